# Optimizing a Trainium2 kernel written in Bass

```python
import math
import jax, jax.numpy as jnp
from jax import lax
import numpy as np

D_MODEL = 1024
BATCH = 2
SEQ = 8192
DEPTH = 2

GRID_W = 64
N_MIXERS = 2
NA_HEADS = 16
NA_HEAD_DIM = D_MODEL // NA_HEADS
NA_ROWS_MAX = 8
NA_KW = 16
NA_QB = 16
NA_KC = NA_QB + NA_KW
NEG_INF = -1e30
RW_HEAD = 64
RW_HEADS = D_MODEL // RW_HEAD
RW_DECAY_LORA = max(32, int(round(1.8 * D_MODEL ** 0.5 / 32)) * 32)
RW_ICLR_LORA = max(32, int(round(1.8 * D_MODEL ** 0.5 / 32)) * 32)
RW_GATE_LORA = max(32, int(round(0.6 * D_MODEL ** 0.8 / 32)) * 32)
RW_GN_EPS = 64e-5
D_FF = 2816
LN_EPS = 1e-5
DN_ALPHA = (2 * DEPTH) ** 0.25
DN_BETA = (8 * DEPTH) ** -0.25

kernel_name = "hybrid_natten_rwkv7_macaron_deepnorm"


def layer_norm(x, g, b):
    xf = x.astype(jnp.float32)
    mu = jnp.mean(xf, axis=-1, keepdims=True)
    var = jnp.mean(jnp.square(xf - mu), axis=-1, keepdims=True)
    return ((xf - mu) * lax.rsqrt(var + LN_EPS) * g + b).astype(x.dtype)


def swiglu(x, w_in, w_out):
    gate, up = jnp.split(x @ w_in, 2, axis=-1)
    return (jax.nn.silu(gate) * up) @ w_out


def neighbourhood_attention(x, w_qkv, b_qkv, rpb, w_o, b_o):
    B, S, D = x.shape
    rows = S // GRID_W
    kr = min(NA_ROWS_MAX, rows)
    ncb = GRID_W // NA_QB
    qkv = (x @ w_qkv + b_qkv).reshape(B, S, 3, NA_HEADS, NA_HEAD_DIM)
    q = qkv[:, :, 0] * (NA_HEAD_DIM ** -0.5)
    k = qkv[:, :, 1]
    v = qkv[:, :, 2]
    q_rows = q.reshape(B, rows, ncb, NA_QB, NA_HEADS, NA_HEAD_DIM).transpose(1, 0, 4, 2, 3, 5)
    k_grid = k.reshape(B, rows, GRID_W, NA_HEADS, NA_HEAD_DIM).transpose(0, 3, 1, 2, 4)
    v_grid = v.reshape(B, rows, GRID_W, NA_HEADS, NA_HEAD_DIM).transpose(0, 3, 1, 2, 4)
    blk = jnp.arange(ncb)
    kcol = jnp.clip(blk * NA_QB - NA_KW // 2, 0, GRID_W - NA_KC)[:, None] + jnp.arange(NA_KC)
    qcol = blk[:, None] * NA_QB + jnp.arange(NA_QB)
    qstart = jnp.clip(qcol - NA_KW // 2, 0, GRID_W - NA_KW)
    col_ok = (kcol[:, None, :] >= qstart[..., None]) & (kcol[:, None, :] < qstart[..., None] + NA_KW)
    dc = jnp.clip(kcol[:, None, :] - qcol[..., None] + NA_KW - 1, 0, 2 * NA_KW - 2)
    rpb_c = rpb[:, :, dc]

    def row_block(args):
        r, q_r = args
        rs = jnp.clip(r - kr // 2, 0, rows - kr)
        k_r = lax.dynamic_slice_in_dim(k_grid, rs, kr, axis=2)[:, :, :, kcol]
        v_r = lax.dynamic_slice_in_dim(v_grid, rs, kr, axis=2)[:, :, :, kcol]
        dr = rs + jnp.arange(kr) - r + NA_ROWS_MAX - 1
        bias = rpb_c[:, dr].transpose(0, 2, 3, 1, 4)
        s = jnp.einsum('bhnqd,bhrncd->bhnqrc', q_r, k_r, preferred_element_type=jnp.float32) + bias
        s = jnp.where(col_ok[:, :, None, :], s, NEG_INF)
        p = jax.nn.softmax(s, axis=(-2, -1))
        return jnp.einsum('bhnqrc,bhrncd->bhnqd', p.astype(v_r.dtype), v_r)

    o = lax.map(row_block, (jnp.arange(rows), q_rows))
    o = o.transpose(1, 0, 3, 4, 2, 5).reshape(B, S, D)
    return o @ w_o + b_o


def rwkv7_bidir(x, mix, w_rkv, w0, w1, w2, a0, a1, a2, g1, g2, k_k, k_a, r_k, lnx_g, lnx_b, w_out):
    B, S, D = x.shape
    H, N = RW_HEADS, RW_HEAD
    f32 = jnp.float32
    zero = jnp.zeros_like(x[:, :1])
    x_prev = jnp.concatenate([zero, x[:, :-1]], axis=1)
    x_next = jnp.concatenate([x[:, 1:], zero], axis=1)
    xx = 0.5 * (x_prev + x_next) - x
    xr, xw, xk, xv, xa, xg = (x + xx * mix[i] for i in range(6))
    r = xr @ w_rkv[0]
    k = xk @ w_rkv[1]
    v = xv @ w_rkv[2]
    wl = w0[:, None, None, :] + jnp.einsum('zbsl,zld->zbsd', jnp.tanh(jnp.einsum('bsd,zdl->zbsl', xw, w1)), w2)
    decay = jnp.exp(-jnp.exp(-jax.nn.softplus(-wl.astype(f32)) - 0.5))
    a = jax.nn.sigmoid(a0[:, None, None, :] + jnp.einsum('zbsl,zld->zbsd', jnp.einsum('bsd,zdl->zbsl', xa, a1), a2)).astype(f32)
    g = jax.nn.sigmoid(xg @ g1) @ g2
    kk = (k * k_k).astype(f32).reshape(B, S, H, N)
    kk = (kk / jnp.maximum(jnp.linalg.norm(kk, axis=-1, keepdims=True), 1e-12)).reshape(B, S, D)
    k_z = k.astype(f32)[None] * (1.0 + (a - 1.0) * k_a)

    def to_scan(t):
        t = jnp.stack([t[0], t[1, :, ::-1]]).astype(f32)
        return t.reshape(2, B, S, H, N).transpose(2, 0, 1, 3, 4)

    rf, vf = r.astype(f32), v.astype(f32)
    xs = (to_scan(jnp.stack([rf, rf])), to_scan(decay), to_scan(k_z), to_scan(jnp.stack([vf, vf])),
          to_scan(jnp.stack([-kk, -kk])), to_scan(kk[None] * a))

    def step(state, inp):
        r_t, w_t, k_t, v_t, a_t, b_t = inp
        sa = jnp.einsum('zbhij,zbhj->zbhi', state, a_t)
        state = state * w_t[..., None, :] + sa[..., None] * b_t[..., None, :] + v_t[..., None] * k_t[..., None, :]
        return state, jnp.einsum('zbhij,zbhj->zbhi', state, r_t)

    _, ys = lax.scan(step, jnp.zeros((2, B, H, N, N), f32), xs)
    y = (ys[:, 0] + ys[::-1, 1]).transpose(1, 0, 2, 3)
    mu = jnp.mean(y, axis=-1, keepdims=True)
    var = jnp.mean(jnp.square(y - mu), axis=-1, keepdims=True)
    y = ((y - mu) * lax.rsqrt(var + RW_GN_EPS)).reshape(B, S, D) * lnx_g + lnx_b
    rk = (rf.reshape(B, S, H, N)[None] * k_z.reshape(2, B, S, H, N) * r_k).sum(axis=-1, keepdims=True).sum(axis=0)
    bonus = (rk * vf.reshape(B, S, H, N)).reshape(B, S, D)
    return ((y + bonus) * g).astype(x.dtype) @ w_out


def setup_inputs(seed: int = 0) -> dict:
    key = jax.random.key(seed)
    ks = iter(jax.random.split(key, 32))
    D, F = D_MODEL, D_FF
    n_a = (DEPTH + 1) // 2
    n_b = DEPTH // 2
    nrm = lambda shape, s: jax.random.normal(next(ks), shape, jnp.float32) * s
    return {
        'x': nrm((BATCH, SEQ, D), 1.0),
        'ffn_w_in': nrm((DEPTH, 2, D, 2 * F), D ** -0.5),
        'ffn_w_out': nrm((DEPTH, 2, F, D), F ** -0.5 * DN_BETA),
        'ln_g': 1.0 + nrm((DEPTH, 3, D), 0.02),
        'ln_b': nrm((DEPTH, 3, D), 0.02),
        'na_w_qkv': nrm((n_a, D, 3 * D), D ** -0.5),
        'na_b_qkv': nrm((n_a, 3 * D), 0.02),
        'na_rpb': nrm((n_a, NA_HEADS, 2 * NA_ROWS_MAX - 1, 2 * NA_KW - 1), 0.02),
        'na_w_o': nrm((n_a, D, D), D ** -0.5 * DN_BETA),
        'na_b_o': nrm((n_a, D), 0.02),
        'rw_mix': jax.random.uniform(next(ks), (n_b, 6, D), jnp.float32),
        'rw_w_rkv': nrm((n_b, 3, D, D), D ** -0.5),
        'rw_w0': jax.random.uniform(next(ks), (n_b, 2, D), jnp.float32, -6.0, 0.0),
        'rw_w1': nrm((n_b, 2, D, RW_DECAY_LORA), D ** -0.5),
        'rw_w2': nrm((n_b, 2, RW_DECAY_LORA, D), 0.5 * RW_DECAY_LORA ** -0.5),
        'rw_a0': nrm((n_b, 2, D), 0.1),
        'rw_a1': nrm((n_b, 2, D, RW_ICLR_LORA), D ** -0.5),
        'rw_a2': nrm((n_b, 2, RW_ICLR_LORA, D), 0.5 * RW_ICLR_LORA ** -0.5),
        'rw_g1': nrm((n_b, D, RW_GATE_LORA), D ** -0.5),
        'rw_g2': nrm((n_b, RW_GATE_LORA, D), RW_GATE_LORA ** -0.5),
        'rw_k_k': 0.85 + nrm((n_b, D), 0.02),
        'rw_k_a': 1.0 + nrm((n_b, D), 0.02),
        'rw_r_k': nrm((n_b, RW_HEADS, RW_HEAD), 0.1),
        'rw_lnx_g': 1.0 + nrm((n_b, D), 0.02),
        'rw_lnx_b': nrm((n_b, D), 0.02),
        'rw_w_out': nrm((n_b, D, D), D ** -0.5 * DN_BETA),
    }


def reference(x, ffn_w_in, ffn_w_out, ln_g, ln_b, na_w_qkv, na_b_qkv, na_rpb, na_w_o, na_b_o,
              rw_mix, rw_w_rkv, rw_w0, rw_w1, rw_w2, rw_a0, rw_a1, rw_a2, rw_g1, rw_g2,
              rw_k_k, rw_k_a, rw_r_k, rw_lnx_g, rw_lnx_b, rw_w_out):
    for i in range(DEPTH):
        j = i // N_MIXERS
        x = layer_norm(DN_ALPHA * x + 0.5 * swiglu(x, ffn_w_in[i, 0], ffn_w_out[i, 0]), ln_g[i, 0], ln_b[i, 0])
        if i % N_MIXERS == 0:
            m = neighbourhood_attention(x, na_w_qkv[j], na_b_qkv[j], na_rpb[j], na_w_o[j], na_b_o[j])
        else:
            m = rwkv7_bidir(x, rw_mix[j], rw_w_rkv[j], rw_w0[j], rw_w1[j], rw_w2[j], rw_a0[j], rw_a1[j], rw_a2[j],
                            rw_g1[j], rw_g2[j], rw_k_k[j], rw_k_a[j], rw_r_k[j], rw_lnx_g[j], rw_lnx_b[j], rw_w_out[j])
        x = layer_norm(DN_ALPHA * x + m, ln_g[i, 1], ln_b[i, 1])
        x = layer_norm(DN_ALPHA * x + 0.5 * swiglu(x, ffn_w_in[i, 1], ffn_w_out[i, 1]), ln_g[i, 2], ln_b[i, 2])
    return x
```

```python
import numpy as np
from concourse.bass_utils import run_bass_kernel_spmd
import numpy as np
from contextlib import ExitStack
import concourse.bass as bass
import concourse.mybir as mybir

F32 = mybir.dt.float32
BF16 = mybir.dt.bfloat16
AF = mybir.ActivationFunctionType
ALU = mybir.AluOpType
AX = mybir.AxisListType


class Sched:
    SEM_ROT = 12000
    N_DMA_SEM = 24

    def __init__(self, nc, es):
        self.nc = nc
        self.es = es
        self.eng = {"pe": nc.tensor, "act": nc.scalar, "dve": nc.vector, "pool": nc.gpsimd, "sp": nc.sync}
        self.sems = []
        self.cur = {}
        for e in self.eng:
            self.cur[e] = [self._new_sem(f"p_{e}"), 0]
        self.waited = {e: {} for e in self.eng}
        self.last_w = {}
        self.readers = {}
        self.dma_sems = [[self._new_sem(f"dma{i}"), 0] for i in range(self.N_DMA_SEM)]
        self.dma_rr = 0
        self.n_inst = {e: 0 for e in self.eng}
        self.n_wait = {e: 0 for e in self.eng}
        self.pending = {e: False for e in self.eng}
        self.clock = {e: {} for e in self.eng}
        self.tclock = {}
        self.n_fused = {}

    def _new_sem(self, name):
        h = self.es.enter_context(self.nc.semaphore(f"{name}_{len(self.sems)}"))
        self.sems.append(h)
        return len(self.sems) - 1

    def _need(self, e, ticket):
        if ticket is None:
            return False
        sid, val = ticket
        ck = self.clock.setdefault(e, {})
        if ck.get(sid, 0) >= val:
            return False
        for e2, c in self.cur.items():
            if c[0] == sid and val > c[1]:
                assert e2 == e, f"{e} waits on pending ticket of {e2}"
                return False
        ck[sid] = val
        snap = self.tclock.get(ticket)
        if snap:
            for s2, v2 in snap.items():
                if ck.get(s2, 0) < v2:
                    ck[s2] = v2
        return True

    def _wait(self, e, ticket):
        if self._need(e, ticket):
            self.eng[e].wait_ge(self.sems[ticket[0]], ticket[1])
            self.n_wait[e] += 1

    def fence(self):
        ts = []
        for e2, c in self.cur.items():
            assert not self.pending[e2]
            if c[1] > 0:
                ts.append((c[0], c[1]))
        for s_ in self.dma_sems:
            if s_[1] > 0:
                ts.append((s_[0], s_[1]))
        self.fence_tickets = ts
        self.fence_done = set()

    def _collect(self, e, reads, writes):
        rd, wr = [], []
        if getattr(self, "fence_tickets", None) and e not in self.fence_done:
            self.fence_done.add(e)
            for t in self.fence_tickets:
                if self._need(e, t):
                    rd.append(t)
        for k in reads:
            t = self.last_w.get(k)
            if self._need(e, t):
                rd.append(t)
        for k in writes:
            t = self.last_w.get(k)
            if self._need(e, t):
                wr.append(t)
            for sid, val in self.readers.get(k, {}).items():
                if self._need(e, (sid, val)):
                    wr.append((sid, val))
        return rd, wr

    def _deps(self, e, reads, writes):
        rd, wr = self._collect(e, reads, writes)
        for t in rd + wr:
            self.eng[e].wait_ge(self.sems[t[0]], t[1])
            self.n_wait[e] += 1

    def _record(self, ticket, reads, writes):
        sid, val = ticket
        for k in reads:
            r = self.readers.setdefault(k, {})
            if r.get(sid, 0) < val:
                r[sid] = val
        for k in writes:
            self.last_w[k] = ticket
            self.readers[k] = {}

    def op(self, e, fn, reads=(), writes=(), signal=True):
        rd, wr = self._collect(e, reads, writes)
        fuse = None
        if e == "pe":
            if wr:
                fuse = wr.pop()
        elif e in ("act", "dve", "pool"):
            if wr:
                fuse = wr.pop()
            elif rd:
                fuse = rd.pop()
        for t in rd + wr:
            self.eng[e].wait_ge(self.sems[t[0]], t[1])
            self.n_wait[e] += 1
        c = self.cur[e]
        if signal and c[1] >= self.SEM_ROT and not self.pending[e]:
            c[0] = self._new_sem(f"p_{e}")
            c[1] = 0
        inst = fn()
        if fuse is not None:
            inst._wait_ge(self.sems[fuse[0]], fuse[1])
            self.n_fused[e] = self.n_fused.get(e, 0) + 1
        self.n_inst[e] += 1
        if signal:
            c[1] += 1
            inst.then_inc(self.sems[c[0]], 1)
            ticket = (c[0], c[1])
            self.pending[e] = False
            self.tclock[ticket] = dict(self.clock.get(e, {}))
        else:
            ticket = (c[0], c[1] + 1)
            self.pending[e] = True
        self._record(ticket, reads, writes)
        return ticket

    def dma(self, q, out, in_, reads=(), writes=(), **kw):
        self._deps(q, reads, writes)
        if q == "pool":
            slot = [self._new_sem("swdma"), 0]
            self.dma_sems.append(slot)
        else:
            slot = self.dma_sems[self.dma_rr]
            self.dma_rr = (self.dma_rr + 1) % self.N_DMA_SEM
        if slot[1] > 0:
            self._wait(q, (slot[0], slot[1]))
        slot[1] += 16
        self.eng[q].dma_start(out=out, in_=in_, **kw).then_inc(self.sems[slot[0]], 16)
        self.n_inst[q] += 1
        ticket = (slot[0], slot[1])
        self.tclock[ticket] = dict(self.clock.get(q, {}))
        self._record(ticket, reads, writes)
        return ticket

    def finish(self, e="sp"):
        for k, t in list(self.last_w.items()):
            self._wait(e, t)
        for s in self.dma_sems:
            if s[1] > 0:
                self._wait(e, (s[0], s[1]))


D = 1024; FF = 2816; NFC = 22
GROUPS = [(0, 4), (4, 4), (8, 4), (12, 4), (16, 4), (20, 2)]
ALPHA = 4 ** 0.25
LN_EPS = 1e-5


class FfnBufs:
    def __init__(self, nc, es, T, with_ffn=True):
        self.T = T
        self.NT = T // 512
        sb = lambda n, s, d: es.enter_context(nc.sbuf_tensor(n, s, d))
        if with_ffn:
            self.alloc_ffn(nc, es)
        self.sq = [sb(f"sq{i}", [128, 8, 512], BF16) for i in range(1)]
        self.yb = [sb(f"yb{i}", [128, 8, 512], BF16) for i in range(1)]
        self.mean_s = sb("mean_s", [128, 512], F32)
        self.m2 = sb("m2", [128, 512], F32)
        self.rstd = sb("rstd", [128, 512], F32)
        self.t1 = [sb(f"t1_{i}", [128, 512], F32) for i in range(2)]
        self.t2 = [sb(f"t2_{i}", [128, 512], F32) for i in range(2)]
        self.ones = sb("ones", [128, 128], BF16)
        self.gcol = sb("gcol", [128, 8], F32)
        self.bcol = sb("bcol", [128, 8], F32)
        self.epsc = sb("epsc", [128, 1], F32)
        self.wslot = 0

    def alloc_ffn(self, nc, es):
        sb = lambda n, s, d: es.enter_context(nc.sbuf_tensor(n, s, d))
        T = self.T
        self.hh = sb("hh", [128, 4, T], BF16)
        self.wi = [sb(f"wi{i}", [128, 2, 8, 512], BF16) for i in range(2)]
        self.wo = [sb(f"wo{i}", [128, 4, 1024], BF16) for i in range(2)]
        self.sg = [sb(f"sg{i}", [128, 512], F32) for i in range(2)]

    def init_consts(self, S, nc):
        S.op("dve", lambda: nc.vector.memset(self.ones[:], 1.0 / 1024), writes=[("ones",)])
        S.op("dve", lambda: nc.vector.memset(self.epsc[:], LN_EPS), writes=[("epsc",)])


def emit_ffn_ln(S, nc, B, xT, xb, ps, w_in, w_out, ln_g, ln_b, tag, ntiles=None):
    NT = B.NT if ntiles is None else ntiles
    tsl = lambda tt: slice(tt * 512, (tt + 1) * 512)
    w_in_v = w_in.rearrange("(c p) (u f) -> p c u f", p=128, u=2)
    w_out_v = w_out.rearrange("(j p) d -> p j d", p=128)

    def load_group(g):
        f0, n = GROUPS[g]
        slot = B.wslot; B.wslot ^= 1
        for u in range(2):
            S.dma("pool", B.wi[slot][:, u, :, 0:n * 128], w_in_v[:, :, u, f0 * 128:(f0 + n) * 128],
                  writes=[("wi", slot, u)])
        S.dma("pool", B.wo[slot][:, 0:n, :], w_out_v[:, f0:f0 + n, :], writes=[("wo", slot)])
        return slot

    slots = {0: load_group(0)}
    for tt in range(NT):
        for c in range(8):
            S.op("act", lambda: nc.scalar.activation(out=xT[:, c, tsl(tt)], in_=xT[:, c, tsl(tt)], func=AF.Identity, scale=float(ALPHA)),
                 reads=[("xT", c, tt)], writes=[("xT", c, tt)])
    pa = 0
    pb = 0
    for g in range(len(GROUPS)):
        f0, n = GROUPS[g]
        slot = slots[g]
        if g + 1 < len(GROUPS):
            slots[g + 1] = load_group(g + 1)
        for tt in range(NT):
            for j in range(n):
                bg = 2 * pa; bu = 2 * pa + 1; pa ^= 1
                for (bank, u) in ((bg, 0), (bu, 1)):
                    for c in range(8):
                        S.op("pe", lambda: nc.tensor.matmul(ps[:, bank, :], B.wi[slot][:, u, c, j * 128:(j + 1) * 128], xb[:, c, tsl(tt)], start=(c == 0), stop=(c == 7)),
                             reads=[("wi", slot, u), ("xb", c, tt)], writes=[("ps", bank)], signal=(c == 7))
                sgi = (tt * n + j) % 2
                S.op("act", lambda: nc.scalar.activation(out=B.sg[sgi][:], in_=ps[:, bg, :], func=AF.Silu),
                     reads=[("ps", bg)], writes=[("sg", sgi)])
                S.op("dve", lambda: nc.vector.tensor_tensor(out=B.hh[:, j, tsl(tt)], in0=ps[:, bu, :], in1=B.sg[sgi][:], op=ALU.mult),
                     reads=[("ps", bu), ("sg", sgi)], writes=[("hh", j, tt)])
        for tt in range(NT):
            for dc in range(8):
                bank = 4 + pb; pb ^= 1
                for j in range(n):
                    S.op("pe", lambda: nc.tensor.matmul(ps[:, bank, :], B.wo[slot][:, j, dc * 128:(dc + 1) * 128], B.hh[:, j, tsl(tt)], start=(j == 0), stop=(j == n - 1)),
                         reads=[("wo", slot), ("hh", j, tt)], writes=[("ps", bank)], signal=(j == n - 1))
                S.op("dve", lambda: nc.vector.scalar_tensor_tensor(out=xT[:, dc, tsl(tt)], in0=ps[:, bank, :], scalar=0.5, in1=xT[:, dc, tsl(tt)], op0=ALU.mult, op1=ALU.add),
                     reads=[("ps", bank), ("xT", dc, tt)], writes=[("xT", dc, tt)])
    emit_ln(S, nc, B, xT, xb, ps, ln_g, ln_b, NT)


def emit_ln(S, nc, B, xT, xb, ps, ln_g, ln_b, NT, xb_tiles=None):
    tsl = lambda tt: slice(tt * 512, (tt + 1) * 512)
    S.dma("sp", B.gcol[:], ln_g.rearrange("(c p) -> p c", p=128), writes=[("gcol",)], allow_slow_non_contiguous=True)
    S.dma("sp", B.bcol[:], ln_b.rearrange("(c p) -> p c", p=128), writes=[("bcol",)], allow_slow_non_contiguous=True)
    for tt in range(NT):
        i2 = 0
        for c in range(8):
            S.op("act", lambda: nc.scalar.activation(out=B.sq[i2][:, c, :], in_=xT[:, c, tsl(tt)], func=AF.Square),
                 reads=[("xT", c, tt)], writes=[("sq", i2, c)])
            S.op("pool", lambda: nc.gpsimd.tensor_copy(out=B.yb[i2][:, c, :], in_=xT[:, c, tsl(tt)]),
                 reads=[("xT", c, tt)], writes=[("yb", i2, c)])
        for c in range(8):
            S.op("pe", lambda: nc.tensor.matmul(ps[:, 6, :], B.ones[:], B.yb[i2][:, c, :], start=(c == 0), stop=(c == 7)),
                 reads=[("ones",), ("yb", i2, c)], writes=[("ps", 6)], signal=(c == 7))
        for c in range(8):
            S.op("pe", lambda: nc.tensor.matmul(ps[:, 7, :], B.ones[:], B.sq[i2][:, c, :], start=(c == 0), stop=(c == 7)),
                 reads=[("ones",), ("sq", i2, c)], writes=[("ps", 7)], signal=(c == 7))
        S.op("act", lambda: nc.scalar.copy(out=B.mean_s[:], in_=ps[:, 6, :]), reads=[("ps", 6)], writes=[("mean_s",)])
        S.op("dve", lambda: nc.vector.tensor_tensor(out=B.m2[:], in0=ps[:, 6, :], in1=B.mean_s[:], op=ALU.mult),
             reads=[("ps", 6), ("mean_s",)], writes=[("m2",)])
        S.op("dve", lambda: nc.vector.tensor_tensor(out=B.m2[:], in0=ps[:, 7, :], in1=B.m2[:], op=ALU.subtract),
             reads=[("ps", 7), ("m2",)], writes=[("m2",)])
        S.op("act", lambda: nc.scalar.activation(out=B.m2[:], in_=B.m2[:], func=AF.Sqrt, bias=B.epsc[:], scale=1.0),
             reads=[("m2",), ("epsc",)], writes=[("m2",)])
        S.op("dve", lambda: nc.vector.reciprocal(out=B.rstd[:], in_=B.m2[:]), reads=[("m2",)], writes=[("rstd",)])
        for c in range(8):
            k = c % 2
            S.op("dve", lambda: nc.vector.tensor_tensor(out=B.t1[k][:], in0=xT[:, c, tsl(tt)], in1=ps[:, 6, :], op=ALU.subtract),
                 reads=[("xT", c, tt), ("ps", 6)], writes=[("t1", k)])
            S.op("pool", lambda: nc.gpsimd.tensor_tensor(out=B.t2[k][:], in0=B.t1[k][:], in1=B.rstd[:], op=ALU.mult),
                 reads=[("t1", k), ("rstd",)], writes=[("t2", k)])
            S.op("act", lambda: nc.scalar.activation(out=xT[:, c, tsl(tt)], in_=B.t2[k][:], func=AF.Identity, bias=B.bcol[:, c:c + 1], scale=B.gcol[:, c:c + 1]),
                 reads=[("t2", k), ("gcol",), ("bcol",)], writes=[("xT", c, tt)])
            S.op("act", lambda: nc.scalar.activation(out=xb[:, c, tsl(tt)], in_=B.t2[k][:], func=AF.Identity, bias=B.bcol[:, c:c + 1], scale=B.gcol[:, c:c + 1]),
                 reads=[("t2", k), ("gcol",), ("bcol",)], writes=[("xb", c, tt)])


def build_ffn_prog(T, n_ffn):
    nc = bass.Bass("TRN2", target_bir_lowering=False)
    xT_d = nc.dram_tensor("xT", [D, T], F32, kind="ExternalInput").ap()
    w_in = nc.dram_tensor("w_in", [n_ffn, D, 2 * FF], F32, kind="ExternalInput").ap()
    w_out = nc.dram_tensor("w_out", [n_ffn, FF, D], F32, kind="ExternalInput").ap()
    lng = nc.dram_tensor("lng", [n_ffn, D], F32, kind="ExternalInput").ap()
    lnb = nc.dram_tensor("lnb", [n_ffn, D], F32, kind="ExternalInput").ap()
    yT_d = nc.dram_tensor("yT", [D, T], F32, kind="ExternalOutput").ap()
    with ExitStack() as es:
        S = Sched(nc, es)
        xT = es.enter_context(nc.sbuf_tensor("xTs", [128, 8, T], F32))
        xb = es.enter_context(nc.sbuf_tensor("xbs", [128, 8, T], BF16))
        ps = es.enter_context(nc.psum_tensor("ps", [128, 8, 512], F32))
        B = FfnBufs(nc, es, T)
        NT = T // 512
        B.init_consts(S, nc)
        xv = xT_d.rearrange("(c p) t -> p c t", p=128)
        yv = yT_d.rearrange("(c p) t -> p c t", p=128)
        for tt in range(NT):
            S.dma("sp", xT[:, :, tt * 512:(tt + 1) * 512], xv[:, :, tt * 512:(tt + 1) * 512],
                  writes=[("xT", c, tt) for c in range(8)])
            for c in range(8):
                S.op("act", lambda: nc.scalar.copy(out=xb[:, c, tt * 512:(tt + 1) * 512], in_=xT[:, c, tt * 512:(tt + 1) * 512]),
                     reads=[("xT", c, tt)], writes=[("xb", c, tt)])
        for i in range(n_ffn):
            emit_ffn_ln(S, nc, B, xT, xb, ps, w_in[i], w_out[i], lng[i], lnb[i], f"f{i}")
        for tt in range(NT):
            S.dma("sp", yv[:, :, tt * 512:(tt + 1) * 512], xT[:, :, tt * 512:(tt + 1) * 512],
                  reads=[("xT", c, tt) for c in range(8)])
        S.finish("sp")
    return nc


NH = 16; HD = 64; NHP = 8


def na_pat(r, NR):
    if r < 4:
        return 1 + r
    if r >= NR - 3:
        return 5 + (r - (NR - 3))
    return 0


def na_halo_rows(r0, NR, rows):
    G = []
    for L in range(NR + 8):
        g = r0 - 4 + L
        if g < 0:
            g = g + 8
        elif g >= rows:
            g = rows - 8 + (g - rows)
        g = min(max(g, 0), rows - 1)
        G.append(g)
    return G


def na_tables(rpb, r0, NR, rows):
    G = np.array(na_halo_rows(r0, NR, rows))
    reps = {0: min(NR // 2, NR - 4)}
    for r in range(NR):
        p = na_pat(r, NR)
        if p != 0:
            reps[p] = r
    tab = np.empty((NHP, 8, 128, 4, 128), np.float32)
    qc = np.arange(64)
    qstart = np.clip(qc - 8, 0, 48)
    for p in range(8):
        r = reps.get(p, reps[0])
        R = r0 + r
        rs = min(max(R - 4, 0), rows - 8)
        kL = r + np.arange(8)
        kG = G[kL]
        row_ok = (kG >= rs) & (kG < rs + 8)
        dr = np.clip(kG - R + 7, 0, 14)
        kcol = np.arange(64)
        col_ok = (kcol[None, :] >= qstart[:, None]) & (kcol[None, :] < qstart[:, None] + 16)
        dc = np.clip(kcol[None, :] - qc[:, None] + 15, 0, 30)
        b = rpb[:, dr][:, :, dc]
        b = np.transpose(b, (0, 1, 3, 2))
        ok = row_ok[:, None, None] & np.transpose(col_ok)[None, :, :]
        b = np.where(ok[None], b, np.float32(-1e30)).astype(np.float32)
        b = b.reshape(NHP, 2, 512, 64)
        b = b.reshape(NHP, 2, 4, 128, 64)
        tab[:, p] = np.transpose(b, (0, 3, 2, 1, 4)).reshape(NHP, 128, 4, 128)
    return tab


def emit_na(S, nc, es, ps, xh_d, x_own_d, w_qkv, b_qkv, tab_d, w_o, b_o, ln_g, ln_b, NR, yT_d):
    T = NR * 64; TH = (NR + 8) * 64
    NT = T // 512
    NTH = TH // 512
    sb = lambda es_, n, s, d: es_.enter_context(nc.sbuf_tensor(n, s, d))
    oT = sb(es, "oT", [128, 8, T], BF16)
    tsl = lambda tt: slice(tt * 512, (tt + 1) * 512)
    with ExitStack() as es2:
        xb = sb(es2, "na_xb", [128, 8, TH], BF16)
        tabs = sb(es2, "na_tab", [128, 8, 4, 128], F32)
        KT = [sb(es2, f"na_KT{i}", [128, TH], BF16) for i in range(2)]
        Ve4 = sb(es2, "na_Ve4", [128, TH // 128, 512], BF16)
        Vo4 = sb(es2, "na_Vo4", [128, TH // 128, 512], BF16)
        wv4 = sb(es2, "na_wv4", [128, 8, 512], BF16)
        QBD = [sb(es2, f"na_Q{i}", [128, NR, 2, 64], BF16) for i in range(2)]
        wq = [sb(es2, f"na_wq{i}", [128, 2, 8, 128], BF16) for i in range(2)]
        sbt = [sb(es2, f"na_sb{i}", [128, 512], F32) for i in range(4)]
        PT = [sb(es2, f"na_PT{i}", [128, 512], BF16) for i in range(4)]
        rc = [sb(es2, f"na_rc{i}", [128, 128], F32) for i in range(4)]
        bcols = sb(es2, "na_bc", [128, 24], F32)
        bvrow = sb(es2, "na_bvrow", [1, 1024], BF16)
        bvb = sb(es2, "na_bvb", [128, 1024], F32)
        ones_r = sb(es2, "na_ones_r", [1, 128], BF16)
        ones_k = sb(es2, "na_ones_k", [128, 128], BF16)

        S.op("dve", lambda: nc.vector.memset(ones_r[:], 1.0), writes=[("ones_r",)])
        S.op("dve", lambda: nc.vector.memset(ones_k[:], 1.0), writes=[("ones_k",)])
        for i in range(2):
            S.op("pool", lambda: nc.gpsimd.memset(QBD[i][:], 0.0), writes=[("QBD", i)])
        S.dma("sp", bcols[:], b_qkv.rearrange("(j p) -> p j", p=128), writes=[("bcols",)], allow_slow_non_contiguous=True)
        S.dma("pool", bvrow[:], b_qkv[2048:3072].rearrange("(o n) -> o n", o=1), writes=[("bvrow",)])
        xhv = xh_d.rearrange("(c p) t -> p c t", p=128)
        for tt in range(NTH):
            S.dma("pool", xb[:, :, tsl(tt)], xhv[:, :, tsl(tt)], writes=[("nxb", tt)])
        for h2 in range(2):
            S.op("pe", lambda: nc.tensor.matmul(ps[:, 6, :], ones_r[0:1, :], bvrow[0:1, h2 * 512:(h2 + 1) * 512], start=True, stop=True),
                 reads=[("ones_r",), ("bvrow",)], writes=[("ps", 6)])
            S.op("act", lambda: nc.scalar.copy(out=bvb[:, h2 * 512:(h2 + 1) * 512], in_=ps[:, 6, :]), reads=[("ps", 6)], writes=[("bvb", h2)])
        wv = w_qkv.rearrange("(c p) n -> p c n", p=128)
        tabv = tab_d

        def load_w(hp, slot):
            for k in range(2):
                S.dma("pool", wq[slot][:, k, :, :], wv[:, :, k * 1024 + hp * 128:k * 1024 + (hp + 1) * 128], writes=[("wq", slot, k)])

        load_w(0, 0)
        pa = 0
        for hp in range(NHP):
            slot = hp % 2
            if hp + 1 < NHP:
                load_w(hp + 1, 1 - slot)
            S.dma("sp", tabs[:].rearrange("p a k n -> p a (k n)"), tabv[hp].rearrange("a p k n -> p a (k n)"), writes=[("tabs",)])
            for tt in range(NTH):
                bank = pa; pa = (pa + 1) % 4
                for c in range(8):
                    S.op("pe", lambda: nc.tensor.matmul(ps[:, bank, :], wq[slot][:, 1, c, :], xb[:, c, tsl(tt)], start=(c == 0), stop=(c == 7)),
                         reads=[("wq", slot, 1), ("nxb", tt)], writes=[("ps", bank)], signal=(c == 7))
                S.op("act", lambda: nc.scalar.activation(out=KT[slot][:, tsl(tt)], in_=ps[:, bank, :], func=AF.Identity, bias=bcols[:, 8 + hp:9 + hp], scale=1.0),
                     reads=[("ps", bank), ("bcols",)], writes=[("KT", slot, tt)])
            for tt in range(NT):
                bank = pa; pa = (pa + 1) % 4
                for c in range(8):
                    S.op("pe", lambda: nc.tensor.matmul(ps[:, bank, :], wq[slot][:, 0, c, :], xb[:, c, 256 + tt * 512:256 + (tt + 1) * 512], start=(c == 0), stop=(c == 7)),
                         reads=[("wq", slot, 0)] + [("nxb", t2) for t2 in range(NTH)], writes=[("ps", bank)], signal=(c == 7))
                for hd in range(2):
                    pr = slice(hd * 64, (hd + 1) * 64)
                    S.op("dve", lambda: nc.vector.tensor_scalar(out=QBD[slot][pr, tt * 8:(tt + 1) * 8, hd, :], in0=ps[pr, bank, :].rearrange("p (r q) -> p r q", q=64),
                                                                scalar1=bcols[pr, hp:hp + 1], scalar2=0.125, op0=ALU.add, op1=ALU.mult),
                         reads=[("ps", bank), ("bcols",)], writes=[("QBD", slot)])
            nch = TH // 128
            if hp % 4 == 0:
                hg = hp // 4
                S.dma("pool", wv4[:], wv[:, :, 2048 + hg * 512:2048 + (hg + 1) * 512], writes=[("wv4",)])
                for (Vx, off, cnt, nm) in ((Ve4, 0, nch, "Ve"), (Vo4, 64, nch - 1, "Vo")):
                    for j in range(cnt):
                        bank = pa; pa = (pa + 1) % 4
                        for c in range(8):
                            S.op("pe", lambda: nc.tensor.matmul(ps[:, bank, :], xb[:, c, off + j * 128:off + (j + 1) * 128], wv4[:, c, :], start=(c == 0), stop=(c == 7)),
                                 reads=[("wv4",)] + [("nxb", t2) for t2 in range(NTH)], writes=[("ps", bank)], signal=(c == 7))
                        S.op("dve", lambda: nc.vector.tensor_tensor(out=Vx[:, j, :], in0=ps[:, bank, :], in1=bvb[:, hg * 512:(hg + 1) * 512], op=ALU.add),
                             reads=[("ps", bank), ("bvb", hg)], writes=[(nm, j)])
            def stage1(r):
                nonlocal pa
                pat = na_pat(r, NR)
                i2 = r % 4
                bankS = pa; pa = (pa + 1) % 4
                tok0 = r * 64
                for kc in range(4):
                    S.op("pe", lambda: nc.tensor.matmul(ps[:, bankS, kc * 128:(kc + 1) * 128], KT[slot][:, tok0 + kc * 128:tok0 + (kc + 1) * 128], QBD[slot][:, r, :, :].rearrange("p a q -> p (a q)"), start=True, stop=True),
                         reads=[("KT", slot, t2) for t2 in range(NTH)] + [("QBD", slot)], writes=[("ps", bankS)], signal=(kc == 3))
                S.op("dve", lambda: nc.vector.tensor_tensor(out=sbt[i2][:], in0=ps[:, bankS, :], in1=tabs[:, pat, :, :].rearrange("p k n -> p (k n)"), op=ALU.add),
                     reads=[("ps", bankS), ("tabs",)], writes=[("sbt", i2)])
                S.op("act", lambda: nc.scalar.activation(out=PT[i2][:], in_=sbt[i2][:], func=AF.Exp),
                     reads=[("sbt", i2)], writes=[("PT", i2)])

            def stage2(r):
                i2 = r % 4
                bankO = 4 + i2
                if r % 2 == 0:
                    Vx, j0, nm = Ve4, r // 2, "Ve"
                else:
                    Vx, j0, nm = Vo4, (r - 1) // 2, "Vo"
                h4 = hp % 4
                for kc in range(4):
                    S.op("pe", lambda: nc.tensor.matmul(ps[:, bankO, 0:128], Vx[:, j0 + kc, h4 * 128:(h4 + 1) * 128], PT[i2][:, kc * 128:(kc + 1) * 128], start=(kc == 0), stop=(kc == 3)),
                         reads=[(nm, j0 + kc), ("PT", i2)], writes=[("ps", bankO)], signal=False)
                for kc in range(4):
                    S.op("pe", lambda: nc.tensor.matmul(ps[:, bankO, 128:256], ones_k[:], PT[i2][:, kc * 128:(kc + 1) * 128], start=(kc == 0), stop=(kc == 3)),
                         reads=[("ones_k",), ("PT", i2)], writes=[("ps", bankO)], signal=(kc == 3))
                S.op("act", lambda: nc.scalar.activation(out=rc[i2][:], in_=ps[:, bankO, 128:256], func=AF.Ln),
                     reads=[("ps", bankO)], writes=[("rc", i2)])
                S.op("act", lambda: nc.scalar.activation(out=rc[i2][:], in_=rc[i2][:], func=AF.Exp, scale=-1.0),
                     reads=[("rc", i2)], writes=[("rc", i2)])
                for hd in range(2):
                    pr = slice(hd * 64, (hd + 1) * 64)
                    S.op("dve", lambda: nc.vector.tensor_tensor(out=oT[pr, hp, r * 64:(r + 1) * 64], in0=ps[pr, bankO, hd * 64:(hd + 1) * 64], in1=rc[i2][pr, hd * 64:(hd + 1) * 64], op=ALU.mult),
                         reads=[("ps", bankO), ("rc", i2)], writes=[("oT", hp, r // 8)])

            stage1(0)
            if NR > 1:
                stage1(1)
            for r in range(NR):
                if r + 2 < NR:
                    stage1(r + 2)
                stage2(r)
    S.fence()
    with ExitStack() as es3:
        xT = sb(es3, "xTs", [128, 8, T], F32)
        xbo = sb(es3, "xbs", [128, 8, T], BF16)
        LB = FfnBufs(nc, es3, T, with_ffn=False)
        LB.init_consts(S, nc)
        wo = sb(es3, "na_wo", [128, 8, 1024], BF16)
        bo = sb(es3, "na_bo", [128, 8], F32)
        S.dma("pool", wo[:], w_o.rearrange("(h p) n -> p h n", p=128), writes=[("nwo",)])
        S.dma("sp", bo[:], b_o.rearrange("(c p) -> p c", p=128), writes=[("nbo",)], allow_slow_non_contiguous=True)
        xov = x_own_d.rearrange("(c p) t -> p c t", p=128)
        pb = 0
        for tt in range(NT):
            S.dma("sp", xT[:, :, tsl(tt)], xov[:, :, tsl(tt)], writes=[("xT", c, tt) for c in range(8)])
            for dc in range(8):
                bank = pb; pb = (pb + 1) % 4
                for hp in range(8):
                    S.op("pe", lambda: nc.tensor.matmul(ps[:, bank, :], wo[:, hp, dc * 128:(dc + 1) * 128], oT[:, hp, tsl(tt)], start=(hp == 0), stop=(hp == 7)),
                         reads=[("nwo",), ("oT", hp, tt)], writes=[("ps", bank)], signal=(hp == 7))
                S.op("pool", lambda: nc.gpsimd.tensor_scalar(out=xT[:, dc, tsl(tt)], in0=xT[:, dc, tsl(tt)], scalar1=float(ALPHA), scalar2=bo[:, dc:dc + 1], op0=ALU.mult, op1=ALU.add),
                     reads=[("xT", dc, tt), ("nbo",)], writes=[("xT", dc, tt)])
                S.op("dve", lambda: nc.vector.tensor_tensor(out=xT[:, dc, tsl(tt)], in0=ps[:, bank, :], in1=xT[:, dc, tsl(tt)], op=ALU.add),
                     reads=[("ps", bank), ("xT", dc, tt)], writes=[("xT", dc, tt)])
        emit_ln(S, nc, LB, xT, xbo, ps, ln_g, ln_b, NT)
        yv = yT_d.rearrange("(c p) t -> p c t", p=128)
        for tt in range(NT):
            S.dma("sp", yv[:, :, tsl(tt)], xT[:, :, tsl(tt)], reads=[("xT", c, tt) for c in range(8)])
        S.finish("sp")


def build_na_prog(NR):
    T = NR * 64; TH = (NR + 8) * 64
    nc = bass.Bass("TRN2", target_bir_lowering=False)
    dt = lambda n, s: nc.dram_tensor(n, s, F32, kind="ExternalInput").ap()
    xh = dt("xh", [D, TH]); xo = dt("xo", [D, T])
    w_qkv = dt("w_qkv", [D, 3 * D]); b_qkv = dt("b_qkv", [3 * D]); tab = dt("tab", [NHP, 8, 128, 4, 128])
    w_o = dt("w_o", [D, D]); b_o = dt("b_o", [D]); lng = dt("lng", [D]); lnb = dt("lnb", [D])
    yT_d = nc.dram_tensor("yT", [D, T], F32, kind="ExternalOutput").ap()
    with ExitStack() as es:
        S = Sched(nc, es)
        ps = es.enter_context(nc.psum_tensor("ps", [128, 8, 512], F32))
        emit_na(S, nc, es, ps, xh, xo, w_qkv, b_qkv, tab, w_o, b_o, lng, lnb, NR, yT_d)
    return nc


C0 = float(np.exp(-0.5))
CH = 64


def rw_consts():
    s = np.arange(128)[:, None]; t = np.arange(128)[None, :]
    same = (s // 64) == (t // 64)
    Sm = (same & (t > s)).astype(np.float32)
    Im = (same & (t >= s)).astype(np.float32)
    maskSI = np.concatenate([Sm, Im, Sm, Im], axis=1)
    maskTS = (same & (t < s)).astype(np.float32)
    ident = np.eye(128, dtype=np.float32)
    blk = same.astype(np.float32)
    return np.concatenate([maskSI, maskTS, ident, blk], axis=1)


STOP = ""


def emit_rw1(S, nc, es, ps, d, SL, TS=256, SD=BF16):
    NTI = SL // TS
    NCK = TS // CH
    NPR = TS // 128
    sb = lambda n, s, dt: es.enter_context(nc.sbuf_tensor(n, s, dt))
    cst = sb("rw_cst", [128, 896], F32)
    S.dma("sp", cst[:], d["consts"], writes=[("cst",)])
    maskSI = cst[:, 0:512]; maskTS = cst[:, 512:640]; identf = cst[:, 640:768]; blkf = cst[:, 768:896]
    ident_s = identf
    if SD != F32:
        ident_sd = sb("rw_identsd", [128, 128], SD)
        S.op("dve", lambda: nc.vector.tensor_copy(out=ident_sd[:], in_=identf), reads=[("cst",)], writes=[("identsd",)])
        ident_s = ident_sd[:]
    ones64 = sb("rw_ones64", [128, CH], F32)
    S.op("dve", lambda: nc.vector.memset(ones64[:], 1.0), writes=[("ones64",)])
    mixc = sb("rw_mixc", [128, 6, 8], F32)
    S.dma("sp", mixc[:], d["mix"].rearrange("i (c p) -> p i c", p=128), writes=[("mixc",)], allow_slow_non_contiguous=True)
    cols = sb("rw_cols", [128, 5, 4], F32)
    for i, nm in enumerate(["w0", "a0", "k_k", "k_a", "r_k"]):
        S.dma("sp", cols[:, i, :], d[nm].rearrange("(c p) -> p c", p=128), writes=[("cols", i)], allow_slow_non_contiguous=True)
    W3 = sb("rw_W3", [128, 3, 8, 512], BF16)
    for i, nm in enumerate(["w_r", "w_k", "w_v"]):
        S.dma("pool", W3[:, i, :, :], d[nm].rearrange("(c p) n -> p c n", p=128), writes=[("W3", i)])
    w1b = sb("rw_w1b", [128, 8, 64], BF16); a1b = sb("rw_a1b", [128, 8, 64], BF16); g1b = sb("rw_g1b", [128, 8, 160], BF16)
    S.dma("pool", w1b[:], d["w1"].rearrange("(c p) n -> p c n", p=128), writes=[("w1b",)])
    S.dma("pool", a1b[:], d["a1"].rearrange("(c p) n -> p c n", p=128), writes=[("a1b",)])
    S.dma("pool", g1b[:], d["g1"].rearrange("(c p) n -> p c n", p=128), writes=[("g1b",)])
    w2b = sb("rw_w2b", [64, 512], BF16); a2b = sb("rw_a2b", [64, 512], BF16)
    g2a = sb("rw_g2a", [128, 512], BF16); g2b = sb("rw_g2b", [128, 512], BF16)
    S.dma("pool", w2b[:], d["w2"], writes=[("w2b",)])
    S.dma("pool", a2b[:], d["a2"], writes=[("a2b",)])
    S.dma("pool", g2a[:], d["g2"][0:128, :], writes=[("g2a",)])
    S.op("dve", lambda: nc.vector.memset(g2b[:], 0.0), writes=[("g2b",)])
    S.dma("pool", g2b[0:32, :], d["g2"][128:160, :], writes=[("g2b",)])
    xt = [sb("rw_xt0", [128, 8, TS + 2], F32)] * 2
    xs = sb("rw_xs", [128, TS], F32)
    xx = sb("rw_xx", [128, TS], F32)
    xm = sb("rw_xm", [128, 6, 8, TS], BF16)
    hw = sb("rw_hw", [64, TS], BF16); ha = sb("rw_ha", [64, TS], BF16)
    hga = sb("rw_hga", [128, TS], BF16); hgb = sb("rw_hgb", [128, TS], BF16)
    S.op("dve", lambda: nc.vector.memset(hgb[:], 0.0), writes=[("hgb",)])
    ft = lambda n: [sb(f"rw_{n}{i}", [128, TS], F32) for i in range(2)]
    def ft1(n):
        t = sb(f"rw_{n}", [128, TS], F32)
        return [t, t]
    rT = ft("rT"); kT = ft("kT"); vT = ft("vT"); gT = ft1("gT"); sg = ft("sg"); asg = ft("asg")
    kq = ft1("kq"); kq2 = ft1("kq2"); rn = ft1("rn"); kk = ft("kk"); t1 = ft1("t1"); kz = ft("kz"); prod = ft1("prod")
    cs = ft1("cs"); E1 = [[sb(f"rw_E1_{q}_{i}", [128, TS], F32) for i in range(4)] for q in range(2)]; E2 = ft1("E2"); E3 = ft1("E3"); dd = ft1("dd"); bb = ft1("bb")
    af = ft("af")
    BKf = [sb(f"rw_BKf{i}", [128, 2, TS], F32) for i in range(2)]
    Hat = [sb(f"rw_Hat{i}", [128, 2, TS], F32) for i in range(2)]
    RA = [[sb(f"rw_RA{q}_{i}", [128, 2, TS], SD) for i in range(4)] for q in range(2)]
    RAf1 = [[sb(f"rw_RAf{q}_{i}", [128, TS], F32) for i in range(4)] for q in range(2)]
    LBt = [[sb(f"rw_LB{q}_{i}", [128, 2, TS], SD) for i in range(4)] for q in range(2)]
    TM = [[[sb(f"rw_TM{q}_{c}_{p}", [128, 4, 128], SD) for p in range(NPR)] for c in range(4)] for q in range(2)]
    NU = 2 * 2 * NPR
    AM = [sb(f"rw_AM{u}", [128, 512], SD) for u in range(NU)]
    Mk = [[sb(f"rw_M{u}_{i}", [128, 128], SD) for i in range(2)] for u in range(NU)]
    Nk = [[sb(f"rw_N{u}_{i}", [128, 128], SD) for i in range(2)] for u in range(NU)]
    Xs = [sb(f"rw_Xs{u}", [128, 128], SD) for u in range(NU)]
    ATM = [sb(f"rw_ATM{u}", [128, 64], SD) for u in range(NU)]
    Ws = [sb(f"rw_Ws{u}", [128, 64], SD) for u in range(NU)]
    UV = [sb(f"rw_UV{u}", [128, 64], SD) for u in range(NU)]
    GT = [sb(f"rw_GT{c}", [128, NCK, 64], F32) for c in range(4)]
    Hf = [sb(f"rw_Hf{c}", [128, NCK, 64], F32) for c in range(4)]
    RH = [sb(f"rw_RH{c}", [128, TS], F32) for c in range(4)]
    YV = [[sb(f"rw_YV{c}_{p}", [128, 128], F32) for p in range(NPR)] for c in range(4)]
    ST = [sb(f"rw_ST{c}", [128, 64], F32) for c in range(4)]
    yt = [[sb(f"rw_yt{c}_{p}", [128, 128], F32) for p in range(NPR)] for c in range(4)]
    for c in range(4):
        S.op("dve", lambda: nc.vector.memset(ST[c][:], 0.0), writes=[("ST", c, 0), ("ST", c, 1)])

    xv = d["xT"].rearrange("(c p) t -> p c t", p=128)
    pr = [0]

    def bank():
        b = pr[0]; pr[0] = (pr[0] + 1) % 8
        return b

    def gen_abc(ti):
        par = ti % 2
        t0 = ti * TS
        xs_ = 0
        X = xt[xs_]
        lo = t0 - 1 if ti > 0 else t0
        hi = t0 + TS + 1 if ti < NTI - 1 else t0 + TS
        if ti == 0:
            S.op("pool", lambda: nc.gpsimd.memset(X[:, :, 0:1], 0.0), writes=[("xt", xs_)])
        if ti == NTI - 1:
            S.op("pool", lambda: nc.gpsimd.memset(X[:, :, TS + 1:TS + 2], 0.0), writes=[("xt", xs_)])
        S.dma("sp", X[:, :, (lo - t0 + 1):(hi - t0 + 1)], xv[:, :, lo:hi], writes=[("xt", xs_)])
        for c in range(8):
            S.op("pool", lambda: nc.gpsimd.tensor_tensor(out=xs[:], in0=X[:, c, 0:TS], in1=X[:, c, 2:TS + 2], op=ALU.add),
                 reads=[("xt", xs_)], writes=[("xs",)])
            S.op("dve", lambda: nc.vector.scalar_tensor_tensor(out=xx[:], in0=xs[:], scalar=0.5, in1=X[:, c, 1:TS + 1], op0=ALU.mult, op1=ALU.subtract),
                 reads=[("xs",), ("xt", xs_)], writes=[("xx",)])
            for i in range(6):
                S.op("dve", lambda: nc.vector.scalar_tensor_tensor(out=xm[:, i, c, :], in0=xx[:], scalar=mixc[:, i, c:c + 1], in1=X[:, c, 1:TS + 1], op0=ALU.mult, op1=ALU.add),
                     reads=[("xx",), ("xt", xs_), ("mixc",)], writes=[("xm", i, c)])
            yield
        yield
        def proj(out_ap, lhs_fn, mi, keys, M=128):
            for c in range(8):
                S.op("pe", lambda: nc.tensor.matmul(out_ap, lhs_fn(c), xm[:, mi, c, :], start=(c == 0), stop=(c == 7)),
                     reads=keys + [("xm", mi, c)], writes=[("ps", bk)], signal=(c == 7))
        bk = bank()
        proj(ps[0:64, bk, 0:TS], lambda c: w1b[:, c, :], 1, [("w1b",)])
        S.op("act", lambda: nc.scalar.activation(out=hw[:], in_=ps[0:64, bk, 0:TS], func=AF.Tanh), reads=[("ps", bk)], writes=[("hw",)])
        bk = bank()
        proj(ps[0:64, bk, 0:TS], lambda c: a1b[:, c, :], 4, [("a1b",)])
        S.op("act", lambda: nc.scalar.copy(out=ha[:], in_=ps[0:64, bk, 0:TS]), reads=[("ps", bk)], writes=[("ha",)])
        bk = bank()
        proj(ps[:, bk, 0:TS], lambda c: g1b[:, c, 0:128], 5, [("g1b",)])
        S.op("act", lambda: nc.scalar.activation(out=hga[:], in_=ps[:, bk, 0:TS], func=AF.Sigmoid), reads=[("ps", bk)], writes=[("hga",)])
        bk = bank()
        proj(ps[0:32, bk, 0:TS], lambda c: g1b[:, c, 128:160], 5, [("g1b",)])
        S.op("act", lambda: nc.scalar.activation(out=hgb[0:32, :], in_=ps[0:32, bk, 0:TS], func=AF.Sigmoid), reads=[("ps", bk)], writes=[("hgb",)])
        for cc in range(4):
            f = cc % 2
            csl = slice(cc * 128, (cc + 1) * 128)
            tsl = slice(t0, t0 + TS)
            bk = bank(); proj(ps[:, bk, 0:TS], lambda c: W3[:, 0, c, csl], 0, [("W3", 0)])
            S.op("act", lambda: nc.scalar.copy(out=rT[f][:], in_=ps[:, bk, 0:TS]), reads=[("ps", bk)], writes=[("rT", f)])
            yield
            bk = bank(); proj(ps[:, bk, 0:TS], lambda c: W3[:, 1, c, csl], 2, [("W3", 1)])
            S.op("act", lambda: nc.scalar.copy(out=kT[f][:], in_=ps[:, bk, 0:TS]), reads=[("ps", bk)], writes=[("kT", f)])
            yield
            bk = bank(); proj(ps[:, bk, 0:TS], lambda c: W3[:, 2, c, csl], 3, [("W3", 2)])
            S.op("act", lambda: nc.scalar.copy(out=vT[f][:], in_=ps[:, bk, 0:TS]), reads=[("ps", bk)], writes=[("vT", f)])
            S.dma("sp", d["v_out"][csl, tsl], vT[f][:], reads=[("vT", f)])
            bk = bank()
            S.op("pe", lambda: nc.tensor.matmul(ps[:, bk, 0:TS], w2b[:, csl], hw[:], start=True, stop=True), reads=[("w2b",), ("hw",)], writes=[("ps", bk)])
            S.op("act", lambda: nc.scalar.activation(out=sg[f][:], in_=ps[:, bk, 0:TS], func=AF.Sigmoid, bias=cols[:, 0, cc:cc + 1], scale=1.0),
                 reads=[("ps", bk), ("cols", 0)], writes=[("sg", f)])
            bk = bank()
            S.op("pe", lambda: nc.tensor.matmul(ps[:, bk, 0:TS], a2b[:, csl], ha[:], start=True, stop=True), reads=[("a2b",), ("ha",)], writes=[("ps", bk)])
            S.op("act", lambda: nc.scalar.activation(out=asg[f][:], in_=ps[:, bk, 0:TS], func=AF.Sigmoid, bias=cols[:, 1, cc:cc + 1], scale=1.0),
                 reads=[("ps", bk), ("cols", 1)], writes=[("asg", f)])
            bk = bank()
            S.op("pe", lambda: nc.tensor.matmul(ps[:, bk, 0:TS], g2a[:, csl], hga[:], start=True, stop=False), reads=[("g2a",), ("hga",)], writes=[("ps", bk)], signal=False)
            S.op("pe", lambda: nc.tensor.matmul(ps[:, bk, 0:TS], g2b[:, csl], hgb[:], start=False, stop=True), reads=[("g2b",), ("hgb",)], writes=[("ps", bk)])
            S.op("act", lambda: nc.scalar.copy(out=gT[f][:], in_=ps[:, bk, 0:TS]), reads=[("ps", bk)], writes=[("gT", 0)])
            S.dma("sp", d["g_out"][csl, tsl], gT[f][:], reads=[("gT", 0)])
            yield
            S.op("dve", lambda: nc.vector.tensor_scalar(out=kq[f][:], in0=kT[f][:], scalar1=cols[:, 2, cc:cc + 1], scalar2=None, op0=ALU.mult),
                 reads=[("kT", f), ("cols", 2)], writes=[("kq", 0)])
            S.op("pool", lambda: nc.gpsimd.tensor_tensor(out=kq2[f][:], in0=kq[f][:], in1=kq[f][:], op=ALU.mult), reads=[("kq", 0)], writes=[("kq2", 0)])
            bk = bank()
            S.op("pe", lambda: nc.tensor.matmul(ps[:, bk, 0:TS], blkf, kq2[f][:], start=True, stop=True), reads=[("cst",), ("kq2", 0)], writes=[("ps", bk)])
            S.op("dve", lambda: nc.vector.tensor_scalar(out=rn[f][:], in0=ps[:, bk, 0:TS], scalar1=1e-24, scalar2=None, op0=ALU.max),
                 reads=[("ps", bk)], writes=[("rn", 0)])
            S.op("act", lambda: nc.scalar.activation(out=rn[f][:], in_=rn[f][:], func=AF.Ln), reads=[("rn", 0)], writes=[("rn", 0)])
            S.op("act", lambda: nc.scalar.activation(out=rn[f][:], in_=rn[f][:], func=AF.Exp, scale=-0.5), reads=[("rn", 0)], writes=[("rn", 0)])
            S.op("dve", lambda: nc.vector.tensor_tensor(out=kk[f][:], in0=kq[f][:], in1=rn[f][:], op=ALU.mult), reads=[("kq", 0), ("rn", 0)], writes=[("kk", f)])
            S.op("dve", lambda: nc.vector.tensor_scalar(out=t1[f][:], in0=asg[f][:], scalar1=-1.0, scalar2=cols[:, 3, cc:cc + 1], op0=ALU.add, op1=ALU.mult),
                 reads=[("asg", f), ("cols", 3)], writes=[("t1", 0)])
            S.op("dve", lambda: nc.vector.scalar_tensor_tensor(out=kz[f][:], in0=t1[f][:], scalar=1.0, in1=kT[f][:], op0=ALU.add, op1=ALU.mult),
                 reads=[("t1", 0), ("kT", f)], writes=[("kz", f)])
            S.op("dve", lambda: nc.vector.scalar_tensor_tensor(out=prod[f][:], in0=rT[f][:], scalar=cols[:, 4, cc:cc + 1], in1=kz[f][:], op0=ALU.mult, op1=ALU.mult),
                 reads=[("rT", f), ("kz", f), ("cols", 4)], writes=[("prod", 0)])
            S.dma("sp", d["p_out"][csl, tsl], prod[f][:], reads=[("prod", 0)])
            yield
            for ck in range(NCK):
                ksl = slice(ck * CH, (ck + 1) * CH)
                S.op("dve", lambda: nc.vector.tensor_tensor_scan(out=cs[f][:, ksl], data0=ones64[:], data1=sg[f][:, ksl], initial=0.0, op0=ALU.mult, op1=ALU.add),
                     reads=[("sg", f), ("ones64",)], writes=[("cs", 0)])
            S.op("act", lambda: nc.scalar.activation(out=E1[par][cc][:], in_=cs[f][:], func=AF.Exp, scale=-C0), reads=[("cs", 0)], writes=[("E1", par, cc)])
            S.op("act", lambda: nc.scalar.activation(out=E2[f][:], in_=cs[f][:], func=AF.Exp, scale=C0), reads=[("cs", 0)], writes=[("E2", 0)])
            S.op("pool", lambda: nc.gpsimd.tensor_tensor(out=dd[f][:], in0=cs[f][:], in1=sg[f][:], op=ALU.subtract), reads=[("cs", 0), ("sg", f)], writes=[("dd", 0)])
            S.op("act", lambda: nc.scalar.activation(out=E3[f][:], in_=dd[f][:], func=AF.Exp, scale=-C0), reads=[("dd", 0)], writes=[("E3", 0)])
            yield
            S.op("dve", lambda: nc.vector.scalar_tensor_tensor(out=af[f][:], in0=kk[f][:], scalar=-1.0, in1=E3[f][:], op0=ALU.mult, op1=ALU.mult),
                 reads=[("kk", f), ("E3", 0)], writes=[("af", f)])
            S.op("act", lambda: nc.scalar.copy(out=RA[par][cc][:, 0, :], in_=af[f][:]), reads=[("af", f)], writes=[("RA", par, cc, 0)])
            S.op("dve", lambda: nc.vector.tensor_tensor(out=RAf1[par][cc][:], in0=rT[f][:], in1=E1[par][cc][:], op=ALU.mult), reads=[("rT", f), ("E1", par, cc)], writes=[("RAf1", par, cc)])
            S.op("act", lambda: nc.scalar.copy(out=RA[par][cc][:, 1, :], in_=RAf1[par][cc][:]), reads=[("RAf1", par, cc)], writes=[("RA", par, cc, 1)])
            S.op("pool", lambda: nc.gpsimd.tensor_tensor(out=bb[f][:], in0=kk[f][:], in1=asg[f][:], op=ALU.mult), reads=[("kk", f), ("asg", f)], writes=[("bb", 0)])
            S.op("dve", lambda: nc.vector.tensor_tensor(out=BKf[f][:, 0, :], in0=bb[f][:], in1=E2[f][:], op=ALU.mult), reads=[("bb", 0), ("E2", 0)], writes=[("BKf", f, 0)])
            S.op("dve", lambda: nc.vector.tensor_tensor(out=BKf[f][:, 1, :], in0=kz[f][:], in1=E2[f][:], op=ALU.mult), reads=[("kz", f), ("E2", 0)], writes=[("BKf", f, 1)])
            S.op("act", lambda: nc.scalar.copy(out=LBt[par][cc][:], in_=BKf[f][:]), reads=[("BKf", f, 0), ("BKf", f, 1)], writes=[("LBt", par, cc)])
            for ck in range(NCK):
                ksl = slice(ck * CH, (ck + 1) * CH)
                e = ck * CH + CH - 1
                S.op("dve", lambda: nc.vector.tensor_scalar(out=Hat[f][:, :, ksl], in0=BKf[f][:, :, ksl], scalar1=E1[par][cc][:, e:e + 1], scalar2=None, op0=ALU.mult),
                     reads=[("BKf", f, 0), ("BKf", f, 1), ("E1", par, cc)], writes=[("Hat", f)])
            yield
            for p in range(NPR):
                psl = slice(p * 128, (p + 1) * 128)
                bk = bank()
                srcs = [(af[f][:, psl], ("af", f)), (Hat[f][:, 0, psl], ("Hat", f)), (Hat[f][:, 1, psl], ("Hat", f)), (vT[f][:, psl], ("vT", f))]
                for i, (src, key) in enumerate(srcs):
                    S.op("pe", lambda: nc.tensor.transpose(ps[:, bk, i * 128:(i + 1) * 128], src, identf), reads=[key, ("cst",)], writes=[("ps", bk)], signal=(i == 3))
                S.op("act", lambda: nc.scalar.copy(out=TM[par][cc][p][:].rearrange("p a n -> p (a n)"), in_=ps[:, bk, :]), reads=[("ps", bk)], writes=[("TM", par, cc, p)])
        yield

    def gen_de(ti):
        par = ti % 2
        t0 = ti * TS
        for ccg in range(2):
            units = [(cc, hd, p) for cc in (2 * ccg, 2 * ccg + 1) for hd in range(2) for p in range(NPR)]
            def uid(cc, hd, p):
                return ((cc % 2) * 2 + hd) * NPR + p
            for (cc, hd, p) in units:
                u = uid(cc, hd, p)
                hs = slice(hd * 64, hd * 64 + 64); tk = slice(p * 128, (p + 1) * 128)
                bk = bank()
                S.op("pe", lambda: nc.tensor.matmul(ps[:, bk, 0:256], LBt[par][cc][hs, 0, tk], RA[par][cc][hs, :, tk], start=True, stop=True),
                     reads=[("LBt", par, cc), ("RA", par, cc, 0), ("RA", par, cc, 1)], writes=[("ps", bk)], signal=False)
                S.op("pe", lambda: nc.tensor.matmul(ps[:, bk, 256:512], LBt[par][cc][hs, 1, tk], RA[par][cc][hs, :, tk], start=True, stop=True),
                     reads=[("LBt", par, cc), ("RA", par, cc, 0), ("RA", par, cc, 1)], writes=[("ps", bk)])
                S.op("dve", lambda: nc.vector.tensor_tensor(out=AM[u][:], in0=ps[:, bk, :], in1=maskSI, op=ALU.mult), reads=[("ps", bk), ("cst",)], writes=[("AM", u)])
                bk = bank()
                S.op("pe", lambda: nc.tensor.matmul(ps[:, bk, 0:128], RA[par][cc][hs, 0, tk], LBt[par][cc][hs, 0, tk], start=True, stop=True),
                     reads=[("LBt", par, cc), ("RA", par, cc, 0)], writes=[("ps", bk)])
                S.op("dve", lambda: nc.vector.tensor_tensor(out=Nk[u][0][:], in0=ps[:, bk, 0:128], in1=maskTS, op=ALU.mult), reads=[("ps", bk), ("cst",)], writes=[("Nk", u, 0)])
                S.op("pool", lambda: nc.gpsimd.tensor_tensor(out=Xs[u][:], in0=AM[u][:, 0:128], in1=identf, op=ALU.add), reads=[("AM", u), ("cst",)], writes=[("Xs", u)])
                yield
            for k in range(1, 6):
                for (cc, hd, p) in units:
                    u = uid(cc, hd, p)
                    Mprev = AM[u][:, 0:128] if k == 1 else Mk[u][(k - 1) % 2][:]
                    Mkey = ("AM", u) if k == 1 else ("Mk", u, (k - 1) % 2)
                    Nprev = Nk[u][(k - 1) % 2][:]
                    Nkey = ("Nk", u, (k - 1) % 2)
                    bk = bank()
                    if k <= 4:
                        S.op("pe", lambda: nc.tensor.matmul(ps[:, bk, 0:128], Nprev, Mprev, start=True, stop=True), reads=[Mkey, Nkey], writes=[("ps", bk)], signal=False)
                    S.op("pe", lambda: nc.tensor.matmul(ps[:, bk, 128:256], Mprev, Nprev, start=True, stop=True), reads=[Mkey, Nkey], writes=[("ps", bk)])
                    if k <= 4:
                        S.op("act", lambda: nc.scalar.copy(out=Mk[u][k % 2][:], in_=ps[:, bk, 0:128]), reads=[("ps", bk)], writes=[("Mk", u, k % 2)])
                    S.op("act", lambda: nc.scalar.copy(out=Nk[u][k % 2][:], in_=ps[:, bk, 128:256]), reads=[("ps", bk)], writes=[("Nk", u, k % 2)])
                    yield
                for (cc, hd, p) in units:
                    u = uid(cc, hd, p)
                    bk = bank()
                    Xkey = ("Xs", u)
                    S.op("pe", lambda: nc.tensor.matmul(ps[:, bk, 0:128], Nk[u][k % 2][:], Xs[u][:], start=True, stop=True), reads=[("Nk", u, k % 2), Xkey], writes=[("ps", bk)])
                    S.op("dve", lambda: nc.vector.tensor_tensor(out=Xs[u][:], in0=ps[:, bk, 0:128], in1=Xs[u][:], op=ALU.add), reads=[("ps", bk), ("Xs", u)], writes=[("Xs", u)])
                    yield
            Xkeyf = lambda u: ("Xs", u)
            for (cc, hd, p) in units:
                u = uid(cc, hd, p)
                hs = slice(hd * 64, hd * 64 + 64)
                bk = bank()
                S.op("pe", lambda: nc.tensor.matmul(ps[:, bk, 0:64], Xs[u][:], TM[par][cc][p][:, 0, hs], start=True, stop=True), reads=[Xkeyf(u), ("TM", par, cc, p)], writes=[("ps", bk)], signal=False)
                S.op("pe", lambda: nc.tensor.matmul(ps[:, bk, 64:128], AM[u][:, 256:384], TM[par][cc][p][:, 3, hs], start=True, stop=True), reads=[("AM", u), ("TM", par, cc, p)], writes=[("ps", bk)])
                S.op("act", lambda: nc.scalar.copy(out=ATM[u][:], in_=ps[:, bk, 0:64]), reads=[("ps", bk)], writes=[("ATM", u)])
                S.op("act", lambda: nc.scalar.copy(out=Ws[u][:], in_=ps[:, bk, 64:128]), reads=[("ps", bk)], writes=[("Ws", u)])
                yield
            for (cc, hd, p) in units:
                u = uid(cc, hd, p)
                hs = slice(hd * 64, hd * 64 + 64); tk = slice(p * 128, (p + 1) * 128)
                bk = bank()
                S.op("pe", lambda: nc.tensor.matmul(ps[:, bk, 0:64], Xs[u][:], Ws[u][:], start=True, stop=True), reads=[Xkeyf(u), ("Ws", u)], writes=[("ps", bk)])
                S.op("act", lambda: nc.scalar.copy(out=UV[u][:], in_=ps[:, bk, 0:64]), reads=[("ps", bk)], writes=[("UV", u)])
                bk2 = bank()
                S.op("pe", lambda: nc.tensor.matmul(ps[hs, bk2, 0:128], ATM[u][:], AM[u][:, 128:256], start=True, stop=True), reads=[("ATM", u), ("AM", u)], writes=[("ps", bk2)])
                S.op("dve", lambda: nc.vector.tensor_tensor(out=RH[cc][hs, tk], in0=ps[hs, bk2, 0:128], in1=RAf1[par][cc][hs, tk], op=ALU.add),
                     reads=[("ps", bk2), ("RAf1", par, cc)], writes=[("RH", cc, hd)])
                yield
            for (cc, hd, p) in units:
                u = uid(cc, hd, p)
                hs = slice(hd * 64, hd * 64 + 64)
                bk = bank()
                S.op("pe", lambda: nc.tensor.matmul(ps[:, bk, 0:64], AM[u][:, 128:256], UV[u][:], start=True, stop=False), reads=[("AM", u), ("UV", u)], writes=[("ps", bk)], signal=False)
                S.op("pe", lambda: nc.tensor.matmul(ps[:, bk, 0:64], AM[u][:, 384:512], TM[par][cc][p][:, 3, hs], start=False, stop=True), reads=[("AM", u), ("TM", par, cc, p)], writes=[("ps", bk)])
                S.op("act", lambda: nc.scalar.copy(out=YV[cc][p][:, hs], in_=ps[:, bk, 0:64]), reads=[("ps", bk)], writes=[("YV", cc, p, hd)])
                for q in range(2):
                    pb = slice(q * 64, q * 64 + 64)
                    ck = p * 2 + q
                    e = ck * CH + CH - 1
                    bk = bank()
                    S.op("pe", lambda: nc.tensor.matmul(ps[hs, bk, 0:64], ATM[u][pb, :], TM[par][cc][p][pb, 1, hs], start=True, stop=True), reads=[("ATM", u), ("TM", par, cc, p)], writes=[("ps", bk)])
                    S.op("dve", lambda: nc.vector.scalar_tensor_tensor(out=GT[cc][hs, ck, :], in0=identf[hs, hs], scalar=E1[par][cc][hs, e:e + 1], in1=ps[hs, bk, 0:64], op0=ALU.mult, op1=ALU.add),
                         reads=[("ps", bk), ("cst",), ("E1", par, cc)], writes=[("GT", cc, hd)])
                    bk = bank()
                    S.op("pe", lambda: nc.tensor.matmul(ps[hs, bk, 0:64], TM[par][cc][p][pb, 1, hs], UV[u][pb, :], start=True, stop=False), reads=[("TM", par, cc, p), ("UV", u)], writes=[("ps", bk)], signal=False)
                    S.op("pe", lambda: nc.tensor.matmul(ps[hs, bk, 0:64], TM[par][cc][p][pb, 2, hs], TM[par][cc][p][pb, 3, hs], start=False, stop=True), reads=[("TM", par, cc, p)], writes=[("ps", bk)])
                    S.op("act", lambda: nc.scalar.copy(out=Hf[cc][hs, ck, :], in_=ps[hs, bk, 0:64]), reads=[("ps", bk)], writes=[("Hf", cc, hd)])
                    yield
        for ck in range(NCK):
            p = ck // 2; q = ck % 2
            pb = slice(q * 64, q * 64 + 64)
            for cc in range(4):
                for hd in range(2):
                    hs = slice(hd * 64, hd * 64 + 64)
                    bkY = bank(); bkS = bank()
                    S.op("pe", lambda: nc.tensor.matmul(ps[pb, bkY, 0:64], RH[cc][hs, ck * CH:(ck + 1) * CH], ST[cc][hs, :], start=True, stop=True),
                         reads=[("RH", cc, hd), ("ST", cc, hd)], writes=[("ps", bkY)])
                    S.op("pe", lambda: nc.tensor.matmul(ps[hs, bkS, 0:64], GT[cc][hs, ck, :], ST[cc][hs, :], start=True, stop=True),
                         reads=[("GT", cc, hd), ("ST", cc, hd)], writes=[("ps", bkS)])
                    S.op("dve", lambda: nc.vector.tensor_tensor(out=ST[cc][hs, :], in0=ps[hs, bkS, 0:64], in1=Hf[cc][hs, ck, :], op=ALU.add),
                         reads=[("ps", bkS), ("Hf", cc, hd)], writes=[("ST", cc, hd)])
                    S.op("dve", lambda: nc.vector.tensor_tensor(out=yt[cc][p][pb, hs], in0=ps[pb, bkY, 0:64], in1=YV[cc][p][pb, hs], op=ALU.add),
                         reads=[("ps", bkY), ("YV", cc, p, hd)], writes=[("yt", cc, p)])
                yield
                if q == 1:
                    S.dma("sp", d["y_out"][t0 + p * 128:t0 + (p + 1) * 128, cc * 128:(cc + 1) * 128], yt[cc][p][:], reads=[("yt", cc, p)])
        yield

    def drain(g):
        n = 0
        for _ in g:
            n += 1
        return n

    n_abc = drain(gen_abc(0))
    n_de = None
    for ti in range(NTI):
        ga = gen_abc(ti + 1) if ti + 1 < NTI else None
        gd = gen_de(ti)
        ca = 0; cd = 0
        while ga is not None or gd is not None:
            fa = ca / n_abc if ga is not None else 2.0
            fd = cd / n_de if (gd is not None and n_de) else (ca / n_abc if gd is not None else 2.0)
            if gd is not None and (ga is None or fd <= fa):
                try:
                    next(gd); cd += 1
                except StopIteration:
                    gd = None
                    if n_de is None:
                        n_de = max(cd, 1)
            else:
                try:
                    next(ga); ca += 1
                except StopIteration:
                    ga = None
        if n_de is None:
            n_de = max(cd, 1)
    S.finish("sp")


def build_rw1_prog(SL, TS=256, SD=BF16):
    nc = bass.Bass("TRN2", target_bir_lowering=False)
    dt = lambda n, s: nc.dram_tensor(n, s, F32, kind="ExternalInput").ap()
    d = {"xT": dt("xT", [D, SL]), "mix": dt("mix", [6, D]), "consts": dt("consts", [128, 896])}
    for nm in ("w_r", "w_k", "w_v"):
        d[nm] = dt(nm, [D, 512])
    d["w1"] = dt("w1", [D, 64]); d["w2"] = dt("w2", [64, 512]); d["w0"] = dt("w0", [512])
    d["a1"] = dt("a1", [D, 64]); d["a2"] = dt("a2", [64, 512]); d["a0"] = dt("a0", [512])
    d["g1"] = dt("g1", [D, 160]); d["g2"] = dt("g2", [160, 512])
    for nm in ("k_k", "k_a", "r_k"):
        d[nm] = dt(nm, [512])
    do = lambda n, s: nc.dram_tensor(n, s, F32, kind="ExternalOutput").ap()
    d["y_out"] = do("y_out", [SL, 512]); d["p_out"] = do("p_out", [512, SL]); d["v_out"] = do("v_out", [512, SL]); d["g_out"] = do("g_out", [512, SL])
    with ExitStack() as es:
        S = Sched(nc, es)
        ps = es.enter_context(nc.psum_tensor("ps", [128, 8, 512], F32))
        emit_rw1(S, nc, es, ps, d, SL, TS, SD)
    return nc


GN_EPS = 64e-5


def emit_rw2(S, nc, es, ps, d, T, xT, xb, LB):
    NT = T // 512
    tsl = lambda tt: slice(tt * 512, (tt + 1) * 512)
    sb = lambda es_, n, s, dt: es_.enter_context(nc.sbuf_tensor(n, s, dt))
    with ExitStack() as es2:
        cst = sb(es2, "r2_cst", [128, 896], F32)
        S.dma("sp", cst[:], d["consts"], writes=[("cst2",)])
        blkf = cst[:, 768:896]
        wout = sb(es2, "r2_wout", [128, 8, 1024], BF16)
        S.dma("pool", wout[:], d["w_out"].rearrange("(c p) n -> p c n", p=128), writes=[("r2wout",)])
        gcol = sb(es2, "r2_gcol", [128, 8], F32); bcol = sb(es2, "r2_bcol", [128, 8], F32); epsg = sb(es2, "r2_eps", [128, 1], F32)
        S.dma("sp", gcol[:], d["lnx_g"].rearrange("(c p) -> p c", p=128), writes=[("r2g",)], allow_slow_non_contiguous=True)
        S.dma("sp", bcol[:], d["lnx_b"].rearrange("(c p) -> p c", p=128), writes=[("r2b",)], allow_slow_non_contiguous=True)
        S.op("dve", lambda: nc.vector.memset(epsg[:], GN_EPS), writes=[("r2eps",)])
        names = ["yf", "yb", "pf", "pb", "v", "g"]
        tin = {n: [sb(es2, f"r2_{n}{i}", [128, 512], F32) for i in range(2)] for n in names}
        tmp = {n: [sb(es2, f"r2_t{n}{i}", [128, 512], F32) for i in range(2)] for n in ["y", "ysq", "mean", "a", "b", "pp"]}
        dv = {n: d[n].rearrange("(c p) t -> p c t", p=128) for n in names + ["x"]}
        it = 0
        for tt in range(NT):
            S.dma("sp", xT[:, :, tsl(tt)], dv["x"][:, :, tsl(tt)], writes=[("xT", c, tt) for c in range(8)])
            for c in range(8):
                i = it % 2; it += 1
                for n in names:
                    S.dma("sp", tin[n][i][:], dv[n][:, c, tsl(tt)], writes=[("r2in", n, i)])
                Y = tmp["y"][i]; YS = tmp["ysq"][i]; MN = tmp["mean"][i]; A = tmp["a"][i]; Bt = tmp["b"][i]; PP = tmp["pp"][i]
                S.op("dve", lambda: nc.vector.tensor_tensor(out=Y[:], in0=tin["yf"][i][:], in1=tin["yb"][i][:], op=ALU.add),
                     reads=[("r2in", "yf", i), ("r2in", "yb", i)], writes=[("r2y", i)])
                S.op("act", lambda: nc.scalar.activation(out=YS[:], in_=Y[:], func=AF.Square), reads=[("r2y", i)], writes=[("r2ysq", i)])
                S.op("pool", lambda: nc.gpsimd.tensor_tensor(out=PP[:], in0=tin["pf"][i][:], in1=tin["pb"][i][:], op=ALU.add),
                     reads=[("r2in", "pf", i), ("r2in", "pb", i)], writes=[("r2pp", i)])
                b1 = 0 + 3 * (it % 2); b2 = b1 + 1; b3 = b1 + 2
                S.op("pe", lambda: nc.tensor.matmul(ps[:, b1, :], blkf, Y[:], start=True, stop=True), reads=[("cst2",), ("r2y", i)], writes=[("ps", b1)])
                S.op("pe", lambda: nc.tensor.matmul(ps[:, b2, :], blkf, YS[:], start=True, stop=True), reads=[("cst2",), ("r2ysq", i)], writes=[("ps", b2)])
                S.op("pe", lambda: nc.tensor.matmul(ps[:, b3, :], blkf, PP[:], start=True, stop=True), reads=[("cst2",), ("r2pp", i)], writes=[("ps", b3)])
                S.op("act", lambda: nc.scalar.activation(out=MN[:], in_=ps[:, b1, :], func=AF.Identity, scale=1.0 / 64), reads=[("ps", b1)], writes=[("r2mean", i)])
                S.op("dve", lambda: nc.vector.tensor_tensor(out=A[:], in0=ps[:, b1, :], in1=MN[:], op=ALU.mult), reads=[("ps", b1), ("r2mean", i)], writes=[("r2a", i)])
                S.op("dve", lambda: nc.vector.tensor_tensor(out=A[:], in0=ps[:, b2, :], in1=A[:], op=ALU.subtract), reads=[("ps", b2), ("r2a", i)], writes=[("r2a", i)])
                S.op("act", lambda: nc.scalar.activation(out=A[:], in_=A[:], func=AF.Sqrt, bias=epsg[:], scale=1.0 / 64), reads=[("r2a", i), ("r2eps",)], writes=[("r2a", i)])
                S.op("dve", lambda: nc.vector.reciprocal(out=A[:], in_=A[:]), reads=[("r2a", i)], writes=[("r2a", i)])
                S.op("pool", lambda: nc.gpsimd.tensor_tensor(out=Bt[:], in0=Y[:], in1=MN[:], op=ALU.subtract), reads=[("r2y", i), ("r2mean", i)], writes=[("r2b_", i)])
                S.op("pool", lambda: nc.gpsimd.tensor_tensor(out=Bt[:], in0=Bt[:], in1=A[:], op=ALU.mult), reads=[("r2b_", i), ("r2a", i)], writes=[("r2b_", i)])
                S.op("act", lambda: nc.scalar.activation(out=Bt[:], in_=Bt[:], func=AF.Identity, bias=bcol[:, c:c + 1], scale=gcol[:, c:c + 1]),
                     reads=[("r2b_", i), ("r2g",), ("r2b",)], writes=[("r2b_", i)])
                S.op("dve", lambda: nc.vector.tensor_tensor(out=YS[:], in0=ps[:, b3, :], in1=tin["v"][i][:], op=ALU.mult), reads=[("ps", b3), ("r2in", "v", i)], writes=[("r2ysq", i)])
                S.op("pool", lambda: nc.gpsimd.tensor_tensor(out=Bt[:], in0=Bt[:], in1=YS[:], op=ALU.add), reads=[("r2b_", i), ("r2ysq", i)], writes=[("r2b_", i)])
                S.op("dve", lambda: nc.vector.tensor_tensor(out=xb[:, c, tsl(tt)], in0=Bt[:], in1=tin["g"][i][:], op=ALU.mult), reads=[("r2b_", i), ("r2in", "g", i)], writes=[("xb", c, tt)])
            for dc in range(8):
                bank = 6 + (dc % 2)
                for c in range(8):
                    S.op("pe", lambda: nc.tensor.matmul(ps[:, bank, :], wout[:, c, dc * 128:(dc + 1) * 128], xb[:, c, tsl(tt)], start=(c == 0), stop=(c == 7)),
                         reads=[("r2wout",), ("xb", c, tt)], writes=[("ps", bank)], signal=(c == 7))
                S.op("pool", lambda: nc.gpsimd.tensor_scalar(out=xT[:, dc, tsl(tt)], in0=xT[:, dc, tsl(tt)], scalar1=float(ALPHA), scalar2=0.0, op0=ALU.mult, op1=ALU.add),
                     reads=[("xT", dc, tt)], writes=[("xT", dc, tt)])
                S.op("dve", lambda: nc.vector.tensor_tensor(out=xT[:, dc, tsl(tt)], in0=ps[:, bank, :], in1=xT[:, dc, tsl(tt)], op=ALU.add),
                     reads=[("ps", bank), ("xT", dc, tt)], writes=[("xT", dc, tt)])
    S.fence()


def build_rw2_prog(T):
    nc = bass.Bass("TRN2", target_bir_lowering=False)
    dt = lambda n, s: nc.dram_tensor(n, s, F32, kind="ExternalInput").ap()
    d = {n: dt(n, [D, T]) for n in ["x", "yf", "yb", "pf", "pb", "v", "g"]}
    d["consts"] = dt("consts", [128, 896])
    d["lnx_g"] = dt("lnx_g", [D]); d["lnx_b"] = dt("lnx_b", [D]); d["w_out"] = dt("w_out", [D, D])
    lng = dt("lng", [2, D]); lnb = dt("lnb", [2, D])
    w_in = dt("w_in", [D, 2 * FF]); w_outf = dt("w_outf", [FF, D])
    yT_d = nc.dram_tensor("yT", [D, T], F32, kind="ExternalOutput").ap()
    with ExitStack() as es:
        S = Sched(nc, es)
        ps = es.enter_context(nc.psum_tensor("ps", [128, 8, 512], F32))
        xT = es.enter_context(nc.sbuf_tensor("xTs", [128, 8, T], F32))
        xb = es.enter_context(nc.sbuf_tensor("xbs", [128, 8, T], BF16))
        B = FfnBufs(nc, es, T, with_ffn=False)
        B.init_consts(S, nc)
        emit_rw2(S, nc, es, ps, d, T, xT, xb, B)
        NT = T // 512
        emit_ln(S, nc, B, xT, xb, ps, lng[0], lnb[0], NT)
        B.alloc_ffn(nc, es)
        emit_ffn_ln(S, nc, B, xT, xb, ps, w_in, w_outf, lng[1], lnb[1], "f")
        yv = yT_d.rearrange("(c p) t -> p c t", p=128)
        for tt in range(NT):
            S.dma("sp", yv[:, :, tt * 512:(tt + 1) * 512], xT[:, :, tt * 512:(tt + 1) * 512], reads=[("xT", c, tt) for c in range(8)])
        S.finish("sp")
    return nc


_PROGS = {}


def _prog(key, fn):
    if key not in _PROGS:
        _PROGS[key] = fn()
    return _PROGS[key]


def _run(nc, in_maps):
    res = run_bass_kernel_spmd(nc, in_maps, core_ids=list(range(8)))
    return res.results


def _tok(c):
    b = c // 4
    t0 = (c % 4) * 2048
    return b, t0


def _tr(a):
    return np.ascontiguousarray(a.T)


def kernel(x, ffn_w_in, ffn_w_out, ln_g, ln_b, na_w_qkv, na_b_qkv, na_rpb, na_w_o, na_b_o,
           rw_mix, rw_w_rkv, rw_w0, rw_w1, rw_w2, rw_a0, rw_a1, rw_a2, rw_g1, rw_g2,
           rw_k_k, rw_k_a, rw_r_k, rw_lnx_g, rw_lnx_b, rw_w_out):
    f32 = np.float32
    A = lambda a: np.ascontiguousarray(np.asarray(a, dtype=f32))
    x = A(x)
    Bsz, SEQ, Dm = x.shape
    T = 2048
    nc1 = _prog("ffn1", lambda: build_ffn_prog(T, 1))
    ins = []
    for c in range(8):
        b, t0 = _tok(c)
        ins.append({"xT": _tr(x[b, t0:t0 + T]), "w_in": A(ffn_w_in[0, 0:1]), "w_out": A(ffn_w_out[0, 0:1]),
                    "lng": A(ln_g[0, 0:1]), "lnb": A(ln_b[0, 0:1])})
    r = _run(nc1, ins)
    x1 = np.empty_like(x)
    for c in range(8):
        b, t0 = _tok(c)
        x1[b, t0:t0 + T] = r[c]["yT"].T
    nc2 = _prog("na", lambda: build_na_prog(32))
    ins = []
    for c in range(8):
        b, t0 = _tok(c)
        r0 = (c % 4) * 32
        G = na_halo_rows(r0, 32, 128)
        xg = x1[b].reshape(128, 64, Dm)
        ins.append({"xh": _tr(xg[G].reshape(-1, Dm)), "xo": _tr(x1[b, t0:t0 + T]), "w_qkv": A(na_w_qkv[0]), "b_qkv": A(na_b_qkv[0]),
                    "tab": na_tables(A(na_rpb[0]), r0, 32, 128), "w_o": A(na_w_o[0]), "b_o": A(na_b_o[0]),
                    "lng": A(ln_g[0, 1]), "lnb": A(ln_b[0, 1])})
    r = _run(nc2, ins)
    x2 = np.empty_like(x)
    for c in range(8):
        b, t0 = _tok(c)
        x2[b, t0:t0 + T] = r[c]["yT"].T
    nc3 = _prog("ffn2", lambda: build_ffn_prog(T, 2))
    w_in2 = A(np.stack([ffn_w_in[0, 1], ffn_w_in[1, 0]])); w_out2 = A(np.stack([ffn_w_out[0, 1], ffn_w_out[1, 0]]))
    lng2 = A(np.stack([ln_g[0, 2], ln_g[1, 0]])); lnb2 = A(np.stack([ln_b[0, 2], ln_b[1, 0]]))
    ins = []
    for c in range(8):
        b, t0 = _tok(c)
        ins.append({"xT": _tr(x2[b, t0:t0 + T]), "w_in": w_in2, "w_out": w_out2, "lng": lng2, "lnb": lnb2})
    r = _run(nc3, ins)
    x4 = np.empty_like(x)
    for c in range(8):
        b, t0 = _tok(c)
        x4[b, t0:t0 + T] = r[c]["yT"].T
    nc4 = _prog("rw1", lambda: build_rw1_prog(SEQ, 256, BF16))
    consts = rw_consts()
    ins = []
    for c in range(8):
        b = c // 4; z = (c // 2) % 2; hg = c % 2
        cs_ = slice(hg * 512, (hg + 1) * 512)
        xs_ = x4[b][::-1] if z == 1 else x4[b]
        ins.append({"xT": _tr(xs_), "mix": A(rw_mix[0]), "consts": consts,
                    "w_r": A(rw_w_rkv[0, 0][:, cs_]), "w_k": A(rw_w_rkv[0, 1][:, cs_]), "w_v": A(rw_w_rkv[0, 2][:, cs_]),
                    "w1": A(rw_w1[0, z]), "w2": A(rw_w2[0, z][:, cs_]), "w0": A(rw_w0[0, z][cs_]),
                    "a1": A(rw_a1[0, z]), "a2": A(rw_a2[0, z][:, cs_]), "a0": A(rw_a0[0, z][cs_]),
                    "g1": A(rw_g1[0]), "g2": A(rw_g2[0][:, cs_]),
                    "k_k": A(rw_k_k[0][cs_]), "k_a": A(rw_k_a[0][cs_]), "r_k": A(np.reshape(rw_r_k[0], (-1,))[cs_])})
    r = _run(nc4, ins)
    yT = np.empty((2, Bsz, Dm, SEQ), f32); pT = np.empty((2, Bsz, Dm, SEQ), f32)
    vT = np.empty((Bsz, Dm, SEQ), f32); gT = np.empty((Bsz, Dm, SEQ), f32)
    for c in range(8):
        b = c // 4; z = (c // 2) % 2; hg = c % 2
        cs_ = slice(hg * 512, (hg + 1) * 512)
        yo = r[c]["y_out"]; po = r[c]["p_out"]
        if z == 1:
            yT[z, b, cs_] = yo[::-1].T
            pT[z, b, cs_] = po[:, ::-1]
        else:
            yT[z, b, cs_] = yo.T
            pT[z, b, cs_] = po
            vT[b, cs_] = r[c]["v_out"]; gT[b, cs_] = r[c]["g_out"]
    nc5 = _prog("rw2", lambda: build_rw2_prog(T))
    ins = []
    lng5 = A(np.stack([ln_g[1, 1], ln_g[1, 2]])); lnb5 = A(np.stack([ln_b[1, 1], ln_b[1, 2]]))
    for c in range(8):
        b, t0 = _tok(c)
        ts_ = slice(t0, t0 + T)
        cp = lambda a: np.ascontiguousarray(a[:, ts_])
        ins.append({"x": _tr(x4[b, ts_]), "yf": cp(yT[0, b]), "yb": cp(yT[1, b]), "pf": cp(pT[0, b]), "pb": cp(pT[1, b]),
                    "v": cp(vT[b]), "g": cp(gT[b]), "consts": consts, "lnx_g": A(rw_lnx_g[0]), "lnx_b": A(rw_lnx_b[0]),
                    "w_out": A(rw_w_out[0]), "lng": lng5, "lnb": lnb5, "w_in": A(ffn_w_in[1, 1]), "w_outf": A(ffn_w_out[1, 1])})
    r = _run(nc5, ins)
    out = np.empty_like(x)
    for c in range(8):
        b, t0 = _tok(c)
        out[b, t0:t0 + T] = r[c]["yT"].T
    return out
```

```python
import numpy as np
from concourse.bass_utils import run_bass_kernel_spmd
import numpy as np
from contextlib import ExitStack
import concourse.bass as bass
import concourse.mybir as mybir

F32 = mybir.dt.float32
BF16 = mybir.dt.bfloat16
AF = mybir.ActivationFunctionType
ALU = mybir.AluOpType
AX = mybir.AxisListType


class Sched:
    SEM_ROT = 12000
    N_DMA_SEM = 24

    def __init__(self, nc, es):
        self.nc = nc
        self.es = es
        self.eng = {"pe": nc.tensor, "act": nc.scalar, "dve": nc.vector, "pool": nc.gpsimd, "sp": nc.sync}
        self.sems = []
        self.cur = {}
        for e in self.eng:
            self.cur[e] = [self._new_sem(f"p_{e}"), 0]
        self.waited = {e: {} for e in self.eng}
        self.last_w = {}
        self.readers = {}
        self.dma_sems = [[self._new_sem(f"dma{i}"), 0] for i in range(self.N_DMA_SEM)]
        self.dma_rr = 0
        self.n_inst = {e: 0 for e in self.eng}
        self.n_wait = {e: 0 for e in self.eng}
        self.pending = {e: False for e in self.eng}
        self.clock = {e: {} for e in self.eng}
        self.tclock = {}
        self.n_fused = {}

    def _new_sem(self, name):
        h = self.es.enter_context(self.nc.semaphore(f"{name}_{len(self.sems)}"))
        self.sems.append(h)
        return len(self.sems) - 1

    def _need(self, e, ticket):
        if ticket is None:
            return False
        sid, val = ticket
        ck = self.clock.setdefault(e, {})
        if ck.get(sid, 0) >= val:
            return False
        for e2, c in self.cur.items():
            if c[0] == sid and val > c[1]:
                assert e2 == e, f"{e} waits on pending ticket of {e2}"
                return False
        ck[sid] = val
        snap = self.tclock.get(ticket)
        if snap:
            for s2, v2 in snap.items():
                if ck.get(s2, 0) < v2:
                    ck[s2] = v2
        return True

    def _wait(self, e, ticket):
        if self._need(e, ticket):
            self.eng[e].wait_ge(self.sems[ticket[0]], ticket[1])
            self.n_wait[e] += 1

    def fence(self):
        ts = []
        for e2, c in self.cur.items():
            assert not self.pending[e2]
            if c[1] > 0:
                ts.append((c[0], c[1]))
        for s_ in self.dma_sems:
            if s_[1] > 0:
                ts.append((s_[0], s_[1]))
        self.fence_tickets = ts
        self.fence_done = set()

    def _collect(self, e, reads, writes):
        rd, wr = [], []
        if getattr(self, "fence_tickets", None) and e not in self.fence_done:
            self.fence_done.add(e)
            for t in self.fence_tickets:
                if self._need(e, t):
                    rd.append(t)
        for k in reads:
            t = self.last_w.get(k)
            if self._need(e, t):
                rd.append(t)
        for k in writes:
            t = self.last_w.get(k)
            if self._need(e, t):
                wr.append(t)
            for sid, val in self.readers.get(k, {}).items():
                if self._need(e, (sid, val)):
                    wr.append((sid, val))
        return rd, wr

    def _deps(self, e, reads, writes):
        rd, wr = self._collect(e, reads, writes)
        for t in rd + wr:
            self.eng[e].wait_ge(self.sems[t[0]], t[1])
            self.n_wait[e] += 1

    def _record(self, ticket, reads, writes):
        sid, val = ticket
        for k in reads:
            r = self.readers.setdefault(k, {})
            if r.get(sid, 0) < val:
                r[sid] = val
        for k in writes:
            self.last_w[k] = ticket
            self.readers[k] = {}

    def op(self, e, fn, reads=(), writes=(), signal=True):
        rd, wr = self._collect(e, reads, writes)
        fuse = None
        if e == "pe":
            if wr:
                fuse = wr.pop()
        elif e in ("act", "dve", "pool"):
            if wr:
                fuse = wr.pop()
            elif rd:
                fuse = rd.pop()
        for t in rd + wr:
            self.eng[e].wait_ge(self.sems[t[0]], t[1])
            self.n_wait[e] += 1
        c = self.cur[e]
        if signal and c[1] >= self.SEM_ROT and not self.pending[e]:
            c[0] = self._new_sem(f"p_{e}")
            c[1] = 0
        inst = fn()
        if fuse is not None:
            inst._wait_ge(self.sems[fuse[0]], fuse[1])
            self.n_fused[e] = self.n_fused.get(e, 0) + 1
        self.n_inst[e] += 1
        if signal:
            c[1] += 1
            inst.then_inc(self.sems[c[0]], 1)
            ticket = (c[0], c[1])
            self.pending[e] = False
            self.tclock[ticket] = dict(self.clock.get(e, {}))
        else:
            ticket = (c[0], c[1] + 1)
            self.pending[e] = True
        self._record(ticket, reads, writes)
        return ticket

    def dma(self, q, out, in_, reads=(), writes=(), **kw):
        self._deps(q, reads, writes)
        if q == "pool":
            slot = [self._new_sem("swdma"), 0]
            self.dma_sems.append(slot)
        else:
            slot = self.dma_sems[self.dma_rr]
            self.dma_rr = (self.dma_rr + 1) % self.N_DMA_SEM
        if slot[1] > 0:
            self._wait(q, (slot[0], slot[1]))
        slot[1] += 16
        self.eng[q].dma_start(out=out, in_=in_, **kw).then_inc(self.sems[slot[0]], 16)
        self.n_inst[q] += 1
        ticket = (slot[0], slot[1])
        self.tclock[ticket] = dict(self.clock.get(q, {}))
        self._record(ticket, reads, writes)
        return ticket

    def finish(self, e="sp"):
        for k, t in list(self.last_w.items()):
            self._wait(e, t)
        for s in self.dma_sems:
            if s[1] > 0:
                self._wait(e, (s[0], s[1]))


D = 1024; FF = 2816; NFC = 22
GROUPS = [(0, 4), (4, 4), (8, 4), (12, 4), (16, 4), (20, 2)]
ALPHA = 4 ** 0.25
LN_EPS = 1e-5


class FfnBufs:
    def __init__(self, nc, es, T, with_ffn=True):
        self.T = T
        self.NT = T // 512
        sb = lambda n, s, d: es.enter_context(nc.sbuf_tensor(n, s, d))
        if with_ffn:
            self.alloc_ffn(nc, es)
        self.sq = [sb(f"sq{i}", [128, 8, 512], BF16) for i in range(1)]
        self.yb = [sb(f"yb{i}", [128, 8, 512], BF16) for i in range(1)]
        self.mean_s = sb("mean_s", [128, 512], F32)
        self.m2 = sb("m2", [128, 512], F32)
        self.rstd = sb("rstd", [128, 512], F32)
        self.t1 = [sb(f"t1_{i}", [128, 512], F32) for i in range(2)]
        self.t2 = [sb(f"t2_{i}", [128, 512], F32) for i in range(2)]
        self.ones = sb("ones", [128, 128], BF16)
        self.gcol = sb("gcol", [128, 8], F32)
        self.bcol = sb("bcol", [128, 8], F32)
        self.epsc = sb("epsc", [128, 1], F32)
        self.wslot = 0

    def alloc_ffn(self, nc, es):
        sb = lambda n, s, d: es.enter_context(nc.sbuf_tensor(n, s, d))
        T = self.T
        self.hh = sb("hh", [128, 4, T], BF16)
        self.wi = [sb(f"wi{i}", [128, 2, 8, 512], BF16) for i in range(2)]
        self.wo = [sb(f"wo{i}", [128, 4, 1024], BF16) for i in range(2)]
        self.sg = [sb(f"sg{i}", [128, 512], F32) for i in range(2)]

    def init_consts(self, S, nc):
        S.op("dve", lambda: nc.vector.memset(self.ones[:], 1.0 / 1024), writes=[("ones",)])
        S.op("dve", lambda: nc.vector.memset(self.epsc[:], LN_EPS), writes=[("epsc",)])


def emit_ffn_ln(S, nc, B, xT, xb, ps, w_in, w_out, ln_g, ln_b, tag, ntiles=None):
    NT = B.NT if ntiles is None else ntiles
    tsl = lambda tt: slice(tt * 512, (tt + 1) * 512)
    w_in_v = w_in.rearrange("(c p) (u f) -> p c u f", p=128, u=2)
    w_out_v = w_out.rearrange("(j p) d -> p j d", p=128)

    def load_group(g):
        f0, n = GROUPS[g]
        slot = B.wslot; B.wslot ^= 1
        for u in range(2):
            S.dma("pool", B.wi[slot][:, u, :, 0:n * 128], w_in_v[:, :, u, f0 * 128:(f0 + n) * 128],
                  writes=[("wi", slot, u)])
        S.dma("pool", B.wo[slot][:, 0:n, :], w_out_v[:, f0:f0 + n, :], writes=[("wo", slot)])
        return slot

    slots = {0: load_group(0)}
    for tt in range(NT):
        for c in range(8):
            S.op("act", lambda: nc.scalar.activation(out=xT[:, c, tsl(tt)], in_=xT[:, c, tsl(tt)], func=AF.Identity, scale=float(ALPHA)),
                 reads=[("xT", c, tt)], writes=[("xT", c, tt)])
    pa = 0
    pb = 0
    for g in range(len(GROUPS)):
        f0, n = GROUPS[g]
        slot = slots[g]
        if g + 1 < len(GROUPS):
            slots[g + 1] = load_group(g + 1)
        for tt in range(NT):
            for j in range(n):
                bg = 2 * pa; bu = 2 * pa + 1; pa ^= 1
                for (bank, u) in ((bg, 0), (bu, 1)):
                    for c in range(8):
                        S.op("pe", lambda: nc.tensor.matmul(ps[:, bank, :], B.wi[slot][:, u, c, j * 128:(j + 1) * 128], xb[:, c, tsl(tt)], start=(c == 0), stop=(c == 7)),
                             reads=[("wi", slot, u), ("xb", c, tt)], writes=[("ps", bank)], signal=(c == 7))
                sgi = (tt * n + j) % 2
                S.op("act", lambda: nc.scalar.activation(out=B.sg[sgi][:], in_=ps[:, bg, :], func=AF.Silu),
                     reads=[("ps", bg)], writes=[("sg", sgi)])
                S.op("dve", lambda: nc.vector.tensor_tensor(out=B.hh[:, j, tsl(tt)], in0=ps[:, bu, :], in1=B.sg[sgi][:], op=ALU.mult),
                     reads=[("ps", bu), ("sg", sgi)], writes=[("hh", j, tt)])
        for tt in range(NT):
            for dc in range(8):
                bank = 4 + pb; pb ^= 1
                for j in range(n):
                    S.op("pe", lambda: nc.tensor.matmul(ps[:, bank, :], B.wo[slot][:, j, dc * 128:(dc + 1) * 128], B.hh[:, j, tsl(tt)], start=(j == 0), stop=(j == n - 1)),
                         reads=[("wo", slot), ("hh", j, tt)], writes=[("ps", bank)], signal=(j == n - 1))
                S.op("dve", lambda: nc.vector.scalar_tensor_tensor(out=xT[:, dc, tsl(tt)], in0=ps[:, bank, :], scalar=0.5, in1=xT[:, dc, tsl(tt)], op0=ALU.mult, op1=ALU.add),
                     reads=[("ps", bank), ("xT", dc, tt)], writes=[("xT", dc, tt)])
    emit_ln(S, nc, B, xT, xb, ps, ln_g, ln_b, NT)


def emit_ln(S, nc, B, xT, xb, ps, ln_g, ln_b, NT, xb_tiles=None):
    tsl = lambda tt: slice(tt * 512, (tt + 1) * 512)
    S.dma("sp", B.gcol[:], ln_g.rearrange("(c p) -> p c", p=128), writes=[("gcol",)], allow_slow_non_contiguous=True)
    S.dma("sp", B.bcol[:], ln_b.rearrange("(c p) -> p c", p=128), writes=[("bcol",)], allow_slow_non_contiguous=True)
    for tt in range(NT):
        i2 = 0
        for c in range(8):
            S.op("act", lambda: nc.scalar.activation(out=B.sq[i2][:, c, :], in_=xT[:, c, tsl(tt)], func=AF.Square),
                 reads=[("xT", c, tt)], writes=[("sq", i2, c)])
            S.op("pool", lambda: nc.gpsimd.tensor_copy(out=B.yb[i2][:, c, :], in_=xT[:, c, tsl(tt)]),
                 reads=[("xT", c, tt)], writes=[("yb", i2, c)])
        for c in range(8):
            S.op("pe", lambda: nc.tensor.matmul(ps[:, 6, :], B.ones[:], B.yb[i2][:, c, :], start=(c == 0), stop=(c == 7)),
                 reads=[("ones",), ("yb", i2, c)], writes=[("ps", 6)], signal=(c == 7))
        for c in range(8):
            S.op("pe", lambda: nc.tensor.matmul(ps[:, 7, :], B.ones[:], B.sq[i2][:, c, :], start=(c == 0), stop=(c == 7)),
                 reads=[("ones",), ("sq", i2, c)], writes=[("ps", 7)], signal=(c == 7))
        S.op("act", lambda: nc.scalar.copy(out=B.mean_s[:], in_=ps[:, 6, :]), reads=[("ps", 6)], writes=[("mean_s",)])
        S.op("dve", lambda: nc.vector.tensor_tensor(out=B.m2[:], in0=ps[:, 6, :], in1=B.mean_s[:], op=ALU.mult),
             reads=[("ps", 6), ("mean_s",)], writes=[("m2",)])
        S.op("dve", lambda: nc.vector.tensor_tensor(out=B.m2[:], in0=ps[:, 7, :], in1=B.m2[:], op=ALU.subtract),
             reads=[("ps", 7), ("m2",)], writes=[("m2",)])
        S.op("act", lambda: nc.scalar.activation(out=B.m2[:], in_=B.m2[:], func=AF.Ln, bias=B.epsc[:], scale=1.0),
             reads=[("m2",), ("epsc",)], writes=[("m2",)])
        S.op("act", lambda: nc.scalar.activation(out=B.rstd[:], in_=B.m2[:], func=AF.Exp, scale=-0.5), reads=[("m2",)], writes=[("rstd",)])
        for c in range(8):
            k = c % 2
            S.op("dve", lambda: nc.vector.tensor_tensor(out=B.t1[k][:], in0=xT[:, c, tsl(tt)], in1=ps[:, 6, :], op=ALU.subtract),
                 reads=[("xT", c, tt), ("ps", 6)], writes=[("t1", k)])
            S.op("pool" if c % 2 else "dve", lambda: (nc.gpsimd if c % 2 else nc.vector).tensor_tensor(out=B.t2[k][:], in0=B.t1[k][:], in1=B.rstd[:], op=ALU.mult),
                 reads=[("t1", k), ("rstd",)], writes=[("t2", k)])
            S.op("act", lambda: nc.scalar.activation(out=xT[:, c, tsl(tt)], in_=B.t2[k][:], func=AF.Identity, bias=B.bcol[:, c:c + 1], scale=B.gcol[:, c:c + 1]),
                 reads=[("t2", k), ("gcol",), ("bcol",)], writes=[("xT", c, tt)])
            S.op("act", lambda: nc.scalar.activation(out=xb[:, c, tsl(tt)], in_=B.t2[k][:], func=AF.Identity, bias=B.bcol[:, c:c + 1], scale=B.gcol[:, c:c + 1]),
                 reads=[("t2", k), ("gcol",), ("bcol",)], writes=[("xb", c, tt)])


def build_ffn_prog(T, n_ffn):
    nc = bass.Bass("TRN2", target_bir_lowering=False)
    xT_d = nc.dram_tensor("xT", [D, T], F32, kind="ExternalInput").ap()
    w_in = nc.dram_tensor("w_in", [n_ffn, D, 2 * FF], F32, kind="ExternalInput").ap()
    w_out = nc.dram_tensor("w_out", [n_ffn, FF, D], F32, kind="ExternalInput").ap()
    lng = nc.dram_tensor("lng", [n_ffn, D], F32, kind="ExternalInput").ap()
    lnb = nc.dram_tensor("lnb", [n_ffn, D], F32, kind="ExternalInput").ap()
    yT_d = nc.dram_tensor("yT", [D, T], F32, kind="ExternalOutput").ap()
    with ExitStack() as es:
        S = Sched(nc, es)
        xT = es.enter_context(nc.sbuf_tensor("xTs", [128, 8, T], F32))
        xb = es.enter_context(nc.sbuf_tensor("xbs", [128, 8, T], BF16))
        ps = es.enter_context(nc.psum_tensor("ps", [128, 8, 512], F32))
        B = FfnBufs(nc, es, T)
        NT = T // 512
        B.init_consts(S, nc)
        xv = xT_d.rearrange("(c p) t -> p c t", p=128)
        yv = yT_d.rearrange("(c p) t -> p c t", p=128)
        for tt in range(NT):
            S.dma("sp", xT[:, :, tt * 512:(tt + 1) * 512], xv[:, :, tt * 512:(tt + 1) * 512],
                  writes=[("xT", c, tt) for c in range(8)])
            for c in range(8):
                S.op("act", lambda: nc.scalar.copy(out=xb[:, c, tt * 512:(tt + 1) * 512], in_=xT[:, c, tt * 512:(tt + 1) * 512]),
                     reads=[("xT", c, tt)], writes=[("xb", c, tt)])
        for i in range(n_ffn):
            emit_ffn_ln(S, nc, B, xT, xb, ps, w_in[i], w_out[i], lng[i], lnb[i], f"f{i}")
        for tt in range(NT):
            S.dma("sp", yv[:, :, tt * 512:(tt + 1) * 512], xT[:, :, tt * 512:(tt + 1) * 512],
                  reads=[("xT", c, tt) for c in range(8)])
        S.finish("sp")
    return nc


NH = 16; HD = 64; NHP = 8


def na_pat(r, NR):
    if r < 4:
        return 1 + r
    if r >= NR - 3:
        return 5 + (r - (NR - 3))
    return 0


def na_halo_rows(r0, NR, rows):
    G = []
    for L in range(NR + 8):
        g = r0 - 4 + L
        if g < 0:
            g = g + 8
        elif g >= rows:
            g = rows - 8 + (g - rows)
        g = min(max(g, 0), rows - 1)
        G.append(g)
    return G


def na_tables(rpb, r0, NR, rows):
    G = np.array(na_halo_rows(r0, NR, rows))
    reps = {0: min(NR // 2, NR - 4)}
    for r in range(NR):
        p = na_pat(r, NR)
        if p != 0:
            reps[p] = r
    tab = np.empty((NHP, 8, 128, 4, 128), np.float32)
    qc = np.arange(64)
    qstart = np.clip(qc - 8, 0, 48)
    for p in range(8):
        r = reps.get(p, reps[0])
        R = r0 + r
        rs = min(max(R - 4, 0), rows - 8)
        kL = r + np.arange(8)
        kG = G[kL]
        row_ok = (kG >= rs) & (kG < rs + 8)
        dr = np.clip(kG - R + 7, 0, 14)
        kcol = np.arange(64)
        col_ok = (kcol[None, :] >= qstart[:, None]) & (kcol[None, :] < qstart[:, None] + 16)
        dc = np.clip(kcol[None, :] - qc[:, None] + 15, 0, 30)
        b = rpb[:, dr][:, :, dc]
        b = np.transpose(b, (0, 1, 3, 2))
        ok = row_ok[:, None, None] & np.transpose(col_ok)[None, :, :]
        b = np.where(ok[None], b, np.float32(-1e30)).astype(np.float32)
        b = b.reshape(NHP, 2, 512, 64)
        b = b.reshape(NHP, 2, 4, 128, 64)
        tab[:, p] = np.transpose(b, (0, 3, 2, 1, 4)).reshape(NHP, 128, 4, 128)
    return tab


def emit_na(S, nc, es, ps, xh_d, x_own_d, w_qkv, b_qkv, tab_d, w_o, b_o, ln_g, ln_b, NR, yT_d):
    T = NR * 64; TH = (NR + 8) * 64
    NT = T // 512
    NTH = TH // 512
    sb = lambda es_, n, s, d: es_.enter_context(nc.sbuf_tensor(n, s, d))
    oT = sb(es, "oT", [128, 8, T], BF16)
    tsl = lambda tt: slice(tt * 512, (tt + 1) * 512)
    with ExitStack() as es2:
        xb = sb(es2, "na_xb", [128, 8, TH], BF16)
        tabs = sb(es2, "na_tab", [128, 8, 4, 128], F32)
        KT = [sb(es2, f"na_KT{i}", [128, TH], BF16) for i in range(2)]
        Ve4 = sb(es2, "na_Ve4", [128, TH // 128, 512], BF16)
        Vo4 = sb(es2, "na_Vo4", [128, TH // 128, 512], BF16)
        wv4 = sb(es2, "na_wv4", [128, 8, 512], BF16)
        QBD = [sb(es2, f"na_Q{i}", [128, NR, 2, 64], BF16) for i in range(2)]
        wq = [sb(es2, f"na_wq{i}", [128, 2, 8, 128], BF16) for i in range(2)]
        sbt = [sb(es2, f"na_sb{i}", [128, 512], F32) for i in range(4)]
        PT = [sb(es2, f"na_PT{i}", [128, 512], BF16) for i in range(4)]
        rc = [sb(es2, f"na_rc{i}", [128, 128], F32) for i in range(4)]
        bcols = sb(es2, "na_bc", [128, 24], F32)
        bvrow = sb(es2, "na_bvrow", [1, 1024], BF16)
        bvb = sb(es2, "na_bvb", [128, 1024], F32)
        ones_r = sb(es2, "na_ones_r", [1, 128], BF16)
        ones_k = sb(es2, "na_ones_k", [128, 128], BF16)

        S.op("dve", lambda: nc.vector.memset(ones_r[:], 1.0), writes=[("ones_r",)])
        S.op("dve", lambda: nc.vector.memset(ones_k[:], 1.0), writes=[("ones_k",)])
        for i in range(2):
            S.op("pool", lambda: nc.gpsimd.memset(QBD[i][:], 0.0), writes=[("QBD", i)])
        S.dma("sp", bcols[:], b_qkv.rearrange("(j p) -> p j", p=128), writes=[("bcols",)], allow_slow_non_contiguous=True)
        S.dma("pool", bvrow[:], b_qkv[2048:3072].rearrange("(o n) -> o n", o=1), writes=[("bvrow",)])
        xhv = xh_d.rearrange("(c p) t -> p c t", p=128)
        for tt in range(NTH):
            S.dma("pool", xb[:, :, tsl(tt)], xhv[:, :, tsl(tt)], writes=[("nxb", tt)])
        for h2 in range(2):
            S.op("pe", lambda: nc.tensor.matmul(ps[:, 6, :], ones_r[0:1, :], bvrow[0:1, h2 * 512:(h2 + 1) * 512], start=True, stop=True),
                 reads=[("ones_r",), ("bvrow",)], writes=[("ps", 6)])
            S.op("act", lambda: nc.scalar.copy(out=bvb[:, h2 * 512:(h2 + 1) * 512], in_=ps[:, 6, :]), reads=[("ps", 6)], writes=[("bvb", h2)])
        wv = w_qkv.rearrange("(c p) n -> p c n", p=128)
        tabv = tab_d

        def load_w(hp, slot):
            for k in range(2):
                S.dma("pool", wq[slot][:, k, :, :], wv[:, :, k * 1024 + hp * 128:k * 1024 + (hp + 1) * 128], writes=[("wq", slot, k)])

        load_w(0, 0)
        pa = 0
        for hp in range(NHP):
            slot = hp % 2
            if hp + 1 < NHP:
                load_w(hp + 1, 1 - slot)
            S.dma("sp", tabs[:].rearrange("p a k n -> p a (k n)"), tabv[hp].rearrange("a p k n -> p a (k n)"), writes=[("tabs",)])
            for tt in range(NTH):
                bank = pa; pa = (pa + 1) % 4
                for c in range(8):
                    S.op("pe", lambda: nc.tensor.matmul(ps[:, bank, :], wq[slot][:, 1, c, :], xb[:, c, tsl(tt)], start=(c == 0), stop=(c == 7)),
                         reads=[("wq", slot, 1), ("nxb", tt)], writes=[("ps", bank)], signal=(c == 7))
                S.op("act", lambda: nc.scalar.activation(out=KT[slot][:, tsl(tt)], in_=ps[:, bank, :], func=AF.Identity, bias=bcols[:, 8 + hp:9 + hp], scale=1.0),
                     reads=[("ps", bank), ("bcols",)], writes=[("KT", slot, tt)])
            for tt in range(NT):
                bank = pa; pa = (pa + 1) % 4
                for c in range(8):
                    S.op("pe", lambda: nc.tensor.matmul(ps[:, bank, :], wq[slot][:, 0, c, :], xb[:, c, 256 + tt * 512:256 + (tt + 1) * 512], start=(c == 0), stop=(c == 7)),
                         reads=[("wq", slot, 0)] + [("nxb", t2) for t2 in range(NTH)], writes=[("ps", bank)], signal=(c == 7))
                for hd in range(2):
                    pr = slice(hd * 64, (hd + 1) * 64)
                    S.op("dve", lambda: nc.vector.tensor_scalar(out=QBD[slot][pr, tt * 8:(tt + 1) * 8, hd, :], in0=ps[pr, bank, :].rearrange("p (r q) -> p r q", q=64),
                                                                scalar1=bcols[pr, hp:hp + 1], scalar2=0.125, op0=ALU.add, op1=ALU.mult),
                         reads=[("ps", bank), ("bcols",)], writes=[("QBD", slot)])
            nch = TH // 128
            if hp % 4 == 0:
                hg = hp // 4
                S.dma("pool", wv4[:], wv[:, :, 2048 + hg * 512:2048 + (hg + 1) * 512], writes=[("wv4",)])
                for (Vx, off, cnt, nm) in ((Ve4, 0, nch, "Ve"), (Vo4, 64, nch - 1, "Vo")):
                    for j in range(cnt):
                        bank = pa; pa = (pa + 1) % 4
                        for c in range(8):
                            S.op("pe", lambda: nc.tensor.matmul(ps[:, bank, :], xb[:, c, off + j * 128:off + (j + 1) * 128], wv4[:, c, :], start=(c == 0), stop=(c == 7)),
                                 reads=[("wv4",)] + [("nxb", t2) for t2 in range(NTH)], writes=[("ps", bank)], signal=(c == 7))
                        S.op("dve", lambda: nc.vector.tensor_tensor(out=Vx[:, j, :], in0=ps[:, bank, :], in1=bvb[:, hg * 512:(hg + 1) * 512], op=ALU.add),
                             reads=[("ps", bank), ("bvb", hg)], writes=[(nm, j)])
            def stage1(r):
                nonlocal pa
                pat = na_pat(r, NR)
                i2 = r % 4
                bankS = pa; pa = (pa + 1) % 4
                tok0 = r * 64
                for kc in range(4):
                    S.op("pe", lambda: nc.tensor.matmul(ps[:, bankS, kc * 128:(kc + 1) * 128], KT[slot][:, tok0 + kc * 128:tok0 + (kc + 1) * 128], QBD[slot][:, r, :, :].rearrange("p a q -> p (a q)"), start=True, stop=True),
                         reads=[("KT", slot, t2) for t2 in range(NTH)] + [("QBD", slot)], writes=[("ps", bankS)], signal=(kc == 3))
                S.op("dve", lambda: nc.vector.tensor_tensor(out=sbt[i2][:], in0=ps[:, bankS, :], in1=tabs[:, pat, :, :].rearrange("p k n -> p (k n)"), op=ALU.add),
                     reads=[("ps", bankS), ("tabs",)], writes=[("sbt", i2)])
                S.op("act", lambda: nc.scalar.activation(out=PT[i2][:], in_=sbt[i2][:], func=AF.Exp),
                     reads=[("sbt", i2)], writes=[("PT", i2)])

            def stage2(r):
                i2 = r % 4
                bankO = 4 + i2
                if r % 2 == 0:
                    Vx, j0, nm = Ve4, r // 2, "Ve"
                else:
                    Vx, j0, nm = Vo4, (r - 1) // 2, "Vo"
                h4 = hp % 4
                for kc in range(4):
                    S.op("pe", lambda: nc.tensor.matmul(ps[:, bankO, 0:128], Vx[:, j0 + kc, h4 * 128:(h4 + 1) * 128], PT[i2][:, kc * 128:(kc + 1) * 128], start=(kc == 0), stop=(kc == 3)),
                         reads=[(nm, j0 + kc), ("PT", i2)], writes=[("ps", bankO)], signal=False)
                for kc in range(4):
                    S.op("pe", lambda: nc.tensor.matmul(ps[:, bankO, 128:256], ones_k[:], PT[i2][:, kc * 128:(kc + 1) * 128], start=(kc == 0), stop=(kc == 3)),
                         reads=[("ones_k",), ("PT", i2)], writes=[("ps", bankO)], signal=(kc == 3))
                S.op("act", lambda: nc.scalar.activation(out=rc[i2][:], in_=ps[:, bankO, 128:256], func=AF.Ln),
                     reads=[("ps", bankO)], writes=[("rc", i2)])
                S.op("act", lambda: nc.scalar.activation(out=rc[i2][:], in_=rc[i2][:], func=AF.Exp, scale=-1.0),
                     reads=[("rc", i2)], writes=[("rc", i2)])
                for hd in range(2):
                    pr = slice(hd * 64, (hd + 1) * 64)
                    S.op("dve", lambda: nc.vector.tensor_tensor(out=oT[pr, hp, r * 64:(r + 1) * 64], in0=ps[pr, bankO, hd * 64:(hd + 1) * 64], in1=rc[i2][pr, hd * 64:(hd + 1) * 64], op=ALU.mult),
                         reads=[("ps", bankO), ("rc", i2)], writes=[("oT", hp, r // 8)])

            stage1(0)
            if NR > 1:
                stage1(1)
            for r in range(NR):
                if r + 2 < NR:
                    stage1(r + 2)
                stage2(r)
    S.fence()
    with ExitStack() as es3:
        xT = sb(es3, "xTs", [128, 8, T], F32)
        xbo = sb(es3, "xbs", [128, 8, T], BF16)
        LB = FfnBufs(nc, es3, T, with_ffn=False)
        LB.init_consts(S, nc)
        wo = sb(es3, "na_wo", [128, 8, 1024], BF16)
        bo = sb(es3, "na_bo", [128, 8], F32)
        S.dma("pool", wo[:], w_o.rearrange("(h p) n -> p h n", p=128), writes=[("nwo",)])
        S.dma("sp", bo[:], b_o.rearrange("(c p) -> p c", p=128), writes=[("nbo",)], allow_slow_non_contiguous=True)
        xov = x_own_d.rearrange("(c p) t -> p c t", p=128)
        pb = 0
        for tt in range(NT):
            S.dma("sp", xT[:, :, tsl(tt)], xov[:, :, tsl(tt)], writes=[("xT", c, tt) for c in range(8)])
            for dc in range(8):
                bank = pb; pb = (pb + 1) % 4
                for hp in range(8):
                    S.op("pe", lambda: nc.tensor.matmul(ps[:, bank, :], wo[:, hp, dc * 128:(dc + 1) * 128], oT[:, hp, tsl(tt)], start=(hp == 0), stop=(hp == 7)),
                         reads=[("nwo",), ("oT", hp, tt)], writes=[("ps", bank)], signal=(hp == 7))
                S.op("pool", lambda: nc.gpsimd.tensor_scalar(out=xT[:, dc, tsl(tt)], in0=xT[:, dc, tsl(tt)], scalar1=float(ALPHA), scalar2=bo[:, dc:dc + 1], op0=ALU.mult, op1=ALU.add),
                     reads=[("xT", dc, tt), ("nbo",)], writes=[("xT", dc, tt)])
                S.op("dve", lambda: nc.vector.tensor_tensor(out=xT[:, dc, tsl(tt)], in0=ps[:, bank, :], in1=xT[:, dc, tsl(tt)], op=ALU.add),
                     reads=[("ps", bank), ("xT", dc, tt)], writes=[("xT", dc, tt)])
        emit_ln(S, nc, LB, xT, xbo, ps, ln_g, ln_b, NT)
        yv = yT_d.rearrange("(c p) t -> p c t", p=128)
        for tt in range(NT):
            S.dma("sp", yv[:, :, tsl(tt)], xT[:, :, tsl(tt)], reads=[("xT", c, tt) for c in range(8)])
        S.finish("sp")


def build_na_prog(NR):
    T = NR * 64; TH = (NR + 8) * 64
    nc = bass.Bass("TRN2", target_bir_lowering=False)
    dt = lambda n, s: nc.dram_tensor(n, s, F32, kind="ExternalInput").ap()
    xh = dt("xh", [D, TH]); xo = dt("xo", [D, T])
    w_qkv = dt("w_qkv", [D, 3 * D]); b_qkv = dt("b_qkv", [3 * D]); tab = dt("tab", [NHP, 8, 128, 4, 128])
    w_o = dt("w_o", [D, D]); b_o = dt("b_o", [D]); lng = dt("lng", [D]); lnb = dt("lnb", [D])
    yT_d = nc.dram_tensor("yT", [D, T], F32, kind="ExternalOutput").ap()
    with ExitStack() as es:
        S = Sched(nc, es)
        ps = es.enter_context(nc.psum_tensor("ps", [128, 8, 512], F32))
        emit_na(S, nc, es, ps, xh, xo, w_qkv, b_qkv, tab, w_o, b_o, lng, lnb, NR, yT_d)
    return nc


C0 = float(np.exp(-0.5))
CH = 64


def rw_consts():
    s = np.arange(128)[:, None]; t = np.arange(128)[None, :]
    same = (s // 64) == (t // 64)
    Sm = (same & (t > s)).astype(np.float32)
    Im = (same & (t >= s)).astype(np.float32)
    maskSI = np.concatenate([Sm, Im, Sm, Im], axis=1)
    maskTS = (same & (t < s)).astype(np.float32)
    ident = np.eye(128, dtype=np.float32)
    blk = same.astype(np.float32)
    return np.concatenate([maskSI, maskTS, ident, blk], axis=1)


STOP = ""


def emit_rw1(S, nc, es, ps, d, SL, TS=256, SD=BF16):
    NTI = SL // TS
    NCK = TS // CH
    NPR = TS // 128
    sb = lambda n, s, dt: es.enter_context(nc.sbuf_tensor(n, s, dt))
    cst = sb("rw_cst", [128, 896], F32)
    S.dma("sp", cst[:], d["consts"], writes=[("cst",)])
    maskSI = cst[:, 0:512]; maskTS = cst[:, 512:640]; identf = cst[:, 640:768]; blkf = cst[:, 768:896]
    ident_s = identf
    if SD != F32:
        ident_sd = sb("rw_identsd", [128, 128], SD)
        S.op("dve", lambda: nc.vector.tensor_copy(out=ident_sd[:], in_=identf), reads=[("cst",)], writes=[("identsd",)])
        ident_s = ident_sd[:]
    ones64 = sb("rw_ones64", [128, CH], F32)
    S.op("dve", lambda: nc.vector.memset(ones64[:], 1.0), writes=[("ones64",)])
    mixc = sb("rw_mixc", [128, 6, 8], F32)
    S.dma("sp", mixc[:], d["mix"].rearrange("i (c p) -> p i c", p=128), writes=[("mixc",)], allow_slow_non_contiguous=True)
    cols = sb("rw_cols", [128, 5, 4], F32)
    for i, nm in enumerate(["w0", "a0", "k_k", "k_a", "r_k"]):
        S.dma("sp", cols[:, i, :], d[nm].rearrange("(c p) -> p c", p=128), writes=[("cols", i)], allow_slow_non_contiguous=True)
    W3 = sb("rw_W3", [128, 3, 8, 512], BF16)
    for i, nm in enumerate(["w_r", "w_k", "w_v"]):
        S.dma("pool", W3[:, i, :, :], d[nm].rearrange("(c p) n -> p c n", p=128), writes=[("W3", i)])
    w1b = sb("rw_w1b", [128, 8, 64], BF16); a1b = sb("rw_a1b", [128, 8, 64], BF16); g1b = sb("rw_g1b", [128, 8, 160], BF16)
    S.dma("pool", w1b[:], d["w1"].rearrange("(c p) n -> p c n", p=128), writes=[("w1b",)])
    S.dma("pool", a1b[:], d["a1"].rearrange("(c p) n -> p c n", p=128), writes=[("a1b",)])
    S.dma("pool", g1b[:], d["g1"].rearrange("(c p) n -> p c n", p=128), writes=[("g1b",)])
    w2b = sb("rw_w2b", [64, 512], BF16); a2b = sb("rw_a2b", [64, 512], BF16)
    g2a = sb("rw_g2a", [128, 512], BF16); g2b = sb("rw_g2b", [128, 512], BF16)
    S.dma("pool", w2b[:], d["w2"], writes=[("w2b",)])
    S.dma("pool", a2b[:], d["a2"], writes=[("a2b",)])
    S.dma("pool", g2a[:], d["g2"][0:128, :], writes=[("g2a",)])
    S.op("dve", lambda: nc.vector.memset(g2b[:], 0.0), writes=[("g2b",)])
    S.dma("pool", g2b[0:32, :], d["g2"][128:160, :], writes=[("g2b",)])
    xt = [sb("rw_xt0", [128, 8, TS + 2], F32)] * 2
    xs = sb("rw_xs", [128, TS], F32)
    xx = sb("rw_xx", [128, TS], F32)
    xm = sb("rw_xm", [128, 6, 8, TS], BF16)
    hw = sb("rw_hw", [64, TS], BF16); ha = sb("rw_ha", [64, TS], BF16)
    hga = sb("rw_hga", [128, TS], BF16); hgb = sb("rw_hgb", [128, TS], BF16)
    S.op("dve", lambda: nc.vector.memset(hgb[:], 0.0), writes=[("hgb",)])
    ft = lambda n: [sb(f"rw_{n}{i}", [128, TS], F32) for i in range(2)]
    def ft1(n):
        t = sb(f"rw_{n}", [128, TS], F32)
        return [t, t]
    rT = ft("rT"); kT = ft("kT"); vT = ft("vT"); gT = ft1("gT"); sg = ft("sg"); asg = ft("asg")
    kq = ft1("kq"); kq2 = ft1("kq2"); rn = ft1("rn"); kk = ft("kk"); t1 = ft1("t1"); kz = ft("kz"); prod = ft1("prod")
    cs = ft1("cs"); E1 = [[sb(f"rw_E1_{q}_{i}", [128, TS], F32) for i in range(4)] for q in range(2)]; E2 = ft1("E2"); E3 = ft1("E3"); dd = ft1("dd"); bb = ft1("bb")
    af = ft("af")
    BKf = [sb(f"rw_BKf{i}", [128, 2, TS], F32) for i in range(2)]
    Hat = [sb(f"rw_Hat{i}", [128, 2, TS], F32) for i in range(2)]
    RA = [[sb(f"rw_RA{q}_{i}", [128, 2, TS], SD) for i in range(4)] for q in range(2)]
    RAf1 = [[sb(f"rw_RAf{q}_{i}", [128, TS], F32) for i in range(4)] for q in range(2)]
    LBt = [[sb(f"rw_LB{q}_{i}", [128, 2, TS], SD) for i in range(4)] for q in range(2)]
    TM = [[[sb(f"rw_TM{q}_{c}_{p}", [128, 4, 128], SD) for p in range(NPR)] for c in range(4)] for q in range(2)]
    NU = 2 * 2 * NPR
    AM = [sb(f"rw_AM{u}", [128, 512], SD) for u in range(NU)]
    Mk = [[sb(f"rw_M{u}_{i}", [128, 128], SD) for i in range(2)] for u in range(NU)]
    Nk = [[sb(f"rw_N{u}_{i}", [128, 128], SD) for i in range(2)] for u in range(NU)]
    Xs = [sb(f"rw_Xs{u}", [128, 128], SD) for u in range(NU)]
    ATM = [sb(f"rw_ATM{u}", [128, 64], SD) for u in range(NU)]
    Ws = [sb(f"rw_Ws{u}", [128, 64], SD) for u in range(NU)]
    UV = [sb(f"rw_UV{u}", [128, 64], SD) for u in range(NU)]
    GT = [sb(f"rw_GT{c}", [128, NCK, 64], F32) for c in range(4)]
    Hf = [sb(f"rw_Hf{c}", [128, NCK, 64], F32) for c in range(4)]
    RH = [sb(f"rw_RH{c}", [128, TS], F32) for c in range(4)]
    YV = [[sb(f"rw_YV{c}_{p}", [128, 128], F32) for p in range(NPR)] for c in range(4)]
    ST = [sb(f"rw_ST{c}", [128, 64], F32) for c in range(4)]
    yt = [[sb(f"rw_yt{c}_{p}", [128, 128], F32) for p in range(NPR)] for c in range(4)]
    for c in range(4):
        S.op("dve", lambda: nc.vector.memset(ST[c][:], 0.0), writes=[("ST", c, 0), ("ST", c, 1)])

    xv = d["xT"].rearrange("(c p) t -> p c t", p=128)
    pr = [0]

    def bank():
        b = pr[0]; pr[0] = (pr[0] + 1) % 8
        return b

    def gen_abc(ti):
        par = ti % 2
        t0 = ti * TS
        xs_ = 0
        X = xt[xs_]
        lo = t0 - 1 if ti > 0 else t0
        hi = t0 + TS + 1 if ti < NTI - 1 else t0 + TS
        if ti == 0:
            S.op("pool", lambda: nc.gpsimd.memset(X[:, :, 0:1], 0.0), writes=[("xt", xs_)])
        if ti == NTI - 1:
            S.op("pool", lambda: nc.gpsimd.memset(X[:, :, TS + 1:TS + 2], 0.0), writes=[("xt", xs_)])
        S.dma("sp", X[:, :, (lo - t0 + 1):(hi - t0 + 1)], xv[:, :, lo:hi], writes=[("xt", xs_)])
        for c in range(8):
            S.op("pool", lambda: nc.gpsimd.tensor_tensor(out=xs[:], in0=X[:, c, 0:TS], in1=X[:, c, 2:TS + 2], op=ALU.add),
                 reads=[("xt", xs_)], writes=[("xs",)])
            S.op("dve", lambda: nc.vector.scalar_tensor_tensor(out=xx[:], in0=xs[:], scalar=0.5, in1=X[:, c, 1:TS + 1], op0=ALU.mult, op1=ALU.subtract),
                 reads=[("xs",), ("xt", xs_)], writes=[("xx",)])
            for i in range(6):
                S.op("dve", lambda: nc.vector.scalar_tensor_tensor(out=xm[:, i, c, :], in0=xx[:], scalar=mixc[:, i, c:c + 1], in1=X[:, c, 1:TS + 1], op0=ALU.mult, op1=ALU.add),
                     reads=[("xx",), ("xt", xs_), ("mixc",)], writes=[("xm", i, c)])
            yield
        yield
        def proj(out_ap, lhs_fn, mi, keys, M=128):
            for c in range(8):
                S.op("pe", lambda: nc.tensor.matmul(out_ap, lhs_fn(c), xm[:, mi, c, :], start=(c == 0), stop=(c == 7)),
                     reads=keys + [("xm", mi, c)], writes=[("ps", bk)], signal=(c == 7))
        bk = bank()
        proj(ps[0:64, bk, 0:TS], lambda c: w1b[:, c, :], 1, [("w1b",)])
        S.op("act", lambda: nc.scalar.activation(out=hw[:], in_=ps[0:64, bk, 0:TS], func=AF.Tanh), reads=[("ps", bk)], writes=[("hw",)])
        bk = bank()
        proj(ps[0:64, bk, 0:TS], lambda c: a1b[:, c, :], 4, [("a1b",)])
        S.op("act", lambda: nc.scalar.copy(out=ha[:], in_=ps[0:64, bk, 0:TS]), reads=[("ps", bk)], writes=[("ha",)])
        bk = bank()
        proj(ps[:, bk, 0:TS], lambda c: g1b[:, c, 0:128], 5, [("g1b",)])
        S.op("act", lambda: nc.scalar.activation(out=hga[:], in_=ps[:, bk, 0:TS], func=AF.Sigmoid), reads=[("ps", bk)], writes=[("hga",)])
        bk = bank()
        proj(ps[0:32, bk, 0:TS], lambda c: g1b[:, c, 128:160], 5, [("g1b",)])
        S.op("act", lambda: nc.scalar.activation(out=hgb[0:32, :], in_=ps[0:32, bk, 0:TS], func=AF.Sigmoid), reads=[("ps", bk)], writes=[("hgb",)])
        for cc in range(4):
            f = cc % 2
            csl = slice(cc * 128, (cc + 1) * 128)
            tsl = slice(t0, t0 + TS)
            bk = bank(); proj(ps[:, bk, 0:TS], lambda c: W3[:, 0, c, csl], 0, [("W3", 0)])
            S.op("act", lambda: nc.scalar.copy(out=rT[f][:], in_=ps[:, bk, 0:TS]), reads=[("ps", bk)], writes=[("rT", f)])
            yield
            bk = bank(); proj(ps[:, bk, 0:TS], lambda c: W3[:, 1, c, csl], 2, [("W3", 1)])
            S.op("act", lambda: nc.scalar.copy(out=kT[f][:], in_=ps[:, bk, 0:TS]), reads=[("ps", bk)], writes=[("kT", f)])
            yield
            bk = bank(); proj(ps[:, bk, 0:TS], lambda c: W3[:, 2, c, csl], 3, [("W3", 2)])
            S.op("act", lambda: nc.scalar.copy(out=vT[f][:], in_=ps[:, bk, 0:TS]), reads=[("ps", bk)], writes=[("vT", f)])
            S.dma("sp", d["v_out"][csl, tsl], vT[f][:], reads=[("vT", f)])
            bk = bank()
            S.op("pe", lambda: nc.tensor.matmul(ps[:, bk, 0:TS], w2b[:, csl], hw[:], start=True, stop=True), reads=[("w2b",), ("hw",)], writes=[("ps", bk)])
            S.op("act", lambda: nc.scalar.activation(out=sg[f][:], in_=ps[:, bk, 0:TS], func=AF.Sigmoid, bias=cols[:, 0, cc:cc + 1], scale=1.0),
                 reads=[("ps", bk), ("cols", 0)], writes=[("sg", f)])
            bk = bank()
            S.op("pe", lambda: nc.tensor.matmul(ps[:, bk, 0:TS], a2b[:, csl], ha[:], start=True, stop=True), reads=[("a2b",), ("ha",)], writes=[("ps", bk)])
            S.op("act", lambda: nc.scalar.activation(out=asg[f][:], in_=ps[:, bk, 0:TS], func=AF.Sigmoid, bias=cols[:, 1, cc:cc + 1], scale=1.0),
                 reads=[("ps", bk), ("cols", 1)], writes=[("asg", f)])
            bk = bank()
            S.op("pe", lambda: nc.tensor.matmul(ps[:, bk, 0:TS], g2a[:, csl], hga[:], start=True, stop=False), reads=[("g2a",), ("hga",)], writes=[("ps", bk)], signal=False)
            S.op("pe", lambda: nc.tensor.matmul(ps[:, bk, 0:TS], g2b[:, csl], hgb[:], start=False, stop=True), reads=[("g2b",), ("hgb",)], writes=[("ps", bk)])
            S.op("act", lambda: nc.scalar.copy(out=gT[f][:], in_=ps[:, bk, 0:TS]), reads=[("ps", bk)], writes=[("gT", 0)])
            S.dma("sp", d["g_out"][csl, tsl], gT[f][:], reads=[("gT", 0)])
            yield
            S.op("dve", lambda: nc.vector.tensor_scalar(out=kq[f][:], in0=kT[f][:], scalar1=cols[:, 2, cc:cc + 1], scalar2=None, op0=ALU.mult),
                 reads=[("kT", f), ("cols", 2)], writes=[("kq", 0)])
            S.op("pool", lambda: nc.gpsimd.tensor_tensor(out=kq2[f][:], in0=kq[f][:], in1=kq[f][:], op=ALU.mult), reads=[("kq", 0)], writes=[("kq2", 0)])
            bk = bank()
            S.op("pe", lambda: nc.tensor.matmul(ps[:, bk, 0:TS], blkf, kq2[f][:], start=True, stop=True), reads=[("cst",), ("kq2", 0)], writes=[("ps", bk)])
            S.op("dve", lambda: nc.vector.tensor_scalar(out=rn[f][:], in0=ps[:, bk, 0:TS], scalar1=1e-24, scalar2=None, op0=ALU.max),
                 reads=[("ps", bk)], writes=[("rn", 0)])
            S.op("act", lambda: nc.scalar.activation(out=rn[f][:], in_=rn[f][:], func=AF.Ln), reads=[("rn", 0)], writes=[("rn", 0)])
            S.op("act", lambda: nc.scalar.activation(out=rn[f][:], in_=rn[f][:], func=AF.Exp, scale=-0.5), reads=[("rn", 0)], writes=[("rn", 0)])
            S.op("dve", lambda: nc.vector.tensor_tensor(out=kk[f][:], in0=kq[f][:], in1=rn[f][:], op=ALU.mult), reads=[("kq", 0), ("rn", 0)], writes=[("kk", f)])
            S.op("dve", lambda: nc.vector.tensor_scalar(out=t1[f][:], in0=asg[f][:], scalar1=-1.0, scalar2=cols[:, 3, cc:cc + 1], op0=ALU.add, op1=ALU.mult),
                 reads=[("asg", f), ("cols", 3)], writes=[("t1", 0)])
            S.op("dve", lambda: nc.vector.scalar_tensor_tensor(out=kz[f][:], in0=t1[f][:], scalar=1.0, in1=kT[f][:], op0=ALU.add, op1=ALU.mult),
                 reads=[("t1", 0), ("kT", f)], writes=[("kz", f)])
            S.op("dve", lambda: nc.vector.scalar_tensor_tensor(out=prod[f][:], in0=rT[f][:], scalar=cols[:, 4, cc:cc + 1], in1=kz[f][:], op0=ALU.mult, op1=ALU.mult),
                 reads=[("rT", f), ("kz", f), ("cols", 4)], writes=[("prod", 0)])
            S.dma("sp", d["p_out"][csl, tsl], prod[f][:], reads=[("prod", 0)])
            yield
            for ck in range(NCK):
                ksl = slice(ck * CH, (ck + 1) * CH)
                S.op("dve", lambda: nc.vector.tensor_tensor_scan(out=cs[f][:, ksl], data0=ones64[:], data1=sg[f][:, ksl], initial=0.0, op0=ALU.mult, op1=ALU.add),
                     reads=[("sg", f), ("ones64",)], writes=[("cs", 0)])
            S.op("act", lambda: nc.scalar.activation(out=E1[par][cc][:], in_=cs[f][:], func=AF.Exp, scale=-C0), reads=[("cs", 0)], writes=[("E1", par, cc)])
            S.op("act", lambda: nc.scalar.activation(out=E2[f][:], in_=cs[f][:], func=AF.Exp, scale=C0), reads=[("cs", 0)], writes=[("E2", 0)])
            S.op("pool", lambda: nc.gpsimd.tensor_tensor(out=dd[f][:], in0=cs[f][:], in1=sg[f][:], op=ALU.subtract), reads=[("cs", 0), ("sg", f)], writes=[("dd", 0)])
            S.op("act", lambda: nc.scalar.activation(out=E3[f][:], in_=dd[f][:], func=AF.Exp, scale=-C0), reads=[("dd", 0)], writes=[("E3", 0)])
            yield
            S.op("dve", lambda: nc.vector.scalar_tensor_tensor(out=af[f][:], in0=kk[f][:], scalar=-1.0, in1=E3[f][:], op0=ALU.mult, op1=ALU.mult),
                 reads=[("kk", f), ("E3", 0)], writes=[("af", f)])
            S.op("act", lambda: nc.scalar.copy(out=RA[par][cc][:, 0, :], in_=af[f][:]), reads=[("af", f)], writes=[("RA", par, cc, 0)])
            S.op("dve", lambda: nc.vector.tensor_tensor(out=RAf1[par][cc][:], in0=rT[f][:], in1=E1[par][cc][:], op=ALU.mult), reads=[("rT", f), ("E1", par, cc)], writes=[("RAf1", par, cc)])
            S.op("act", lambda: nc.scalar.copy(out=RA[par][cc][:, 1, :], in_=RAf1[par][cc][:]), reads=[("RAf1", par, cc)], writes=[("RA", par, cc, 1)])
            S.op("pool", lambda: nc.gpsimd.tensor_tensor(out=bb[f][:], in0=kk[f][:], in1=asg[f][:], op=ALU.mult), reads=[("kk", f), ("asg", f)], writes=[("bb", 0)])
            S.op("dve", lambda: nc.vector.tensor_tensor(out=BKf[f][:, 0, :], in0=bb[f][:], in1=E2[f][:], op=ALU.mult), reads=[("bb", 0), ("E2", 0)], writes=[("BKf", f, 0)])
            S.op("dve", lambda: nc.vector.tensor_tensor(out=BKf[f][:, 1, :], in0=kz[f][:], in1=E2[f][:], op=ALU.mult), reads=[("kz", f), ("E2", 0)], writes=[("BKf", f, 1)])
            S.op("act", lambda: nc.scalar.copy(out=LBt[par][cc][:], in_=BKf[f][:]), reads=[("BKf", f, 0), ("BKf", f, 1)], writes=[("LBt", par, cc)])
            for ck in range(NCK):
                ksl = slice(ck * CH, (ck + 1) * CH)
                e = ck * CH + CH - 1
                S.op("dve", lambda: nc.vector.tensor_scalar(out=Hat[f][:, :, ksl], in0=BKf[f][:, :, ksl], scalar1=E1[par][cc][:, e:e + 1], scalar2=None, op0=ALU.mult),
                     reads=[("BKf", f, 0), ("BKf", f, 1), ("E1", par, cc)], writes=[("Hat", f)])
            yield
            for p in range(NPR):
                psl = slice(p * 128, (p + 1) * 128)
                bk = bank()
                srcs = [(af[f][:, psl], ("af", f)), (Hat[f][:, 0, psl], ("Hat", f)), (Hat[f][:, 1, psl], ("Hat", f)), (vT[f][:, psl], ("vT", f))]
                for i, (src, key) in enumerate(srcs):
                    S.op("pe", lambda: nc.tensor.transpose(ps[:, bk, i * 128:(i + 1) * 128], src, identf), reads=[key, ("cst",)], writes=[("ps", bk)], signal=(i == 3))
                S.op("act", lambda: nc.scalar.copy(out=TM[par][cc][p][:].rearrange("p a n -> p (a n)"), in_=ps[:, bk, :]), reads=[("ps", bk)], writes=[("TM", par, cc, p)])
        yield

    def gen_de(ti):
        par = ti % 2
        t0 = ti * TS
        for ccg in range(2):
            units = [(cc, hd, p) for cc in (2 * ccg, 2 * ccg + 1) for hd in range(2) for p in range(NPR)]
            def uid(cc, hd, p):
                return ((cc % 2) * 2 + hd) * NPR + p
            for (cc, hd, p) in units:
                u = uid(cc, hd, p)
                hs = slice(hd * 64, hd * 64 + 64); tk = slice(p * 128, (p + 1) * 128)
                bk = bank()
                S.op("pe", lambda: nc.tensor.matmul(ps[:, bk, 0:256], LBt[par][cc][hs, 0, tk], RA[par][cc][hs, :, tk], start=True, stop=True),
                     reads=[("LBt", par, cc), ("RA", par, cc, 0), ("RA", par, cc, 1)], writes=[("ps", bk)], signal=False)
                S.op("pe", lambda: nc.tensor.matmul(ps[:, bk, 256:512], LBt[par][cc][hs, 1, tk], RA[par][cc][hs, :, tk], start=True, stop=True),
                     reads=[("LBt", par, cc), ("RA", par, cc, 0), ("RA", par, cc, 1)], writes=[("ps", bk)])
                S.op("dve", lambda: nc.vector.tensor_tensor(out=AM[u][:], in0=ps[:, bk, :], in1=maskSI, op=ALU.mult), reads=[("ps", bk), ("cst",)], writes=[("AM", u)])
                bk = bank()
                S.op("pe", lambda: nc.tensor.matmul(ps[:, bk, 0:128], RA[par][cc][hs, 0, tk], LBt[par][cc][hs, 0, tk], start=True, stop=True),
                     reads=[("LBt", par, cc), ("RA", par, cc, 0)], writes=[("ps", bk)])
                S.op("dve", lambda: nc.vector.tensor_tensor(out=Nk[u][0][:], in0=ps[:, bk, 0:128], in1=maskTS, op=ALU.mult), reads=[("ps", bk), ("cst",)], writes=[("Nk", u, 0)])
                S.op("pool", lambda: nc.gpsimd.tensor_tensor(out=Xs[u][:], in0=AM[u][:, 0:128], in1=identf, op=ALU.add), reads=[("AM", u), ("cst",)], writes=[("Xs", u)])
                yield
            for k in range(1, 6):
                for (cc, hd, p) in units:
                    u = uid(cc, hd, p)
                    Mprev = AM[u][:, 0:128] if k == 1 else Mk[u][(k - 1) % 2][:]
                    Mkey = ("AM", u) if k == 1 else ("Mk", u, (k - 1) % 2)
                    Nprev = Nk[u][(k - 1) % 2][:]
                    Nkey = ("Nk", u, (k - 1) % 2)
                    bk = bank()
                    if k <= 4:
                        S.op("pe", lambda: nc.tensor.matmul(ps[:, bk, 0:128], Nprev, Mprev, start=True, stop=True), reads=[Mkey, Nkey], writes=[("ps", bk)], signal=False)
                    S.op("pe", lambda: nc.tensor.matmul(ps[:, bk, 128:256], Mprev, Nprev, start=True, stop=True), reads=[Mkey, Nkey], writes=[("ps", bk)])
                    if k <= 4:
                        S.op("act", lambda: nc.scalar.copy(out=Mk[u][k % 2][:], in_=ps[:, bk, 0:128]), reads=[("ps", bk)], writes=[("Mk", u, k % 2)])
                    S.op("act", lambda: nc.scalar.copy(out=Nk[u][k % 2][:], in_=ps[:, bk, 128:256]), reads=[("ps", bk)], writes=[("Nk", u, k % 2)])
                    yield
                for (cc, hd, p) in units:
                    u = uid(cc, hd, p)
                    bk = bank()
                    Xkey = ("Xs", u)
                    S.op("pe", lambda: nc.tensor.matmul(ps[:, bk, 0:128], Nk[u][k % 2][:], Xs[u][:], start=True, stop=True), reads=[("Nk", u, k % 2), Xkey], writes=[("ps", bk)])
                    S.op("dve", lambda: nc.vector.tensor_tensor(out=Xs[u][:], in0=ps[:, bk, 0:128], in1=Xs[u][:], op=ALU.add), reads=[("ps", bk), ("Xs", u)], writes=[("Xs", u)])
                    yield
            Xkeyf = lambda u: ("Xs", u)
            for (cc, hd, p) in units:
                u = uid(cc, hd, p)
                hs = slice(hd * 64, hd * 64 + 64)
                bk = bank()
                S.op("pe", lambda: nc.tensor.matmul(ps[:, bk, 0:64], Xs[u][:], TM[par][cc][p][:, 0, hs], start=True, stop=True), reads=[Xkeyf(u), ("TM", par, cc, p)], writes=[("ps", bk)], signal=False)
                S.op("pe", lambda: nc.tensor.matmul(ps[:, bk, 64:128], AM[u][:, 256:384], TM[par][cc][p][:, 3, hs], start=True, stop=True), reads=[("AM", u), ("TM", par, cc, p)], writes=[("ps", bk)])
                S.op("act", lambda: nc.scalar.copy(out=ATM[u][:], in_=ps[:, bk, 0:64]), reads=[("ps", bk)], writes=[("ATM", u)])
                S.op("act", lambda: nc.scalar.copy(out=Ws[u][:], in_=ps[:, bk, 64:128]), reads=[("ps", bk)], writes=[("Ws", u)])
                yield
            for (cc, hd, p) in units:
                u = uid(cc, hd, p)
                hs = slice(hd * 64, hd * 64 + 64); tk = slice(p * 128, (p + 1) * 128)
                bk = bank()
                S.op("pe", lambda: nc.tensor.matmul(ps[:, bk, 0:64], Xs[u][:], Ws[u][:], start=True, stop=True), reads=[Xkeyf(u), ("Ws", u)], writes=[("ps", bk)])
                S.op("act", lambda: nc.scalar.copy(out=UV[u][:], in_=ps[:, bk, 0:64]), reads=[("ps", bk)], writes=[("UV", u)])
                bk2 = bank()
                S.op("pe", lambda: nc.tensor.matmul(ps[hs, bk2, 0:128], ATM[u][:], AM[u][:, 128:256], start=True, stop=True), reads=[("ATM", u), ("AM", u)], writes=[("ps", bk2)])
                S.op("dve", lambda: nc.vector.tensor_tensor(out=RH[cc][hs, tk], in0=ps[hs, bk2, 0:128], in1=RAf1[par][cc][hs, tk], op=ALU.add),
                     reads=[("ps", bk2), ("RAf1", par, cc)], writes=[("RH", cc, hd)])
                yield
            for (cc, hd, p) in units:
                u = uid(cc, hd, p)
                hs = slice(hd * 64, hd * 64 + 64)
                bk = bank()
                S.op("pe", lambda: nc.tensor.matmul(ps[:, bk, 0:64], AM[u][:, 128:256], UV[u][:], start=True, stop=False), reads=[("AM", u), ("UV", u)], writes=[("ps", bk)], signal=False)
                S.op("pe", lambda: nc.tensor.matmul(ps[:, bk, 0:64], AM[u][:, 384:512], TM[par][cc][p][:, 3, hs], start=False, stop=True), reads=[("AM", u), ("TM", par, cc, p)], writes=[("ps", bk)])
                S.op("act", lambda: nc.scalar.copy(out=YV[cc][p][:, hs], in_=ps[:, bk, 0:64]), reads=[("ps", bk)], writes=[("YV", cc, p, hd)])
                for q in range(2):
                    pb = slice(q * 64, q * 64 + 64)
                    ck = p * 2 + q
                    e = ck * CH + CH - 1
                    bk = bank()
                    S.op("pe", lambda: nc.tensor.matmul(ps[hs, bk, 0:64], ATM[u][pb, :], TM[par][cc][p][pb, 1, hs], start=True, stop=True), reads=[("ATM", u), ("TM", par, cc, p)], writes=[("ps", bk)])
                    S.op("dve", lambda: nc.vector.scalar_tensor_tensor(out=GT[cc][hs, ck, :], in0=identf[hs, hs], scalar=E1[par][cc][hs, e:e + 1], in1=ps[hs, bk, 0:64], op0=ALU.mult, op1=ALU.add),
                         reads=[("ps", bk), ("cst",), ("E1", par, cc)], writes=[("GT", cc, hd)])
                    bk = bank()
                    S.op("pe", lambda: nc.tensor.matmul(ps[hs, bk, 0:64], TM[par][cc][p][pb, 1, hs], UV[u][pb, :], start=True, stop=False), reads=[("TM", par, cc, p), ("UV", u)], writes=[("ps", bk)], signal=False)
                    S.op("pe", lambda: nc.tensor.matmul(ps[hs, bk, 0:64], TM[par][cc][p][pb, 2, hs], TM[par][cc][p][pb, 3, hs], start=False, stop=True), reads=[("TM", par, cc, p)], writes=[("ps", bk)])
                    S.op("act", lambda: nc.scalar.copy(out=Hf[cc][hs, ck, :], in_=ps[hs, bk, 0:64]), reads=[("ps", bk)], writes=[("Hf", cc, hd)])
                    yield
        for ck in range(NCK):
            p = ck // 2; q = ck % 2
            pb = slice(q * 64, q * 64 + 64)
            for cc in range(4):
                for hd in range(2):
                    hs = slice(hd * 64, hd * 64 + 64)
                    bkY = bank(); bkS = bank()
                    S.op("pe", lambda: nc.tensor.matmul(ps[pb, bkY, 0:64], RH[cc][hs, ck * CH:(ck + 1) * CH], ST[cc][hs, :], start=True, stop=True),
                         reads=[("RH", cc, hd), ("ST", cc, hd)], writes=[("ps", bkY)])
                    S.op("pe", lambda: nc.tensor.matmul(ps[hs, bkS, 0:64], GT[cc][hs, ck, :], ST[cc][hs, :], start=True, stop=True),
                         reads=[("GT", cc, hd), ("ST", cc, hd)], writes=[("ps", bkS)])
                    S.op("dve", lambda: nc.vector.tensor_tensor(out=ST[cc][hs, :], in0=ps[hs, bkS, 0:64], in1=Hf[cc][hs, ck, :], op=ALU.add),
                         reads=[("ps", bkS), ("Hf", cc, hd)], writes=[("ST", cc, hd)])
                    S.op("dve", lambda: nc.vector.tensor_tensor(out=yt[cc][p][pb, hs], in0=ps[pb, bkY, 0:64], in1=YV[cc][p][pb, hs], op=ALU.add),
                         reads=[("ps", bkY), ("YV", cc, p, hd)], writes=[("yt", cc, p)])
                yield
                if q == 1:
                    S.dma("sp", d["y_out"][t0 + p * 128:t0 + (p + 1) * 128, cc * 128:(cc + 1) * 128], yt[cc][p][:], reads=[("yt", cc, p)])
        yield

    def drain(g):
        n = 0
        for _ in g:
            n += 1
        return n

    n_abc = drain(gen_abc(0))
    n_de = None
    for ti in range(NTI):
        ga = gen_abc(ti + 1) if ti + 1 < NTI else None
        gd = gen_de(ti)
        ca = 0; cd = 0
        while ga is not None or gd is not None:
            fa = ca / n_abc if ga is not None else 2.0
            fd = cd / n_de if (gd is not None and n_de) else (ca / n_abc if gd is not None else 2.0)
            if gd is not None and (ga is None or fd <= fa):
                try:
                    next(gd); cd += 1
                except StopIteration:
                    gd = None
                    if n_de is None:
                        n_de = max(cd, 1)
            else:
                try:
                    next(ga); ca += 1
                except StopIteration:
                    ga = None
        if n_de is None:
            n_de = max(cd, 1)
    S.finish("sp")


def build_rw1_prog(SL, TS=256, SD=BF16):
    nc = bass.Bass("TRN2", target_bir_lowering=False)
    dt = lambda n, s: nc.dram_tensor(n, s, F32, kind="ExternalInput").ap()
    d = {"xT": dt("xT", [D, SL]), "mix": dt("mix", [6, D]), "consts": dt("consts", [128, 896])}
    for nm in ("w_r", "w_k", "w_v"):
        d[nm] = dt(nm, [D, 512])
    d["w1"] = dt("w1", [D, 64]); d["w2"] = dt("w2", [64, 512]); d["w0"] = dt("w0", [512])
    d["a1"] = dt("a1", [D, 64]); d["a2"] = dt("a2", [64, 512]); d["a0"] = dt("a0", [512])
    d["g1"] = dt("g1", [D, 160]); d["g2"] = dt("g2", [160, 512])
    for nm in ("k_k", "k_a", "r_k"):
        d[nm] = dt(nm, [512])
    do = lambda n, s: nc.dram_tensor(n, s, F32, kind="ExternalOutput").ap()
    d["y_out"] = do("y_out", [SL, 512]); d["p_out"] = do("p_out", [512, SL]); d["v_out"] = do("v_out", [512, SL]); d["g_out"] = do("g_out", [512, SL])
    with ExitStack() as es:
        S = Sched(nc, es)
        ps = es.enter_context(nc.psum_tensor("ps", [128, 8, 512], F32))
        emit_rw1(S, nc, es, ps, d, SL, TS, SD)
    return nc


GN_EPS = 64e-5


def emit_rw2(S, nc, es, ps, d, T, xT, xb, LB):
    NT = T // 512
    tsl = lambda tt: slice(tt * 512, (tt + 1) * 512)
    sb = lambda es_, n, s, dt: es_.enter_context(nc.sbuf_tensor(n, s, dt))
    with ExitStack() as es2:
        cst = sb(es2, "r2_cst", [128, 896], F32)
        S.dma("sp", cst[:], d["consts"], writes=[("cst2",)])
        blkf = cst[:, 768:896]
        wout = sb(es2, "r2_wout", [128, 8, 1024], BF16)
        S.dma("pool", wout[:], d["w_out"].rearrange("(c p) n -> p c n", p=128), writes=[("r2wout",)])
        gcol = sb(es2, "r2_gcol", [128, 8], F32); bcol = sb(es2, "r2_bcol", [128, 8], F32); epsg = sb(es2, "r2_eps", [128, 1], F32)
        S.dma("sp", gcol[:], d["lnx_g"].rearrange("(c p) -> p c", p=128), writes=[("r2g",)], allow_slow_non_contiguous=True)
        S.dma("sp", bcol[:], d["lnx_b"].rearrange("(c p) -> p c", p=128), writes=[("r2b",)], allow_slow_non_contiguous=True)
        S.op("dve", lambda: nc.vector.memset(epsg[:], GN_EPS), writes=[("r2eps",)])
        names = ["yf", "yb", "pf", "pb", "v", "g"]
        tin = {n: [sb(es2, f"r2_{n}{i}", [128, 512], F32) for i in range(2)] for n in names}
        tmp = {n: [sb(es2, f"r2_t{n}{i}", [128, 512], F32) for i in range(2)] for n in ["y", "ysq", "mean", "a", "b", "pp"]}
        dv = {n: d[n].rearrange("(c p) t -> p c t", p=128) for n in names + ["x"]}
        it = 0
        for tt in range(NT):
            S.dma("sp", xT[:, :, tsl(tt)], dv["x"][:, :, tsl(tt)], writes=[("xT", c, tt) for c in range(8)])
            for c in range(8):
                i = it % 2; it += 1
                for n in names:
                    S.dma("sp", tin[n][i][:], dv[n][:, c, tsl(tt)], writes=[("r2in", n, i)])
                Y = tmp["y"][i]; YS = tmp["ysq"][i]; MN = tmp["mean"][i]; A = tmp["a"][i]; Bt = tmp["b"][i]; PP = tmp["pp"][i]
                S.op("dve", lambda: nc.vector.tensor_tensor(out=Y[:], in0=tin["yf"][i][:], in1=tin["yb"][i][:], op=ALU.add),
                     reads=[("r2in", "yf", i), ("r2in", "yb", i)], writes=[("r2y", i)])
                S.op("act", lambda: nc.scalar.activation(out=YS[:], in_=Y[:], func=AF.Square), reads=[("r2y", i)], writes=[("r2ysq", i)])
                S.op("pool", lambda: nc.gpsimd.tensor_tensor(out=PP[:], in0=tin["pf"][i][:], in1=tin["pb"][i][:], op=ALU.add),
                     reads=[("r2in", "pf", i), ("r2in", "pb", i)], writes=[("r2pp", i)])
                b1 = 0 + 3 * (it % 2); b2 = b1 + 1; b3 = b1 + 2
                S.op("pe", lambda: nc.tensor.matmul(ps[:, b1, :], blkf, Y[:], start=True, stop=True), reads=[("cst2",), ("r2y", i)], writes=[("ps", b1)])
                S.op("pe", lambda: nc.tensor.matmul(ps[:, b2, :], blkf, YS[:], start=True, stop=True), reads=[("cst2",), ("r2ysq", i)], writes=[("ps", b2)])
                S.op("pe", lambda: nc.tensor.matmul(ps[:, b3, :], blkf, PP[:], start=True, stop=True), reads=[("cst2",), ("r2pp", i)], writes=[("ps", b3)])
                S.op("act", lambda: nc.scalar.activation(out=MN[:], in_=ps[:, b1, :], func=AF.Identity, scale=1.0 / 64), reads=[("ps", b1)], writes=[("r2mean", i)])
                S.op("dve", lambda: nc.vector.tensor_tensor(out=A[:], in0=ps[:, b1, :], in1=MN[:], op=ALU.mult), reads=[("ps", b1), ("r2mean", i)], writes=[("r2a", i)])
                S.op("dve", lambda: nc.vector.tensor_tensor(out=A[:], in0=ps[:, b2, :], in1=A[:], op=ALU.subtract), reads=[("ps", b2), ("r2a", i)], writes=[("r2a", i)])
                S.op("act", lambda: nc.scalar.activation(out=A[:], in_=A[:], func=AF.Ln, bias=epsg[:], scale=1.0 / 64), reads=[("r2a", i), ("r2eps",)], writes=[("r2a", i)])
                S.op("act", lambda: nc.scalar.activation(out=A[:], in_=A[:], func=AF.Exp, scale=-0.5), reads=[("r2a", i)], writes=[("r2a", i)])
                S.op("pool", lambda: nc.gpsimd.tensor_tensor(out=Bt[:], in0=Y[:], in1=MN[:], op=ALU.subtract), reads=[("r2y", i), ("r2mean", i)], writes=[("r2b_", i)])
                S.op("pool", lambda: nc.gpsimd.tensor_tensor(out=Bt[:], in0=Bt[:], in1=A[:], op=ALU.mult), reads=[("r2b_", i), ("r2a", i)], writes=[("r2b_", i)])
                S.op("act", lambda: nc.scalar.activation(out=Bt[:], in_=Bt[:], func=AF.Identity, bias=bcol[:, c:c + 1], scale=gcol[:, c:c + 1]),
                     reads=[("r2b_", i), ("r2g",), ("r2b",)], writes=[("r2b_", i)])
                S.op("dve", lambda: nc.vector.tensor_tensor(out=YS[:], in0=ps[:, b3, :], in1=tin["v"][i][:], op=ALU.mult), reads=[("ps", b3), ("r2in", "v", i)], writes=[("r2ysq", i)])
                S.op("pool", lambda: nc.gpsimd.tensor_tensor(out=Bt[:], in0=Bt[:], in1=YS[:], op=ALU.add), reads=[("r2b_", i), ("r2ysq", i)], writes=[("r2b_", i)])
                S.op("dve", lambda: nc.vector.tensor_tensor(out=xb[:, c, tsl(tt)], in0=Bt[:], in1=tin["g"][i][:], op=ALU.mult), reads=[("r2b_", i), ("r2in", "g", i)], writes=[("xb", c, tt)])
            for dc in range(8):
                bank = 6 + (dc % 2)
                for c in range(8):
                    S.op("pe", lambda: nc.tensor.matmul(ps[:, bank, :], wout[:, c, dc * 128:(dc + 1) * 128], xb[:, c, tsl(tt)], start=(c == 0), stop=(c == 7)),
                         reads=[("r2wout",), ("xb", c, tt)], writes=[("ps", bank)], signal=(c == 7))
                S.op("pool", lambda: nc.gpsimd.tensor_scalar(out=xT[:, dc, tsl(tt)], in0=xT[:, dc, tsl(tt)], scalar1=float(ALPHA), scalar2=0.0, op0=ALU.mult, op1=ALU.add),
                     reads=[("xT", dc, tt)], writes=[("xT", dc, tt)])
                S.op("dve", lambda: nc.vector.tensor_tensor(out=xT[:, dc, tsl(tt)], in0=ps[:, bank, :], in1=xT[:, dc, tsl(tt)], op=ALU.add),
                     reads=[("ps", bank), ("xT", dc, tt)], writes=[("xT", dc, tt)])
    S.fence()


def build_rw2_prog(T):
    nc = bass.Bass("TRN2", target_bir_lowering=False)
    dt = lambda n, s: nc.dram_tensor(n, s, F32, kind="ExternalInput").ap()
    d = {n: dt(n, [D, T]) for n in ["x", "yf", "yb", "pf", "pb", "v", "g"]}
    d["consts"] = dt("consts", [128, 896])
    d["lnx_g"] = dt("lnx_g", [D]); d["lnx_b"] = dt("lnx_b", [D]); d["w_out"] = dt("w_out", [D, D])
    lng = dt("lng", [2, D]); lnb = dt("lnb", [2, D])
    w_in = dt("w_in", [D, 2 * FF]); w_outf = dt("w_outf", [FF, D])
    yT_d = nc.dram_tensor("yT", [D, T], F32, kind="ExternalOutput").ap()
    with ExitStack() as es:
        S = Sched(nc, es)
        ps = es.enter_context(nc.psum_tensor("ps", [128, 8, 512], F32))
        xT = es.enter_context(nc.sbuf_tensor("xTs", [128, 8, T], F32))
        xb = es.enter_context(nc.sbuf_tensor("xbs", [128, 8, T], BF16))
        B = FfnBufs(nc, es, T, with_ffn=False)
        B.init_consts(S, nc)
        emit_rw2(S, nc, es, ps, d, T, xT, xb, B)
        NT = T // 512
        emit_ln(S, nc, B, xT, xb, ps, lng[0], lnb[0], NT)
        B.alloc_ffn(nc, es)
        emit_ffn_ln(S, nc, B, xT, xb, ps, w_in, w_outf, lng[1], lnb[1], "f")
        yv = yT_d.rearrange("(c p) t -> p c t", p=128)
        for tt in range(NT):
            S.dma("sp", yv[:, :, tt * 512:(tt + 1) * 512], xT[:, :, tt * 512:(tt + 1) * 512], reads=[("xT", c, tt) for c in range(8)])
        S.finish("sp")
    return nc


_PROGS = {}


def _prog(key, fn):
    if key not in _PROGS:
        _PROGS[key] = fn()
    return _PROGS[key]


def _run(nc, in_maps):
    res = run_bass_kernel_spmd(nc, in_maps, core_ids=list(range(8)))
    return res.results


def _tok(c):
    b = c // 4
    t0 = (c % 4) * 2048
    return b, t0


def _tr(a):
    return np.ascontiguousarray(a.T)


def kernel(x, ffn_w_in, ffn_w_out, ln_g, ln_b, na_w_qkv, na_b_qkv, na_rpb, na_w_o, na_b_o,
           rw_mix, rw_w_rkv, rw_w0, rw_w1, rw_w2, rw_a0, rw_a1, rw_a2, rw_g1, rw_g2,
           rw_k_k, rw_k_a, rw_r_k, rw_lnx_g, rw_lnx_b, rw_w_out):
    f32 = np.float32
    A = lambda a: np.ascontiguousarray(np.asarray(a, dtype=f32))
    x = A(x)
    Bsz, SEQ, Dm = x.shape
    T = 2048
    nc1 = _prog("ffn1", lambda: build_ffn_prog(T, 1))
    ins = []
    for c in range(8):
        b, t0 = _tok(c)
        ins.append({"xT": _tr(x[b, t0:t0 + T]), "w_in": A(ffn_w_in[0, 0:1]), "w_out": A(ffn_w_out[0, 0:1]),
                    "lng": A(ln_g[0, 0:1]), "lnb": A(ln_b[0, 0:1])})
    r = _run(nc1, ins)
    x1 = np.empty_like(x)
    for c in range(8):
        b, t0 = _tok(c)
        x1[b, t0:t0 + T] = r[c]["yT"].T
    nc2 = _prog("na", lambda: build_na_prog(32))
    ins = []
    for c in range(8):
        b, t0 = _tok(c)
        r0 = (c % 4) * 32
        G = na_halo_rows(r0, 32, 128)
        xg = x1[b].reshape(128, 64, Dm)
        ins.append({"xh": _tr(xg[G].reshape(-1, Dm)), "xo": _tr(x1[b, t0:t0 + T]), "w_qkv": A(na_w_qkv[0]), "b_qkv": A(na_b_qkv[0]),
                    "tab": na_tables(A(na_rpb[0]), r0, 32, 128), "w_o": A(na_w_o[0]), "b_o": A(na_b_o[0]),
                    "lng": A(ln_g[0, 1]), "lnb": A(ln_b[0, 1])})
    r = _run(nc2, ins)
    x2 = np.empty_like(x)
    for c in range(8):
        b, t0 = _tok(c)
        x2[b, t0:t0 + T] = r[c]["yT"].T
    nc3 = _prog("ffn2", lambda: build_ffn_prog(T, 2))
    w_in2 = A(np.stack([ffn_w_in[0, 1], ffn_w_in[1, 0]])); w_out2 = A(np.stack([ffn_w_out[0, 1], ffn_w_out[1, 0]]))
    lng2 = A(np.stack([ln_g[0, 2], ln_g[1, 0]])); lnb2 = A(np.stack([ln_b[0, 2], ln_b[1, 0]]))
    ins = []
    for c in range(8):
        b, t0 = _tok(c)
        ins.append({"xT": _tr(x2[b, t0:t0 + T]), "w_in": w_in2, "w_out": w_out2, "lng": lng2, "lnb": lnb2})
    r = _run(nc3, ins)
    x4 = np.empty_like(x)
    for c in range(8):
        b, t0 = _tok(c)
        x4[b, t0:t0 + T] = r[c]["yT"].T
    nc4 = _prog("rw1", lambda: build_rw1_prog(SEQ, 256, BF16))
    consts = rw_consts()
    ins = []
    for c in range(8):
        b = c // 4; z = (c // 2) % 2; hg = c % 2
        cs_ = slice(hg * 512, (hg + 1) * 512)
        xs_ = x4[b][::-1] if z == 1 else x4[b]
        ins.append({"xT": _tr(xs_), "mix": A(rw_mix[0]), "consts": consts,
                    "w_r": A(rw_w_rkv[0, 0][:, cs_]), "w_k": A(rw_w_rkv[0, 1][:, cs_]), "w_v": A(rw_w_rkv[0, 2][:, cs_]),
                    "w1": A(rw_w1[0, z]), "w2": A(rw_w2[0, z][:, cs_]), "w0": A(rw_w0[0, z][cs_]),
                    "a1": A(rw_a1[0, z]), "a2": A(rw_a2[0, z][:, cs_]), "a0": A(rw_a0[0, z][cs_]),
                    "g1": A(rw_g1[0]), "g2": A(rw_g2[0][:, cs_]),
                    "k_k": A(rw_k_k[0][cs_]), "k_a": A(rw_k_a[0][cs_]), "r_k": A(np.reshape(rw_r_k[0], (-1,))[cs_])})
    r = _run(nc4, ins)
    yT = np.empty((2, Bsz, Dm, SEQ), f32); pT = np.empty((2, Bsz, Dm, SEQ), f32)
    vT = np.empty((Bsz, Dm, SEQ), f32); gT = np.empty((Bsz, Dm, SEQ), f32)
    for c in range(8):
        b = c // 4; z = (c // 2) % 2; hg = c % 2
        cs_ = slice(hg * 512, (hg + 1) * 512)
        yo = r[c]["y_out"]; po = r[c]["p_out"]
        if z == 1:
            yT[z, b, cs_] = yo[::-1].T
            pT[z, b, cs_] = po[:, ::-1]
        else:
            yT[z, b, cs_] = yo.T
            pT[z, b, cs_] = po
            vT[b, cs_] = r[c]["v_out"]; gT[b, cs_] = r[c]["g_out"]
    nc5 = _prog("rw2", lambda: build_rw2_prog(T))
    ins = []
    lng5 = A(np.stack([ln_g[1, 1], ln_g[1, 2]])); lnb5 = A(np.stack([ln_b[1, 1], ln_b[1, 2]]))
    for c in range(8):
        b, t0 = _tok(c)
        ts_ = slice(t0, t0 + T)
        cp = lambda a: np.ascontiguousarray(a[:, ts_])
        ins.append({"x": _tr(x4[b, ts_]), "yf": cp(yT[0, b]), "yb": cp(yT[1, b]), "pf": cp(pT[0, b]), "pb": cp(pT[1, b]),
                    "v": cp(vT[b]), "g": cp(gT[b]), "consts": consts, "lnx_g": A(rw_lnx_g[0]), "lnx_b": A(rw_lnx_b[0]),
                    "w_out": A(rw_w_out[0]), "lng": lng5, "lnb": lnb5, "w_in": A(ffn_w_in[1, 1]), "w_outf": A(ffn_w_out[1, 1])})
    r = _run(nc5, ins)
    out = np.empty_like(x)
    for c in range(8):
        b, t0 = _tok(c)
        out[b, t0:t0 + T] = r[c]["yT"].T
    return out
```

```python
import numpy as np
from concourse.bass_utils import run_bass_kernel_spmd
import numpy as np
from contextlib import ExitStack
import concourse.bass as bass
import concourse.mybir as mybir

F32 = mybir.dt.float32
BF16 = mybir.dt.bfloat16
AF = mybir.ActivationFunctionType
ALU = mybir.AluOpType
AX = mybir.AxisListType


class Sched:
    SEM_ROT = 12000
    N_DMA_SEM = 24

    def __init__(self, nc, es):
        self.nc = nc
        self.es = es
        self.eng = {"pe": nc.tensor, "act": nc.scalar, "dve": nc.vector, "pool": nc.gpsimd, "sp": nc.sync}
        self.sems = []
        self.cur = {}
        for e in self.eng:
            self.cur[e] = [self._new_sem(f"p_{e}"), 0]
        self.waited = {e: {} for e in self.eng}
        self.last_w = {}
        self.readers = {}
        self.dma_sems = [[self._new_sem(f"dma{i}"), 0] for i in range(self.N_DMA_SEM)]
        self.dma_rr = 0
        self.n_inst = {e: 0 for e in self.eng}
        self.n_wait = {e: 0 for e in self.eng}
        self.pending = {e: False for e in self.eng}
        self.clock = {e: {} for e in self.eng}
        self.tclock = {}
        self.n_fused = {}

    def _new_sem(self, name):
        h = self.es.enter_context(self.nc.semaphore(f"{name}_{len(self.sems)}"))
        self.sems.append(h)
        return len(self.sems) - 1

    def _need(self, e, ticket):
        if ticket is None:
            return False
        sid, val = ticket
        ck = self.clock.setdefault(e, {})
        if ck.get(sid, 0) >= val:
            return False
        for e2, c in self.cur.items():
            if c[0] == sid and val > c[1]:
                assert e2 == e, f"{e} waits on pending ticket of {e2}"
                return False
        ck[sid] = val
        snap = self.tclock.get(ticket)
        if snap:
            for s2, v2 in snap.items():
                if ck.get(s2, 0) < v2:
                    ck[s2] = v2
        return True

    def _wait(self, e, ticket):
        if self._need(e, ticket):
            self.eng[e].wait_ge(self.sems[ticket[0]], ticket[1])
            self.n_wait[e] += 1

    def fence(self):
        ts = []
        for e2, c in self.cur.items():
            assert not self.pending[e2]
            if c[1] > 0:
                ts.append((c[0], c[1]))
        for s_ in self.dma_sems:
            if s_[1] > 0:
                ts.append((s_[0], s_[1]))
        self.fence_tickets = ts
        self.fence_done = set()

    def _collect(self, e, reads, writes):
        rd, wr = [], []
        if getattr(self, "fence_tickets", None) and e not in self.fence_done:
            self.fence_done.add(e)
            for t in self.fence_tickets:
                if self._need(e, t):
                    rd.append(t)
        for k in reads:
            t = self.last_w.get(k)
            if self._need(e, t):
                rd.append(t)
        for k in writes:
            t = self.last_w.get(k)
            if self._need(e, t):
                wr.append(t)
            for sid, val in self.readers.get(k, {}).items():
                if self._need(e, (sid, val)):
                    wr.append((sid, val))
        return rd, wr

    def _deps(self, e, reads, writes):
        rd, wr = self._collect(e, reads, writes)
        for t in rd + wr:
            self.eng[e].wait_ge(self.sems[t[0]], t[1])
            self.n_wait[e] += 1

    def _record(self, ticket, reads, writes):
        sid, val = ticket
        for k in reads:
            r = self.readers.setdefault(k, {})
            if r.get(sid, 0) < val:
                r[sid] = val
        for k in writes:
            self.last_w[k] = ticket
            self.readers[k] = {}

    def op(self, e, fn, reads=(), writes=(), signal=True):
        rd, wr = self._collect(e, reads, writes)
        fuse = None
        if e == "pe":
            if wr:
                fuse = wr.pop()
        elif e in ("act", "dve", "pool"):
            if wr:
                fuse = wr.pop()
            elif rd:
                fuse = rd.pop()
        for t in rd + wr:
            self.eng[e].wait_ge(self.sems[t[0]], t[1])
            self.n_wait[e] += 1
        c = self.cur[e]
        if signal and c[1] >= self.SEM_ROT and not self.pending[e]:
            c[0] = self._new_sem(f"p_{e}")
            c[1] = 0
        inst = fn()
        if fuse is not None:
            inst._wait_ge(self.sems[fuse[0]], fuse[1])
            self.n_fused[e] = self.n_fused.get(e, 0) + 1
        self.n_inst[e] += 1
        if signal:
            c[1] += 1
            inst.then_inc(self.sems[c[0]], 1)
            ticket = (c[0], c[1])
            self.pending[e] = False
            self.tclock[ticket] = dict(self.clock.get(e, {}))
        else:
            ticket = (c[0], c[1] + 1)
            self.pending[e] = True
        self._record(ticket, reads, writes)
        return ticket

    def dma(self, q, out, in_, reads=(), writes=(), **kw):
        self._deps(q, reads, writes)
        if q == "pool":
            slot = [self._new_sem("swdma"), 0]
            self.dma_sems.append(slot)
        else:
            slot = self.dma_sems[self.dma_rr]
            self.dma_rr = (self.dma_rr + 1) % self.N_DMA_SEM
        if slot[1] > 0:
            self._wait(q, (slot[0], slot[1]))
        slot[1] += 16
        self.eng[q].dma_start(out=out, in_=in_, **kw).then_inc(self.sems[slot[0]], 16)
        self.n_inst[q] += 1
        ticket = (slot[0], slot[1])
        self.tclock[ticket] = dict(self.clock.get(q, {}))
        self._record(ticket, reads, writes)
        return ticket

    def finish(self, e="sp"):
        for k, t in list(self.last_w.items()):
            self._wait(e, t)
        for s in self.dma_sems:
            if s[1] > 0:
                self._wait(e, (s[0], s[1]))


D = 1024; FF = 2816; NFC = 22
GROUPS = [(0, 4), (4, 4), (8, 4), (12, 4), (16, 4), (20, 2)]
ALPHA = 4 ** 0.25
LN_EPS = 1e-5


class FfnBufs:
    def __init__(self, nc, es, T, with_ffn=True):
        self.T = T
        self.NT = T // 512
        sb = lambda n, s, d: es.enter_context(nc.sbuf_tensor(n, s, d))
        if with_ffn:
            self.alloc_ffn(nc, es)
        self.sq = [sb(f"sq{i}", [128, 8, 512], BF16) for i in range(1)]
        self.yb = [sb(f"yb{i}", [128, 8, 512], BF16) for i in range(1)]
        self.mean_s = sb("mean_s", [128, 512], F32)
        self.m2 = sb("m2", [128, 512], F32)
        self.rstd = sb("rstd", [128, 512], F32)
        self.t1 = [sb(f"t1_{i}", [128, 512], F32) for i in range(2)]
        self.t2 = [sb(f"t2_{i}", [128, 512], F32) for i in range(2)]
        self.ones = sb("ones", [128, 128], BF16)
        self.gcol = sb("gcol", [128, 8], F32)
        self.bcol = sb("bcol", [128, 8], F32)
        self.epsc = sb("epsc", [128, 1], F32)
        self.wslot = 0

    def alloc_ffn(self, nc, es):
        sb = lambda n, s, d: es.enter_context(nc.sbuf_tensor(n, s, d))
        T = self.T
        self.hh = sb("hh", [128, 4, T], BF16)
        self.wi = [sb(f"wi{i}", [128, 2, 8, 512], BF16) for i in range(2)]
        self.wo = [sb(f"wo{i}", [128, 4, 1024], BF16) for i in range(2)]
        self.sg = [sb(f"sg{i}", [128, 512], F32) for i in range(2)]

    def init_consts(self, S, nc):
        S.op("dve", lambda: nc.vector.memset(self.ones[:], 1.0 / 1024), writes=[("ones",)])
        S.op("dve", lambda: nc.vector.memset(self.epsc[:], LN_EPS), writes=[("epsc",)])


def emit_ffn_ln(S, nc, B, xT, xb, ps, w_in, w_out, ln_g, ln_b, tag, ntiles=None):
    NT = B.NT if ntiles is None else ntiles
    tsl = lambda tt: slice(tt * 512, (tt + 1) * 512)
    w_in_v = w_in.rearrange("(c p) (u f) -> p c u f", p=128, u=2)
    w_out_v = w_out.rearrange("(j p) d -> p j d", p=128)

    def load_group(g):
        f0, n = GROUPS[g]
        slot = B.wslot; B.wslot ^= 1
        for u in range(2):
            S.dma("pool", B.wi[slot][:, u, :, 0:n * 128], w_in_v[:, :, u, f0 * 128:(f0 + n) * 128],
                  writes=[("wi", slot, u)])
        S.dma("pool", B.wo[slot][:, 0:n, :], w_out_v[:, f0:f0 + n, :], writes=[("wo", slot)])
        return slot

    slots = {0: load_group(0)}
    for tt in range(NT):
        for c in range(8):
            S.op("act", lambda: nc.scalar.activation(out=xT[:, c, tsl(tt)], in_=xT[:, c, tsl(tt)], func=AF.Identity, scale=float(ALPHA)),
                 reads=[("xT", c, tt)], writes=[("xT", c, tt)])
    pa = 0
    pb = 0
    for g in range(len(GROUPS)):
        f0, n = GROUPS[g]
        slot = slots[g]
        if g + 1 < len(GROUPS):
            slots[g + 1] = load_group(g + 1)
        for tt in range(NT):
            for j in range(n):
                bg = 2 * pa; bu = 2 * pa + 1; pa ^= 1
                for (bank, u) in ((bg, 0), (bu, 1)):
                    for c in range(8):
                        S.op("pe", lambda: nc.tensor.matmul(ps[:, bank, :], B.wi[slot][:, u, c, j * 128:(j + 1) * 128], xb[:, c, tsl(tt)], start=(c == 0), stop=(c == 7)),
                             reads=[("wi", slot, u), ("xb", c, tt)], writes=[("ps", bank)], signal=(c == 7))
                sgi = (tt * n + j) % 2
                S.op("act", lambda: nc.scalar.activation(out=B.sg[sgi][:], in_=ps[:, bg, :], func=AF.Silu),
                     reads=[("ps", bg)], writes=[("sg", sgi)])
                S.op("dve", lambda: nc.vector.tensor_tensor(out=B.hh[:, j, tsl(tt)], in0=ps[:, bu, :], in1=B.sg[sgi][:], op=ALU.mult),
                     reads=[("ps", bu), ("sg", sgi)], writes=[("hh", j, tt)])
        for tt in range(NT):
            for dc in range(8):
                bank = 4 + pb; pb ^= 1
                for j in range(n):
                    S.op("pe", lambda: nc.tensor.matmul(ps[:, bank, :], B.wo[slot][:, j, dc * 128:(dc + 1) * 128], B.hh[:, j, tsl(tt)], start=(j == 0), stop=(j == n - 1)),
                         reads=[("wo", slot), ("hh", j, tt)], writes=[("ps", bank)], signal=(j == n - 1))
                S.op("dve", lambda: nc.vector.scalar_tensor_tensor(out=xT[:, dc, tsl(tt)], in0=ps[:, bank, :], scalar=0.5, in1=xT[:, dc, tsl(tt)], op0=ALU.mult, op1=ALU.add),
                     reads=[("ps", bank), ("xT", dc, tt)], writes=[("xT", dc, tt)])
    emit_ln(S, nc, B, xT, xb, ps, ln_g, ln_b, NT)


def emit_ln(S, nc, B, xT, xb, ps, ln_g, ln_b, NT, xb_tiles=None):
    tsl = lambda tt: slice(tt * 512, (tt + 1) * 512)
    S.dma("sp", B.gcol[:], ln_g.rearrange("(c p) -> p c", p=128), writes=[("gcol",)], allow_slow_non_contiguous=True)
    S.dma("sp", B.bcol[:], ln_b.rearrange("(c p) -> p c", p=128), writes=[("bcol",)], allow_slow_non_contiguous=True)
    for tt in range(NT):
        i2 = 0
        for c in range(8):
            S.op("act", lambda: nc.scalar.activation(out=B.sq[i2][:, c, :], in_=xT[:, c, tsl(tt)], func=AF.Square),
                 reads=[("xT", c, tt)], writes=[("sq", i2, c)])
            S.op("pool", lambda: nc.gpsimd.tensor_copy(out=B.yb[i2][:, c, :], in_=xT[:, c, tsl(tt)]),
                 reads=[("xT", c, tt)], writes=[("yb", i2, c)])
        for c in range(8):
            S.op("pe", lambda: nc.tensor.matmul(ps[:, 6, :], B.ones[:], B.yb[i2][:, c, :], start=(c == 0), stop=(c == 7)),
                 reads=[("ones",), ("yb", i2, c)], writes=[("ps", 6)], signal=(c == 7))
        for c in range(8):
            S.op("pe", lambda: nc.tensor.matmul(ps[:, 7, :], B.ones[:], B.sq[i2][:, c, :], start=(c == 0), stop=(c == 7)),
                 reads=[("ones",), ("sq", i2, c)], writes=[("ps", 7)], signal=(c == 7))
        S.op("act", lambda: nc.scalar.copy(out=B.mean_s[:], in_=ps[:, 6, :]), reads=[("ps", 6)], writes=[("mean_s",)])
        S.op("dve", lambda: nc.vector.tensor_tensor(out=B.m2[:], in0=ps[:, 6, :], in1=B.mean_s[:], op=ALU.mult),
             reads=[("ps", 6), ("mean_s",)], writes=[("m2",)])
        S.op("dve", lambda: nc.vector.tensor_tensor(out=B.m2[:], in0=ps[:, 7, :], in1=B.m2[:], op=ALU.subtract),
             reads=[("ps", 7), ("m2",)], writes=[("m2",)])
        S.op("act", lambda: nc.scalar.activation(out=B.m2[:], in_=B.m2[:], func=AF.Ln, bias=B.epsc[:], scale=1.0),
             reads=[("m2",), ("epsc",)], writes=[("m2",)])
        S.op("act", lambda: nc.scalar.activation(out=B.rstd[:], in_=B.m2[:], func=AF.Exp, scale=-0.5), reads=[("m2",)], writes=[("rstd",)])
        for c in range(8):
            k = c % 2
            S.op("dve", lambda: nc.vector.tensor_tensor(out=B.t1[k][:], in0=xT[:, c, tsl(tt)], in1=ps[:, 6, :], op=ALU.subtract),
                 reads=[("xT", c, tt), ("ps", 6)], writes=[("t1", k)])
            S.op("pool" if c % 2 else "dve", lambda: (nc.gpsimd if c % 2 else nc.vector).tensor_tensor(out=B.t2[k][:], in0=B.t1[k][:], in1=B.rstd[:], op=ALU.mult),
                 reads=[("t1", k), ("rstd",)], writes=[("t2", k)])
            S.op("act", lambda: nc.scalar.activation(out=xT[:, c, tsl(tt)], in_=B.t2[k][:], func=AF.Identity, bias=B.bcol[:, c:c + 1], scale=B.gcol[:, c:c + 1]),
                 reads=[("t2", k), ("gcol",), ("bcol",)], writes=[("xT", c, tt)])
            S.op("act", lambda: nc.scalar.activation(out=xb[:, c, tsl(tt)], in_=B.t2[k][:], func=AF.Identity, bias=B.bcol[:, c:c + 1], scale=B.gcol[:, c:c + 1]),
                 reads=[("t2", k), ("gcol",), ("bcol",)], writes=[("xb", c, tt)])


def build_ffn_prog(T, n_ffn):
    nc = bass.Bass("TRN2", target_bir_lowering=False)
    xT_d = nc.dram_tensor("xT", [D, T], F32, kind="ExternalInput").ap()
    w_in = nc.dram_tensor("w_in", [n_ffn, D, 2 * FF], F32, kind="ExternalInput").ap()
    w_out = nc.dram_tensor("w_out", [n_ffn, FF, D], F32, kind="ExternalInput").ap()
    lng = nc.dram_tensor("lng", [n_ffn, D], F32, kind="ExternalInput").ap()
    lnb = nc.dram_tensor("lnb", [n_ffn, D], F32, kind="ExternalInput").ap()
    yT_d = nc.dram_tensor("yT", [D, T], F32, kind="ExternalOutput").ap()
    with ExitStack() as es:
        S = Sched(nc, es)
        xT = es.enter_context(nc.sbuf_tensor("xTs", [128, 8, T], F32))
        xb = es.enter_context(nc.sbuf_tensor("xbs", [128, 8, T], BF16))
        ps = es.enter_context(nc.psum_tensor("ps", [128, 8, 512], F32))
        B = FfnBufs(nc, es, T)
        NT = T // 512
        B.init_consts(S, nc)
        xv = xT_d.rearrange("(c p) t -> p c t", p=128)
        yv = yT_d.rearrange("(c p) t -> p c t", p=128)
        for tt in range(NT):
            S.dma("sp", xT[:, :, tt * 512:(tt + 1) * 512], xv[:, :, tt * 512:(tt + 1) * 512],
                  writes=[("xT", c, tt) for c in range(8)])
            for c in range(8):
                S.op("act", lambda: nc.scalar.copy(out=xb[:, c, tt * 512:(tt + 1) * 512], in_=xT[:, c, tt * 512:(tt + 1) * 512]),
                     reads=[("xT", c, tt)], writes=[("xb", c, tt)])
        for i in range(n_ffn):
            emit_ffn_ln(S, nc, B, xT, xb, ps, w_in[i], w_out[i], lng[i], lnb[i], f"f{i}")
        for tt in range(NT):
            S.dma("sp", yv[:, :, tt * 512:(tt + 1) * 512], xT[:, :, tt * 512:(tt + 1) * 512],
                  reads=[("xT", c, tt) for c in range(8)])
        S.finish("sp")
    return nc


NH = 16; HD = 64; NHP = 8


def na_pat(r, NR):
    if r < 4:
        return 1 + r
    if r >= NR - 3:
        return 5 + (r - (NR - 3))
    return 0


def na_halo_rows(r0, NR, rows):
    G = []
    for L in range(NR + 8):
        g = r0 - 4 + L
        if g < 0:
            g = g + 8
        elif g >= rows:
            g = rows - 8 + (g - rows)
        g = min(max(g, 0), rows - 1)
        G.append(g)
    return G


def na_tables(rpb, r0, NR, rows):
    G = np.array(na_halo_rows(r0, NR, rows))
    reps = {0: min(NR // 2, NR - 4)}
    for r in range(NR):
        p = na_pat(r, NR)
        if p != 0:
            reps[p] = r
    tab = np.empty((NHP, 8, 128, 4, 128), np.float32)
    qc = np.arange(64)
    qstart = np.clip(qc - 8, 0, 48)
    for p in range(8):
        r = reps.get(p, reps[0])
        R = r0 + r
        rs = min(max(R - 4, 0), rows - 8)
        kL = r + np.arange(8)
        kG = G[kL]
        row_ok = (kG >= rs) & (kG < rs + 8)
        dr = np.clip(kG - R + 7, 0, 14)
        kcol = np.arange(64)
        col_ok = (kcol[None, :] >= qstart[:, None]) & (kcol[None, :] < qstart[:, None] + 16)
        dc = np.clip(kcol[None, :] - qc[:, None] + 15, 0, 30)
        b = rpb[:, dr][:, :, dc]
        b = np.transpose(b, (0, 1, 3, 2))
        ok = row_ok[:, None, None] & np.transpose(col_ok)[None, :, :]
        b = np.where(ok[None], b, np.float32(-1e30)).astype(np.float32)
        b = b.reshape(NHP, 2, 512, 64)
        b = b.reshape(NHP, 2, 4, 128, 64)
        tab[:, p] = np.transpose(b, (0, 3, 2, 1, 4)).reshape(NHP, 128, 4, 128)
    return tab


def emit_na(S, nc, es, ps, xh_d, x_own_d, w_qkv, b_qkv, tab_d, w_o, b_o, ln_g, ln_b, NR, yT_d):
    T = NR * 64; TH = (NR + 8) * 64
    NT = T // 512
    NTH = TH // 512
    sb = lambda es_, n, s, d: es_.enter_context(nc.sbuf_tensor(n, s, d))
    oT = sb(es, "oT", [128, 8, T], BF16)
    tsl = lambda tt: slice(tt * 512, (tt + 1) * 512)
    with ExitStack() as es2:
        xb = sb(es2, "na_xb", [128, 8, TH], BF16)
        tabs = sb(es2, "na_tab", [128, 8, 4, 128], F32)
        KT = [sb(es2, f"na_KT{i}", [128, TH], BF16) for i in range(2)]
        Ve4 = sb(es2, "na_Ve4", [128, TH // 128, 512], BF16)
        Vo4 = sb(es2, "na_Vo4", [128, TH // 128, 512], BF16)
        wv4 = sb(es2, "na_wv4", [128, 8, 512], BF16)
        QBD = [sb(es2, f"na_Q{i}", [128, NR, 2, 64], BF16) for i in range(2)]
        wq = [sb(es2, f"na_wq{i}", [128, 2, 8, 128], BF16) for i in range(2)]
        sbt = [sb(es2, f"na_sb{i}", [128, 512], F32) for i in range(4)]
        PT = [sb(es2, f"na_PT{i}", [128, 512], BF16) for i in range(4)]
        rc = [sb(es2, f"na_rc{i}", [128, 128], F32) for i in range(4)]
        bcols = sb(es2, "na_bc", [128, 24], F32)
        bvrow = sb(es2, "na_bvrow", [1, 1024], BF16)
        bvb = sb(es2, "na_bvb", [128, 1024], F32)
        ones_r = sb(es2, "na_ones_r", [1, 128], BF16)
        ones_k = sb(es2, "na_ones_k", [128, 128], BF16)

        S.op("dve", lambda: nc.vector.memset(ones_r[:], 1.0), writes=[("ones_r",)])
        S.op("dve", lambda: nc.vector.memset(ones_k[:], 1.0), writes=[("ones_k",)])
        for i in range(2):
            S.op("pool", lambda: nc.gpsimd.memset(QBD[i][:], 0.0), writes=[("QBD", i)])
        S.dma("sp", bcols[:], b_qkv.rearrange("(j p) -> p j", p=128), writes=[("bcols",)], allow_slow_non_contiguous=True)
        S.dma("pool", bvrow[:], b_qkv[2048:3072].rearrange("(o n) -> o n", o=1), writes=[("bvrow",)])
        xhv = xh_d.rearrange("(c p) t -> p c t", p=128)
        for tt in range(NTH):
            S.dma("pool", xb[:, :, tsl(tt)], xhv[:, :, tsl(tt)], writes=[("nxb", tt)])
        for h2 in range(2):
            S.op("pe", lambda: nc.tensor.matmul(ps[:, 6, :], ones_r[0:1, :], bvrow[0:1, h2 * 512:(h2 + 1) * 512], start=True, stop=True),
                 reads=[("ones_r",), ("bvrow",)], writes=[("ps", 6)])
            S.op("act", lambda: nc.scalar.copy(out=bvb[:, h2 * 512:(h2 + 1) * 512], in_=ps[:, 6, :]), reads=[("ps", 6)], writes=[("bvb", h2)])
        wv = w_qkv.rearrange("(c p) n -> p c n", p=128)
        tabv = tab_d

        def load_w(hp, slot):
            for k in range(2):
                S.dma("pool", wq[slot][:, k, :, :], wv[:, :, k * 1024 + hp * 128:k * 1024 + (hp + 1) * 128], writes=[("wq", slot, k)])

        load_w(0, 0)
        pa = 0
        for hp in range(NHP):
            slot = hp % 2
            if hp + 1 < NHP:
                load_w(hp + 1, 1 - slot)
            S.dma("sp", tabs[:].rearrange("p a k n -> p a (k n)"), tabv[hp].rearrange("a p k n -> p a (k n)"), writes=[("tabs",)])
            for tt in range(NTH):
                bank = pa; pa = (pa + 1) % 4
                for c in range(8):
                    S.op("pe", lambda: nc.tensor.matmul(ps[:, bank, :], wq[slot][:, 1, c, :], xb[:, c, tsl(tt)], start=(c == 0), stop=(c == 7)),
                         reads=[("wq", slot, 1), ("nxb", tt)], writes=[("ps", bank)], signal=(c == 7))
                S.op("act", lambda: nc.scalar.activation(out=KT[slot][:, tsl(tt)], in_=ps[:, bank, :], func=AF.Identity, bias=bcols[:, 8 + hp:9 + hp], scale=1.0),
                     reads=[("ps", bank), ("bcols",)], writes=[("KT", slot, tt)])
            for tt in range(NT):
                bank = pa; pa = (pa + 1) % 4
                for c in range(8):
                    S.op("pe", lambda: nc.tensor.matmul(ps[:, bank, :], wq[slot][:, 0, c, :], xb[:, c, 256 + tt * 512:256 + (tt + 1) * 512], start=(c == 0), stop=(c == 7)),
                         reads=[("wq", slot, 0)] + [("nxb", t2) for t2 in range(NTH)], writes=[("ps", bank)], signal=(c == 7))
                for hd in range(2):
                    pr = slice(hd * 64, (hd + 1) * 64)
                    S.op("dve", lambda: nc.vector.tensor_scalar(out=QBD[slot][pr, tt * 8:(tt + 1) * 8, hd, :], in0=ps[pr, bank, :].rearrange("p (r q) -> p r q", q=64),
                                                                scalar1=bcols[pr, hp:hp + 1], scalar2=0.125, op0=ALU.add, op1=ALU.mult),
                         reads=[("ps", bank), ("bcols",)], writes=[("QBD", slot)])
            nch = TH // 128
            if hp % 4 == 0:
                hg = hp // 4
                S.dma("pool", wv4[:], wv[:, :, 2048 + hg * 512:2048 + (hg + 1) * 512], writes=[("wv4",)])
                for (Vx, off, cnt, nm) in ((Ve4, 0, nch, "Ve"), (Vo4, 64, nch - 1, "Vo")):
                    for j in range(cnt):
                        bank = pa; pa = (pa + 1) % 4
                        for c in range(8):
                            S.op("pe", lambda: nc.tensor.matmul(ps[:, bank, :], xb[:, c, off + j * 128:off + (j + 1) * 128], wv4[:, c, :], start=(c == 0), stop=(c == 7)),
                                 reads=[("wv4",)] + [("nxb", t2) for t2 in range(NTH)], writes=[("ps", bank)], signal=(c == 7))
                        S.op("dve", lambda: nc.vector.tensor_tensor(out=Vx[:, j, :], in0=ps[:, bank, :], in1=bvb[:, hg * 512:(hg + 1) * 512], op=ALU.add),
                             reads=[("ps", bank), ("bvb", hg)], writes=[(nm, j)])
            def stage1(r):
                nonlocal pa
                pat = na_pat(r, NR)
                i2 = r % 4
                bankS = pa; pa = (pa + 1) % 4
                tok0 = r * 64
                for kc in range(4):
                    S.op("pe", lambda: nc.tensor.matmul(ps[:, bankS, kc * 128:(kc + 1) * 128], KT[slot][:, tok0 + kc * 128:tok0 + (kc + 1) * 128], QBD[slot][:, r, :, :].rearrange("p a q -> p (a q)"), start=True, stop=True),
                         reads=[("KT", slot, t2) for t2 in range(NTH)] + [("QBD", slot)], writes=[("ps", bankS)], signal=(kc == 3))
                S.op("dve", lambda: nc.vector.tensor_tensor(out=sbt[i2][:], in0=ps[:, bankS, :], in1=tabs[:, pat, :, :].rearrange("p k n -> p (k n)"), op=ALU.add),
                     reads=[("ps", bankS), ("tabs",)], writes=[("sbt", i2)])
                S.op("act", lambda: nc.scalar.activation(out=PT[i2][:], in_=sbt[i2][:], func=AF.Exp),
                     reads=[("sbt", i2)], writes=[("PT", i2)])

            def stage2(r):
                i2 = r % 4
                bankO = 4 + i2
                if r % 2 == 0:
                    Vx, j0, nm = Ve4, r // 2, "Ve"
                else:
                    Vx, j0, nm = Vo4, (r - 1) // 2, "Vo"
                h4 = hp % 4
                for kc in range(4):
                    S.op("pe", lambda: nc.tensor.matmul(ps[:, bankO, 0:128], Vx[:, j0 + kc, h4 * 128:(h4 + 1) * 128], PT[i2][:, kc * 128:(kc + 1) * 128], start=(kc == 0), stop=(kc == 3)),
                         reads=[(nm, j0 + kc), ("PT", i2)], writes=[("ps", bankO)], signal=False)
                for kc in range(4):
                    S.op("pe", lambda: nc.tensor.matmul(ps[:, bankO, 128:256], ones_k[:], PT[i2][:, kc * 128:(kc + 1) * 128], start=(kc == 0), stop=(kc == 3)),
                         reads=[("ones_k",), ("PT", i2)], writes=[("ps", bankO)], signal=(kc == 3))
                S.op("act", lambda: nc.scalar.activation(out=rc[i2][:], in_=ps[:, bankO, 128:256], func=AF.Ln),
                     reads=[("ps", bankO)], writes=[("rc", i2)])
                S.op("act", lambda: nc.scalar.activation(out=rc[i2][:], in_=rc[i2][:], func=AF.Exp, scale=-1.0),
                     reads=[("rc", i2)], writes=[("rc", i2)])
                for hd in range(2):
                    pr = slice(hd * 64, (hd + 1) * 64)
                    S.op("dve", lambda: nc.vector.tensor_tensor(out=oT[pr, hp, r * 64:(r + 1) * 64], in0=ps[pr, bankO, hd * 64:(hd + 1) * 64], in1=rc[i2][pr, hd * 64:(hd + 1) * 64], op=ALU.mult),
                         reads=[("ps", bankO), ("rc", i2)], writes=[("oT", hp, r // 8)])

            stage1(0)
            if NR > 1:
                stage1(1)
            for r in range(NR):
                if r + 2 < NR:
                    stage1(r + 2)
                stage2(r)
    S.fence()
    with ExitStack() as es3:
        xT = sb(es3, "xTs", [128, 8, T], F32)
        xbo = sb(es3, "xbs", [128, 8, T], BF16)
        LB = FfnBufs(nc, es3, T, with_ffn=False)
        LB.init_consts(S, nc)
        wo = sb(es3, "na_wo", [128, 8, 1024], BF16)
        bo = sb(es3, "na_bo", [128, 8], F32)
        S.dma("pool", wo[:], w_o.rearrange("(h p) n -> p h n", p=128), writes=[("nwo",)])
        S.dma("sp", bo[:], b_o.rearrange("(c p) -> p c", p=128), writes=[("nbo",)], allow_slow_non_contiguous=True)
        xov = x_own_d.rearrange("(c p) t -> p c t", p=128)
        pb = 0
        for tt in range(NT):
            S.dma("sp", xT[:, :, tsl(tt)], xov[:, :, tsl(tt)], writes=[("xT", c, tt) for c in range(8)])
            for dc in range(8):
                bank = pb; pb = (pb + 1) % 4
                for hp in range(8):
                    S.op("pe", lambda: nc.tensor.matmul(ps[:, bank, :], wo[:, hp, dc * 128:(dc + 1) * 128], oT[:, hp, tsl(tt)], start=(hp == 0), stop=(hp == 7)),
                         reads=[("nwo",), ("oT", hp, tt)], writes=[("ps", bank)], signal=(hp == 7))
                S.op("pool", lambda: nc.gpsimd.tensor_scalar(out=xT[:, dc, tsl(tt)], in0=xT[:, dc, tsl(tt)], scalar1=float(ALPHA), scalar2=bo[:, dc:dc + 1], op0=ALU.mult, op1=ALU.add),
                     reads=[("xT", dc, tt), ("nbo",)], writes=[("xT", dc, tt)])
                S.op("dve", lambda: nc.vector.tensor_tensor(out=xT[:, dc, tsl(tt)], in0=ps[:, bank, :], in1=xT[:, dc, tsl(tt)], op=ALU.add),
                     reads=[("ps", bank), ("xT", dc, tt)], writes=[("xT", dc, tt)])
        emit_ln(S, nc, LB, xT, xbo, ps, ln_g, ln_b, NT)
        yv = yT_d.rearrange("(c p) t -> p c t", p=128)
        for tt in range(NT):
            S.dma("sp", yv[:, :, tsl(tt)], xT[:, :, tsl(tt)], reads=[("xT", c, tt) for c in range(8)])
        S.finish("sp")


def build_na_prog(NR):
    T = NR * 64; TH = (NR + 8) * 64
    nc = bass.Bass("TRN2", target_bir_lowering=False)
    dt = lambda n, s: nc.dram_tensor(n, s, F32, kind="ExternalInput").ap()
    xh = dt("xh", [D, TH]); xo = dt("xo", [D, T])
    w_qkv = dt("w_qkv", [D, 3 * D]); b_qkv = dt("b_qkv", [3 * D]); tab = dt("tab", [NHP, 8, 128, 4, 128])
    w_o = dt("w_o", [D, D]); b_o = dt("b_o", [D]); lng = dt("lng", [D]); lnb = dt("lnb", [D])
    yT_d = nc.dram_tensor("yT", [D, T], F32, kind="ExternalOutput").ap()
    with ExitStack() as es:
        S = Sched(nc, es)
        ps = es.enter_context(nc.psum_tensor("ps", [128, 8, 512], F32))
        emit_na(S, nc, es, ps, xh, xo, w_qkv, b_qkv, tab, w_o, b_o, lng, lnb, NR, yT_d)
    return nc


C0 = float(np.exp(-0.5))
CH = 64


def rw_consts():
    s = np.arange(128)[:, None]; t = np.arange(128)[None, :]
    same = (s // 64) == (t // 64)
    Sm = (same & (t > s)).astype(np.float32)
    Im = (same & (t >= s)).astype(np.float32)
    maskSI = np.concatenate([Sm, Im, Sm, Im], axis=1)
    maskTS = (same & (t < s)).astype(np.float32)
    ident = np.eye(128, dtype=np.float32)
    blk = same.astype(np.float32)
    return np.concatenate([maskSI, maskTS, ident, blk], axis=1)


STOP = ""


def emit_rw1(S, nc, es, ps, d, SL, TS=256, SD=BF16):
    NTI = SL // TS
    NCK = TS // CH
    NPR = TS // 128
    sb = lambda n, s, dt: es.enter_context(nc.sbuf_tensor(n, s, dt))
    cst = sb("rw_cst", [128, 896], F32)
    S.dma("sp", cst[:], d["consts"], writes=[("cst",)])
    maskSI = cst[:, 0:512]; maskTS = cst[:, 512:640]; identf = cst[:, 640:768]; blkf = cst[:, 768:896]
    ident_s = identf
    if SD != F32:
        ident_sd = sb("rw_identsd", [128, 128], SD)
        S.op("dve", lambda: nc.vector.tensor_copy(out=ident_sd[:], in_=identf), reads=[("cst",)], writes=[("identsd",)])
        ident_s = ident_sd[:]
    ones64 = sb("rw_ones64", [128, CH], F32)
    S.op("dve", lambda: nc.vector.memset(ones64[:], 1.0), writes=[("ones64",)])
    mixc = sb("rw_mixc", [128, 6, 8], F32)
    S.dma("sp", mixc[:], d["mix"].rearrange("i (c p) -> p i c", p=128), writes=[("mixc",)], allow_slow_non_contiguous=True)
    cols = sb("rw_cols", [128, 5, 4], F32)
    for i, nm in enumerate(["w0", "a0", "k_k", "k_a", "r_k"]):
        S.dma("sp", cols[:, i, :], d[nm].rearrange("(c p) -> p c", p=128), writes=[("cols", i)], allow_slow_non_contiguous=True)
    W3 = sb("rw_W3", [128, 3, 8, 512], BF16)
    for i, nm in enumerate(["w_r", "w_k", "w_v"]):
        S.dma("pool", W3[:, i, :, :], d[nm].rearrange("(c p) n -> p c n", p=128), writes=[("W3", i)])
    w1b = sb("rw_w1b", [128, 8, 64], BF16); a1b = sb("rw_a1b", [128, 8, 64], BF16); g1b = sb("rw_g1b", [128, 8, 160], BF16)
    S.dma("pool", w1b[:], d["w1"].rearrange("(c p) n -> p c n", p=128), writes=[("w1b",)])
    S.dma("pool", a1b[:], d["a1"].rearrange("(c p) n -> p c n", p=128), writes=[("a1b",)])
    S.dma("pool", g1b[:], d["g1"].rearrange("(c p) n -> p c n", p=128), writes=[("g1b",)])
    w2b = sb("rw_w2b", [64, 512], BF16); a2b = sb("rw_a2b", [64, 512], BF16)
    g2a = sb("rw_g2a", [128, 512], BF16); g2b = sb("rw_g2b", [128, 512], BF16)
    S.dma("pool", w2b[:], d["w2"], writes=[("w2b",)])
    S.dma("pool", a2b[:], d["a2"], writes=[("a2b",)])
    S.dma("pool", g2a[:], d["g2"][0:128, :], writes=[("g2a",)])
    S.op("dve", lambda: nc.vector.memset(g2b[:], 0.0), writes=[("g2b",)])
    S.dma("pool", g2b[0:32, :], d["g2"][128:160, :], writes=[("g2b",)])
    xt = [sb("rw_xt0", [128, 8, TS + 2], F32)] * 2
    xs = sb("rw_xs", [128, TS], F32)
    xx = sb("rw_xx", [128, TS], F32)
    mixt = [sb(f"rw_mixt{i}", [128, TS], F32) for i in range(3)]
    xm = sb("rw_xm", [128, 6, 8, TS], BF16)
    hw = sb("rw_hw", [64, TS], BF16); ha = sb("rw_ha", [64, TS], BF16)
    hga = sb("rw_hga", [128, TS], BF16); hgb = sb("rw_hgb", [128, TS], BF16)
    S.op("dve", lambda: nc.vector.memset(hgb[:], 0.0), writes=[("hgb",)])
    ft = lambda n: [sb(f"rw_{n}{i}", [128, TS], F32) for i in range(2)]
    def ft1(n):
        t = sb(f"rw_{n}", [128, TS], F32)
        return [t, t]
    rT = ft("rT"); kT = ft("kT"); vT = ft("vT"); gT = ft1("gT"); sg = ft("sg"); asg = ft("asg")
    kq = ft1("kq"); kq2 = ft1("kq2"); rn = ft1("rn"); kk = ft("kk"); t1 = ft1("t1"); kz = ft("kz"); prod = ft1("prod")
    cs = ft1("cs"); E1 = [[sb(f"rw_E1_{q}_{i}", [128, TS], F32) for i in range(4)] for q in range(2)]; E2 = ft1("E2"); E3 = ft1("E3"); dd = ft1("dd"); bb = ft1("bb")
    af = ft("af")
    BKf = [sb(f"rw_BKf{i}", [128, 2, TS], F32) for i in range(2)]
    Hat = [sb(f"rw_Hat{i}", [128, 2, TS], F32) for i in range(2)]
    RA = [[sb(f"rw_RA{q}_{i}", [128, 2, TS], SD) for i in range(4)] for q in range(2)]
    RAf1 = [[sb(f"rw_RAf{q}_{i}", [128, TS], F32) for i in range(4)] for q in range(2)]
    LBt = [[sb(f"rw_LB{q}_{i}", [128, 2, TS], SD) for i in range(4)] for q in range(2)]
    TM = [[[sb(f"rw_TM{q}_{c}_{p}", [128, 4, 128], SD) for p in range(NPR)] for c in range(4)] for q in range(2)]
    NU = 2 * 2 * NPR
    AM = [sb(f"rw_AM{u}", [128, 512], SD) for u in range(NU)]
    Mk = [[sb(f"rw_M{u}_{i}", [128, 128], SD) for i in range(2)] for u in range(NU)]
    Nk = [[sb(f"rw_N{u}_{i}", [128, 128], SD) for i in range(2)] for u in range(NU)]
    Xs = [sb(f"rw_Xs{u}", [128, 128], SD) for u in range(NU)]
    ATM = [sb(f"rw_ATM{u}", [128, 64], SD) for u in range(NU)]
    Ws = [sb(f"rw_Ws{u}", [128, 64], SD) for u in range(NU)]
    UV = [sb(f"rw_UV{u}", [128, 64], SD) for u in range(NU)]
    GT = [sb(f"rw_GT{c}", [128, NCK, 64], F32) for c in range(4)]
    Hf = [sb(f"rw_Hf{c}", [128, NCK, 64], F32) for c in range(4)]
    RH = [sb(f"rw_RH{c}", [128, TS], F32) for c in range(4)]
    YV = [[sb(f"rw_YV{c}_{p}", [128, 128], F32) for p in range(NPR)] for c in range(4)]
    ST = [sb(f"rw_ST{c}", [128, 64], F32) for c in range(4)]
    yt = [[sb(f"rw_yt{c}_{p}", [128, 128], F32) for p in range(NPR)] for c in range(4)]
    for c in range(4):
        S.op("dve", lambda: nc.vector.memset(ST[c][:], 0.0), writes=[("ST", c, 0), ("ST", c, 1)])

    xv = d["xT"].rearrange("(c p) t -> p c t", p=128)
    pr = [0]

    def bank():
        b = pr[0]; pr[0] = (pr[0] + 1) % 8
        return b

    def gen_abc(ti):
        par = ti % 2
        t0 = ti * TS
        xs_ = 0
        X = xt[xs_]
        lo = t0 - 1 if ti > 0 else t0
        hi = t0 + TS + 1 if ti < NTI - 1 else t0 + TS
        if ti == 0:
            S.op("pool", lambda: nc.gpsimd.memset(X[:, :, 0:1], 0.0), writes=[("xt", xs_)])
        if ti == NTI - 1:
            S.op("pool", lambda: nc.gpsimd.memset(X[:, :, TS + 1:TS + 2], 0.0), writes=[("xt", xs_)])
        S.dma("sp", X[:, :, (lo - t0 + 1):(hi - t0 + 1)], xv[:, :, lo:hi], writes=[("xt", xs_)])
        for c in range(8):
            S.op("pool", lambda: nc.gpsimd.tensor_tensor(out=xs[:], in0=X[:, c, 0:TS], in1=X[:, c, 2:TS + 2], op=ALU.add),
                 reads=[("xt", xs_)], writes=[("xs",)])
            S.op("dve", lambda: nc.vector.scalar_tensor_tensor(out=xx[:], in0=xs[:], scalar=0.5, in1=X[:, c, 1:TS + 1], op0=ALU.mult, op1=ALU.subtract),
                 reads=[("xs",), ("xt", xs_)], writes=[("xx",)])
            for i in range(3):
                S.op("dve", lambda: nc.vector.scalar_tensor_tensor(out=xm[:, i, c, :], in0=xx[:], scalar=mixc[:, i, c:c + 1], in1=X[:, c, 1:TS + 1], op0=ALU.mult, op1=ALU.add),
                     reads=[("xx",), ("xt", xs_), ("mixc",)], writes=[("xm", i, c)])
            for i in (3, 4, 5):
                mt = mixt[i - 3]
                S.op("act", lambda: nc.scalar.activation(out=mt[:], in_=xx[:], func=AF.Identity, scale=mixc[:, i, c:c + 1]),
                     reads=[("xx",), ("mixc",)], writes=[("mixt", i)])
                S.op("pool", lambda: nc.gpsimd.tensor_tensor(out=xm[:, i, c, :], in0=mt[:], in1=X[:, c, 1:TS + 1], op=ALU.add),
                     reads=[("mixt", i), ("xt", xs_)], writes=[("xm", i, c)])
            yield
        yield
        def proj(out_ap, lhs_fn, mi, keys, M=128):
            for c in range(8):
                S.op("pe", lambda: nc.tensor.matmul(out_ap, lhs_fn(c), xm[:, mi, c, :], start=(c == 0), stop=(c == 7)),
                     reads=keys + [("xm", mi, c)], writes=[("ps", bk)], signal=(c == 7))
        bk = bank()
        proj(ps[0:64, bk, 0:TS], lambda c: w1b[:, c, :], 1, [("w1b",)])
        S.op("act", lambda: nc.scalar.activation(out=hw[:], in_=ps[0:64, bk, 0:TS], func=AF.Tanh), reads=[("ps", bk)], writes=[("hw",)])
        bk = bank()
        proj(ps[0:64, bk, 0:TS], lambda c: a1b[:, c, :], 4, [("a1b",)])
        S.op("act", lambda: nc.scalar.copy(out=ha[:], in_=ps[0:64, bk, 0:TS]), reads=[("ps", bk)], writes=[("ha",)])
        bk = bank()
        proj(ps[:, bk, 0:TS], lambda c: g1b[:, c, 0:128], 5, [("g1b",)])
        S.op("act", lambda: nc.scalar.activation(out=hga[:], in_=ps[:, bk, 0:TS], func=AF.Sigmoid), reads=[("ps", bk)], writes=[("hga",)])
        bk = bank()
        proj(ps[0:32, bk, 0:TS], lambda c: g1b[:, c, 128:160], 5, [("g1b",)])
        S.op("act", lambda: nc.scalar.activation(out=hgb[0:32, :], in_=ps[0:32, bk, 0:TS], func=AF.Sigmoid), reads=[("ps", bk)], writes=[("hgb",)])
        for cc in range(4):
            f = cc % 2
            csl = slice(cc * 128, (cc + 1) * 128)
            tsl = slice(t0, t0 + TS)
            bk = bank(); proj(ps[:, bk, 0:TS], lambda c: W3[:, 0, c, csl], 0, [("W3", 0)])
            S.op("act", lambda: nc.scalar.copy(out=rT[f][:], in_=ps[:, bk, 0:TS]), reads=[("ps", bk)], writes=[("rT", f)])
            yield
            bk = bank(); proj(ps[:, bk, 0:TS], lambda c: W3[:, 1, c, csl], 2, [("W3", 1)])
            S.op("act", lambda: nc.scalar.copy(out=kT[f][:], in_=ps[:, bk, 0:TS]), reads=[("ps", bk)], writes=[("kT", f)])
            yield
            bk = bank(); proj(ps[:, bk, 0:TS], lambda c: W3[:, 2, c, csl], 3, [("W3", 2)])
            S.op("act", lambda: nc.scalar.copy(out=vT[f][:], in_=ps[:, bk, 0:TS]), reads=[("ps", bk)], writes=[("vT", f)])
            S.dma("sp", d["v_out"][csl, tsl], vT[f][:], reads=[("vT", f)])
            bk = bank()
            S.op("pe", lambda: nc.tensor.matmul(ps[:, bk, 0:TS], w2b[:, csl], hw[:], start=True, stop=True), reads=[("w2b",), ("hw",)], writes=[("ps", bk)])
            S.op("act", lambda: nc.scalar.activation(out=sg[f][:], in_=ps[:, bk, 0:TS], func=AF.Sigmoid, bias=cols[:, 0, cc:cc + 1], scale=1.0),
                 reads=[("ps", bk), ("cols", 0)], writes=[("sg", f)])
            bk = bank()
            S.op("pe", lambda: nc.tensor.matmul(ps[:, bk, 0:TS], a2b[:, csl], ha[:], start=True, stop=True), reads=[("a2b",), ("ha",)], writes=[("ps", bk)])
            S.op("act", lambda: nc.scalar.activation(out=asg[f][:], in_=ps[:, bk, 0:TS], func=AF.Sigmoid, bias=cols[:, 1, cc:cc + 1], scale=1.0),
                 reads=[("ps", bk), ("cols", 1)], writes=[("asg", f)])
            bk = bank()
            S.op("pe", lambda: nc.tensor.matmul(ps[:, bk, 0:TS], g2a[:, csl], hga[:], start=True, stop=False), reads=[("g2a",), ("hga",)], writes=[("ps", bk)], signal=False)
            S.op("pe", lambda: nc.tensor.matmul(ps[:, bk, 0:TS], g2b[:, csl], hgb[:], start=False, stop=True), reads=[("g2b",), ("hgb",)], writes=[("ps", bk)])
            S.op("act", lambda: nc.scalar.copy(out=gT[f][:], in_=ps[:, bk, 0:TS]), reads=[("ps", bk)], writes=[("gT", 0)])
            S.dma("sp", d["g_out"][csl, tsl], gT[f][:], reads=[("gT", 0)])
            yield
            S.op("dve", lambda: nc.vector.tensor_scalar(out=kq[f][:], in0=kT[f][:], scalar1=cols[:, 2, cc:cc + 1], scalar2=None, op0=ALU.mult),
                 reads=[("kT", f), ("cols", 2)], writes=[("kq", 0)])
            S.op("pool", lambda: nc.gpsimd.tensor_tensor(out=kq2[f][:], in0=kq[f][:], in1=kq[f][:], op=ALU.mult), reads=[("kq", 0)], writes=[("kq2", 0)])
            bk = bank()
            S.op("pe", lambda: nc.tensor.matmul(ps[:, bk, 0:TS], blkf, kq2[f][:], start=True, stop=True), reads=[("cst",), ("kq2", 0)], writes=[("ps", bk)])
            S.op("dve", lambda: nc.vector.tensor_scalar(out=rn[f][:], in0=ps[:, bk, 0:TS], scalar1=1e-24, scalar2=None, op0=ALU.max),
                 reads=[("ps", bk)], writes=[("rn", 0)])
            S.op("act", lambda: nc.scalar.activation(out=rn[f][:], in_=rn[f][:], func=AF.Ln), reads=[("rn", 0)], writes=[("rn", 0)])
            S.op("act", lambda: nc.scalar.activation(out=rn[f][:], in_=rn[f][:], func=AF.Exp, scale=-0.5), reads=[("rn", 0)], writes=[("rn", 0)])
            S.op("pool", lambda: nc.gpsimd.tensor_tensor(out=kk[f][:], in0=kq[f][:], in1=rn[f][:], op=ALU.mult), reads=[("kq", 0), ("rn", 0)], writes=[("kk", f)])
            S.op("dve", lambda: nc.vector.tensor_scalar(out=t1[f][:], in0=asg[f][:], scalar1=-1.0, scalar2=cols[:, 3, cc:cc + 1], op0=ALU.add, op1=ALU.mult),
                 reads=[("asg", f), ("cols", 3)], writes=[("t1", 0)])
            S.op("dve", lambda: nc.vector.scalar_tensor_tensor(out=kz[f][:], in0=t1[f][:], scalar=1.0, in1=kT[f][:], op0=ALU.add, op1=ALU.mult),
                 reads=[("t1", 0), ("kT", f)], writes=[("kz", f)])
            S.op("dve", lambda: nc.vector.scalar_tensor_tensor(out=prod[f][:], in0=rT[f][:], scalar=cols[:, 4, cc:cc + 1], in1=kz[f][:], op0=ALU.mult, op1=ALU.mult),
                 reads=[("rT", f), ("kz", f), ("cols", 4)], writes=[("prod", 0)])
            S.dma("sp", d["p_out"][csl, tsl], prod[f][:], reads=[("prod", 0)])
            yield
            for ck in range(NCK):
                ksl = slice(ck * CH, (ck + 1) * CH)
                S.op("dve", lambda: nc.vector.tensor_tensor_scan(out=cs[f][:, ksl], data0=ones64[:], data1=sg[f][:, ksl], initial=0.0, op0=ALU.mult, op1=ALU.add),
                     reads=[("sg", f), ("ones64",)], writes=[("cs", 0)])
            S.op("act", lambda: nc.scalar.activation(out=E1[par][cc][:], in_=cs[f][:], func=AF.Exp, scale=-C0), reads=[("cs", 0)], writes=[("E1", par, cc)])
            S.op("act", lambda: nc.scalar.activation(out=E2[f][:], in_=cs[f][:], func=AF.Exp, scale=C0), reads=[("cs", 0)], writes=[("E2", 0)])
            S.op("pool", lambda: nc.gpsimd.tensor_tensor(out=dd[f][:], in0=cs[f][:], in1=sg[f][:], op=ALU.subtract), reads=[("cs", 0), ("sg", f)], writes=[("dd", 0)])
            S.op("act", lambda: nc.scalar.activation(out=E3[f][:], in_=dd[f][:], func=AF.Exp, scale=-C0), reads=[("dd", 0)], writes=[("E3", 0)])
            yield
            S.op("dve", lambda: nc.vector.scalar_tensor_tensor(out=af[f][:], in0=kk[f][:], scalar=-1.0, in1=E3[f][:], op0=ALU.mult, op1=ALU.mult),
                 reads=[("kk", f), ("E3", 0)], writes=[("af", f)])
            S.op("act", lambda: nc.scalar.copy(out=RA[par][cc][:, 0, :], in_=af[f][:]), reads=[("af", f)], writes=[("RA", par, cc, 0)])
            S.op("pool", lambda: nc.gpsimd.tensor_tensor(out=RAf1[par][cc][:], in0=rT[f][:], in1=E1[par][cc][:], op=ALU.mult), reads=[("rT", f), ("E1", par, cc)], writes=[("RAf1", par, cc)])
            S.op("act", lambda: nc.scalar.copy(out=RA[par][cc][:, 1, :], in_=RAf1[par][cc][:]), reads=[("RAf1", par, cc)], writes=[("RA", par, cc, 1)])
            S.op("pool", lambda: nc.gpsimd.tensor_tensor(out=bb[f][:], in0=kk[f][:], in1=asg[f][:], op=ALU.mult), reads=[("kk", f), ("asg", f)], writes=[("bb", 0)])
            S.op("pool", lambda: nc.gpsimd.tensor_tensor(out=BKf[f][:, 0, :], in0=bb[f][:], in1=E2[f][:], op=ALU.mult), reads=[("bb", 0), ("E2", 0)], writes=[("BKf", f, 0)])
            S.op("pool", lambda: nc.gpsimd.tensor_tensor(out=BKf[f][:, 1, :], in0=kz[f][:], in1=E2[f][:], op=ALU.mult), reads=[("kz", f), ("E2", 0)], writes=[("BKf", f, 1)])
            S.op("act", lambda: nc.scalar.copy(out=LBt[par][cc][:], in_=BKf[f][:]), reads=[("BKf", f, 0), ("BKf", f, 1)], writes=[("LBt", par, cc)])
            for ck in range(NCK):
                ksl = slice(ck * CH, (ck + 1) * CH)
                e = ck * CH + CH - 1
                S.op("dve", lambda: nc.vector.tensor_scalar(out=Hat[f][:, :, ksl], in0=BKf[f][:, :, ksl], scalar1=E1[par][cc][:, e:e + 1], scalar2=None, op0=ALU.mult),
                     reads=[("BKf", f, 0), ("BKf", f, 1), ("E1", par, cc)], writes=[("Hat", f)])
            yield
            for p in range(NPR):
                psl = slice(p * 128, (p + 1) * 128)
                bk = bank()
                srcs = [(af[f][:, psl], ("af", f)), (Hat[f][:, 0, psl], ("Hat", f)), (Hat[f][:, 1, psl], ("Hat", f)), (vT[f][:, psl], ("vT", f))]
                for i, (src, key) in enumerate(srcs):
                    S.op("pe", lambda: nc.tensor.transpose(ps[:, bk, i * 128:(i + 1) * 128], src, identf), reads=[key, ("cst",)], writes=[("ps", bk)], signal=(i == 3))
                S.op("act", lambda: nc.scalar.copy(out=TM[par][cc][p][:].rearrange("p a n -> p (a n)"), in_=ps[:, bk, :]), reads=[("ps", bk)], writes=[("TM", par, cc, p)])
        yield

    def gen_de(ti):
        par = ti % 2
        t0 = ti * TS
        for ccg in range(2):
            units = [(cc, hd, p) for cc in (2 * ccg, 2 * ccg + 1) for hd in range(2) for p in range(NPR)]
            def uid(cc, hd, p):
                return ((cc % 2) * 2 + hd) * NPR + p
            for (cc, hd, p) in units:
                u = uid(cc, hd, p)
                hs = slice(hd * 64, hd * 64 + 64); tk = slice(p * 128, (p + 1) * 128)
                bk = bank()
                S.op("pe", lambda: nc.tensor.matmul(ps[:, bk, 0:256], LBt[par][cc][hs, 0, tk], RA[par][cc][hs, :, tk], start=True, stop=True),
                     reads=[("LBt", par, cc), ("RA", par, cc, 0), ("RA", par, cc, 1)], writes=[("ps", bk)], signal=False)
                S.op("pe", lambda: nc.tensor.matmul(ps[:, bk, 256:512], LBt[par][cc][hs, 1, tk], RA[par][cc][hs, :, tk], start=True, stop=True),
                     reads=[("LBt", par, cc), ("RA", par, cc, 0), ("RA", par, cc, 1)], writes=[("ps", bk)])
                S.op("dve", lambda: nc.vector.tensor_tensor(out=AM[u][:], in0=ps[:, bk, :], in1=maskSI, op=ALU.mult), reads=[("ps", bk), ("cst",)], writes=[("AM", u)])
                bk = bank()
                S.op("pe", lambda: nc.tensor.matmul(ps[:, bk, 0:128], RA[par][cc][hs, 0, tk], LBt[par][cc][hs, 0, tk], start=True, stop=True),
                     reads=[("LBt", par, cc), ("RA", par, cc, 0)], writes=[("ps", bk)])
                S.op("dve", lambda: nc.vector.tensor_tensor(out=Nk[u][0][:], in0=ps[:, bk, 0:128], in1=maskTS, op=ALU.mult), reads=[("ps", bk), ("cst",)], writes=[("Nk", u, 0)])
                S.op("pool", lambda: nc.gpsimd.tensor_tensor(out=Xs[u][:], in0=AM[u][:, 0:128], in1=identf, op=ALU.add), reads=[("AM", u), ("cst",)], writes=[("Xs", u)])
                yield
            for k in range(1, 6):
                for (cc, hd, p) in units:
                    u = uid(cc, hd, p)
                    Mprev = AM[u][:, 0:128] if k == 1 else Mk[u][(k - 1) % 2][:]
                    Mkey = ("AM", u) if k == 1 else ("Mk", u, (k - 1) % 2)
                    Nprev = Nk[u][(k - 1) % 2][:]
                    Nkey = ("Nk", u, (k - 1) % 2)
                    bk = bank()
                    if k <= 4:
                        S.op("pe", lambda: nc.tensor.matmul(ps[:, bk, 0:128], Nprev, Mprev, start=True, stop=True), reads=[Mkey, Nkey], writes=[("ps", bk)], signal=False)
                    S.op("pe", lambda: nc.tensor.matmul(ps[:, bk, 128:256], Mprev, Nprev, start=True, stop=True), reads=[Mkey, Nkey], writes=[("ps", bk)])
                    if k <= 4:
                        S.op("act", lambda: nc.scalar.copy(out=Mk[u][k % 2][:], in_=ps[:, bk, 0:128]), reads=[("ps", bk)], writes=[("Mk", u, k % 2)])
                    S.op("act", lambda: nc.scalar.copy(out=Nk[u][k % 2][:], in_=ps[:, bk, 128:256]), reads=[("ps", bk)], writes=[("Nk", u, k % 2)])
                    yield
                for (cc, hd, p) in units:
                    u = uid(cc, hd, p)
                    bk = bank()
                    Xkey = ("Xs", u)
                    S.op("pe", lambda: nc.tensor.matmul(ps[:, bk, 0:128], Nk[u][k % 2][:], Xs[u][:], start=True, stop=True), reads=[("Nk", u, k % 2), Xkey], writes=[("ps", bk)])
                    S.op("dve", lambda: nc.vector.tensor_tensor(out=Xs[u][:], in0=ps[:, bk, 0:128], in1=Xs[u][:], op=ALU.add), reads=[("ps", bk), ("Xs", u)], writes=[("Xs", u)])
                    yield
            Xkeyf = lambda u: ("Xs", u)
            for (cc, hd, p) in units:
                u = uid(cc, hd, p)
                hs = slice(hd * 64, hd * 64 + 64)
                bk = bank()
                S.op("pe", lambda: nc.tensor.matmul(ps[:, bk, 0:64], Xs[u][:], TM[par][cc][p][:, 0, hs], start=True, stop=True), reads=[Xkeyf(u), ("TM", par, cc, p)], writes=[("ps", bk)], signal=False)
                S.op("pe", lambda: nc.tensor.matmul(ps[:, bk, 64:128], AM[u][:, 256:384], TM[par][cc][p][:, 3, hs], start=True, stop=True), reads=[("AM", u), ("TM", par, cc, p)], writes=[("ps", bk)])
                S.op("act", lambda: nc.scalar.copy(out=ATM[u][:], in_=ps[:, bk, 0:64]), reads=[("ps", bk)], writes=[("ATM", u)])
                S.op("act", lambda: nc.scalar.copy(out=Ws[u][:], in_=ps[:, bk, 64:128]), reads=[("ps", bk)], writes=[("Ws", u)])
                yield
            for (cc, hd, p) in units:
                u = uid(cc, hd, p)
                hs = slice(hd * 64, hd * 64 + 64); tk = slice(p * 128, (p + 1) * 128)
                bk = bank()
                S.op("pe", lambda: nc.tensor.matmul(ps[:, bk, 0:64], Xs[u][:], Ws[u][:], start=True, stop=True), reads=[Xkeyf(u), ("Ws", u)], writes=[("ps", bk)])
                S.op("act", lambda: nc.scalar.copy(out=UV[u][:], in_=ps[:, bk, 0:64]), reads=[("ps", bk)], writes=[("UV", u)])
                bk2 = bank()
                S.op("pe", lambda: nc.tensor.matmul(ps[hs, bk2, 0:128], ATM[u][:], AM[u][:, 128:256], start=True, stop=True), reads=[("ATM", u), ("AM", u)], writes=[("ps", bk2)])
                S.op("dve", lambda: nc.vector.tensor_tensor(out=RH[cc][hs, tk], in0=ps[hs, bk2, 0:128], in1=RAf1[par][cc][hs, tk], op=ALU.add),
                     reads=[("ps", bk2), ("RAf1", par, cc)], writes=[("RH", cc, hd)])
                yield
            for (cc, hd, p) in units:
                u = uid(cc, hd, p)
                hs = slice(hd * 64, hd * 64 + 64)
                bk = bank()
                S.op("pe", lambda: nc.tensor.matmul(ps[:, bk, 0:64], AM[u][:, 128:256], UV[u][:], start=True, stop=False), reads=[("AM", u), ("UV", u)], writes=[("ps", bk)], signal=False)
                S.op("pe", lambda: nc.tensor.matmul(ps[:, bk, 0:64], AM[u][:, 384:512], TM[par][cc][p][:, 3, hs], start=False, stop=True), reads=[("AM", u), ("TM", par, cc, p)], writes=[("ps", bk)])
                S.op("act", lambda: nc.scalar.copy(out=YV[cc][p][:, hs], in_=ps[:, bk, 0:64]), reads=[("ps", bk)], writes=[("YV", cc, p, hd)])
                for q in range(2):
                    pb = slice(q * 64, q * 64 + 64)
                    ck = p * 2 + q
                    e = ck * CH + CH - 1
                    bk = bank()
                    S.op("pe", lambda: nc.tensor.matmul(ps[hs, bk, 0:64], ATM[u][pb, :], TM[par][cc][p][pb, 1, hs], start=True, stop=True), reads=[("ATM", u), ("TM", par, cc, p)], writes=[("ps", bk)])
                    S.op("dve", lambda: nc.vector.scalar_tensor_tensor(out=GT[cc][hs, ck, :], in0=identf[hs, hs], scalar=E1[par][cc][hs, e:e + 1], in1=ps[hs, bk, 0:64], op0=ALU.mult, op1=ALU.add),
                         reads=[("ps", bk), ("cst",), ("E1", par, cc)], writes=[("GT", cc, hd)])
                    bk = bank()
                    S.op("pe", lambda: nc.tensor.matmul(ps[hs, bk, 0:64], TM[par][cc][p][pb, 1, hs], UV[u][pb, :], start=True, stop=False), reads=[("TM", par, cc, p), ("UV", u)], writes=[("ps", bk)], signal=False)
                    S.op("pe", lambda: nc.tensor.matmul(ps[hs, bk, 0:64], TM[par][cc][p][pb, 2, hs], TM[par][cc][p][pb, 3, hs], start=False, stop=True), reads=[("TM", par, cc, p)], writes=[("ps", bk)])
                    S.op("act", lambda: nc.scalar.copy(out=Hf[cc][hs, ck, :], in_=ps[hs, bk, 0:64]), reads=[("ps", bk)], writes=[("Hf", cc, hd)])
                    yield
        for ck in range(NCK):
            p = ck // 2; q = ck % 2
            pb = slice(q * 64, q * 64 + 64)
            for cc in range(4):
                for hd in range(2):
                    hs = slice(hd * 64, hd * 64 + 64)
                    bkY = bank(); bkS = bank()
                    S.op("pe", lambda: nc.tensor.matmul(ps[pb, bkY, 0:64], RH[cc][hs, ck * CH:(ck + 1) * CH], ST[cc][hs, :], start=True, stop=True),
                         reads=[("RH", cc, hd), ("ST", cc, hd)], writes=[("ps", bkY)])
                    S.op("pe", lambda: nc.tensor.matmul(ps[hs, bkS, 0:64], GT[cc][hs, ck, :], ST[cc][hs, :], start=True, stop=True),
                         reads=[("GT", cc, hd), ("ST", cc, hd)], writes=[("ps", bkS)])
                    S.op("dve", lambda: nc.vector.tensor_tensor(out=ST[cc][hs, :], in0=ps[hs, bkS, 0:64], in1=Hf[cc][hs, ck, :], op=ALU.add),
                         reads=[("ps", bkS), ("Hf", cc, hd)], writes=[("ST", cc, hd)])
                    S.op("dve", lambda: nc.vector.tensor_tensor(out=yt[cc][p][pb, hs], in0=ps[pb, bkY, 0:64], in1=YV[cc][p][pb, hs], op=ALU.add),
                         reads=[("ps", bkY), ("YV", cc, p, hd)], writes=[("yt", cc, p)])
                yield
                if q == 1:
                    S.dma("sp", d["y_out"][t0 + p * 128:t0 + (p + 1) * 128, cc * 128:(cc + 1) * 128], yt[cc][p][:], reads=[("yt", cc, p)])
        yield

    def drain(g):
        n = 0
        for _ in g:
            n += 1
        return n

    n_abc = drain(gen_abc(0))
    n_de = None
    for ti in range(NTI):
        ga = gen_abc(ti + 1) if ti + 1 < NTI else None
        gd = gen_de(ti)
        ca = 0; cd = 0
        while ga is not None or gd is not None:
            fa = ca / n_abc if ga is not None else 2.0
            fd = cd / n_de if (gd is not None and n_de) else (ca / n_abc if gd is not None else 2.0)
            if gd is not None and (ga is None or fd <= fa):
                try:
                    next(gd); cd += 1
                except StopIteration:
                    gd = None
                    if n_de is None:
                        n_de = max(cd, 1)
            else:
                try:
                    next(ga); ca += 1
                except StopIteration:
                    ga = None
        if n_de is None:
            n_de = max(cd, 1)
    S.finish("sp")


def build_rw1_prog(SL, TS=256, SD=BF16):
    nc = bass.Bass("TRN2", target_bir_lowering=False)
    dt = lambda n, s: nc.dram_tensor(n, s, F32, kind="ExternalInput").ap()
    d = {"xT": dt("xT", [D, SL]), "mix": dt("mix", [6, D]), "consts": dt("consts", [128, 896])}
    for nm in ("w_r", "w_k", "w_v"):
        d[nm] = dt(nm, [D, 512])
    d["w1"] = dt("w1", [D, 64]); d["w2"] = dt("w2", [64, 512]); d["w0"] = dt("w0", [512])
    d["a1"] = dt("a1", [D, 64]); d["a2"] = dt("a2", [64, 512]); d["a0"] = dt("a0", [512])
    d["g1"] = dt("g1", [D, 160]); d["g2"] = dt("g2", [160, 512])
    for nm in ("k_k", "k_a", "r_k"):
        d[nm] = dt(nm, [512])
    do = lambda n, s: nc.dram_tensor(n, s, F32, kind="ExternalOutput").ap()
    d["y_out"] = do("y_out", [SL, 512]); d["p_out"] = do("p_out", [512, SL]); d["v_out"] = do("v_out", [512, SL]); d["g_out"] = do("g_out", [512, SL])
    with ExitStack() as es:
        S = Sched(nc, es)
        ps = es.enter_context(nc.psum_tensor("ps", [128, 8, 512], F32))
        emit_rw1(S, nc, es, ps, d, SL, TS, SD)
    return nc


GN_EPS = 64e-5


def emit_rw2(S, nc, es, ps, d, T, xT, xb, LB):
    NT = T // 512
    tsl = lambda tt: slice(tt * 512, (tt + 1) * 512)
    sb = lambda es_, n, s, dt: es_.enter_context(nc.sbuf_tensor(n, s, dt))
    with ExitStack() as es2:
        cst = sb(es2, "r2_cst", [128, 896], F32)
        S.dma("sp", cst[:], d["consts"], writes=[("cst2",)])
        blkf = cst[:, 768:896]
        wout = sb(es2, "r2_wout", [128, 8, 1024], BF16)
        S.dma("pool", wout[:], d["w_out"].rearrange("(c p) n -> p c n", p=128), writes=[("r2wout",)])
        gcol = sb(es2, "r2_gcol", [128, 8], F32); bcol = sb(es2, "r2_bcol", [128, 8], F32); epsg = sb(es2, "r2_eps", [128, 1], F32)
        S.dma("sp", gcol[:], d["lnx_g"].rearrange("(c p) -> p c", p=128), writes=[("r2g",)], allow_slow_non_contiguous=True)
        S.dma("sp", bcol[:], d["lnx_b"].rearrange("(c p) -> p c", p=128), writes=[("r2b",)], allow_slow_non_contiguous=True)
        S.op("dve", lambda: nc.vector.memset(epsg[:], GN_EPS), writes=[("r2eps",)])
        names = ["yf", "yb", "pf", "pb", "v", "g"]
        tin = {n: [sb(es2, f"r2_{n}{i}", [128, 512], F32) for i in range(2)] for n in names}
        tmp = {n: [sb(es2, f"r2_t{n}{i}", [128, 512], F32) for i in range(2)] for n in ["y", "ysq", "mean", "a", "b", "pp"]}
        dv = {n: d[n].rearrange("(c p) t -> p c t", p=128) for n in names + ["x"]}
        it = 0
        for tt in range(NT):
            S.dma("sp", xT[:, :, tsl(tt)], dv["x"][:, :, tsl(tt)], writes=[("xT", c, tt) for c in range(8)])
            for c in range(8):
                i = it % 2; it += 1
                for n in names:
                    S.dma("sp", tin[n][i][:], dv[n][:, c, tsl(tt)], writes=[("r2in", n, i)])
                Y = tmp["y"][i]; YS = tmp["ysq"][i]; MN = tmp["mean"][i]; A = tmp["a"][i]; Bt = tmp["b"][i]; PP = tmp["pp"][i]
                S.op("dve", lambda: nc.vector.tensor_tensor(out=Y[:], in0=tin["yf"][i][:], in1=tin["yb"][i][:], op=ALU.add),
                     reads=[("r2in", "yf", i), ("r2in", "yb", i)], writes=[("r2y", i)])
                S.op("act", lambda: nc.scalar.activation(out=YS[:], in_=Y[:], func=AF.Square), reads=[("r2y", i)], writes=[("r2ysq", i)])
                S.op("pool", lambda: nc.gpsimd.tensor_tensor(out=PP[:], in0=tin["pf"][i][:], in1=tin["pb"][i][:], op=ALU.add),
                     reads=[("r2in", "pf", i), ("r2in", "pb", i)], writes=[("r2pp", i)])
                b1 = 0 + 3 * (it % 2); b2 = b1 + 1; b3 = b1 + 2
                S.op("pe", lambda: nc.tensor.matmul(ps[:, b1, :], blkf, Y[:], start=True, stop=True), reads=[("cst2",), ("r2y", i)], writes=[("ps", b1)])
                S.op("pe", lambda: nc.tensor.matmul(ps[:, b2, :], blkf, YS[:], start=True, stop=True), reads=[("cst2",), ("r2ysq", i)], writes=[("ps", b2)])
                S.op("pe", lambda: nc.tensor.matmul(ps[:, b3, :], blkf, PP[:], start=True, stop=True), reads=[("cst2",), ("r2pp", i)], writes=[("ps", b3)])
                S.op("act", lambda: nc.scalar.activation(out=MN[:], in_=ps[:, b1, :], func=AF.Identity, scale=1.0 / 64), reads=[("ps", b1)], writes=[("r2mean", i)])
                S.op("dve", lambda: nc.vector.tensor_tensor(out=A[:], in0=ps[:, b1, :], in1=MN[:], op=ALU.mult), reads=[("ps", b1), ("r2mean", i)], writes=[("r2a", i)])
                S.op("dve", lambda: nc.vector.tensor_tensor(out=A[:], in0=ps[:, b2, :], in1=A[:], op=ALU.subtract), reads=[("ps", b2), ("r2a", i)], writes=[("r2a", i)])
                S.op("act", lambda: nc.scalar.activation(out=A[:], in_=A[:], func=AF.Ln, bias=epsg[:], scale=1.0 / 64), reads=[("r2a", i), ("r2eps",)], writes=[("r2a", i)])
                S.op("act", lambda: nc.scalar.activation(out=A[:], in_=A[:], func=AF.Exp, scale=-0.5), reads=[("r2a", i)], writes=[("r2a", i)])
                S.op("pool", lambda: nc.gpsimd.tensor_tensor(out=Bt[:], in0=Y[:], in1=MN[:], op=ALU.subtract), reads=[("r2y", i), ("r2mean", i)], writes=[("r2b_", i)])
                S.op("pool", lambda: nc.gpsimd.tensor_tensor(out=Bt[:], in0=Bt[:], in1=A[:], op=ALU.mult), reads=[("r2b_", i), ("r2a", i)], writes=[("r2b_", i)])
                S.op("act", lambda: nc.scalar.activation(out=Bt[:], in_=Bt[:], func=AF.Identity, bias=bcol[:, c:c + 1], scale=gcol[:, c:c + 1]),
                     reads=[("r2b_", i), ("r2g",), ("r2b",)], writes=[("r2b_", i)])
                S.op("dve", lambda: nc.vector.tensor_tensor(out=YS[:], in0=ps[:, b3, :], in1=tin["v"][i][:], op=ALU.mult), reads=[("ps", b3), ("r2in", "v", i)], writes=[("r2ysq", i)])
                S.op("pool", lambda: nc.gpsimd.tensor_tensor(out=Bt[:], in0=Bt[:], in1=YS[:], op=ALU.add), reads=[("r2b_", i), ("r2ysq", i)], writes=[("r2b_", i)])
                S.op("dve", lambda: nc.vector.tensor_tensor(out=xb[:, c, tsl(tt)], in0=Bt[:], in1=tin["g"][i][:], op=ALU.mult), reads=[("r2b_", i), ("r2in", "g", i)], writes=[("xb", c, tt)])
            for dc in range(8):
                bank = 6 + (dc % 2)
                for c in range(8):
                    S.op("pe", lambda: nc.tensor.matmul(ps[:, bank, :], wout[:, c, dc * 128:(dc + 1) * 128], xb[:, c, tsl(tt)], start=(c == 0), stop=(c == 7)),
                         reads=[("r2wout",), ("xb", c, tt)], writes=[("ps", bank)], signal=(c == 7))
                S.op("pool", lambda: nc.gpsimd.tensor_scalar(out=xT[:, dc, tsl(tt)], in0=xT[:, dc, tsl(tt)], scalar1=float(ALPHA), scalar2=0.0, op0=ALU.mult, op1=ALU.add),
                     reads=[("xT", dc, tt)], writes=[("xT", dc, tt)])
                S.op("dve", lambda: nc.vector.tensor_tensor(out=xT[:, dc, tsl(tt)], in0=ps[:, bank, :], in1=xT[:, dc, tsl(tt)], op=ALU.add),
                     reads=[("ps", bank), ("xT", dc, tt)], writes=[("xT", dc, tt)])
    S.fence()


def build_rw2_prog(T):
    nc = bass.Bass("TRN2", target_bir_lowering=False)
    dt = lambda n, s: nc.dram_tensor(n, s, F32, kind="ExternalInput").ap()
    d = {n: dt(n, [D, T]) for n in ["x", "yf", "yb", "pf", "pb", "v", "g"]}
    d["consts"] = dt("consts", [128, 896])
    d["lnx_g"] = dt("lnx_g", [D]); d["lnx_b"] = dt("lnx_b", [D]); d["w_out"] = dt("w_out", [D, D])
    lng = dt("lng", [2, D]); lnb = dt("lnb", [2, D])
    w_in = dt("w_in", [D, 2 * FF]); w_outf = dt("w_outf", [FF, D])
    yT_d = nc.dram_tensor("yT", [D, T], F32, kind="ExternalOutput").ap()
    with ExitStack() as es:
        S = Sched(nc, es)
        ps = es.enter_context(nc.psum_tensor("ps", [128, 8, 512], F32))
        xT = es.enter_context(nc.sbuf_tensor("xTs", [128, 8, T], F32))
        xb = es.enter_context(nc.sbuf_tensor("xbs", [128, 8, T], BF16))
        B = FfnBufs(nc, es, T, with_ffn=False)
        B.init_consts(S, nc)
        emit_rw2(S, nc, es, ps, d, T, xT, xb, B)
        NT = T // 512
        emit_ln(S, nc, B, xT, xb, ps, lng[0], lnb[0], NT)
        B.alloc_ffn(nc, es)
        emit_ffn_ln(S, nc, B, xT, xb, ps, w_in, w_outf, lng[1], lnb[1], "f")
        yv = yT_d.rearrange("(c p) t -> p c t", p=128)
        for tt in range(NT):
            S.dma("sp", yv[:, :, tt * 512:(tt + 1) * 512], xT[:, :, tt * 512:(tt + 1) * 512], reads=[("xT", c, tt) for c in range(8)])
        S.finish("sp")
    return nc


_PROGS = {}


def _prog(key, fn):
    if key not in _PROGS:
        _PROGS[key] = fn()
    return _PROGS[key]


def _run(nc, in_maps):
    res = run_bass_kernel_spmd(nc, in_maps, core_ids=list(range(8)))
    return res.results


def _tok(c):
    b = c // 4
    t0 = (c % 4) * 2048
    return b, t0


def _tr(a):
    return np.ascontiguousarray(a.T)


def kernel(x, ffn_w_in, ffn_w_out, ln_g, ln_b, na_w_qkv, na_b_qkv, na_rpb, na_w_o, na_b_o,
           rw_mix, rw_w_rkv, rw_w0, rw_w1, rw_w2, rw_a0, rw_a1, rw_a2, rw_g1, rw_g2,
           rw_k_k, rw_k_a, rw_r_k, rw_lnx_g, rw_lnx_b, rw_w_out):
    f32 = np.float32
    A = lambda a: np.ascontiguousarray(np.asarray(a, dtype=f32))
    x = A(x)
    Bsz, SEQ, Dm = x.shape
    T = 2048
    nc1 = _prog("ffn1", lambda: build_ffn_prog(T, 1))
    ins = []
    for c in range(8):
        b, t0 = _tok(c)
        ins.append({"xT": _tr(x[b, t0:t0 + T]), "w_in": A(ffn_w_in[0, 0:1]), "w_out": A(ffn_w_out[0, 0:1]),
                    "lng": A(ln_g[0, 0:1]), "lnb": A(ln_b[0, 0:1])})
    r = _run(nc1, ins)
    x1 = np.empty_like(x)
    for c in range(8):
        b, t0 = _tok(c)
        x1[b, t0:t0 + T] = r[c]["yT"].T
    nc2 = _prog("na", lambda: build_na_prog(32))
    ins = []
    for c in range(8):
        b, t0 = _tok(c)
        r0 = (c % 4) * 32
        G = na_halo_rows(r0, 32, 128)
        xg = x1[b].reshape(128, 64, Dm)
        ins.append({"xh": _tr(xg[G].reshape(-1, Dm)), "xo": _tr(x1[b, t0:t0 + T]), "w_qkv": A(na_w_qkv[0]), "b_qkv": A(na_b_qkv[0]),
                    "tab": na_tables(A(na_rpb[0]), r0, 32, 128), "w_o": A(na_w_o[0]), "b_o": A(na_b_o[0]),
                    "lng": A(ln_g[0, 1]), "lnb": A(ln_b[0, 1])})
    r = _run(nc2, ins)
    x2 = np.empty_like(x)
    for c in range(8):
        b, t0 = _tok(c)
        x2[b, t0:t0 + T] = r[c]["yT"].T
    nc3 = _prog("ffn2", lambda: build_ffn_prog(T, 2))
    w_in2 = A(np.stack([ffn_w_in[0, 1], ffn_w_in[1, 0]])); w_out2 = A(np.stack([ffn_w_out[0, 1], ffn_w_out[1, 0]]))
    lng2 = A(np.stack([ln_g[0, 2], ln_g[1, 0]])); lnb2 = A(np.stack([ln_b[0, 2], ln_b[1, 0]]))
    ins = []
    for c in range(8):
        b, t0 = _tok(c)
        ins.append({"xT": _tr(x2[b, t0:t0 + T]), "w_in": w_in2, "w_out": w_out2, "lng": lng2, "lnb": lnb2})
    r = _run(nc3, ins)
    x4 = np.empty_like(x)
    for c in range(8):
        b, t0 = _tok(c)
        x4[b, t0:t0 + T] = r[c]["yT"].T
    nc4 = _prog("rw1", lambda: build_rw1_prog(SEQ, 256, BF16))
    consts = rw_consts()
    ins = []
    for c in range(8):
        b = c // 4; z = (c // 2) % 2; hg = c % 2
        cs_ = slice(hg * 512, (hg + 1) * 512)
        xs_ = x4[b][::-1] if z == 1 else x4[b]
        ins.append({"xT": _tr(xs_), "mix": A(rw_mix[0]), "consts": consts,
                    "w_r": A(rw_w_rkv[0, 0][:, cs_]), "w_k": A(rw_w_rkv[0, 1][:, cs_]), "w_v": A(rw_w_rkv[0, 2][:, cs_]),
                    "w1": A(rw_w1[0, z]), "w2": A(rw_w2[0, z][:, cs_]), "w0": A(rw_w0[0, z][cs_]),
                    "a1": A(rw_a1[0, z]), "a2": A(rw_a2[0, z][:, cs_]), "a0": A(rw_a0[0, z][cs_]),
                    "g1": A(rw_g1[0]), "g2": A(rw_g2[0][:, cs_]),
                    "k_k": A(rw_k_k[0][cs_]), "k_a": A(rw_k_a[0][cs_]), "r_k": A(np.reshape(rw_r_k[0], (-1,))[cs_])})
    r = _run(nc4, ins)
    yT = np.empty((2, Bsz, Dm, SEQ), f32); pT = np.empty((2, Bsz, Dm, SEQ), f32)
    vT = np.empty((Bsz, Dm, SEQ), f32); gT = np.empty((Bsz, Dm, SEQ), f32)
    for c in range(8):
        b = c // 4; z = (c // 2) % 2; hg = c % 2
        cs_ = slice(hg * 512, (hg + 1) * 512)
        yo = r[c]["y_out"]; po = r[c]["p_out"]
        if z == 1:
            yT[z, b, cs_] = yo[::-1].T
            pT[z, b, cs_] = po[:, ::-1]
        else:
            yT[z, b, cs_] = yo.T
            pT[z, b, cs_] = po
            vT[b, cs_] = r[c]["v_out"]; gT[b, cs_] = r[c]["g_out"]
    nc5 = _prog("rw2", lambda: build_rw2_prog(T))
    ins = []
    lng5 = A(np.stack([ln_g[1, 1], ln_g[1, 2]])); lnb5 = A(np.stack([ln_b[1, 1], ln_b[1, 2]]))
    for c in range(8):
        b, t0 = _tok(c)
        ts_ = slice(t0, t0 + T)
        cp = lambda a: np.ascontiguousarray(a[:, ts_])
        ins.append({"x": _tr(x4[b, ts_]), "yf": cp(yT[0, b]), "yb": cp(yT[1, b]), "pf": cp(pT[0, b]), "pb": cp(pT[1, b]),
                    "v": cp(vT[b]), "g": cp(gT[b]), "consts": consts, "lnx_g": A(rw_lnx_g[0]), "lnx_b": A(rw_lnx_b[0]),
                    "w_out": A(rw_w_out[0]), "lng": lng5, "lnb": lnb5, "w_in": A(ffn_w_in[1, 1]), "w_outf": A(ffn_w_out[1, 1])})
    r = _run(nc5, ins)
    out = np.empty_like(x)
    for c in range(8):
        b, t0 = _tok(c)
        out[b, t0:t0 + T] = r[c]["yT"].T
    return out
```

```python
import numpy as np
from concourse.bass_utils import run_bass_kernel_spmd
import numpy as np
from contextlib import ExitStack
import concourse.bass as bass
import concourse.mybir as mybir

F32 = mybir.dt.float32
BF16 = mybir.dt.bfloat16
AF = mybir.ActivationFunctionType
ALU = mybir.AluOpType
AX = mybir.AxisListType


class Sched:
    SEM_ROT = 12000
    N_DMA_SEM = 24

    def __init__(self, nc, es):
        self.nc = nc
        self.es = es
        self.eng = {"pe": nc.tensor, "act": nc.scalar, "dve": nc.vector, "pool": nc.gpsimd, "sp": nc.sync}
        self.sems = []
        self.cur = {}
        for e in self.eng:
            self.cur[e] = [self._new_sem(f"p_{e}"), 0]
        self.waited = {e: {} for e in self.eng}
        self.last_w = {}
        self.readers = {}
        self.dma_sems = [[self._new_sem(f"dma{i}"), 0] for i in range(self.N_DMA_SEM)]
        self.dma_rr = 0
        self.n_inst = {e: 0 for e in self.eng}
        self.n_wait = {e: 0 for e in self.eng}
        self.pending = {e: False for e in self.eng}
        self.clock = {e: {} for e in self.eng}
        self.tclock = {}
        self.n_fused = {}

    def _new_sem(self, name):
        h = self.es.enter_context(self.nc.semaphore(f"{name}_{len(self.sems)}"))
        self.sems.append(h)
        return len(self.sems) - 1

    def _need(self, e, ticket):
        if ticket is None:
            return False
        sid, val = ticket
        ck = self.clock.setdefault(e, {})
        if ck.get(sid, 0) >= val:
            return False
        for e2, c in self.cur.items():
            if c[0] == sid and val > c[1]:
                assert e2 == e, f"{e} waits on pending ticket of {e2}"
                return False
        ck[sid] = val
        snap = self.tclock.get(ticket)
        if snap:
            for s2, v2 in snap.items():
                if ck.get(s2, 0) < v2:
                    ck[s2] = v2
        return True

    def _wait(self, e, ticket):
        if self._need(e, ticket):
            self.eng[e].wait_ge(self.sems[ticket[0]], ticket[1])
            self.n_wait[e] += 1

    def fence(self):
        ts = []
        for e2, c in self.cur.items():
            assert not self.pending[e2]
            if c[1] > 0:
                ts.append((c[0], c[1]))
        for s_ in self.dma_sems:
            if s_[1] > 0:
                ts.append((s_[0], s_[1]))
        self.fence_tickets = ts
        self.fence_done = set()

    def _collect(self, e, reads, writes):
        rd, wr = [], []
        if getattr(self, "fence_tickets", None) and e not in self.fence_done:
            self.fence_done.add(e)
            for t in self.fence_tickets:
                if self._need(e, t):
                    rd.append(t)
        for k in reads:
            t = self.last_w.get(k)
            if self._need(e, t):
                rd.append(t)
        for k in writes:
            t = self.last_w.get(k)
            if self._need(e, t):
                wr.append(t)
            for sid, val in self.readers.get(k, {}).items():
                if self._need(e, (sid, val)):
                    wr.append((sid, val))
        return rd, wr

    def _deps(self, e, reads, writes):
        rd, wr = self._collect(e, reads, writes)
        for t in rd + wr:
            self.eng[e].wait_ge(self.sems[t[0]], t[1])
            self.n_wait[e] += 1

    def _record(self, ticket, reads, writes):
        sid, val = ticket
        for k in reads:
            r = self.readers.setdefault(k, {})
            if r.get(sid, 0) < val:
                r[sid] = val
        for k in writes:
            self.last_w[k] = ticket
            self.readers[k] = {}

    def op(self, e, fn, reads=(), writes=(), signal=True):
        rd, wr = self._collect(e, reads, writes)
        fuse = None
        if e == "pe":
            if wr:
                fuse = wr.pop()
        elif e in ("act", "dve", "pool"):
            if wr:
                fuse = wr.pop()
            elif rd:
                fuse = rd.pop()
        for t in rd + wr:
            self.eng[e].wait_ge(self.sems[t[0]], t[1])
            self.n_wait[e] += 1
        c = self.cur[e]
        if signal and c[1] >= self.SEM_ROT and not self.pending[e]:
            c[0] = self._new_sem(f"p_{e}")
            c[1] = 0
        inst = fn()
        if fuse is not None:
            inst._wait_ge(self.sems[fuse[0]], fuse[1])
            self.n_fused[e] = self.n_fused.get(e, 0) + 1
        self.n_inst[e] += 1
        if signal:
            c[1] += 1
            inst.then_inc(self.sems[c[0]], 1)
            ticket = (c[0], c[1])
            self.pending[e] = False
            self.tclock[ticket] = dict(self.clock.get(e, {}))
        else:
            ticket = (c[0], c[1] + 1)
            self.pending[e] = True
        self._record(ticket, reads, writes)
        return ticket

    def dma(self, q, out, in_, reads=(), writes=(), **kw):
        self._deps(q, reads, writes)
        if q == "pool":
            slot = [self._new_sem("swdma"), 0]
            self.dma_sems.append(slot)
        else:
            slot = self.dma_sems[self.dma_rr]
            self.dma_rr = (self.dma_rr + 1) % self.N_DMA_SEM
        if slot[1] > 0:
            self._wait(q, (slot[0], slot[1]))
        slot[1] += 16
        self.eng[q].dma_start(out=out, in_=in_, **kw).then_inc(self.sems[slot[0]], 16)
        self.n_inst[q] += 1
        ticket = (slot[0], slot[1])
        self.tclock[ticket] = dict(self.clock.get(q, {}))
        self._record(ticket, reads, writes)
        return ticket

    def finish(self, e="sp"):
        for k, t in list(self.last_w.items()):
            self._wait(e, t)
        for s in self.dma_sems:
            if s[1] > 0:
                self._wait(e, (s[0], s[1]))


D = 1024; FF = 2816; NFC = 22
GROUPS = [(0, 2), (2, 4), (6, 4), (10, 4), (14, 4), (18, 4)]
ALPHA = 4 ** 0.25
LN_EPS = 1e-5


class FfnBufs:
    def __init__(self, nc, es, T, with_ffn=True):
        self.T = T
        self.NT = T // 512
        sb = lambda n, s, d: es.enter_context(nc.sbuf_tensor(n, s, d))
        if with_ffn:
            self.alloc_ffn(nc, es)
        self.sq = [sb(f"sq{i}", [128, 8, 512], BF16) for i in range(1)]
        self.yb = [sb(f"yb{i}", [128, 8, 512], BF16) for i in range(1)]
        self.mean_s = sb("mean_s", [128, 512], F32)
        self.m2 = sb("m2", [128, 512], F32)
        self.rstd = sb("rstd", [128, 512], F32)
        self.t1 = [sb(f"t1_{i}", [128, 512], F32) for i in range(2)]
        self.t2 = [sb(f"t2_{i}", [128, 512], F32) for i in range(2)]
        self.ones = sb("ones", [128, 128], BF16)
        self.gcol = sb("gcol", [128, 8], F32)
        self.bcol = sb("bcol", [128, 8], F32)
        self.epsc = sb("epsc", [128, 1], F32)
        self.wslot = 0

    def alloc_ffn(self, nc, es):
        sb = lambda n, s, d: es.enter_context(nc.sbuf_tensor(n, s, d))
        T = self.T
        self.hh = sb("hh", [128, 4, T], BF16)
        self.wi = [sb(f"wi{i}", [128, 2, 8, 512], BF16) for i in range(2)]
        self.wo = [sb(f"wo{i}", [128, 4, 1024], BF16) for i in range(2)]
        self.sg = [sb(f"sg{i}", [128, 512], F32) for i in range(2)]

    def init_consts(self, S, nc):
        S.op("dve", lambda: nc.vector.memset(self.ones[:], 1.0 / 1024), writes=[("ones",)])
        S.op("dve", lambda: nc.vector.memset(self.epsc[:], LN_EPS), writes=[("epsc",)])


def emit_ffn_ln(S, nc, B, xT, xb, ps, w_in, w_out, ln_g, ln_b, tag, ntiles=None):
    NT = B.NT if ntiles is None else ntiles
    tsl = lambda tt: slice(tt * 512, (tt + 1) * 512)
    w_in_v = w_in.rearrange("(c p) (u f) -> p c u f", p=128, u=2)
    w_out_v = w_out.rearrange("(j p) d -> p j d", p=128)

    def load_group(g):
        f0, n = GROUPS[g]
        slot = B.wslot; B.wslot ^= 1
        for u in range(2):
            S.dma("pool", B.wi[slot][:, u, :, 0:n * 128], w_in_v[:, :, u, f0 * 128:(f0 + n) * 128],
                  writes=[("wi", slot, u)])
        S.dma("pool", B.wo[slot][:, 0:n, :], w_out_v[:, f0:f0 + n, :], writes=[("wo", slot)])
        return slot

    slots = {0: load_group(0)}
    for tt in range(NT):
        for c in range(8):
            S.op("act", lambda: nc.scalar.activation(out=xT[:, c, tsl(tt)], in_=xT[:, c, tsl(tt)], func=AF.Identity, scale=float(ALPHA)),
                 reads=[("xT", c, tt)], writes=[("xT", c, tt)])
    pa = 0
    pb = 0
    for g in range(len(GROUPS)):
        f0, n = GROUPS[g]
        slot = slots[g]
        if g + 1 < len(GROUPS):
            slots[g + 1] = load_group(g + 1)
        for tt in range(NT):
            for j in range(n):
                bg = 2 * pa; bu = 2 * pa + 1; pa ^= 1
                for (bank, u) in ((bg, 0), (bu, 1)):
                    for c in range(8):
                        S.op("pe", lambda: nc.tensor.matmul(ps[:, bank, :], B.wi[slot][:, u, c, j * 128:(j + 1) * 128], xb[:, c, tsl(tt)], start=(c == 0), stop=(c == 7)),
                             reads=[("wi", slot, u), ("xb", c, tt)], writes=[("ps", bank)], signal=(c == 7))
                sgi = (tt * n + j) % 2
                S.op("act", lambda: nc.scalar.activation(out=B.sg[sgi][:], in_=ps[:, bg, :], func=AF.Silu),
                     reads=[("ps", bg)], writes=[("sg", sgi)])
                S.op("dve", lambda: nc.vector.tensor_tensor(out=B.hh[:, j, tsl(tt)], in0=ps[:, bu, :], in1=B.sg[sgi][:], op=ALU.mult),
                     reads=[("ps", bu), ("sg", sgi)], writes=[("hh", j, tt)])
        for tt in range(NT):
            for dc in range(8):
                bank = 4 + pb; pb ^= 1
                for j in range(n):
                    S.op("pe", lambda: nc.tensor.matmul(ps[:, bank, :], B.wo[slot][:, j, dc * 128:(dc + 1) * 128], B.hh[:, j, tsl(tt)], start=(j == 0), stop=(j == n - 1)),
                         reads=[("wo", slot), ("hh", j, tt)], writes=[("ps", bank)], signal=(j == n - 1))
                S.op("dve", lambda: nc.vector.scalar_tensor_tensor(out=xT[:, dc, tsl(tt)], in0=ps[:, bank, :], scalar=0.5, in1=xT[:, dc, tsl(tt)], op0=ALU.mult, op1=ALU.add),
                     reads=[("ps", bank), ("xT", dc, tt)], writes=[("xT", dc, tt)])
    emit_ln(S, nc, B, xT, xb, ps, ln_g, ln_b, NT)


def emit_ln(S, nc, B, xT, xb, ps, ln_g, ln_b, NT, xb_tiles=None):
    tsl = lambda tt: slice(tt * 512, (tt + 1) * 512)
    S.dma("sp", B.gcol[:], ln_g.rearrange("(c p) -> p c", p=128), writes=[("gcol",)], allow_slow_non_contiguous=True)
    S.dma("sp", B.bcol[:], ln_b.rearrange("(c p) -> p c", p=128), writes=[("bcol",)], allow_slow_non_contiguous=True)
    for tt in range(NT):
        i2 = 0
        for c in range(8):
            S.op("act", lambda: nc.scalar.activation(out=B.sq[i2][:, c, :], in_=xT[:, c, tsl(tt)], func=AF.Square),
                 reads=[("xT", c, tt)], writes=[("sq", i2, c)])
            S.op("pool", lambda: nc.gpsimd.tensor_copy(out=B.yb[i2][:, c, :], in_=xT[:, c, tsl(tt)]),
                 reads=[("xT", c, tt)], writes=[("yb", i2, c)])
        for c in range(8):
            S.op("pe", lambda: nc.tensor.matmul(ps[:, 6, :], B.ones[:], B.yb[i2][:, c, :], start=(c == 0), stop=(c == 7)),
                 reads=[("ones",), ("yb", i2, c)], writes=[("ps", 6)], signal=(c == 7))
        for c in range(8):
            S.op("pe", lambda: nc.tensor.matmul(ps[:, 7, :], B.ones[:], B.sq[i2][:, c, :], start=(c == 0), stop=(c == 7)),
                 reads=[("ones",), ("sq", i2, c)], writes=[("ps", 7)], signal=(c == 7))
        S.op("act", lambda: nc.scalar.copy(out=B.mean_s[:], in_=ps[:, 6, :]), reads=[("ps", 6)], writes=[("mean_s",)])
        S.op("dve", lambda: nc.vector.tensor_tensor(out=B.m2[:], in0=ps[:, 6, :], in1=B.mean_s[:], op=ALU.mult),
             reads=[("ps", 6), ("mean_s",)], writes=[("m2",)])
        S.op("dve", lambda: nc.vector.tensor_tensor(out=B.m2[:], in0=ps[:, 7, :], in1=B.m2[:], op=ALU.subtract),
             reads=[("ps", 7), ("m2",)], writes=[("m2",)])
        S.op("act", lambda: nc.scalar.activation(out=B.m2[:], in_=B.m2[:], func=AF.Ln, bias=B.epsc[:], scale=1.0),
             reads=[("m2",), ("epsc",)], writes=[("m2",)])
        S.op("act", lambda: nc.scalar.activation(out=B.rstd[:], in_=B.m2[:], func=AF.Exp, scale=-0.5), reads=[("m2",)], writes=[("rstd",)])
        for c in range(8):
            k = c % 2
            S.op("dve", lambda: nc.vector.tensor_tensor(out=B.t1[k][:], in0=xT[:, c, tsl(tt)], in1=ps[:, 6, :], op=ALU.subtract),
                 reads=[("xT", c, tt), ("ps", 6)], writes=[("t1", k)])
            S.op("pool" if c % 2 else "dve", lambda: (nc.gpsimd if c % 2 else nc.vector).tensor_tensor(out=B.t2[k][:], in0=B.t1[k][:], in1=B.rstd[:], op=ALU.mult),
                 reads=[("t1", k), ("rstd",)], writes=[("t2", k)])
            S.op("act", lambda: nc.scalar.activation(out=xT[:, c, tsl(tt)], in_=B.t2[k][:], func=AF.Identity, bias=B.bcol[:, c:c + 1], scale=B.gcol[:, c:c + 1]),
                 reads=[("t2", k), ("gcol",), ("bcol",)], writes=[("xT", c, tt)])
            S.op("act", lambda: nc.scalar.activation(out=xb[:, c, tsl(tt)], in_=B.t2[k][:], func=AF.Identity, bias=B.bcol[:, c:c + 1], scale=B.gcol[:, c:c + 1]),
                 reads=[("t2", k), ("gcol",), ("bcol",)], writes=[("xb", c, tt)])


def build_ffn_prog(T, n_ffn):
    nc = bass.Bass("TRN2", target_bir_lowering=False)
    xT_d = nc.dram_tensor("xT", [D, T], F32, kind="ExternalInput").ap()
    w_in = nc.dram_tensor("w_in", [n_ffn, D, 2 * FF], F32, kind="ExternalInput").ap()
    w_out = nc.dram_tensor("w_out", [n_ffn, FF, D], F32, kind="ExternalInput").ap()
    lng = nc.dram_tensor("lng", [n_ffn, D], F32, kind="ExternalInput").ap()
    lnb = nc.dram_tensor("lnb", [n_ffn, D], F32, kind="ExternalInput").ap()
    yT_d = nc.dram_tensor("yT", [D, T], F32, kind="ExternalOutput").ap()
    with ExitStack() as es:
        S = Sched(nc, es)
        xT = es.enter_context(nc.sbuf_tensor("xTs", [128, 8, T], F32))
        xb = es.enter_context(nc.sbuf_tensor("xbs", [128, 8, T], BF16))
        ps = es.enter_context(nc.psum_tensor("ps", [128, 8, 512], F32))
        B = FfnBufs(nc, es, T)
        NT = T // 512
        B.init_consts(S, nc)
        xv = xT_d.rearrange("(c p) t -> p c t", p=128)
        yv = yT_d.rearrange("(c p) t -> p c t", p=128)
        for tt in range(NT):
            S.dma("sp", xT[:, :, tt * 512:(tt + 1) * 512], xv[:, :, tt * 512:(tt + 1) * 512],
                  writes=[("xT", c, tt) for c in range(8)])
            for c in range(8):
                S.op("act", lambda: nc.scalar.copy(out=xb[:, c, tt * 512:(tt + 1) * 512], in_=xT[:, c, tt * 512:(tt + 1) * 512]),
                     reads=[("xT", c, tt)], writes=[("xb", c, tt)])
        for i in range(n_ffn):
            emit_ffn_ln(S, nc, B, xT, xb, ps, w_in[i], w_out[i], lng[i], lnb[i], f"f{i}")
        for tt in range(NT):
            S.dma("sp", yv[:, :, tt * 512:(tt + 1) * 512], xT[:, :, tt * 512:(tt + 1) * 512],
                  reads=[("xT", c, tt) for c in range(8)])
        S.finish("sp")
    return nc


NH = 16; HD = 64; NHP = 8


def na_pat(r, NR):
    if r < 4:
        return 1 + r
    if r >= NR - 3:
        return 5 + (r - (NR - 3))
    return 0


def na_halo_rows(r0, NR, rows):
    G = []
    for L in range(NR + 8):
        g = r0 - 4 + L
        if g < 0:
            g = g + 8
        elif g >= rows:
            g = rows - 8 + (g - rows)
        g = min(max(g, 0), rows - 1)
        G.append(g)
    return G


def na_tables(rpb, r0, NR, rows):
    G = np.array(na_halo_rows(r0, NR, rows))
    reps = {0: min(NR // 2, NR - 4)}
    for r in range(NR):
        p = na_pat(r, NR)
        if p != 0:
            reps[p] = r
    tab = np.empty((NHP, 8, 128, 4, 128), np.float32)
    qc = np.arange(64)
    qstart = np.clip(qc - 8, 0, 48)
    for p in range(8):
        r = reps.get(p, reps[0])
        R = r0 + r
        rs = min(max(R - 4, 0), rows - 8)
        kL = r + np.arange(8)
        kG = G[kL]
        row_ok = (kG >= rs) & (kG < rs + 8)
        dr = np.clip(kG - R + 7, 0, 14)
        kcol = np.arange(64)
        col_ok = (kcol[None, :] >= qstart[:, None]) & (kcol[None, :] < qstart[:, None] + 16)
        dc = np.clip(kcol[None, :] - qc[:, None] + 15, 0, 30)
        b = rpb[:, dr][:, :, dc]
        b = np.transpose(b, (0, 1, 3, 2))
        ok = row_ok[:, None, None] & np.transpose(col_ok)[None, :, :]
        b = np.where(ok[None], b, np.float32(-1e30)).astype(np.float32)
        b = b.reshape(NHP, 2, 512, 64)
        b = b.reshape(NHP, 2, 4, 128, 64)
        tab[:, p] = np.transpose(b, (0, 3, 2, 1, 4)).reshape(NHP, 128, 4, 128)
    return tab


def emit_na(S, nc, es, ps, xh_d, x_own_d, w_qkv, b_qkv, tab_d, w_o, b_o, ln_g, ln_b, NR, yT_d):
    T = NR * 64; TH = (NR + 8) * 64
    NT = T // 512
    NTH = TH // 512
    sb = lambda es_, n, s, d: es_.enter_context(nc.sbuf_tensor(n, s, d))
    oT = sb(es, "oT", [128, 8, T], BF16)
    tsl = lambda tt: slice(tt * 512, (tt + 1) * 512)
    with ExitStack() as es2:
        xb = sb(es2, "na_xb", [128, 8, TH], BF16)
        tabs2 = [sb(es2, f"na_tab{i}", [128, 8, 4, 128], F32) for i in range(2)]
        KT = [sb(es2, f"na_KT{i}", [128, TH], BF16) for i in range(2)]
        Ve4 = sb(es2, "na_Ve4", [128, TH // 128, 512], BF16)
        Vo4 = sb(es2, "na_Vo4", [128, TH // 128, 512], BF16)
        wv4 = sb(es2, "na_wv4", [128, 8, 512], BF16)
        QBD = [sb(es2, f"na_Q{i}", [128, NR, 2, 64], BF16) for i in range(2)]
        wq = [sb(es2, f"na_wq{i}", [128, 2, 8, 128], BF16) for i in range(2)]
        sbt = [sb(es2, f"na_sb{i}", [128, 512], F32) for i in range(4)]
        PT = [sb(es2, f"na_PT{i}", [128, 512], BF16) for i in range(4)]
        rc = [sb(es2, f"na_rc{i}", [128, 128], F32) for i in range(4)]
        bcols = sb(es2, "na_bc", [128, 24], F32)
        bvrow = sb(es2, "na_bvrow", [1, 1024], BF16)
        bvb = sb(es2, "na_bvb", [128, 1024], F32)
        ones_r = sb(es2, "na_ones_r", [1, 128], BF16)
        ones_k = sb(es2, "na_ones_k", [128, 128], BF16)

        S.op("dve", lambda: nc.vector.memset(ones_r[:], 1.0), writes=[("ones_r",)])
        S.op("dve", lambda: nc.vector.memset(ones_k[:], 1.0), writes=[("ones_k",)])
        for i in range(2):
            S.op("pool", lambda: nc.gpsimd.memset(QBD[i][:], 0.0), writes=[("QBD", i)])
        S.dma("sp", bcols[:], b_qkv.rearrange("(j p) -> p j", p=128), writes=[("bcols",)], allow_slow_non_contiguous=True)
        S.dma("pool", bvrow[:], b_qkv[2048:3072].rearrange("(o n) -> o n", o=1), writes=[("bvrow",)])
        xhv = xh_d.rearrange("(c p) t -> p c t", p=128)
        for tt in range(NTH):
            S.dma("pool", xb[:, :, tsl(tt)], xhv[:, :, tsl(tt)], writes=[("nxb", tt)])
        for h2 in range(2):
            S.op("pe", lambda: nc.tensor.matmul(ps[:, 6, :], ones_r[0:1, :], bvrow[0:1, h2 * 512:(h2 + 1) * 512], start=True, stop=True),
                 reads=[("ones_r",), ("bvrow",)], writes=[("ps", 6)])
            S.op("act", lambda: nc.scalar.copy(out=bvb[:, h2 * 512:(h2 + 1) * 512], in_=ps[:, 6, :]), reads=[("ps", 6)], writes=[("bvb", h2)])
        wv = w_qkv.rearrange("(c p) n -> p c n", p=128)
        tabv = tab_d

        def load_w(hp, slot):
            for k in range(2):
                S.dma("pool", wq[slot][:, k, :, :], wv[:, :, k * 1024 + hp * 128:k * 1024 + (hp + 1) * 128], writes=[("wq", slot, k)])

        load_w(0, 0)
        pa = 0

        def gen_proj(hp):
            nonlocal pa
            slot = hp % 2
            if hp + 1 < NHP:
                load_w(hp + 1, 1 - slot)
            S.dma("sp", tabs2[slot][:].rearrange("p a k n -> p a (k n)"), tabv[hp].rearrange("a p k n -> p a (k n)"), writes=[("tabs", slot)])
            for tt in range(NTH):
                bank = pa; pa = (pa + 1) % 4
                for c in range(8):
                    S.op("pe", lambda: nc.tensor.matmul(ps[:, bank, :], wq[slot][:, 1, c, :], xb[:, c, tsl(tt)], start=(c == 0), stop=(c == 7)),
                         reads=[("wq", slot, 1), ("nxb", tt)], writes=[("ps", bank)], signal=(c == 7))
                S.op("act", lambda: nc.scalar.activation(out=KT[slot][:, tsl(tt)], in_=ps[:, bank, :], func=AF.Identity, bias=bcols[:, 8 + hp:9 + hp], scale=1.0),
                     reads=[("ps", bank), ("bcols",)], writes=[("KT", slot, tt)])
                yield
            for tt in range(NT):
                bank = pa; pa = (pa + 1) % 4
                for c in range(8):
                    S.op("pe", lambda: nc.tensor.matmul(ps[:, bank, :], wq[slot][:, 0, c, :], xb[:, c, 256 + tt * 512:256 + (tt + 1) * 512], start=(c == 0), stop=(c == 7)),
                         reads=[("wq", slot, 0)] + [("nxb", t2) for t2 in range(NTH)], writes=[("ps", bank)], signal=(c == 7))
                for hd in range(2):
                    pr = slice(hd * 64, (hd + 1) * 64)
                    S.op("dve", lambda: nc.vector.tensor_scalar(out=QBD[slot][pr, tt * 8:(tt + 1) * 8, hd, :], in0=ps[pr, bank, :].rearrange("p (r q) -> p r q", q=64),
                                                                scalar1=bcols[pr, hp:hp + 1], scalar2=0.125, op0=ALU.add, op1=ALU.mult),
                         reads=[("ps", bank), ("bcols",)], writes=[("QBD", slot)])
                yield
            yield

        def gen_rows(hp):
            nonlocal pa
            slot = hp % 2
            nch = TH // 128
            if hp % 4 == 0:
                hg = hp // 4
                S.dma("pool", wv4[:], wv[:, :, 2048 + hg * 512:2048 + (hg + 1) * 512], writes=[("wv4",)])
                for (Vx, off, cnt, nm) in ((Ve4, 0, nch, "Ve"), (Vo4, 64, nch - 1, "Vo")):
                    for j in range(cnt):
                        bank = pa; pa = (pa + 1) % 4
                        for c in range(8):
                            S.op("pe", lambda: nc.tensor.matmul(ps[:, bank, :], xb[:, c, off + j * 128:off + (j + 1) * 128], wv4[:, c, :], start=(c == 0), stop=(c == 7)),
                                 reads=[("wv4",)] + [("nxb", t2) for t2 in range(NTH)], writes=[("ps", bank)], signal=(c == 7))
                        S.op("dve", lambda: nc.vector.tensor_tensor(out=Vx[:, j, :], in0=ps[:, bank, :], in1=bvb[:, hg * 512:(hg + 1) * 512], op=ALU.add),
                             reads=[("ps", bank), ("bvb", hg)], writes=[(nm, j)])
                        yield
            def stage1(r):
                nonlocal pa
                pat = na_pat(r, NR)
                i2 = r % 4
                bankS = pa; pa = (pa + 1) % 4
                tok0 = r * 64
                for kc in range(4):
                    S.op("pe", lambda: nc.tensor.matmul(ps[:, bankS, kc * 128:(kc + 1) * 128], KT[slot][:, tok0 + kc * 128:tok0 + (kc + 1) * 128], QBD[slot][:, r, :, :].rearrange("p a q -> p (a q)"), start=True, stop=True),
                         reads=[("KT", slot, t2) for t2 in range(NTH)] + [("QBD", slot)], writes=[("ps", bankS)], signal=(kc == 3))
                S.op("dve", lambda: nc.vector.tensor_tensor(out=sbt[i2][:], in0=ps[:, bankS, :], in1=tabs2[slot][:, pat, :, :].rearrange("p k n -> p (k n)"), op=ALU.add),
                     reads=[("ps", bankS), ("tabs", slot)], writes=[("sbt", i2)])
                S.op("act", lambda: nc.scalar.activation(out=PT[i2][:], in_=sbt[i2][:], func=AF.Exp),
                     reads=[("sbt", i2)], writes=[("PT", i2)])

            def stage2(r):
                i2 = r % 4
                bankO = 4 + i2
                if r % 2 == 0:
                    Vx, j0, nm = Ve4, r // 2, "Ve"
                else:
                    Vx, j0, nm = Vo4, (r - 1) // 2, "Vo"
                h4 = hp % 4
                for kc in range(4):
                    S.op("pe", lambda: nc.tensor.matmul(ps[:, bankO, 0:128], Vx[:, j0 + kc, h4 * 128:(h4 + 1) * 128], PT[i2][:, kc * 128:(kc + 1) * 128], start=(kc == 0), stop=(kc == 3)),
                         reads=[(nm, j0 + kc), ("PT", i2)], writes=[("ps", bankO)], signal=False)
                for kc in range(4):
                    S.op("pe", lambda: nc.tensor.matmul(ps[:, bankO, 128:256], ones_k[:], PT[i2][:, kc * 128:(kc + 1) * 128], start=(kc == 0), stop=(kc == 3)),
                         reads=[("ones_k",), ("PT", i2)], writes=[("ps", bankO)], signal=(kc == 3))
                S.op("act", lambda: nc.scalar.activation(out=rc[i2][:], in_=ps[:, bankO, 128:256], func=AF.Ln),
                     reads=[("ps", bankO)], writes=[("rc", i2)])
                S.op("act", lambda: nc.scalar.activation(out=rc[i2][:], in_=rc[i2][:], func=AF.Exp, scale=-1.0),
                     reads=[("rc", i2)], writes=[("rc", i2)])
                for hd in range(2):
                    pr = slice(hd * 64, (hd + 1) * 64)
                    S.op("dve", lambda: nc.vector.tensor_tensor(out=oT[pr, hp, r * 64:(r + 1) * 64], in0=ps[pr, bankO, hd * 64:(hd + 1) * 64], in1=rc[i2][pr, hd * 64:(hd + 1) * 64], op=ALU.mult),
                         reads=[("ps", bankO), ("rc", i2)], writes=[("oT", hp, r // 8)])

            stage1(0)
            if NR > 1:
                stage1(1)
            for r in range(NR):
                if r + 2 < NR:
                    stage1(r + 2)
                stage2(r)
                yield

        def drain(g):
            n = 0
            for _ in g:
                n += 1
            return n

        n_pj = drain(gen_proj(0))
        for hp in range(NHP):
            gp = gen_proj(hp + 1) if hp + 1 < NHP else None
            gr = gen_rows(hp)
            cp = 0; cr = 0
            while gp is not None or gr is not None:
                fp = cp / n_pj if gp is not None else 2.0
                fr = cr / (NR + 1) if gr is not None else 2.0
                if gr is not None and (gp is None or fr <= fp):
                    try:
                        next(gr); cr += 1
                    except StopIteration:
                        gr = None
                else:
                    try:
                        next(gp); cp += 1
                    except StopIteration:
                        gp = None
    S.fence()
    with ExitStack() as es3:
        xT = sb(es3, "xTs", [128, 8, T], F32)
        xbo = sb(es3, "xbs", [128, 8, T], BF16)
        LB = FfnBufs(nc, es3, T, with_ffn=False)
        LB.init_consts(S, nc)
        wo = sb(es3, "na_wo", [128, 8, 1024], BF16)
        bo = sb(es3, "na_bo", [128, 8], F32)
        S.dma("pool", wo[:], w_o.rearrange("(h p) n -> p h n", p=128), writes=[("nwo",)])
        S.dma("sp", bo[:], b_o.rearrange("(c p) -> p c", p=128), writes=[("nbo",)], allow_slow_non_contiguous=True)
        xov = x_own_d.rearrange("(c p) t -> p c t", p=128)
        pb = 0
        for tt in range(NT):
            S.dma("sp", xT[:, :, tsl(tt)], xov[:, :, tsl(tt)], writes=[("xT", c, tt) for c in range(8)])
            for dc in range(8):
                bank = pb; pb = (pb + 1) % 4
                for hp in range(8):
                    S.op("pe", lambda: nc.tensor.matmul(ps[:, bank, :], wo[:, hp, dc * 128:(dc + 1) * 128], oT[:, hp, tsl(tt)], start=(hp == 0), stop=(hp == 7)),
                         reads=[("nwo",), ("oT", hp, tt)], writes=[("ps", bank)], signal=(hp == 7))
                S.op("pool", lambda: nc.gpsimd.tensor_scalar(out=xT[:, dc, tsl(tt)], in0=xT[:, dc, tsl(tt)], scalar1=float(ALPHA), scalar2=bo[:, dc:dc + 1], op0=ALU.mult, op1=ALU.add),
                     reads=[("xT", dc, tt), ("nbo",)], writes=[("xT", dc, tt)])
                S.op("dve", lambda: nc.vector.tensor_tensor(out=xT[:, dc, tsl(tt)], in0=ps[:, bank, :], in1=xT[:, dc, tsl(tt)], op=ALU.add),
                     reads=[("ps", bank), ("xT", dc, tt)], writes=[("xT", dc, tt)])
        emit_ln(S, nc, LB, xT, xbo, ps, ln_g, ln_b, NT)
        yv = yT_d.rearrange("(c p) t -> p c t", p=128)
        for tt in range(NT):
            S.dma("sp", yv[:, :, tsl(tt)], xT[:, :, tsl(tt)], reads=[("xT", c, tt) for c in range(8)])
        S.finish("sp")


def build_na_prog(NR):
    T = NR * 64; TH = (NR + 8) * 64
    nc = bass.Bass("TRN2", target_bir_lowering=False)
    dt = lambda n, s: nc.dram_tensor(n, s, F32, kind="ExternalInput").ap()
    xh = dt("xh", [D, TH]); xo = dt("xo", [D, T])
    w_qkv = dt("w_qkv", [D, 3 * D]); b_qkv = dt("b_qkv", [3 * D]); tab = dt("tab", [NHP, 8, 128, 4, 128])
    w_o = dt("w_o", [D, D]); b_o = dt("b_o", [D]); lng = dt("lng", [D]); lnb = dt("lnb", [D])
    yT_d = nc.dram_tensor("yT", [D, T], F32, kind="ExternalOutput").ap()
    with ExitStack() as es:
        S = Sched(nc, es)
        ps = es.enter_context(nc.psum_tensor("ps", [128, 8, 512], F32))
        emit_na(S, nc, es, ps, xh, xo, w_qkv, b_qkv, tab, w_o, b_o, lng, lnb, NR, yT_d)
    return nc


C0 = float(np.exp(-0.5))
CH = 64


def rw_consts():
    s = np.arange(128)[:, None]; t = np.arange(128)[None, :]
    same = (s // 64) == (t // 64)
    Sm = (same & (t > s)).astype(np.float32)
    Im = (same & (t >= s)).astype(np.float32)
    maskSI = np.concatenate([Sm, Im, Sm, Im], axis=1)
    maskTS = (same & (t < s)).astype(np.float32)
    ident = np.eye(128, dtype=np.float32)
    blk = same.astype(np.float32)
    return np.concatenate([maskSI, maskTS, ident, blk], axis=1)


STOP = ""


def emit_rw1(S, nc, es, ps, d, SL, TS=256, SD=BF16):
    NTI = SL // TS
    NCK = TS // CH
    NPR = TS // 128
    sb = lambda n, s, dt: es.enter_context(nc.sbuf_tensor(n, s, dt))
    cst = sb("rw_cst", [128, 896], F32)
    S.dma("sp", cst[:], d["consts"], writes=[("cst",)])
    maskSI = cst[:, 0:512]; maskTS = cst[:, 512:640]; identf = cst[:, 640:768]; blkf = cst[:, 768:896]
    ident_s = identf
    if SD != F32:
        ident_sd = sb("rw_identsd", [128, 128], SD)
        S.op("dve", lambda: nc.vector.tensor_copy(out=ident_sd[:], in_=identf), reads=[("cst",)], writes=[("identsd",)])
        ident_s = ident_sd[:]
    ones64 = sb("rw_ones64", [128, CH], F32)
    S.op("dve", lambda: nc.vector.memset(ones64[:], 1.0), writes=[("ones64",)])
    mixc = sb("rw_mixc", [128, 6, 8], F32)
    S.dma("sp", mixc[:], d["mix"].rearrange("i (c p) -> p i c", p=128), writes=[("mixc",)], allow_slow_non_contiguous=True)
    cols = sb("rw_cols", [128, 5, 4], F32)
    for i, nm in enumerate(["w0", "a0", "k_k", "k_a", "r_k"]):
        S.dma("sp", cols[:, i, :], d[nm].rearrange("(c p) -> p c", p=128), writes=[("cols", i)], allow_slow_non_contiguous=True)
    W3 = sb("rw_W3", [128, 3, 8, 512], BF16)
    for i, nm in enumerate(["w_r", "w_k", "w_v"]):
        S.dma("pool", W3[:, i, :, :], d[nm].rearrange("(c p) n -> p c n", p=128), writes=[("W3", i)])
    w1b = sb("rw_w1b", [128, 8, 64], BF16); a1b = sb("rw_a1b", [128, 8, 64], BF16); g1b = sb("rw_g1b", [128, 8, 160], BF16)
    S.dma("pool", w1b[:], d["w1"].rearrange("(c p) n -> p c n", p=128), writes=[("w1b",)])
    S.dma("pool", a1b[:], d["a1"].rearrange("(c p) n -> p c n", p=128), writes=[("a1b",)])
    S.dma("pool", g1b[:], d["g1"].rearrange("(c p) n -> p c n", p=128), writes=[("g1b",)])
    w2b = sb("rw_w2b", [64, 512], BF16); a2b = sb("rw_a2b", [64, 512], BF16)
    g2a = sb("rw_g2a", [128, 512], BF16); g2b = sb("rw_g2b", [128, 512], BF16)
    S.dma("pool", w2b[:], d["w2"], writes=[("w2b",)])
    S.dma("pool", a2b[:], d["a2"], writes=[("a2b",)])
    S.dma("pool", g2a[:], d["g2"][0:128, :], writes=[("g2a",)])
    S.op("dve", lambda: nc.vector.memset(g2b[:], 0.0), writes=[("g2b",)])
    S.dma("pool", g2b[0:32, :], d["g2"][128:160, :], writes=[("g2b",)])
    xt = [sb("rw_xt0", [128, 8, TS + 2], F32)] * 2
    xs = sb("rw_xs", [128, TS], F32)
    xx = sb("rw_xx", [128, TS], F32)
    mixt = [sb(f"rw_mixt{i}", [128, TS], F32) for i in range(3)]
    xm = sb("rw_xm", [128, 6, 8, TS], BF16)
    hw = sb("rw_hw", [64, TS], BF16); ha = sb("rw_ha", [64, TS], BF16)
    hga = sb("rw_hga", [128, TS], BF16); hgb = sb("rw_hgb", [128, TS], BF16)
    S.op("dve", lambda: nc.vector.memset(hgb[:], 0.0), writes=[("hgb",)])
    ft = lambda n: [sb(f"rw_{n}{i}", [128, TS], F32) for i in range(2)]
    def ft1(n):
        t = sb(f"rw_{n}", [128, TS], F32)
        return [t, t]
    rT = ft("rT"); kT = ft("kT"); vT = ft("vT"); gT = ft1("gT"); sg = ft("sg"); asg = ft("asg")
    kq = ft1("kq"); kq2 = ft1("kq2"); rn = ft1("rn"); kk = ft("kk"); t1 = ft1("t1"); kz = ft("kz"); prod = ft1("prod")
    cs = ft1("cs"); E1 = [[sb(f"rw_E1_{q}_{i}", [128, TS], F32) for i in range(4)] for q in range(2)]; E2 = ft1("E2"); E3 = ft1("E3"); dd = ft1("dd"); bb = ft1("bb")
    af = ft("af")
    BKf = [sb(f"rw_BKf{i}", [128, 2, TS], F32) for i in range(2)]
    Hat = [sb(f"rw_Hat{i}", [128, 2, TS], F32) for i in range(2)]
    RA = [[sb(f"rw_RA{q}_{i}", [128, 2, TS], SD) for i in range(4)] for q in range(2)]
    RAf1 = [[sb(f"rw_RAf{q}_{i}", [128, TS], F32) for i in range(4)] for q in range(2)]
    LBt = [[sb(f"rw_LB{q}_{i}", [128, 2, TS], SD) for i in range(4)] for q in range(2)]
    TM = [[[sb(f"rw_TM{q}_{c}_{p}", [128, 4, 128], SD) for p in range(NPR)] for c in range(4)] for q in range(2)]
    NU = 2 * 2 * NPR
    AM = [sb(f"rw_AM{u}", [128, 512], SD) for u in range(NU)]
    Mk = [[sb(f"rw_M{u}_{i}", [128, 128], SD) for i in range(2)] for u in range(NU)]
    Nk = [[sb(f"rw_N{u}_{i}", [128, 128], SD) for i in range(2)] for u in range(NU)]
    Xs = [sb(f"rw_Xs{u}", [128, 128], SD) for u in range(NU)]
    ATM = [sb(f"rw_ATM{u}", [128, 64], SD) for u in range(NU)]
    Ws = [sb(f"rw_Ws{u}", [128, 64], SD) for u in range(NU)]
    UV = [sb(f"rw_UV{u}", [128, 64], SD) for u in range(NU)]
    GT = [sb(f"rw_GT{c}", [128, NCK, 64], F32) for c in range(4)]
    Hf = [sb(f"rw_Hf{c}", [128, NCK, 64], F32) for c in range(4)]
    RH = [sb(f"rw_RH{c}", [128, TS], F32) for c in range(4)]
    YV = [[sb(f"rw_YV{c}_{p}", [128, 128], F32) for p in range(NPR)] for c in range(4)]
    ST = [sb(f"rw_ST{c}", [128, 64], F32) for c in range(4)]
    yt = [[sb(f"rw_yt{c}_{p}", [128, 128], F32) for p in range(NPR)] for c in range(4)]
    for c in range(4):
        S.op("dve", lambda: nc.vector.memset(ST[c][:], 0.0), writes=[("ST", c, 0), ("ST", c, 1)])

    xv = d["xT"].rearrange("(c p) t -> p c t", p=128)
    pr = [0]

    def bank():
        b = pr[0]; pr[0] = (pr[0] + 1) % 8
        return b

    def gen_abc(ti):
        par = ti % 2
        t0 = ti * TS
        xs_ = 0
        X = xt[xs_]
        lo = t0 - 1 if ti > 0 else t0
        hi = t0 + TS + 1 if ti < NTI - 1 else t0 + TS
        if ti == 0:
            S.op("pool", lambda: nc.gpsimd.memset(X[:, :, 0:1], 0.0), writes=[("xt", xs_)])
        if ti == NTI - 1:
            S.op("pool", lambda: nc.gpsimd.memset(X[:, :, TS + 1:TS + 2], 0.0), writes=[("xt", xs_)])
        S.dma("sp", X[:, :, (lo - t0 + 1):(hi - t0 + 1)], xv[:, :, lo:hi], writes=[("xt", xs_)])
        for c in range(8):
            S.op("pool", lambda: nc.gpsimd.tensor_tensor(out=xs[:], in0=X[:, c, 0:TS], in1=X[:, c, 2:TS + 2], op=ALU.add),
                 reads=[("xt", xs_)], writes=[("xs",)])
            S.op("dve", lambda: nc.vector.scalar_tensor_tensor(out=xx[:], in0=xs[:], scalar=0.5, in1=X[:, c, 1:TS + 1], op0=ALU.mult, op1=ALU.subtract),
                 reads=[("xs",), ("xt", xs_)], writes=[("xx",)])
            for i in range(3):
                S.op("dve", lambda: nc.vector.scalar_tensor_tensor(out=xm[:, i, c, :], in0=xx[:], scalar=mixc[:, i, c:c + 1], in1=X[:, c, 1:TS + 1], op0=ALU.mult, op1=ALU.add),
                     reads=[("xx",), ("xt", xs_), ("mixc",)], writes=[("xm", i, c)])
            for i in (3, 4, 5):
                mt = mixt[i - 3]
                S.op("act", lambda: nc.scalar.activation(out=mt[:], in_=xx[:], func=AF.Identity, scale=mixc[:, i, c:c + 1]),
                     reads=[("xx",), ("mixc",)], writes=[("mixt", i)])
                S.op("pool", lambda: nc.gpsimd.tensor_tensor(out=xm[:, i, c, :], in0=mt[:], in1=X[:, c, 1:TS + 1], op=ALU.add),
                     reads=[("mixt", i), ("xt", xs_)], writes=[("xm", i, c)])
            yield
        yield
        def proj(out_ap, lhs_fn, mi, keys, M=128):
            for c in range(8):
                S.op("pe", lambda: nc.tensor.matmul(out_ap, lhs_fn(c), xm[:, mi, c, :], start=(c == 0), stop=(c == 7)),
                     reads=keys + [("xm", mi, c)], writes=[("ps", bk)], signal=(c == 7))
        bk = bank()
        proj(ps[0:64, bk, 0:TS], lambda c: w1b[:, c, :], 1, [("w1b",)])
        S.op("act", lambda: nc.scalar.activation(out=hw[:], in_=ps[0:64, bk, 0:TS], func=AF.Tanh), reads=[("ps", bk)], writes=[("hw",)])
        bk = bank()
        proj(ps[0:64, bk, 0:TS], lambda c: a1b[:, c, :], 4, [("a1b",)])
        S.op("act", lambda: nc.scalar.copy(out=ha[:], in_=ps[0:64, bk, 0:TS]), reads=[("ps", bk)], writes=[("ha",)])
        bk = bank()
        proj(ps[:, bk, 0:TS], lambda c: g1b[:, c, 0:128], 5, [("g1b",)])
        S.op("act", lambda: nc.scalar.activation(out=hga[:], in_=ps[:, bk, 0:TS], func=AF.Sigmoid), reads=[("ps", bk)], writes=[("hga",)])
        bk = bank()
        proj(ps[0:32, bk, 0:TS], lambda c: g1b[:, c, 128:160], 5, [("g1b",)])
        S.op("act", lambda: nc.scalar.activation(out=hgb[0:32, :], in_=ps[0:32, bk, 0:TS], func=AF.Sigmoid), reads=[("ps", bk)], writes=[("hgb",)])
        for cc in range(4):
            f = cc % 2
            csl = slice(cc * 128, (cc + 1) * 128)
            tsl = slice(t0, t0 + TS)
            bk = bank(); proj(ps[:, bk, 0:TS], lambda c: W3[:, 0, c, csl], 0, [("W3", 0)])
            S.op("act", lambda: nc.scalar.copy(out=rT[f][:], in_=ps[:, bk, 0:TS]), reads=[("ps", bk)], writes=[("rT", f)])
            yield
            bk = bank(); proj(ps[:, bk, 0:TS], lambda c: W3[:, 1, c, csl], 2, [("W3", 1)])
            S.op("act", lambda: nc.scalar.copy(out=kT[f][:], in_=ps[:, bk, 0:TS]), reads=[("ps", bk)], writes=[("kT", f)])
            yield
            bk = bank(); proj(ps[:, bk, 0:TS], lambda c: W3[:, 2, c, csl], 3, [("W3", 2)])
            S.op("act", lambda: nc.scalar.copy(out=vT[f][:], in_=ps[:, bk, 0:TS]), reads=[("ps", bk)], writes=[("vT", f)])
            S.dma("sp", d["v_out"][csl, tsl], vT[f][:], reads=[("vT", f)])
            bk = bank()
            S.op("pe", lambda: nc.tensor.matmul(ps[:, bk, 0:TS], w2b[:, csl], hw[:], start=True, stop=True), reads=[("w2b",), ("hw",)], writes=[("ps", bk)])
            S.op("act", lambda: nc.scalar.activation(out=sg[f][:], in_=ps[:, bk, 0:TS], func=AF.Sigmoid, bias=cols[:, 0, cc:cc + 1], scale=1.0),
                 reads=[("ps", bk), ("cols", 0)], writes=[("sg", f)])
            bk = bank()
            S.op("pe", lambda: nc.tensor.matmul(ps[:, bk, 0:TS], a2b[:, csl], ha[:], start=True, stop=True), reads=[("a2b",), ("ha",)], writes=[("ps", bk)])
            S.op("act", lambda: nc.scalar.activation(out=asg[f][:], in_=ps[:, bk, 0:TS], func=AF.Sigmoid, bias=cols[:, 1, cc:cc + 1], scale=1.0),
                 reads=[("ps", bk), ("cols", 1)], writes=[("asg", f)])
            bk = bank()
            S.op("pe", lambda: nc.tensor.matmul(ps[:, bk, 0:TS], g2a[:, csl], hga[:], start=True, stop=False), reads=[("g2a",), ("hga",)], writes=[("ps", bk)], signal=False)
            S.op("pe", lambda: nc.tensor.matmul(ps[:, bk, 0:TS], g2b[:, csl], hgb[:], start=False, stop=True), reads=[("g2b",), ("hgb",)], writes=[("ps", bk)])
            S.op("act", lambda: nc.scalar.copy(out=gT[f][:], in_=ps[:, bk, 0:TS]), reads=[("ps", bk)], writes=[("gT", 0)])
            S.dma("sp", d["g_out"][csl, tsl], gT[f][:], reads=[("gT", 0)])
            yield
            S.op("dve", lambda: nc.vector.tensor_scalar(out=kq[f][:], in0=kT[f][:], scalar1=cols[:, 2, cc:cc + 1], scalar2=None, op0=ALU.mult),
                 reads=[("kT", f), ("cols", 2)], writes=[("kq", 0)])
            S.op("pool", lambda: nc.gpsimd.tensor_tensor(out=kq2[f][:], in0=kq[f][:], in1=kq[f][:], op=ALU.mult), reads=[("kq", 0)], writes=[("kq2", 0)])
            bk = bank()
            S.op("pe", lambda: nc.tensor.matmul(ps[:, bk, 0:TS], blkf, kq2[f][:], start=True, stop=True), reads=[("cst",), ("kq2", 0)], writes=[("ps", bk)])
            S.op("dve", lambda: nc.vector.tensor_scalar(out=rn[f][:], in0=ps[:, bk, 0:TS], scalar1=1e-24, scalar2=None, op0=ALU.max),
                 reads=[("ps", bk)], writes=[("rn", 0)])
            S.op("act", lambda: nc.scalar.activation(out=rn[f][:], in_=rn[f][:], func=AF.Ln), reads=[("rn", 0)], writes=[("rn", 0)])
            S.op("act", lambda: nc.scalar.activation(out=rn[f][:], in_=rn[f][:], func=AF.Exp, scale=-0.5), reads=[("rn", 0)], writes=[("rn", 0)])
            S.op("pool", lambda: nc.gpsimd.tensor_tensor(out=kk[f][:], in0=kq[f][:], in1=rn[f][:], op=ALU.mult), reads=[("kq", 0), ("rn", 0)], writes=[("kk", f)])
            S.op("dve", lambda: nc.vector.tensor_scalar(out=t1[f][:], in0=asg[f][:], scalar1=-1.0, scalar2=cols[:, 3, cc:cc + 1], op0=ALU.add, op1=ALU.mult),
                 reads=[("asg", f), ("cols", 3)], writes=[("t1", 0)])
            S.op("dve", lambda: nc.vector.scalar_tensor_tensor(out=kz[f][:], in0=t1[f][:], scalar=1.0, in1=kT[f][:], op0=ALU.add, op1=ALU.mult),
                 reads=[("t1", 0), ("kT", f)], writes=[("kz", f)])
            S.op("dve", lambda: nc.vector.scalar_tensor_tensor(out=prod[f][:], in0=rT[f][:], scalar=cols[:, 4, cc:cc + 1], in1=kz[f][:], op0=ALU.mult, op1=ALU.mult),
                 reads=[("rT", f), ("kz", f), ("cols", 4)], writes=[("prod", 0)])
            S.dma("sp", d["p_out"][csl, tsl], prod[f][:], reads=[("prod", 0)])
            yield
            for ck in range(NCK):
                ksl = slice(ck * CH, (ck + 1) * CH)
                S.op("dve", lambda: nc.vector.tensor_tensor_scan(out=cs[f][:, ksl], data0=ones64[:], data1=sg[f][:, ksl], initial=0.0, op0=ALU.mult, op1=ALU.add),
                     reads=[("sg", f), ("ones64",)], writes=[("cs", 0)])
            S.op("act", lambda: nc.scalar.activation(out=E1[par][cc][:], in_=cs[f][:], func=AF.Exp, scale=-C0), reads=[("cs", 0)], writes=[("E1", par, cc)])
            S.op("act", lambda: nc.scalar.activation(out=E2[f][:], in_=cs[f][:], func=AF.Exp, scale=C0), reads=[("cs", 0)], writes=[("E2", 0)])
            S.op("pool", lambda: nc.gpsimd.tensor_tensor(out=dd[f][:], in0=cs[f][:], in1=sg[f][:], op=ALU.subtract), reads=[("cs", 0), ("sg", f)], writes=[("dd", 0)])
            S.op("act", lambda: nc.scalar.activation(out=E3[f][:], in_=dd[f][:], func=AF.Exp, scale=-C0), reads=[("dd", 0)], writes=[("E3", 0)])
            yield
            S.op("dve", lambda: nc.vector.scalar_tensor_tensor(out=af[f][:], in0=kk[f][:], scalar=-1.0, in1=E3[f][:], op0=ALU.mult, op1=ALU.mult),
                 reads=[("kk", f), ("E3", 0)], writes=[("af", f)])
            S.op("act", lambda: nc.scalar.copy(out=RA[par][cc][:, 0, :], in_=af[f][:]), reads=[("af", f)], writes=[("RA", par, cc, 0)])
            S.op("pool", lambda: nc.gpsimd.tensor_tensor(out=RAf1[par][cc][:], in0=rT[f][:], in1=E1[par][cc][:], op=ALU.mult), reads=[("rT", f), ("E1", par, cc)], writes=[("RAf1", par, cc)])
            S.op("act", lambda: nc.scalar.copy(out=RA[par][cc][:, 1, :], in_=RAf1[par][cc][:]), reads=[("RAf1", par, cc)], writes=[("RA", par, cc, 1)])
            S.op("pool", lambda: nc.gpsimd.tensor_tensor(out=bb[f][:], in0=kk[f][:], in1=asg[f][:], op=ALU.mult), reads=[("kk", f), ("asg", f)], writes=[("bb", 0)])
            S.op("pool", lambda: nc.gpsimd.tensor_tensor(out=BKf[f][:, 0, :], in0=bb[f][:], in1=E2[f][:], op=ALU.mult), reads=[("bb", 0), ("E2", 0)], writes=[("BKf", f, 0)])
            S.op("pool", lambda: nc.gpsimd.tensor_tensor(out=BKf[f][:, 1, :], in0=kz[f][:], in1=E2[f][:], op=ALU.mult), reads=[("kz", f), ("E2", 0)], writes=[("BKf", f, 1)])
            S.op("act", lambda: nc.scalar.copy(out=LBt[par][cc][:], in_=BKf[f][:]), reads=[("BKf", f, 0), ("BKf", f, 1)], writes=[("LBt", par, cc)])
            for ck in range(NCK):
                ksl = slice(ck * CH, (ck + 1) * CH)
                e = ck * CH + CH - 1
                S.op("dve", lambda: nc.vector.tensor_scalar(out=Hat[f][:, :, ksl], in0=BKf[f][:, :, ksl], scalar1=E1[par][cc][:, e:e + 1], scalar2=None, op0=ALU.mult),
                     reads=[("BKf", f, 0), ("BKf", f, 1), ("E1", par, cc)], writes=[("Hat", f)])
            yield
            for p in range(NPR):
                psl = slice(p * 128, (p + 1) * 128)
                bk = bank()
                srcs = [(af[f][:, psl], ("af", f)), (Hat[f][:, 0, psl], ("Hat", f)), (Hat[f][:, 1, psl], ("Hat", f)), (vT[f][:, psl], ("vT", f))]
                for i, (src, key) in enumerate(srcs):
                    S.op("pe", lambda: nc.tensor.transpose(ps[:, bk, i * 128:(i + 1) * 128], src, identf), reads=[key, ("cst",)], writes=[("ps", bk)], signal=(i == 3))
                S.op("act", lambda: nc.scalar.copy(out=TM[par][cc][p][:].rearrange("p a n -> p (a n)"), in_=ps[:, bk, :]), reads=[("ps", bk)], writes=[("TM", par, cc, p)])
        yield

    def gen_de(ti):
        par = ti % 2
        t0 = ti * TS
        for ccg in range(2):
            units = [(cc, hd, p) for cc in (2 * ccg, 2 * ccg + 1) for hd in range(2) for p in range(NPR)]
            def uid(cc, hd, p):
                return ((cc % 2) * 2 + hd) * NPR + p
            for (cc, hd, p) in units:
                u = uid(cc, hd, p)
                hs = slice(hd * 64, hd * 64 + 64); tk = slice(p * 128, (p + 1) * 128)
                bk = bank()
                S.op("pe", lambda: nc.tensor.matmul(ps[:, bk, 0:256], LBt[par][cc][hs, 0, tk], RA[par][cc][hs, :, tk], start=True, stop=True),
                     reads=[("LBt", par, cc), ("RA", par, cc, 0), ("RA", par, cc, 1)], writes=[("ps", bk)], signal=False)
                S.op("pe", lambda: nc.tensor.matmul(ps[:, bk, 256:512], LBt[par][cc][hs, 1, tk], RA[par][cc][hs, :, tk], start=True, stop=True),
                     reads=[("LBt", par, cc), ("RA", par, cc, 0), ("RA", par, cc, 1)], writes=[("ps", bk)])
                S.op("dve", lambda: nc.vector.tensor_tensor(out=AM[u][:], in0=ps[:, bk, :], in1=maskSI, op=ALU.mult), reads=[("ps", bk), ("cst",)], writes=[("AM", u)])
                bk = bank()
                S.op("pe", lambda: nc.tensor.matmul(ps[:, bk, 0:128], RA[par][cc][hs, 0, tk], LBt[par][cc][hs, 0, tk], start=True, stop=True),
                     reads=[("LBt", par, cc), ("RA", par, cc, 0)], writes=[("ps", bk)])
                S.op("dve", lambda: nc.vector.tensor_tensor(out=Nk[u][0][:], in0=ps[:, bk, 0:128], in1=maskTS, op=ALU.mult), reads=[("ps", bk), ("cst",)], writes=[("Nk", u, 0)])
                S.op("pool", lambda: nc.gpsimd.tensor_tensor(out=Xs[u][:], in0=AM[u][:, 0:128], in1=identf, op=ALU.add), reads=[("AM", u), ("cst",)], writes=[("Xs", u)])
                yield
            for k in range(1, 6):
                for (cc, hd, p) in units:
                    u = uid(cc, hd, p)
                    Mprev = AM[u][:, 0:128] if k == 1 else Mk[u][(k - 1) % 2][:]
                    Mkey = ("AM", u) if k == 1 else ("Mk", u, (k - 1) % 2)
                    Nprev = Nk[u][(k - 1) % 2][:]
                    Nkey = ("Nk", u, (k - 1) % 2)
                    bk = bank()
                    if k <= 4:
                        S.op("pe", lambda: nc.tensor.matmul(ps[:, bk, 0:128], Nprev, Mprev, start=True, stop=True), reads=[Mkey, Nkey], writes=[("ps", bk)], signal=False)
                    S.op("pe", lambda: nc.tensor.matmul(ps[:, bk, 128:256], Mprev, Nprev, start=True, stop=True), reads=[Mkey, Nkey], writes=[("ps", bk)])
                    if k <= 4:
                        S.op("act", lambda: nc.scalar.copy(out=Mk[u][k % 2][:], in_=ps[:, bk, 0:128]), reads=[("ps", bk)], writes=[("Mk", u, k % 2)])
                    S.op("act", lambda: nc.scalar.copy(out=Nk[u][k % 2][:], in_=ps[:, bk, 128:256]), reads=[("ps", bk)], writes=[("Nk", u, k % 2)])
                    yield
                for (cc, hd, p) in units:
                    u = uid(cc, hd, p)
                    bk = bank()
                    Xkey = ("Xs", u)
                    S.op("pe", lambda: nc.tensor.matmul(ps[:, bk, 0:128], Nk[u][k % 2][:], Xs[u][:], start=True, stop=True), reads=[("Nk", u, k % 2), Xkey], writes=[("ps", bk)])
                    S.op("dve", lambda: nc.vector.tensor_tensor(out=Xs[u][:], in0=ps[:, bk, 0:128], in1=Xs[u][:], op=ALU.add), reads=[("ps", bk), ("Xs", u)], writes=[("Xs", u)])
                    yield
            Xkeyf = lambda u: ("Xs", u)
            for (cc, hd, p) in units:
                u = uid(cc, hd, p)
                hs = slice(hd * 64, hd * 64 + 64)
                bk = bank()
                S.op("pe", lambda: nc.tensor.matmul(ps[:, bk, 0:64], Xs[u][:], TM[par][cc][p][:, 0, hs], start=True, stop=True), reads=[Xkeyf(u), ("TM", par, cc, p)], writes=[("ps", bk)], signal=False)
                S.op("pe", lambda: nc.tensor.matmul(ps[:, bk, 64:128], AM[u][:, 256:384], TM[par][cc][p][:, 3, hs], start=True, stop=True), reads=[("AM", u), ("TM", par, cc, p)], writes=[("ps", bk)])
                S.op("act", lambda: nc.scalar.copy(out=ATM[u][:], in_=ps[:, bk, 0:64]), reads=[("ps", bk)], writes=[("ATM", u)])
                S.op("act", lambda: nc.scalar.copy(out=Ws[u][:], in_=ps[:, bk, 64:128]), reads=[("ps", bk)], writes=[("Ws", u)])
                yield
            for (cc, hd, p) in units:
                u = uid(cc, hd, p)
                hs = slice(hd * 64, hd * 64 + 64); tk = slice(p * 128, (p + 1) * 128)
                bk = bank()
                S.op("pe", lambda: nc.tensor.matmul(ps[:, bk, 0:64], Xs[u][:], Ws[u][:], start=True, stop=True), reads=[Xkeyf(u), ("Ws", u)], writes=[("ps", bk)])
                S.op("act", lambda: nc.scalar.copy(out=UV[u][:], in_=ps[:, bk, 0:64]), reads=[("ps", bk)], writes=[("UV", u)])
                bk2 = bank()
                S.op("pe", lambda: nc.tensor.matmul(ps[hs, bk2, 0:128], ATM[u][:], AM[u][:, 128:256], start=True, stop=True), reads=[("ATM", u), ("AM", u)], writes=[("ps", bk2)])
                S.op("dve", lambda: nc.vector.tensor_tensor(out=RH[cc][hs, tk], in0=ps[hs, bk2, 0:128], in1=RAf1[par][cc][hs, tk], op=ALU.add),
                     reads=[("ps", bk2), ("RAf1", par, cc)], writes=[("RH", cc, hd)])
                yield
            for (cc, hd, p) in units:
                u = uid(cc, hd, p)
                hs = slice(hd * 64, hd * 64 + 64)
                bk = bank()
                S.op("pe", lambda: nc.tensor.matmul(ps[:, bk, 0:64], AM[u][:, 128:256], UV[u][:], start=True, stop=False), reads=[("AM", u), ("UV", u)], writes=[("ps", bk)], signal=False)
                S.op("pe", lambda: nc.tensor.matmul(ps[:, bk, 0:64], AM[u][:, 384:512], TM[par][cc][p][:, 3, hs], start=False, stop=True), reads=[("AM", u), ("TM", par, cc, p)], writes=[("ps", bk)])
                S.op("act", lambda: nc.scalar.copy(out=YV[cc][p][:, hs], in_=ps[:, bk, 0:64]), reads=[("ps", bk)], writes=[("YV", cc, p, hd)])
                for q in range(2):
                    pb = slice(q * 64, q * 64 + 64)
                    ck = p * 2 + q
                    e = ck * CH + CH - 1
                    bk = bank()
                    S.op("pe", lambda: nc.tensor.matmul(ps[hs, bk, 0:64], ATM[u][pb, :], TM[par][cc][p][pb, 1, hs], start=True, stop=True), reads=[("ATM", u), ("TM", par, cc, p)], writes=[("ps", bk)])
                    S.op("dve", lambda: nc.vector.scalar_tensor_tensor(out=GT[cc][hs, ck, :], in0=identf[hs, hs], scalar=E1[par][cc][hs, e:e + 1], in1=ps[hs, bk, 0:64], op0=ALU.mult, op1=ALU.add),
                         reads=[("ps", bk), ("cst",), ("E1", par, cc)], writes=[("GT", cc, hd)])
                    bk = bank()
                    S.op("pe", lambda: nc.tensor.matmul(ps[hs, bk, 0:64], TM[par][cc][p][pb, 1, hs], UV[u][pb, :], start=True, stop=False), reads=[("TM", par, cc, p), ("UV", u)], writes=[("ps", bk)], signal=False)
                    S.op("pe", lambda: nc.tensor.matmul(ps[hs, bk, 0:64], TM[par][cc][p][pb, 2, hs], TM[par][cc][p][pb, 3, hs], start=False, stop=True), reads=[("TM", par, cc, p)], writes=[("ps", bk)])
                    S.op("act", lambda: nc.scalar.copy(out=Hf[cc][hs, ck, :], in_=ps[hs, bk, 0:64]), reads=[("ps", bk)], writes=[("Hf", cc, hd)])
                    yield
        for ck in range(NCK):
            p = ck // 2; q = ck % 2
            pb = slice(q * 64, q * 64 + 64)
            for cc in range(4):
                for hd in range(2):
                    hs = slice(hd * 64, hd * 64 + 64)
                    bkY = bank(); bkS = bank()
                    S.op("pe", lambda: nc.tensor.matmul(ps[pb, bkY, 0:64], RH[cc][hs, ck * CH:(ck + 1) * CH], ST[cc][hs, :], start=True, stop=True),
                         reads=[("RH", cc, hd), ("ST", cc, hd)], writes=[("ps", bkY)])
                    S.op("pe", lambda: nc.tensor.matmul(ps[hs, bkS, 0:64], GT[cc][hs, ck, :], ST[cc][hs, :], start=True, stop=True),
                         reads=[("GT", cc, hd), ("ST", cc, hd)], writes=[("ps", bkS)])
                    S.op("dve", lambda: nc.vector.tensor_tensor(out=ST[cc][hs, :], in0=ps[hs, bkS, 0:64], in1=Hf[cc][hs, ck, :], op=ALU.add),
                         reads=[("ps", bkS), ("Hf", cc, hd)], writes=[("ST", cc, hd)])
                    S.op("dve", lambda: nc.vector.tensor_tensor(out=yt[cc][p][pb, hs], in0=ps[pb, bkY, 0:64], in1=YV[cc][p][pb, hs], op=ALU.add),
                         reads=[("ps", bkY), ("YV", cc, p, hd)], writes=[("yt", cc, p)])
                yield
                if q == 1:
                    S.dma("sp", d["y_out"][t0 + p * 128:t0 + (p + 1) * 128, cc * 128:(cc + 1) * 128], yt[cc][p][:], reads=[("yt", cc, p)])
        yield

    def drain(g):
        n = 0
        for _ in g:
            n += 1
        return n

    n_abc = drain(gen_abc(0))
    n_de = None
    for ti in range(NTI):
        ga = gen_abc(ti + 1) if ti + 1 < NTI else None
        gd = gen_de(ti)
        ca = 0; cd = 0
        while ga is not None or gd is not None:
            fa = ca / n_abc if ga is not None else 2.0
            fd = cd / n_de if (gd is not None and n_de) else (ca / n_abc if gd is not None else 2.0)
            if gd is not None and (ga is None or fd <= fa):
                try:
                    next(gd); cd += 1
                except StopIteration:
                    gd = None
                    if n_de is None:
                        n_de = max(cd, 1)
            else:
                try:
                    next(ga); ca += 1
                except StopIteration:
                    ga = None
        if n_de is None:
            n_de = max(cd, 1)
    S.finish("sp")


def build_rw1_prog(SL, TS=256, SD=BF16):
    nc = bass.Bass("TRN2", target_bir_lowering=False)
    dt = lambda n, s: nc.dram_tensor(n, s, F32, kind="ExternalInput").ap()
    d = {"xT": dt("xT", [D, SL]), "mix": dt("mix", [6, D]), "consts": dt("consts", [128, 896])}
    for nm in ("w_r", "w_k", "w_v"):
        d[nm] = dt(nm, [D, 512])
    d["w1"] = dt("w1", [D, 64]); d["w2"] = dt("w2", [64, 512]); d["w0"] = dt("w0", [512])
    d["a1"] = dt("a1", [D, 64]); d["a2"] = dt("a2", [64, 512]); d["a0"] = dt("a0", [512])
    d["g1"] = dt("g1", [D, 160]); d["g2"] = dt("g2", [160, 512])
    for nm in ("k_k", "k_a", "r_k"):
        d[nm] = dt(nm, [512])
    do = lambda n, s: nc.dram_tensor(n, s, F32, kind="ExternalOutput").ap()
    d["y_out"] = do("y_out", [SL, 512]); d["p_out"] = do("p_out", [512, SL]); d["v_out"] = do("v_out", [512, SL]); d["g_out"] = do("g_out", [512, SL])
    with ExitStack() as es:
        S = Sched(nc, es)
        ps = es.enter_context(nc.psum_tensor("ps", [128, 8, 512], F32))
        emit_rw1(S, nc, es, ps, d, SL, TS, SD)
    return nc


GN_EPS = 64e-5


def emit_rw2(S, nc, es, ps, d, T, xT, xb, LB):
    NT = T // 512
    tsl = lambda tt: slice(tt * 512, (tt + 1) * 512)
    sb = lambda es_, n, s, dt: es_.enter_context(nc.sbuf_tensor(n, s, dt))
    with ExitStack() as es2:
        cst = sb(es2, "r2_cst", [128, 896], F32)
        S.dma("sp", cst[:], d["consts"], writes=[("cst2",)])
        blkf = cst[:, 768:896]
        wout = sb(es2, "r2_wout", [128, 8, 1024], BF16)
        S.dma("pool", wout[:], d["w_out"].rearrange("(c p) n -> p c n", p=128), writes=[("r2wout",)])
        gcol = sb(es2, "r2_gcol", [128, 8], F32); bcol = sb(es2, "r2_bcol", [128, 8], F32); epsg = sb(es2, "r2_eps", [128, 1], F32)
        S.dma("sp", gcol[:], d["lnx_g"].rearrange("(c p) -> p c", p=128), writes=[("r2g",)], allow_slow_non_contiguous=True)
        S.dma("sp", bcol[:], d["lnx_b"].rearrange("(c p) -> p c", p=128), writes=[("r2b",)], allow_slow_non_contiguous=True)
        S.op("dve", lambda: nc.vector.memset(epsg[:], GN_EPS), writes=[("r2eps",)])
        names = ["yf", "yb", "pf", "pb", "v", "g"]
        tin = {n: [sb(es2, f"r2_{n}{i}", [128, 512], F32) for i in range(2)] for n in names}
        tmp = {n: [sb(es2, f"r2_t{n}{i}", [128, 512], F32) for i in range(2)] for n in ["y", "ysq", "mean", "a", "b", "pp"]}
        dv = {n: d[n].rearrange("(c p) t -> p c t", p=128) for n in names + ["x"]}
        it = 0
        for tt in range(NT):
            S.dma("sp", xT[:, :, tsl(tt)], dv["x"][:, :, tsl(tt)], writes=[("xT", c, tt) for c in range(8)])
            for c in range(8):
                i = it % 2; it += 1
                for n in names:
                    S.dma("sp", tin[n][i][:], dv[n][:, c, tsl(tt)], writes=[("r2in", n, i)])
                Y = tmp["y"][i]; YS = tmp["ysq"][i]; MN = tmp["mean"][i]; A = tmp["a"][i]; Bt = tmp["b"][i]; PP = tmp["pp"][i]
                S.op("dve", lambda: nc.vector.tensor_tensor(out=Y[:], in0=tin["yf"][i][:], in1=tin["yb"][i][:], op=ALU.add),
                     reads=[("r2in", "yf", i), ("r2in", "yb", i)], writes=[("r2y", i)])
                S.op("act", lambda: nc.scalar.activation(out=YS[:], in_=Y[:], func=AF.Square), reads=[("r2y", i)], writes=[("r2ysq", i)])
                S.op("pool", lambda: nc.gpsimd.tensor_tensor(out=PP[:], in0=tin["pf"][i][:], in1=tin["pb"][i][:], op=ALU.add),
                     reads=[("r2in", "pf", i), ("r2in", "pb", i)], writes=[("r2pp", i)])
                b1 = 0 + 3 * (it % 2); b2 = b1 + 1; b3 = b1 + 2
                S.op("pe", lambda: nc.tensor.matmul(ps[:, b1, :], blkf, Y[:], start=True, stop=True), reads=[("cst2",), ("r2y", i)], writes=[("ps", b1)])
                S.op("pe", lambda: nc.tensor.matmul(ps[:, b2, :], blkf, YS[:], start=True, stop=True), reads=[("cst2",), ("r2ysq", i)], writes=[("ps", b2)])
                S.op("pe", lambda: nc.tensor.matmul(ps[:, b3, :], blkf, PP[:], start=True, stop=True), reads=[("cst2",), ("r2pp", i)], writes=[("ps", b3)])
                S.op("act", lambda: nc.scalar.activation(out=MN[:], in_=ps[:, b1, :], func=AF.Identity, scale=1.0 / 64), reads=[("ps", b1)], writes=[("r2mean", i)])
                S.op("dve", lambda: nc.vector.tensor_tensor(out=A[:], in0=ps[:, b1, :], in1=MN[:], op=ALU.mult), reads=[("ps", b1), ("r2mean", i)], writes=[("r2a", i)])
                S.op("dve", lambda: nc.vector.tensor_tensor(out=A[:], in0=ps[:, b2, :], in1=A[:], op=ALU.subtract), reads=[("ps", b2), ("r2a", i)], writes=[("r2a", i)])
                S.op("act", lambda: nc.scalar.activation(out=A[:], in_=A[:], func=AF.Ln, bias=epsg[:], scale=1.0 / 64), reads=[("r2a", i), ("r2eps",)], writes=[("r2a", i)])
                S.op("act", lambda: nc.scalar.activation(out=A[:], in_=A[:], func=AF.Exp, scale=-0.5), reads=[("r2a", i)], writes=[("r2a", i)])
                S.op("pool", lambda: nc.gpsimd.tensor_tensor(out=Bt[:], in0=Y[:], in1=MN[:], op=ALU.subtract), reads=[("r2y", i), ("r2mean", i)], writes=[("r2b_", i)])
                S.op("pool", lambda: nc.gpsimd.tensor_tensor(out=Bt[:], in0=Bt[:], in1=A[:], op=ALU.mult), reads=[("r2b_", i), ("r2a", i)], writes=[("r2b_", i)])
                S.op("act", lambda: nc.scalar.activation(out=Bt[:], in_=Bt[:], func=AF.Identity, bias=bcol[:, c:c + 1], scale=gcol[:, c:c + 1]),
                     reads=[("r2b_", i), ("r2g",), ("r2b",)], writes=[("r2b_", i)])
                S.op("dve", lambda: nc.vector.tensor_tensor(out=YS[:], in0=ps[:, b3, :], in1=tin["v"][i][:], op=ALU.mult), reads=[("ps", b3), ("r2in", "v", i)], writes=[("r2ysq", i)])
                S.op("pool", lambda: nc.gpsimd.tensor_tensor(out=Bt[:], in0=Bt[:], in1=YS[:], op=ALU.add), reads=[("r2b_", i), ("r2ysq", i)], writes=[("r2b_", i)])
                S.op("dve", lambda: nc.vector.tensor_tensor(out=xb[:, c, tsl(tt)], in0=Bt[:], in1=tin["g"][i][:], op=ALU.mult), reads=[("r2b_", i), ("r2in", "g", i)], writes=[("xb", c, tt)])
            for dc in range(8):
                bank = 6 + (dc % 2)
                for c in range(8):
                    S.op("pe", lambda: nc.tensor.matmul(ps[:, bank, :], wout[:, c, dc * 128:(dc + 1) * 128], xb[:, c, tsl(tt)], start=(c == 0), stop=(c == 7)),
                         reads=[("r2wout",), ("xb", c, tt)], writes=[("ps", bank)], signal=(c == 7))
                S.op("pool", lambda: nc.gpsimd.tensor_scalar(out=xT[:, dc, tsl(tt)], in0=xT[:, dc, tsl(tt)], scalar1=float(ALPHA), scalar2=0.0, op0=ALU.mult, op1=ALU.add),
                     reads=[("xT", dc, tt)], writes=[("xT", dc, tt)])
                S.op("dve", lambda: nc.vector.tensor_tensor(out=xT[:, dc, tsl(tt)], in0=ps[:, bank, :], in1=xT[:, dc, tsl(tt)], op=ALU.add),
                     reads=[("ps", bank), ("xT", dc, tt)], writes=[("xT", dc, tt)])
    S.fence()


def build_rw2_prog(T):
    nc = bass.Bass("TRN2", target_bir_lowering=False)
    dt = lambda n, s: nc.dram_tensor(n, s, F32, kind="ExternalInput").ap()
    d = {n: dt(n, [D, T]) for n in ["x", "yf", "yb", "pf", "pb", "v", "g"]}
    d["consts"] = dt("consts", [128, 896])
    d["lnx_g"] = dt("lnx_g", [D]); d["lnx_b"] = dt("lnx_b", [D]); d["w_out"] = dt("w_out", [D, D])
    lng = dt("lng", [2, D]); lnb = dt("lnb", [2, D])
    w_in = dt("w_in", [D, 2 * FF]); w_outf = dt("w_outf", [FF, D])
    yT_d = nc.dram_tensor("yT", [D, T], F32, kind="ExternalOutput").ap()
    with ExitStack() as es:
        S = Sched(nc, es)
        ps = es.enter_context(nc.psum_tensor("ps", [128, 8, 512], F32))
        xT = es.enter_context(nc.sbuf_tensor("xTs", [128, 8, T], F32))
        xb = es.enter_context(nc.sbuf_tensor("xbs", [128, 8, T], BF16))
        B = FfnBufs(nc, es, T, with_ffn=False)
        B.init_consts(S, nc)
        emit_rw2(S, nc, es, ps, d, T, xT, xb, B)
        NT = T // 512
        emit_ln(S, nc, B, xT, xb, ps, lng[0], lnb[0], NT)
        B.alloc_ffn(nc, es)
        emit_ffn_ln(S, nc, B, xT, xb, ps, w_in, w_outf, lng[1], lnb[1], "f")
        yv = yT_d.rearrange("(c p) t -> p c t", p=128)
        for tt in range(NT):
            S.dma("sp", yv[:, :, tt * 512:(tt + 1) * 512], xT[:, :, tt * 512:(tt + 1) * 512], reads=[("xT", c, tt) for c in range(8)])
        S.finish("sp")
    return nc


_PROGS = {}


def _prog(key, fn):
    if key not in _PROGS:
        _PROGS[key] = fn()
    return _PROGS[key]


def _run(nc, in_maps):
    res = run_bass_kernel_spmd(nc, in_maps, core_ids=list(range(8)))
    return res.results


def _tok(c):
    b = c // 4
    t0 = (c % 4) * 2048
    return b, t0


def _tr(a):
    return np.ascontiguousarray(a.T)


def kernel(x, ffn_w_in, ffn_w_out, ln_g, ln_b, na_w_qkv, na_b_qkv, na_rpb, na_w_o, na_b_o,
           rw_mix, rw_w_rkv, rw_w0, rw_w1, rw_w2, rw_a0, rw_a1, rw_a2, rw_g1, rw_g2,
           rw_k_k, rw_k_a, rw_r_k, rw_lnx_g, rw_lnx_b, rw_w_out):
    f32 = np.float32
    A = lambda a: np.ascontiguousarray(np.asarray(a, dtype=f32))
    x = A(x)
    Bsz, SEQ, Dm = x.shape
    T = 2048
    nc1 = _prog("ffn1", lambda: build_ffn_prog(T, 1))
    ins = []
    for c in range(8):
        b, t0 = _tok(c)
        ins.append({"xT": _tr(x[b, t0:t0 + T]), "w_in": A(ffn_w_in[0, 0:1]), "w_out": A(ffn_w_out[0, 0:1]),
                    "lng": A(ln_g[0, 0:1]), "lnb": A(ln_b[0, 0:1])})
    r = _run(nc1, ins)
    x1 = np.empty_like(x)
    for c in range(8):
        b, t0 = _tok(c)
        x1[b, t0:t0 + T] = r[c]["yT"].T
    nc2 = _prog("na", lambda: build_na_prog(32))
    ins = []
    for c in range(8):
        b, t0 = _tok(c)
        r0 = (c % 4) * 32
        G = na_halo_rows(r0, 32, 128)
        xg = x1[b].reshape(128, 64, Dm)
        ins.append({"xh": _tr(xg[G].reshape(-1, Dm)), "xo": _tr(x1[b, t0:t0 + T]), "w_qkv": A(na_w_qkv[0]), "b_qkv": A(na_b_qkv[0]),
                    "tab": na_tables(A(na_rpb[0]), r0, 32, 128), "w_o": A(na_w_o[0]), "b_o": A(na_b_o[0]),
                    "lng": A(ln_g[0, 1]), "lnb": A(ln_b[0, 1])})
    r = _run(nc2, ins)
    x2 = np.empty_like(x)
    for c in range(8):
        b, t0 = _tok(c)
        x2[b, t0:t0 + T] = r[c]["yT"].T
    nc3 = _prog("ffn2", lambda: build_ffn_prog(T, 2))
    w_in2 = A(np.stack([ffn_w_in[0, 1], ffn_w_in[1, 0]])); w_out2 = A(np.stack([ffn_w_out[0, 1], ffn_w_out[1, 0]]))
    lng2 = A(np.stack([ln_g[0, 2], ln_g[1, 0]])); lnb2 = A(np.stack([ln_b[0, 2], ln_b[1, 0]]))
    ins = []
    for c in range(8):
        b, t0 = _tok(c)
        ins.append({"xT": _tr(x2[b, t0:t0 + T]), "w_in": w_in2, "w_out": w_out2, "lng": lng2, "lnb": lnb2})
    r = _run(nc3, ins)
    x4 = np.empty_like(x)
    for c in range(8):
        b, t0 = _tok(c)
        x4[b, t0:t0 + T] = r[c]["yT"].T
    nc4 = _prog("rw1", lambda: build_rw1_prog(SEQ, 256, BF16))
    consts = rw_consts()
    ins = []
    for c in range(8):
        b = c // 4; z = (c // 2) % 2; hg = c % 2
        cs_ = slice(hg * 512, (hg + 1) * 512)
        xs_ = x4[b][::-1] if z == 1 else x4[b]
        ins.append({"xT": _tr(xs_), "mix": A(rw_mix[0]), "consts": consts,
                    "w_r": A(rw_w_rkv[0, 0][:, cs_]), "w_k": A(rw_w_rkv[0, 1][:, cs_]), "w_v": A(rw_w_rkv[0, 2][:, cs_]),
                    "w1": A(rw_w1[0, z]), "w2": A(rw_w2[0, z][:, cs_]), "w0": A(rw_w0[0, z][cs_]),
                    "a1": A(rw_a1[0, z]), "a2": A(rw_a2[0, z][:, cs_]), "a0": A(rw_a0[0, z][cs_]),
                    "g1": A(rw_g1[0]), "g2": A(rw_g2[0][:, cs_]),
                    "k_k": A(rw_k_k[0][cs_]), "k_a": A(rw_k_a[0][cs_]), "r_k": A(np.reshape(rw_r_k[0], (-1,))[cs_])})
    r = _run(nc4, ins)
    yT = np.empty((2, Bsz, Dm, SEQ), f32); pT = np.empty((2, Bsz, Dm, SEQ), f32)
    vT = np.empty((Bsz, Dm, SEQ), f32); gT = np.empty((Bsz, Dm, SEQ), f32)
    for c in range(8):
        b = c // 4; z = (c // 2) % 2; hg = c % 2
        cs_ = slice(hg * 512, (hg + 1) * 512)
        yo = r[c]["y_out"]; po = r[c]["p_out"]
        if z == 1:
            yT[z, b, cs_] = yo[::-1].T
            pT[z, b, cs_] = po[:, ::-1]
        else:
            yT[z, b, cs_] = yo.T
            pT[z, b, cs_] = po
            vT[b, cs_] = r[c]["v_out"]; gT[b, cs_] = r[c]["g_out"]
    nc5 = _prog("rw2", lambda: build_rw2_prog(T))
    ins = []
    lng5 = A(np.stack([ln_g[1, 1], ln_g[1, 2]])); lnb5 = A(np.stack([ln_b[1, 1], ln_b[1, 2]]))
    for c in range(8):
        b, t0 = _tok(c)
        ts_ = slice(t0, t0 + T)
        cp = lambda a: np.ascontiguousarray(a[:, ts_])
        ins.append({"x": _tr(x4[b, ts_]), "yf": cp(yT[0, b]), "yb": cp(yT[1, b]), "pf": cp(pT[0, b]), "pb": cp(pT[1, b]),
                    "v": cp(vT[b]), "g": cp(gT[b]), "consts": consts, "lnx_g": A(rw_lnx_g[0]), "lnx_b": A(rw_lnx_b[0]),
                    "w_out": A(rw_w_out[0]), "lng": lng5, "lnb": lnb5, "w_in": A(ffn_w_in[1, 1]), "w_outf": A(ffn_w_out[1, 1])})
    r = _run(nc5, ins)
    out = np.empty_like(x)
    for c in range(8):
        b, t0 = _tok(c)
        out[b, t0:t0 + T] = r[c]["yT"].T
    return out
```

```python
import numpy as np
from concourse.bass_utils import run_bass_kernel_spmd
import numpy as np
from contextlib import ExitStack
import concourse.bass as bass
import concourse.mybir as mybir

F32 = mybir.dt.float32
BF16 = mybir.dt.bfloat16
AF = mybir.ActivationFunctionType
ALU = mybir.AluOpType
AX = mybir.AxisListType


class Sched:
    SEM_ROT = 12000
    N_DMA_SEM = 24

    def __init__(self, nc, es):
        self.nc = nc
        self.es = es
        self.eng = {"pe": nc.tensor, "act": nc.scalar, "dve": nc.vector, "pool": nc.gpsimd, "sp": nc.sync}
        self.sems = []
        self.cur = {}
        for e in self.eng:
            self.cur[e] = [self._new_sem(f"p_{e}"), 0]
        self.waited = {e: {} for e in self.eng}
        self.last_w = {}
        self.readers = {}
        self.dma_sems = [[self._new_sem(f"dma{i}"), 0] for i in range(self.N_DMA_SEM)]
        self.dma_rr = 0
        self.n_inst = {e: 0 for e in self.eng}
        self.n_wait = {e: 0 for e in self.eng}
        self.pending = {e: False for e in self.eng}
        self.clock = {e: {} for e in self.eng}
        self.tclock = {}
        self.n_fused = {}

    def _new_sem(self, name):
        h = self.es.enter_context(self.nc.semaphore(f"{name}_{len(self.sems)}"))
        self.sems.append(h)
        return len(self.sems) - 1

    def _need(self, e, ticket):
        if ticket is None:
            return False
        sid, val = ticket
        ck = self.clock.setdefault(e, {})
        if ck.get(sid, 0) >= val:
            return False
        for e2, c in self.cur.items():
            if c[0] == sid and val > c[1]:
                assert e2 == e, f"{e} waits on pending ticket of {e2}"
                return False
        ck[sid] = val
        snap = self.tclock.get(ticket)
        if snap:
            for s2, v2 in snap.items():
                if ck.get(s2, 0) < v2:
                    ck[s2] = v2
        return True

    def _wait(self, e, ticket):
        if self._need(e, ticket):
            self.eng[e].wait_ge(self.sems[ticket[0]], ticket[1])
            self.n_wait[e] += 1

    def fence(self):
        ts = []
        for e2, c in self.cur.items():
            assert not self.pending[e2]
            if c[1] > 0:
                ts.append((c[0], c[1]))
        for s_ in self.dma_sems:
            if s_[1] > 0:
                ts.append((s_[0], s_[1]))
        self.fence_tickets = ts
        self.fence_done = set()

    def _collect(self, e, reads, writes):
        rd, wr = [], []
        if getattr(self, "fence_tickets", None) and e not in self.fence_done:
            self.fence_done.add(e)
            for t in self.fence_tickets:
                if self._need(e, t):
                    rd.append(t)
        for k in reads:
            t = self.last_w.get(k)
            if self._need(e, t):
                rd.append(t)
        for k in writes:
            t = self.last_w.get(k)
            if self._need(e, t):
                wr.append(t)
            for sid, val in self.readers.get(k, {}).items():
                if self._need(e, (sid, val)):
                    wr.append((sid, val))
        return rd, wr

    def _deps(self, e, reads, writes):
        rd, wr = self._collect(e, reads, writes)
        for t in rd + wr:
            self.eng[e].wait_ge(self.sems[t[0]], t[1])
            self.n_wait[e] += 1

    def _record(self, ticket, reads, writes):
        sid, val = ticket
        for k in reads:
            r = self.readers.setdefault(k, {})
            if r.get(sid, 0) < val:
                r[sid] = val
        for k in writes:
            self.last_w[k] = ticket
            self.readers[k] = {}

    def op(self, e, fn, reads=(), writes=(), signal=True):
        rd, wr = self._collect(e, reads, writes)
        fuse = None
        if e == "pe":
            if wr:
                fuse = wr.pop()
        elif e in ("act", "dve", "pool"):
            if wr:
                fuse = wr.pop()
            elif rd:
                fuse = rd.pop()
        for t in rd + wr:
            self.eng[e].wait_ge(self.sems[t[0]], t[1])
            self.n_wait[e] += 1
        c = self.cur[e]
        if signal and c[1] >= self.SEM_ROT and not self.pending[e]:
            c[0] = self._new_sem(f"p_{e}")
            c[1] = 0
        inst = fn()
        if fuse is not None:
            inst._wait_ge(self.sems[fuse[0]], fuse[1])
            self.n_fused[e] = self.n_fused.get(e, 0) + 1
        self.n_inst[e] += 1
        if signal:
            c[1] += 1
            inst.then_inc(self.sems[c[0]], 1)
            ticket = (c[0], c[1])
            self.pending[e] = False
            self.tclock[ticket] = dict(self.clock.get(e, {}))
        else:
            ticket = (c[0], c[1] + 1)
            self.pending[e] = True
        self._record(ticket, reads, writes)
        return ticket

    def dma(self, q, out, in_, reads=(), writes=(), **kw):
        self._deps(q, reads, writes)
        if q == "pool":
            slot = [self._new_sem("swdma"), 0]
            self.dma_sems.append(slot)
        else:
            slot = self.dma_sems[self.dma_rr]
            self.dma_rr = (self.dma_rr + 1) % self.N_DMA_SEM
        if slot[1] > 0:
            self._wait(q, (slot[0], slot[1]))
        slot[1] += 16
        self.eng[q].dma_start(out=out, in_=in_, **kw).then_inc(self.sems[slot[0]], 16)
        self.n_inst[q] += 1
        ticket = (slot[0], slot[1])
        self.tclock[ticket] = dict(self.clock.get(q, {}))
        self._record(ticket, reads, writes)
        return ticket

    def finish(self, e="sp"):
        for k, t in list(self.last_w.items()):
            self._wait(e, t)
        for s in self.dma_sems:
            if s[1] > 0:
                self._wait(e, (s[0], s[1]))


D = 1024; FF = 2816; NFC = 22
GROUPS = [(0, 2), (2, 4), (6, 4), (10, 4), (14, 4), (18, 4)]
ALPHA = 4 ** 0.25
LN_EPS = 1e-5


class FfnBufs:
    def __init__(self, nc, es, T, with_ffn=True, pfx=""):
        self.pfx = pfx
        self.T = T
        self.NT = T // 512
        sb = lambda n, s, d: es.enter_context(nc.sbuf_tensor(pfx + n, s, d))
        if with_ffn:
            self.alloc_ffn(nc, es)
        self.sq = [sb(f"sq{i}", [128, 8, 512], BF16) for i in range(1)]
        self.yb = [sb(f"yb{i}", [128, 8, 512], BF16) for i in range(1)]
        self.mean_s = sb("mean_s", [128, 512], F32)
        self.m2 = sb("m2", [128, 512], F32)
        self.rstd = sb("rstd", [128, 512], F32)
        self.t1 = [sb(f"t1_{i}", [128, 512], F32) for i in range(2)]
        self.t2 = [sb(f"t2_{i}", [128, 512], F32) for i in range(2)]
        self.ones = sb("ones", [128, 128], BF16)
        self.gcol = sb("gcol", [128, 8], F32)
        self.bcol = sb("bcol", [128, 8], F32)
        self.epsc = sb("epsc", [128, 1], F32)
        self.wslot = 0

    def alloc_ffn(self, nc, es):
        sb = lambda n, s, d: es.enter_context(nc.sbuf_tensor(self.pfx + n, s, d))
        T = self.T
        self.hh = sb("hh", [128, 4, T], BF16)
        self.wi = [sb(f"wi{i}", [128, 2, 8, 512], BF16) for i in range(2)]
        self.wo = [sb(f"wo{i}", [128, 4, 1024], BF16) for i in range(2)]
        self.sg = [sb(f"sg{i}", [128, 512], F32) for i in range(2)]

    def init_consts(self, S, nc):
        S.op("dve", lambda: nc.vector.memset(self.ones[:], 1.0 / 1024), writes=[("ones",)])
        S.op("dve", lambda: nc.vector.memset(self.epsc[:], LN_EPS), writes=[("epsc",)])


def emit_ffn_ln(S, nc, B, xT, xb, ps, w_in, w_out, ln_g, ln_b, tag, ntiles=None):
    NT = B.NT if ntiles is None else ntiles
    tsl = lambda tt: slice(tt * 512, (tt + 1) * 512)
    w_in_v = w_in.rearrange("(c p) (u f) -> p c u f", p=128, u=2)
    w_out_v = w_out.rearrange("(j p) d -> p j d", p=128)

    def load_group(g):
        f0, n = GROUPS[g]
        slot = B.wslot; B.wslot ^= 1
        for u in range(2):
            S.dma("pool", B.wi[slot][:, u, :, 0:n * 128], w_in_v[:, :, u, f0 * 128:(f0 + n) * 128],
                  writes=[("wi", slot, u)])
        S.dma("pool", B.wo[slot][:, 0:n, :], w_out_v[:, f0:f0 + n, :], writes=[("wo", slot)])
        return slot

    slots = {0: load_group(0)}
    for tt in range(NT):
        for c in range(8):
            S.op("act", lambda: nc.scalar.activation(out=xT[:, c, tsl(tt)], in_=xT[:, c, tsl(tt)], func=AF.Identity, scale=float(ALPHA)),
                 reads=[("xT", c, tt)], writes=[("xT", c, tt)])
    pa = 0
    pb = 0
    for g in range(len(GROUPS)):
        f0, n = GROUPS[g]
        slot = slots[g]
        if g + 1 < len(GROUPS):
            slots[g + 1] = load_group(g + 1)
        for tt in range(NT):
            for j in range(n):
                bg = 2 * pa; bu = 2 * pa + 1; pa ^= 1
                for (bank, u) in ((bg, 0), (bu, 1)):
                    for c in range(8):
                        S.op("pe", lambda: nc.tensor.matmul(ps[:, bank, :], B.wi[slot][:, u, c, j * 128:(j + 1) * 128], xb[:, c, tsl(tt)], start=(c == 0), stop=(c == 7)),
                             reads=[("wi", slot, u), ("xb", c, tt)], writes=[("ps", bank)], signal=(c == 7))
                sgi = (tt * n + j) % 2
                S.op("act", lambda: nc.scalar.activation(out=B.sg[sgi][:], in_=ps[:, bg, :], func=AF.Silu),
                     reads=[("ps", bg)], writes=[("sg", sgi)])
                S.op("dve", lambda: nc.vector.tensor_tensor(out=B.hh[:, j, tsl(tt)], in0=ps[:, bu, :], in1=B.sg[sgi][:], op=ALU.mult),
                     reads=[("ps", bu), ("sg", sgi)], writes=[("hh", j, tt)])
        for tt in range(NT):
            for dc in range(8):
                bank = 4 + pb; pb ^= 1
                for j in range(n):
                    S.op("pe", lambda: nc.tensor.matmul(ps[:, bank, :], B.wo[slot][:, j, dc * 128:(dc + 1) * 128], B.hh[:, j, tsl(tt)], start=(j == 0), stop=(j == n - 1)),
                         reads=[("wo", slot), ("hh", j, tt)], writes=[("ps", bank)], signal=(j == n - 1))
                S.op("dve", lambda: nc.vector.scalar_tensor_tensor(out=xT[:, dc, tsl(tt)], in0=ps[:, bank, :], scalar=0.5, in1=xT[:, dc, tsl(tt)], op0=ALU.mult, op1=ALU.add),
                     reads=[("ps", bank), ("xT", dc, tt)], writes=[("xT", dc, tt)])
    emit_ln(S, nc, B, xT, xb, ps, ln_g, ln_b, NT)


def emit_ln(S, nc, B, xT, xb, ps, ln_g, ln_b, NT, xb_tiles=None):
    tsl = lambda tt: slice(tt * 512, (tt + 1) * 512)
    S.dma("sp", B.gcol[:], ln_g.rearrange("(c p) -> p c", p=128), writes=[("gcol",)], allow_slow_non_contiguous=True)
    S.dma("sp", B.bcol[:], ln_b.rearrange("(c p) -> p c", p=128), writes=[("bcol",)], allow_slow_non_contiguous=True)
    for tt in range(NT):
        i2 = 0
        for c in range(8):
            S.op("act", lambda: nc.scalar.activation(out=B.sq[i2][:, c, :], in_=xT[:, c, tsl(tt)], func=AF.Square),
                 reads=[("xT", c, tt)], writes=[("sq", i2, c)])
            S.op("pool", lambda: nc.gpsimd.tensor_copy(out=B.yb[i2][:, c, :], in_=xT[:, c, tsl(tt)]),
                 reads=[("xT", c, tt)], writes=[("yb", i2, c)])
        for c in range(8):
            S.op("pe", lambda: nc.tensor.matmul(ps[:, 6, :], B.ones[:], B.yb[i2][:, c, :], start=(c == 0), stop=(c == 7)),
                 reads=[("ones",), ("yb", i2, c)], writes=[("ps", 6)], signal=(c == 7))
        for c in range(8):
            S.op("pe", lambda: nc.tensor.matmul(ps[:, 7, :], B.ones[:], B.sq[i2][:, c, :], start=(c == 0), stop=(c == 7)),
                 reads=[("ones",), ("sq", i2, c)], writes=[("ps", 7)], signal=(c == 7))
        S.op("act", lambda: nc.scalar.copy(out=B.mean_s[:], in_=ps[:, 6, :]), reads=[("ps", 6)], writes=[("mean_s",)])
        S.op("dve", lambda: nc.vector.tensor_tensor(out=B.m2[:], in0=ps[:, 6, :], in1=B.mean_s[:], op=ALU.mult),
             reads=[("ps", 6), ("mean_s",)], writes=[("m2",)])
        S.op("dve", lambda: nc.vector.tensor_tensor(out=B.m2[:], in0=ps[:, 7, :], in1=B.m2[:], op=ALU.subtract),
             reads=[("ps", 7), ("m2",)], writes=[("m2",)])
        S.op("act", lambda: nc.scalar.activation(out=B.m2[:], in_=B.m2[:], func=AF.Ln, bias=B.epsc[:], scale=1.0),
             reads=[("m2",), ("epsc",)], writes=[("m2",)])
        S.op("act", lambda: nc.scalar.activation(out=B.rstd[:], in_=B.m2[:], func=AF.Exp, scale=-0.5), reads=[("m2",)], writes=[("rstd",)])
        for c in range(8):
            k = c % 2
            S.op("dve", lambda: nc.vector.tensor_tensor(out=B.t1[k][:], in0=xT[:, c, tsl(tt)], in1=ps[:, 6, :], op=ALU.subtract),
                 reads=[("xT", c, tt), ("ps", 6)], writes=[("t1", k)])
            S.op("pool" if c % 2 else "dve", lambda: (nc.gpsimd if c % 2 else nc.vector).tensor_tensor(out=B.t2[k][:], in0=B.t1[k][:], in1=B.rstd[:], op=ALU.mult),
                 reads=[("t1", k), ("rstd",)], writes=[("t2", k)])
            S.op("act", lambda: nc.scalar.activation(out=xT[:, c, tsl(tt)], in_=B.t2[k][:], func=AF.Identity, bias=B.bcol[:, c:c + 1], scale=B.gcol[:, c:c + 1]),
                 reads=[("t2", k), ("gcol",), ("bcol",)], writes=[("xT", c, tt)])
            S.op("act", lambda: nc.scalar.activation(out=xb[:, c, tsl(tt)], in_=B.t2[k][:], func=AF.Identity, bias=B.bcol[:, c:c + 1], scale=B.gcol[:, c:c + 1]),
                 reads=[("t2", k), ("gcol",), ("bcol",)], writes=[("xb", c, tt)])


def build_ffn_prog(T, n_ffn):
    nc = bass.Bass("TRN2", target_bir_lowering=False)
    xT_d = nc.dram_tensor("xT", [D, T], F32, kind="ExternalInput").ap()
    w_in = nc.dram_tensor("w_in", [n_ffn, D, 2 * FF], F32, kind="ExternalInput").ap()
    w_out = nc.dram_tensor("w_out", [n_ffn, FF, D], F32, kind="ExternalInput").ap()
    lng = nc.dram_tensor("lng", [n_ffn, D], F32, kind="ExternalInput").ap()
    lnb = nc.dram_tensor("lnb", [n_ffn, D], F32, kind="ExternalInput").ap()
    yT_d = nc.dram_tensor("yT", [D, T], F32, kind="ExternalOutput").ap()
    with ExitStack() as es:
        S = Sched(nc, es)
        xT = es.enter_context(nc.sbuf_tensor("xTs", [128, 8, T], F32))
        xb = es.enter_context(nc.sbuf_tensor("xbs", [128, 8, T], BF16))
        ps = es.enter_context(nc.psum_tensor("ps", [128, 8, 512], F32))
        B = FfnBufs(nc, es, T)
        NT = T // 512
        B.init_consts(S, nc)
        xv = xT_d.rearrange("(c p) t -> p c t", p=128)
        yv = yT_d.rearrange("(c p) t -> p c t", p=128)
        for tt in range(NT):
            S.dma("sp", xT[:, :, tt * 512:(tt + 1) * 512], xv[:, :, tt * 512:(tt + 1) * 512],
                  writes=[("xT", c, tt) for c in range(8)])
            for c in range(8):
                S.op("act", lambda: nc.scalar.copy(out=xb[:, c, tt * 512:(tt + 1) * 512], in_=xT[:, c, tt * 512:(tt + 1) * 512]),
                     reads=[("xT", c, tt)], writes=[("xb", c, tt)])
        for i in range(n_ffn):
            emit_ffn_ln(S, nc, B, xT, xb, ps, w_in[i], w_out[i], lng[i], lnb[i], f"f{i}")
        for tt in range(NT):
            S.dma("sp", yv[:, :, tt * 512:(tt + 1) * 512], xT[:, :, tt * 512:(tt + 1) * 512],
                  reads=[("xT", c, tt) for c in range(8)])
        S.finish("sp")
    return nc


NH = 16; HD = 64; NHP = 8


def na_pat(r, NR):
    if r < 4:
        return 1 + r
    if r >= NR - 3:
        return 5 + (r - (NR - 3))
    return 0


def na_halo_rows(r0, NR, rows):
    G = []
    for L in range(NR + 8):
        g = r0 - 4 + L
        if g < 0:
            g = g + 8
        elif g >= rows:
            g = rows - 8 + (g - rows)
        g = min(max(g, 0), rows - 1)
        G.append(g)
    return G


def na_tables(rpb, r0, NR, rows):
    G = np.array(na_halo_rows(r0, NR, rows))
    reps = {0: min(NR // 2, NR - 4)}
    for r in range(NR):
        p = na_pat(r, NR)
        if p != 0:
            reps[p] = r
    tab = np.empty((NHP, 8, 128, 4, 128), np.float32)
    qc = np.arange(64)
    qstart = np.clip(qc - 8, 0, 48)
    for p in range(8):
        r = reps.get(p, reps[0])
        R = r0 + r
        rs = min(max(R - 4, 0), rows - 8)
        kL = r + np.arange(8)
        kG = G[kL]
        row_ok = (kG >= rs) & (kG < rs + 8)
        dr = np.clip(kG - R + 7, 0, 14)
        kcol = np.arange(64)
        col_ok = (kcol[None, :] >= qstart[:, None]) & (kcol[None, :] < qstart[:, None] + 16)
        dc = np.clip(kcol[None, :] - qc[:, None] + 15, 0, 30)
        b = rpb[:, dr][:, :, dc]
        b = np.transpose(b, (0, 1, 3, 2))
        ok = row_ok[:, None, None] & np.transpose(col_ok)[None, :, :]
        b = np.where(ok[None], b, np.float32(-1e30)).astype(np.float32)
        b = b.reshape(NHP, 2, 512, 64)
        b = b.reshape(NHP, 2, 4, 128, 64)
        tab[:, p] = np.transpose(b, (0, 3, 2, 1, 4)).reshape(NHP, 128, 4, 128)
    return tab


def emit_na(S, nc, es, ps, xh_d, x_own_d, w_qkv, b_qkv, tab_d, w_o, b_o, ln_g, ln_b, NR, yT_d):
    T = NR * 64; TH = (NR + 8) * 64
    NT = T // 512
    NTH = TH // 512
    sb = lambda es_, n, s, d: es_.enter_context(nc.sbuf_tensor(n, s, d))
    oT = sb(es, "oT", [128, 8, T], BF16)
    tsl = lambda tt: slice(tt * 512, (tt + 1) * 512)
    with ExitStack() as es2:
        xb = sb(es2, "na_xb", [128, 8, TH], BF16)
        tabs2 = [sb(es2, f"na_tab{i}", [128, 8, 4, 128], F32) for i in range(2)]
        KT = [sb(es2, f"na_KT{i}", [128, TH], BF16) for i in range(2)]
        Ve4 = sb(es2, "na_Ve4", [128, TH // 128, 512], BF16)
        Vo4 = sb(es2, "na_Vo4", [128, TH // 128, 512], BF16)
        wv4 = sb(es2, "na_wv4", [128, 8, 512], BF16)
        QBD = [sb(es2, f"na_Q{i}", [128, NR, 2, 64], BF16) for i in range(2)]
        wq = [sb(es2, f"na_wq{i}", [128, 2, 8, 128], BF16) for i in range(2)]
        sbt = [sb(es2, f"na_sb{i}", [128, 512], F32) for i in range(4)]
        PT = [sb(es2, f"na_PT{i}", [128, 512], BF16) for i in range(4)]
        rc = [sb(es2, f"na_rc{i}", [128, 128], F32) for i in range(4)]
        bcols = sb(es2, "na_bc", [128, 24], F32)
        bvrow = sb(es2, "na_bvrow", [1, 1024], BF16)
        bvb = sb(es2, "na_bvb", [128, 1024], F32)
        ones_r = sb(es2, "na_ones_r", [1, 128], BF16)
        ones_k = sb(es2, "na_ones_k", [128, 128], BF16)

        S.op("dve", lambda: nc.vector.memset(ones_r[:], 1.0), writes=[("ones_r",)])
        S.op("dve", lambda: nc.vector.memset(ones_k[:], 1.0), writes=[("ones_k",)])
        for i in range(2):
            S.op("pool", lambda: nc.gpsimd.memset(QBD[i][:], 0.0), writes=[("QBD", i)])
        S.dma("sp", bcols[:], b_qkv.rearrange("(j p) -> p j", p=128), writes=[("bcols",)], allow_slow_non_contiguous=True)
        S.dma("pool", bvrow[:], b_qkv[2048:3072].rearrange("(o n) -> o n", o=1), writes=[("bvrow",)])
        xhv = xh_d.rearrange("(c p) t -> p c t", p=128)
        for tt in range(NTH):
            S.dma("pool", xb[:, :, tsl(tt)], xhv[:, :, tsl(tt)], writes=[("nxb", tt)])
        for h2 in range(2):
            S.op("pe", lambda: nc.tensor.matmul(ps[:, 6, :], ones_r[0:1, :], bvrow[0:1, h2 * 512:(h2 + 1) * 512], start=True, stop=True),
                 reads=[("ones_r",), ("bvrow",)], writes=[("ps", 6)])
            S.op("act", lambda: nc.scalar.copy(out=bvb[:, h2 * 512:(h2 + 1) * 512], in_=ps[:, 6, :]), reads=[("ps", 6)], writes=[("bvb", h2)])
        wv = w_qkv.rearrange("(c p) n -> p c n", p=128)
        tabv = tab_d

        def load_w(hp, slot):
            for k in range(2):
                S.dma("pool", wq[slot][:, k, :, :], wv[:, :, k * 1024 + hp * 128:k * 1024 + (hp + 1) * 128], writes=[("wq", slot, k)])

        load_w(0, 0)
        pa = 0

        def gen_proj(hp):
            nonlocal pa
            slot = hp % 2
            if hp + 1 < NHP:
                load_w(hp + 1, 1 - slot)
            S.dma("sp", tabs2[slot][:].rearrange("p a k n -> p a (k n)"), tabv[hp].rearrange("a p k n -> p a (k n)"), writes=[("tabs", slot)])
            for tt in range(NTH):
                bank = pa; pa = (pa + 1) % 4
                for c in range(8):
                    S.op("pe", lambda: nc.tensor.matmul(ps[:, bank, :], wq[slot][:, 1, c, :], xb[:, c, tsl(tt)], start=(c == 0), stop=(c == 7)),
                         reads=[("wq", slot, 1), ("nxb", tt)], writes=[("ps", bank)], signal=(c == 7))
                S.op("act", lambda: nc.scalar.activation(out=KT[slot][:, tsl(tt)], in_=ps[:, bank, :], func=AF.Identity, bias=bcols[:, 8 + hp:9 + hp], scale=1.0),
                     reads=[("ps", bank), ("bcols",)], writes=[("KT", slot, tt)])
                yield
            for tt in range(NT):
                bank = pa; pa = (pa + 1) % 4
                for c in range(8):
                    S.op("pe", lambda: nc.tensor.matmul(ps[:, bank, :], wq[slot][:, 0, c, :], xb[:, c, 256 + tt * 512:256 + (tt + 1) * 512], start=(c == 0), stop=(c == 7)),
                         reads=[("wq", slot, 0)] + [("nxb", t2) for t2 in range(NTH)], writes=[("ps", bank)], signal=(c == 7))
                for hd in range(2):
                    pr = slice(hd * 64, (hd + 1) * 64)
                    S.op("dve", lambda: nc.vector.tensor_scalar(out=QBD[slot][pr, tt * 8:(tt + 1) * 8, hd, :], in0=ps[pr, bank, :].rearrange("p (r q) -> p r q", q=64),
                                                                scalar1=bcols[pr, hp:hp + 1], scalar2=0.125, op0=ALU.add, op1=ALU.mult),
                         reads=[("ps", bank), ("bcols",)], writes=[("QBD", slot)])
                yield
            yield

        def gen_rows(hp):
            nonlocal pa
            slot = hp % 2
            nch = TH // 128
            if hp % 4 == 0:
                hg = hp // 4
                S.dma("pool", wv4[:], wv[:, :, 2048 + hg * 512:2048 + (hg + 1) * 512], writes=[("wv4",)])
                for (Vx, off, cnt, nm) in ((Ve4, 0, nch, "Ve"), (Vo4, 64, nch - 1, "Vo")):
                    for j in range(cnt):
                        bank = pa; pa = (pa + 1) % 4
                        for c in range(8):
                            S.op("pe", lambda: nc.tensor.matmul(ps[:, bank, :], xb[:, c, off + j * 128:off + (j + 1) * 128], wv4[:, c, :], start=(c == 0), stop=(c == 7)),
                                 reads=[("wv4",)] + [("nxb", t2) for t2 in range(NTH)], writes=[("ps", bank)], signal=(c == 7))
                        S.op("dve", lambda: nc.vector.tensor_tensor(out=Vx[:, j, :], in0=ps[:, bank, :], in1=bvb[:, hg * 512:(hg + 1) * 512], op=ALU.add),
                             reads=[("ps", bank), ("bvb", hg)], writes=[(nm, j)])
                        yield
            def stage1(r):
                nonlocal pa
                pat = na_pat(r, NR)
                i2 = r % 4
                bankS = pa; pa = (pa + 1) % 4
                tok0 = r * 64
                for kc in range(4):
                    S.op("pe", lambda: nc.tensor.matmul(ps[:, bankS, kc * 128:(kc + 1) * 128], KT[slot][:, tok0 + kc * 128:tok0 + (kc + 1) * 128], QBD[slot][:, r, :, :].rearrange("p a q -> p (a q)"), start=True, stop=True),
                         reads=[("KT", slot, t2) for t2 in range(NTH)] + [("QBD", slot)], writes=[("ps", bankS)], signal=(kc == 3))
                S.op("dve", lambda: nc.vector.tensor_tensor(out=sbt[i2][:], in0=ps[:, bankS, :], in1=tabs2[slot][:, pat, :, :].rearrange("p k n -> p (k n)"), op=ALU.add),
                     reads=[("ps", bankS), ("tabs", slot)], writes=[("sbt", i2)])
                S.op("act", lambda: nc.scalar.activation(out=PT[i2][:], in_=sbt[i2][:], func=AF.Exp),
                     reads=[("sbt", i2)], writes=[("PT", i2)])

            def stage2(r):
                i2 = r % 4
                bankO = 4 + i2
                if r % 2 == 0:
                    Vx, j0, nm = Ve4, r // 2, "Ve"
                else:
                    Vx, j0, nm = Vo4, (r - 1) // 2, "Vo"
                h4 = hp % 4
                for kc in range(4):
                    S.op("pe", lambda: nc.tensor.matmul(ps[:, bankO, 0:128], Vx[:, j0 + kc, h4 * 128:(h4 + 1) * 128], PT[i2][:, kc * 128:(kc + 1) * 128], start=(kc == 0), stop=(kc == 3)),
                         reads=[(nm, j0 + kc), ("PT", i2)], writes=[("ps", bankO)], signal=False)
                for kc in range(4):
                    S.op("pe", lambda: nc.tensor.matmul(ps[:, bankO, 128:256], ones_k[:], PT[i2][:, kc * 128:(kc + 1) * 128], start=(kc == 0), stop=(kc == 3)),
                         reads=[("ones_k",), ("PT", i2)], writes=[("ps", bankO)], signal=(kc == 3))
                S.op("act", lambda: nc.scalar.activation(out=rc[i2][:], in_=ps[:, bankO, 128:256], func=AF.Ln),
                     reads=[("ps", bankO)], writes=[("rc", i2)])
                S.op("act", lambda: nc.scalar.activation(out=rc[i2][:], in_=rc[i2][:], func=AF.Exp, scale=-1.0),
                     reads=[("rc", i2)], writes=[("rc", i2)])
                for hd in range(2):
                    pr = slice(hd * 64, (hd + 1) * 64)
                    S.op("dve", lambda: nc.vector.tensor_tensor(out=oT[pr, hp, r * 64:(r + 1) * 64], in0=ps[pr, bankO, hd * 64:(hd + 1) * 64], in1=rc[i2][pr, hd * 64:(hd + 1) * 64], op=ALU.mult),
                         reads=[("ps", bankO), ("rc", i2)], writes=[("oT", hp, r // 8)])

            stage1(0)
            if NR > 1:
                stage1(1)
            for r in range(NR):
                if r + 2 < NR:
                    stage1(r + 2)
                stage2(r)
                yield

        def drain(g):
            n = 0
            for _ in g:
                n += 1
            return n

        n_pj = drain(gen_proj(0))
        for hp in range(NHP):
            gp = gen_proj(hp + 1) if hp + 1 < NHP else None
            gr = gen_rows(hp)
            cp = 0; cr = 0
            while gp is not None or gr is not None:
                fp = cp / n_pj if gp is not None else 2.0
                fr = cr / (NR + 1) if gr is not None else 2.0
                if gr is not None and (gp is None or fr <= fp):
                    try:
                        next(gr); cr += 1
                    except StopIteration:
                        gr = None
                else:
                    try:
                        next(gp); cp += 1
                    except StopIteration:
                        gp = None
    S.fence()
    with ExitStack() as es3:
        xT = sb(es3, "na_xTs", [128, 8, T], F32)
        xbo = sb(es3, "na_xbs", [128, 8, T], BF16)
        LB = FfnBufs(nc, es3, T, with_ffn=False, pfx="na_")
        LB.init_consts(S, nc)
        wo = sb(es3, "na_wo", [128, 8, 1024], BF16)
        bo = sb(es3, "na_bo", [128, 8], F32)
        S.dma("pool", wo[:], w_o.rearrange("(h p) n -> p h n", p=128), writes=[("nwo",)])
        S.dma("sp", bo[:], b_o.rearrange("(c p) -> p c", p=128), writes=[("nbo",)], allow_slow_non_contiguous=True)
        xov = x_own_d.rearrange("(c p) t -> p c t", p=128)
        pb = 0
        for tt in range(NT):
            S.dma("sp", xT[:, :, tsl(tt)], xov[:, :, tsl(tt)], writes=[("xT", c, tt) for c in range(8)])
            for dc in range(8):
                bank = pb; pb = (pb + 1) % 4
                for hp in range(8):
                    S.op("pe", lambda: nc.tensor.matmul(ps[:, bank, :], wo[:, hp, dc * 128:(dc + 1) * 128], oT[:, hp, tsl(tt)], start=(hp == 0), stop=(hp == 7)),
                         reads=[("nwo",), ("oT", hp, tt)], writes=[("ps", bank)], signal=(hp == 7))
                S.op("pool", lambda: nc.gpsimd.tensor_scalar(out=xT[:, dc, tsl(tt)], in0=xT[:, dc, tsl(tt)], scalar1=float(ALPHA), scalar2=bo[:, dc:dc + 1], op0=ALU.mult, op1=ALU.add),
                     reads=[("xT", dc, tt), ("nbo",)], writes=[("xT", dc, tt)])
                S.op("dve", lambda: nc.vector.tensor_tensor(out=xT[:, dc, tsl(tt)], in0=ps[:, bank, :], in1=xT[:, dc, tsl(tt)], op=ALU.add),
                     reads=[("ps", bank), ("xT", dc, tt)], writes=[("xT", dc, tt)])
        emit_ln(S, nc, LB, xT, xbo, ps, ln_g, ln_b, NT)
        yv = yT_d.rearrange("(c p) t -> p c t", p=128)
        for tt in range(NT):
            S.dma("sp", yv[:, :, tsl(tt)], xT[:, :, tsl(tt)], reads=[("xT", c, tt) for c in range(8)])
        S.finish("sp")


def build_na_prog(NR):
    T = NR * 64; TH = (NR + 8) * 64
    nc = bass.Bass("TRN2", target_bir_lowering=False)
    dt = lambda n, s: nc.dram_tensor(n, s, F32, kind="ExternalInput").ap()
    xh = dt("xh", [D, TH]); xo = dt("xo", [D, T])
    w_qkv = dt("w_qkv", [D, 3 * D]); b_qkv = dt("b_qkv", [3 * D]); tab = dt("tab", [NHP, 8, 128, 4, 128])
    w_o = dt("w_o", [D, D]); b_o = dt("b_o", [D]); lng = dt("lng", [D]); lnb = dt("lnb", [D])
    yT_d = nc.dram_tensor("yT", [D, T], F32, kind="ExternalOutput").ap()
    with ExitStack() as es:
        S = Sched(nc, es)
        ps = es.enter_context(nc.psum_tensor("ps", [128, 8, 512], F32))
        emit_na(S, nc, es, ps, xh, xo, w_qkv, b_qkv, tab, w_o, b_o, lng, lnb, NR, yT_d)
    return nc


def build_na_ffn_prog(NR, n_ffn):
    T = NR * 64; TH = (NR + 8) * 64
    nc = bass.Bass("TRN2", target_bir_lowering=False)
    dt = lambda n, s: nc.dram_tensor(n, s, F32, kind="ExternalInput").ap()
    xh = dt("xh", [D, TH]); xo = dt("xo", [D, T])
    w_qkv = dt("w_qkv", [D, 3 * D]); b_qkv = dt("b_qkv", [3 * D]); tab = dt("tab", [NHP, 8, 128, 4, 128])
    w_o = dt("w_o", [D, D]); b_o = dt("b_o", [D]); lng = dt("lng", [D]); lnb = dt("lnb", [D])
    w_in = dt("w_in", [n_ffn, D, 2 * FF]); w_out = dt("w_out", [n_ffn, FF, D])
    flng = dt("flng", [n_ffn, D]); flnb = dt("flnb", [n_ffn, D])
    mid = nc.dram_tensor("x_mid", [D, T], F32, kind="Internal").ap()
    yT_d = nc.dram_tensor("yT", [D, T], F32, kind="ExternalOutput").ap()
    with ExitStack() as es:
        S = Sched(nc, es)
        ps = es.enter_context(nc.psum_tensor("ps", [128, 8, 512], F32))
        with ExitStack() as esn:
            emit_na(S, nc, esn, ps, xh, xo, w_qkv, b_qkv, tab, w_o, b_o, lng, lnb, NR, mid)
        S.fence()
        xT = es.enter_context(nc.sbuf_tensor("xTs", [128, 8, T], F32))
        xb = es.enter_context(nc.sbuf_tensor("xbs", [128, 8, T], BF16))
        B = FfnBufs(nc, es, T)
        B.init_consts(S, nc)
        NT = T // 512
        xv = mid.rearrange("(c p) t -> p c t", p=128)
        yv = yT_d.rearrange("(c p) t -> p c t", p=128)
        for tt in range(NT):
            S.dma("sp", xT[:, :, tt * 512:(tt + 1) * 512], xv[:, :, tt * 512:(tt + 1) * 512],
                  reads=[("xmid", tt)], writes=[("xT", c, tt) for c in range(8)])
            for c in range(8):
                S.op("act", lambda: nc.scalar.copy(out=xb[:, c, tt * 512:(tt + 1) * 512], in_=xT[:, c, tt * 512:(tt + 1) * 512]),
                     reads=[("xT", c, tt)], writes=[("xb", c, tt)])
        for i in range(n_ffn):
            emit_ffn_ln(S, nc, B, xT, xb, ps, w_in[i], w_out[i], flng[i], flnb[i], f"f{i}")
        for tt in range(NT):
            S.dma("sp", yv[:, :, tt * 512:(tt + 1) * 512], xT[:, :, tt * 512:(tt + 1) * 512], reads=[("xT", c, tt) for c in range(8)])
        S.finish("sp")
    return nc


C0 = float(np.exp(-0.5))
CH = 64


def rw_consts():
    s = np.arange(128)[:, None]; t = np.arange(128)[None, :]
    same = (s // 64) == (t // 64)
    Sm = (same & (t > s)).astype(np.float32)
    Im = (same & (t >= s)).astype(np.float32)
    maskSI = np.concatenate([Sm, Im, Sm, Im], axis=1)
    maskTS = (same & (t < s)).astype(np.float32)
    ident = np.eye(128, dtype=np.float32)
    blk = same.astype(np.float32)
    return np.concatenate([maskSI, maskTS, ident, blk], axis=1)


STOP = ""


def emit_rw1(S, nc, es, ps, d, SL, TS=256, SD=BF16):
    NTI = SL // TS
    NCK = TS // CH
    NPR = TS // 128
    sb = lambda n, s, dt: es.enter_context(nc.sbuf_tensor(n, s, dt))
    cst = sb("rw_cst", [128, 896], F32)
    S.dma("sp", cst[:], d["consts"], writes=[("cst",)])
    maskSI = cst[:, 0:512]; maskTS = cst[:, 512:640]; identf = cst[:, 640:768]; blkf = cst[:, 768:896]
    ident_s = identf
    if SD != F32:
        ident_sd = sb("rw_identsd", [128, 128], SD)
        S.op("dve", lambda: nc.vector.tensor_copy(out=ident_sd[:], in_=identf), reads=[("cst",)], writes=[("identsd",)])
        ident_s = ident_sd[:]
    ones64 = sb("rw_ones64", [128, CH], F32)
    S.op("dve", lambda: nc.vector.memset(ones64[:], 1.0), writes=[("ones64",)])
    mixc = sb("rw_mixc", [128, 6, 8], F32)
    S.dma("sp", mixc[:], d["mix"].rearrange("i (c p) -> p i c", p=128), writes=[("mixc",)], allow_slow_non_contiguous=True)
    cols = sb("rw_cols", [128, 5, 4], F32)
    for i, nm in enumerate(["w0", "a0", "k_k", "k_a", "r_k"]):
        S.dma("sp", cols[:, i, :], d[nm].rearrange("(c p) -> p c", p=128), writes=[("cols", i)], allow_slow_non_contiguous=True)
    W3 = sb("rw_W3", [128, 3, 8, 512], BF16)
    for i, nm in enumerate(["w_r", "w_k", "w_v"]):
        S.dma("pool", W3[:, i, :, :], d[nm].rearrange("(c p) n -> p c n", p=128), writes=[("W3", i)])
    w1b = sb("rw_w1b", [128, 8, 64], BF16); a1b = sb("rw_a1b", [128, 8, 64], BF16); g1b = sb("rw_g1b", [128, 8, 160], BF16)
    S.dma("pool", w1b[:], d["w1"].rearrange("(c p) n -> p c n", p=128), writes=[("w1b",)])
    S.dma("pool", a1b[:], d["a1"].rearrange("(c p) n -> p c n", p=128), writes=[("a1b",)])
    S.dma("pool", g1b[:], d["g1"].rearrange("(c p) n -> p c n", p=128), writes=[("g1b",)])
    w2b = sb("rw_w2b", [64, 512], BF16); a2b = sb("rw_a2b", [64, 512], BF16)
    g2a = sb("rw_g2a", [128, 512], BF16); g2b = sb("rw_g2b", [128, 512], BF16)
    S.dma("pool", w2b[:], d["w2"], writes=[("w2b",)])
    S.dma("pool", a2b[:], d["a2"], writes=[("a2b",)])
    S.dma("pool", g2a[:], d["g2"][0:128, :], writes=[("g2a",)])
    S.op("dve", lambda: nc.vector.memset(g2b[:], 0.0), writes=[("g2b",)])
    S.dma("pool", g2b[0:32, :], d["g2"][128:160, :], writes=[("g2b",)])
    xt = [sb("rw_xt0", [128, 8, TS + 2], F32)] * 2
    xs = sb("rw_xs", [128, TS], F32)
    xx = sb("rw_xx", [128, TS], F32)
    mixt = [sb(f"rw_mixt{i}", [128, TS], F32) for i in range(3)]
    xm = sb("rw_xm", [128, 6, 8, TS], BF16)
    hw = sb("rw_hw", [64, TS], BF16); ha = sb("rw_ha", [64, TS], BF16)
    hga = sb("rw_hga", [128, TS], BF16); hgb = sb("rw_hgb", [128, TS], BF16)
    S.op("dve", lambda: nc.vector.memset(hgb[:], 0.0), writes=[("hgb",)])
    ft = lambda n: [sb(f"rw_{n}{i}", [128, TS], F32) for i in range(2)]
    def ft1(n):
        t = sb(f"rw_{n}", [128, TS], F32)
        return [t, t]
    rT = ft("rT"); kT = ft("kT"); vT = ft("vT"); gT = ft1("gT"); sg = ft("sg"); asg = ft("asg")
    kq = ft1("kq"); kq2 = ft1("kq2"); rn = ft1("rn"); kk = ft("kk"); t1 = ft1("t1"); kz = ft("kz"); prod = ft1("prod")
    cs = ft1("cs"); E1 = [[sb(f"rw_E1_{q}_{i}", [128, TS], F32) for i in range(4)] for q in range(2)]; E2 = ft1("E2"); E3 = ft1("E3"); dd = ft1("dd"); bb = ft1("bb")
    af = ft("af")
    BKf = [sb(f"rw_BKf{i}", [128, 2, TS], F32) for i in range(2)]
    Hat = [sb(f"rw_Hat{i}", [128, 2, TS], F32) for i in range(2)]
    RA = [[sb(f"rw_RA{q}_{i}", [128, 2, TS], SD) for i in range(4)] for q in range(2)]
    RAf1 = [[sb(f"rw_RAf{q}_{i}", [128, TS], F32) for i in range(4)] for q in range(2)]
    LBt = [[sb(f"rw_LB{q}_{i}", [128, 2, TS], SD) for i in range(4)] for q in range(2)]
    TM = [[[sb(f"rw_TM{q}_{c}_{p}", [128, 4, 128], SD) for p in range(NPR)] for c in range(4)] for q in range(2)]
    NU = 2 * 2 * NPR
    AM = [sb(f"rw_AM{u}", [128, 512], SD) for u in range(NU)]
    Mk = [[sb(f"rw_M{u}_{i}", [128, 128], SD) for i in range(2)] for u in range(NU)]
    Nk = [[sb(f"rw_N{u}_{i}", [128, 128], SD) for i in range(2)] for u in range(NU)]
    Xs = [sb(f"rw_Xs{u}", [128, 128], SD) for u in range(NU)]
    ATM = [sb(f"rw_ATM{u}", [128, 64], SD) for u in range(NU)]
    Ws = [sb(f"rw_Ws{u}", [128, 64], SD) for u in range(NU)]
    UV = [sb(f"rw_UV{u}", [128, 64], SD) for u in range(NU)]
    GT = [sb(f"rw_GT{c}", [128, NCK, 64], F32) for c in range(4)]
    Hf = [sb(f"rw_Hf{c}", [128, NCK, 64], F32) for c in range(4)]
    RH = [sb(f"rw_RH{c}", [128, TS], F32) for c in range(4)]
    YV = [[sb(f"rw_YV{c}_{p}", [128, 128], F32) for p in range(NPR)] for c in range(4)]
    ST = [sb(f"rw_ST{c}", [128, 64], F32) for c in range(4)]
    yt = [[sb(f"rw_yt{c}_{p}", [128, 128], F32) for p in range(NPR)] for c in range(4)]
    for c in range(4):
        S.op("dve", lambda: nc.vector.memset(ST[c][:], 0.0), writes=[("ST", c, 0), ("ST", c, 1)])

    xv = d["xT"].rearrange("(c p) t -> p c t", p=128)
    pr = [0]

    def bank():
        b = pr[0]; pr[0] = (pr[0] + 1) % 8
        return b

    def gen_abc(ti):
        par = ti % 2
        t0 = ti * TS
        xs_ = 0
        X = xt[xs_]
        lo = t0 - 1 if ti > 0 else t0
        hi = t0 + TS + 1 if ti < NTI - 1 else t0 + TS
        if ti == 0:
            S.op("pool", lambda: nc.gpsimd.memset(X[:, :, 0:1], 0.0), writes=[("xt", xs_)])
        if ti == NTI - 1:
            S.op("pool", lambda: nc.gpsimd.memset(X[:, :, TS + 1:TS + 2], 0.0), writes=[("xt", xs_)])
        S.dma("sp", X[:, :, (lo - t0 + 1):(hi - t0 + 1)], xv[:, :, lo:hi], writes=[("xt", xs_)])
        for c in range(8):
            S.op("pool", lambda: nc.gpsimd.tensor_tensor(out=xs[:], in0=X[:, c, 0:TS], in1=X[:, c, 2:TS + 2], op=ALU.add),
                 reads=[("xt", xs_)], writes=[("xs",)])
            S.op("dve", lambda: nc.vector.scalar_tensor_tensor(out=xx[:], in0=xs[:], scalar=0.5, in1=X[:, c, 1:TS + 1], op0=ALU.mult, op1=ALU.subtract),
                 reads=[("xs",), ("xt", xs_)], writes=[("xx",)])
            for i in range(3):
                S.op("dve", lambda: nc.vector.scalar_tensor_tensor(out=xm[:, i, c, :], in0=xx[:], scalar=mixc[:, i, c:c + 1], in1=X[:, c, 1:TS + 1], op0=ALU.mult, op1=ALU.add),
                     reads=[("xx",), ("xt", xs_), ("mixc",)], writes=[("xm", i, c)])
            for i in (3, 4, 5):
                mt = mixt[i - 3]
                S.op("act", lambda: nc.scalar.activation(out=mt[:], in_=xx[:], func=AF.Identity, scale=mixc[:, i, c:c + 1]),
                     reads=[("xx",), ("mixc",)], writes=[("mixt", i)])
                S.op("pool", lambda: nc.gpsimd.tensor_tensor(out=xm[:, i, c, :], in0=mt[:], in1=X[:, c, 1:TS + 1], op=ALU.add),
                     reads=[("mixt", i), ("xt", xs_)], writes=[("xm", i, c)])
            yield
        yield
        def proj(out_ap, lhs_fn, mi, keys, M=128):
            for c in range(8):
                S.op("pe", lambda: nc.tensor.matmul(out_ap, lhs_fn(c), xm[:, mi, c, :], start=(c == 0), stop=(c == 7)),
                     reads=keys + [("xm", mi, c)], writes=[("ps", bk)], signal=(c == 7))
        bk = bank()
        proj(ps[0:64, bk, 0:TS], lambda c: w1b[:, c, :], 1, [("w1b",)])
        S.op("act", lambda: nc.scalar.activation(out=hw[:], in_=ps[0:64, bk, 0:TS], func=AF.Tanh), reads=[("ps", bk)], writes=[("hw",)])
        bk = bank()
        proj(ps[0:64, bk, 0:TS], lambda c: a1b[:, c, :], 4, [("a1b",)])
        S.op("act", lambda: nc.scalar.copy(out=ha[:], in_=ps[0:64, bk, 0:TS]), reads=[("ps", bk)], writes=[("ha",)])
        bk = bank()
        proj(ps[:, bk, 0:TS], lambda c: g1b[:, c, 0:128], 5, [("g1b",)])
        S.op("act", lambda: nc.scalar.activation(out=hga[:], in_=ps[:, bk, 0:TS], func=AF.Sigmoid), reads=[("ps", bk)], writes=[("hga",)])
        bk = bank()
        proj(ps[0:32, bk, 0:TS], lambda c: g1b[:, c, 128:160], 5, [("g1b",)])
        S.op("act", lambda: nc.scalar.activation(out=hgb[0:32, :], in_=ps[0:32, bk, 0:TS], func=AF.Sigmoid), reads=[("ps", bk)], writes=[("hgb",)])
        for cc in range(4):
            f = cc % 2
            csl = slice(cc * 128, (cc + 1) * 128)
            tsl = slice(t0, t0 + TS)
            bk = bank(); proj(ps[:, bk, 0:TS], lambda c: W3[:, 0, c, csl], 0, [("W3", 0)])
            S.op("act", lambda: nc.scalar.copy(out=rT[f][:], in_=ps[:, bk, 0:TS]), reads=[("ps", bk)], writes=[("rT", f)])
            yield
            bk = bank(); proj(ps[:, bk, 0:TS], lambda c: W3[:, 1, c, csl], 2, [("W3", 1)])
            S.op("act", lambda: nc.scalar.copy(out=kT[f][:], in_=ps[:, bk, 0:TS]), reads=[("ps", bk)], writes=[("kT", f)])
            yield
            bk = bank(); proj(ps[:, bk, 0:TS], lambda c: W3[:, 2, c, csl], 3, [("W3", 2)])
            S.op("act", lambda: nc.scalar.copy(out=vT[f][:], in_=ps[:, bk, 0:TS]), reads=[("ps", bk)], writes=[("vT", f)])
            S.dma("sp", d["v_out"][csl, tsl], vT[f][:], reads=[("vT", f)])
            bk = bank()
            S.op("pe", lambda: nc.tensor.matmul(ps[:, bk, 0:TS], w2b[:, csl], hw[:], start=True, stop=True), reads=[("w2b",), ("hw",)], writes=[("ps", bk)])
            S.op("act", lambda: nc.scalar.activation(out=sg[f][:], in_=ps[:, bk, 0:TS], func=AF.Sigmoid, bias=cols[:, 0, cc:cc + 1], scale=1.0),
                 reads=[("ps", bk), ("cols", 0)], writes=[("sg", f)])
            bk = bank()
            S.op("pe", lambda: nc.tensor.matmul(ps[:, bk, 0:TS], a2b[:, csl], ha[:], start=True, stop=True), reads=[("a2b",), ("ha",)], writes=[("ps", bk)])
            S.op("act", lambda: nc.scalar.activation(out=asg[f][:], in_=ps[:, bk, 0:TS], func=AF.Sigmoid, bias=cols[:, 1, cc:cc + 1], scale=1.0),
                 reads=[("ps", bk), ("cols", 1)], writes=[("asg", f)])
            bk = bank()
            S.op("pe", lambda: nc.tensor.matmul(ps[:, bk, 0:TS], g2a[:, csl], hga[:], start=True, stop=False), reads=[("g2a",), ("hga",)], writes=[("ps", bk)], signal=False)
            S.op("pe", lambda: nc.tensor.matmul(ps[:, bk, 0:TS], g2b[:, csl], hgb[:], start=False, stop=True), reads=[("g2b",), ("hgb",)], writes=[("ps", bk)])
            S.op("act", lambda: nc.scalar.copy(out=gT[f][:], in_=ps[:, bk, 0:TS]), reads=[("ps", bk)], writes=[("gT", 0)])
            S.dma("sp", d["g_out"][csl, tsl], gT[f][:], reads=[("gT", 0)])
            yield
            S.op("dve", lambda: nc.vector.tensor_scalar(out=kq[f][:], in0=kT[f][:], scalar1=cols[:, 2, cc:cc + 1], scalar2=None, op0=ALU.mult),
                 reads=[("kT", f), ("cols", 2)], writes=[("kq", 0)])
            S.op("pool", lambda: nc.gpsimd.tensor_tensor(out=kq2[f][:], in0=kq[f][:], in1=kq[f][:], op=ALU.mult), reads=[("kq", 0)], writes=[("kq2", 0)])
            bk = bank()
            S.op("pe", lambda: nc.tensor.matmul(ps[:, bk, 0:TS], blkf, kq2[f][:], start=True, stop=True), reads=[("cst",), ("kq2", 0)], writes=[("ps", bk)])
            S.op("dve", lambda: nc.vector.tensor_scalar(out=rn[f][:], in0=ps[:, bk, 0:TS], scalar1=1e-24, scalar2=None, op0=ALU.max),
                 reads=[("ps", bk)], writes=[("rn", 0)])
            S.op("act", lambda: nc.scalar.activation(out=rn[f][:], in_=rn[f][:], func=AF.Ln), reads=[("rn", 0)], writes=[("rn", 0)])
            S.op("act", lambda: nc.scalar.activation(out=rn[f][:], in_=rn[f][:], func=AF.Exp, scale=-0.5), reads=[("rn", 0)], writes=[("rn", 0)])
            S.op("pool", lambda: nc.gpsimd.tensor_tensor(out=kk[f][:], in0=kq[f][:], in1=rn[f][:], op=ALU.mult), reads=[("kq", 0), ("rn", 0)], writes=[("kk", f)])
            S.op("dve", lambda: nc.vector.tensor_scalar(out=t1[f][:], in0=asg[f][:], scalar1=-1.0, scalar2=cols[:, 3, cc:cc + 1], op0=ALU.add, op1=ALU.mult),
                 reads=[("asg", f), ("cols", 3)], writes=[("t1", 0)])
            S.op("dve", lambda: nc.vector.scalar_tensor_tensor(out=kz[f][:], in0=t1[f][:], scalar=1.0, in1=kT[f][:], op0=ALU.add, op1=ALU.mult),
                 reads=[("t1", 0), ("kT", f)], writes=[("kz", f)])
            S.op("dve", lambda: nc.vector.scalar_tensor_tensor(out=prod[f][:], in0=rT[f][:], scalar=cols[:, 4, cc:cc + 1], in1=kz[f][:], op0=ALU.mult, op1=ALU.mult),
                 reads=[("rT", f), ("kz", f), ("cols", 4)], writes=[("prod", 0)])
            S.dma("sp", d["p_out"][csl, tsl], prod[f][:], reads=[("prod", 0)])
            yield
            for ck in range(NCK):
                ksl = slice(ck * CH, (ck + 1) * CH)
                S.op("dve", lambda: nc.vector.tensor_tensor_scan(out=cs[f][:, ksl], data0=ones64[:], data1=sg[f][:, ksl], initial=0.0, op0=ALU.mult, op1=ALU.add),
                     reads=[("sg", f), ("ones64",)], writes=[("cs", 0)])
            S.op("act", lambda: nc.scalar.activation(out=E1[par][cc][:], in_=cs[f][:], func=AF.Exp, scale=-C0), reads=[("cs", 0)], writes=[("E1", par, cc)])
            S.op("act", lambda: nc.scalar.activation(out=E2[f][:], in_=cs[f][:], func=AF.Exp, scale=C0), reads=[("cs", 0)], writes=[("E2", 0)])
            S.op("pool", lambda: nc.gpsimd.tensor_tensor(out=dd[f][:], in0=cs[f][:], in1=sg[f][:], op=ALU.subtract), reads=[("cs", 0), ("sg", f)], writes=[("dd", 0)])
            S.op("act", lambda: nc.scalar.activation(out=E3[f][:], in_=dd[f][:], func=AF.Exp, scale=-C0), reads=[("dd", 0)], writes=[("E3", 0)])
            yield
            S.op("dve", lambda: nc.vector.scalar_tensor_tensor(out=af[f][:], in0=kk[f][:], scalar=-1.0, in1=E3[f][:], op0=ALU.mult, op1=ALU.mult),
                 reads=[("kk", f), ("E3", 0)], writes=[("af", f)])
            S.op("act", lambda: nc.scalar.copy(out=RA[par][cc][:, 0, :], in_=af[f][:]), reads=[("af", f)], writes=[("RA", par, cc, 0)])
            S.op("pool", lambda: nc.gpsimd.tensor_tensor(out=RAf1[par][cc][:], in0=rT[f][:], in1=E1[par][cc][:], op=ALU.mult), reads=[("rT", f), ("E1", par, cc)], writes=[("RAf1", par, cc)])
            S.op("act", lambda: nc.scalar.copy(out=RA[par][cc][:, 1, :], in_=RAf1[par][cc][:]), reads=[("RAf1", par, cc)], writes=[("RA", par, cc, 1)])
            S.op("pool", lambda: nc.gpsimd.tensor_tensor(out=bb[f][:], in0=kk[f][:], in1=asg[f][:], op=ALU.mult), reads=[("kk", f), ("asg", f)], writes=[("bb", 0)])
            S.op("pool", lambda: nc.gpsimd.tensor_tensor(out=BKf[f][:, 0, :], in0=bb[f][:], in1=E2[f][:], op=ALU.mult), reads=[("bb", 0), ("E2", 0)], writes=[("BKf", f, 0)])
            S.op("pool", lambda: nc.gpsimd.tensor_tensor(out=BKf[f][:, 1, :], in0=kz[f][:], in1=E2[f][:], op=ALU.mult), reads=[("kz", f), ("E2", 0)], writes=[("BKf", f, 1)])
            S.op("act", lambda: nc.scalar.copy(out=LBt[par][cc][:], in_=BKf[f][:]), reads=[("BKf", f, 0), ("BKf", f, 1)], writes=[("LBt", par, cc)])
            for ck in range(NCK):
                ksl = slice(ck * CH, (ck + 1) * CH)
                e = ck * CH + CH - 1
                S.op("dve", lambda: nc.vector.tensor_scalar(out=Hat[f][:, :, ksl], in0=BKf[f][:, :, ksl], scalar1=E1[par][cc][:, e:e + 1], scalar2=None, op0=ALU.mult),
                     reads=[("BKf", f, 0), ("BKf", f, 1), ("E1", par, cc)], writes=[("Hat", f)])
            yield
            for p in range(NPR):
                psl = slice(p * 128, (p + 1) * 128)
                bk = bank()
                srcs = [(af[f][:, psl], ("af", f)), (Hat[f][:, 0, psl], ("Hat", f)), (Hat[f][:, 1, psl], ("Hat", f)), (vT[f][:, psl], ("vT", f))]
                for i, (src, key) in enumerate(srcs):
                    S.op("pe", lambda: nc.tensor.transpose(ps[:, bk, i * 128:(i + 1) * 128], src, identf), reads=[key, ("cst",)], writes=[("ps", bk)], signal=(i == 3))
                S.op("act", lambda: nc.scalar.copy(out=TM[par][cc][p][:].rearrange("p a n -> p (a n)"), in_=ps[:, bk, :]), reads=[("ps", bk)], writes=[("TM", par, cc, p)])
        yield

    def gen_de(ti):
        par = ti % 2
        t0 = ti * TS
        for ccg in range(2):
            units = [(cc, hd, p) for cc in (2 * ccg, 2 * ccg + 1) for hd in range(2) for p in range(NPR)]
            def uid(cc, hd, p):
                return ((cc % 2) * 2 + hd) * NPR + p
            for (cc, hd, p) in units:
                u = uid(cc, hd, p)
                hs = slice(hd * 64, hd * 64 + 64); tk = slice(p * 128, (p + 1) * 128)
                bk = bank()
                S.op("pe", lambda: nc.tensor.matmul(ps[:, bk, 0:256], LBt[par][cc][hs, 0, tk], RA[par][cc][hs, :, tk], start=True, stop=True),
                     reads=[("LBt", par, cc), ("RA", par, cc, 0), ("RA", par, cc, 1)], writes=[("ps", bk)], signal=False)
                S.op("pe", lambda: nc.tensor.matmul(ps[:, bk, 256:512], LBt[par][cc][hs, 1, tk], RA[par][cc][hs, :, tk], start=True, stop=True),
                     reads=[("LBt", par, cc), ("RA", par, cc, 0), ("RA", par, cc, 1)], writes=[("ps", bk)])
                S.op("dve", lambda: nc.vector.tensor_tensor(out=AM[u][:], in0=ps[:, bk, :], in1=maskSI, op=ALU.mult), reads=[("ps", bk), ("cst",)], writes=[("AM", u)])
                bk = bank()
                S.op("pe", lambda: nc.tensor.matmul(ps[:, bk, 0:128], RA[par][cc][hs, 0, tk], LBt[par][cc][hs, 0, tk], start=True, stop=True),
                     reads=[("LBt", par, cc), ("RA", par, cc, 0)], writes=[("ps", bk)])
                S.op("dve", lambda: nc.vector.tensor_tensor(out=Nk[u][0][:], in0=ps[:, bk, 0:128], in1=maskTS, op=ALU.mult), reads=[("ps", bk), ("cst",)], writes=[("Nk", u, 0)])
                S.op("pool", lambda: nc.gpsimd.tensor_tensor(out=Xs[u][:], in0=AM[u][:, 0:128], in1=identf, op=ALU.add), reads=[("AM", u), ("cst",)], writes=[("Xs", u)])
                yield
            for k in range(1, 6):
                for (cc, hd, p) in units:
                    u = uid(cc, hd, p)
                    Mprev = AM[u][:, 0:128] if k == 1 else Mk[u][(k - 1) % 2][:]
                    Mkey = ("AM", u) if k == 1 else ("Mk", u, (k - 1) % 2)
                    Nprev = Nk[u][(k - 1) % 2][:]
                    Nkey = ("Nk", u, (k - 1) % 2)
                    bk = bank()
                    if k <= 4:
                        S.op("pe", lambda: nc.tensor.matmul(ps[:, bk, 0:128], Nprev, Mprev, start=True, stop=True), reads=[Mkey, Nkey], writes=[("ps", bk)], signal=False)
                    S.op("pe", lambda: nc.tensor.matmul(ps[:, bk, 128:256], Mprev, Nprev, start=True, stop=True), reads=[Mkey, Nkey], writes=[("ps", bk)])
                    if k <= 4:
                        S.op("act", lambda: nc.scalar.copy(out=Mk[u][k % 2][:], in_=ps[:, bk, 0:128]), reads=[("ps", bk)], writes=[("Mk", u, k % 2)])
                    S.op("act", lambda: nc.scalar.copy(out=Nk[u][k % 2][:], in_=ps[:, bk, 128:256]), reads=[("ps", bk)], writes=[("Nk", u, k % 2)])
                    yield
                for (cc, hd, p) in units:
                    u = uid(cc, hd, p)
                    bk = bank()
                    Xkey = ("Xs", u)
                    S.op("pe", lambda: nc.tensor.matmul(ps[:, bk, 0:128], Nk[u][k % 2][:], Xs[u][:], start=True, stop=True), reads=[("Nk", u, k % 2), Xkey], writes=[("ps", bk)])
                    S.op("dve", lambda: nc.vector.tensor_tensor(out=Xs[u][:], in0=ps[:, bk, 0:128], in1=Xs[u][:], op=ALU.add), reads=[("ps", bk), ("Xs", u)], writes=[("Xs", u)])
                    yield
            Xkeyf = lambda u: ("Xs", u)
            for (cc, hd, p) in units:
                u = uid(cc, hd, p)
                hs = slice(hd * 64, hd * 64 + 64)
                bk = bank()
                S.op("pe", lambda: nc.tensor.matmul(ps[:, bk, 0:64], Xs[u][:], TM[par][cc][p][:, 0, hs], start=True, stop=True), reads=[Xkeyf(u), ("TM", par, cc, p)], writes=[("ps", bk)], signal=False)
                S.op("pe", lambda: nc.tensor.matmul(ps[:, bk, 64:128], AM[u][:, 256:384], TM[par][cc][p][:, 3, hs], start=True, stop=True), reads=[("AM", u), ("TM", par, cc, p)], writes=[("ps", bk)])
                S.op("act", lambda: nc.scalar.copy(out=ATM[u][:], in_=ps[:, bk, 0:64]), reads=[("ps", bk)], writes=[("ATM", u)])
                S.op("act", lambda: nc.scalar.copy(out=Ws[u][:], in_=ps[:, bk, 64:128]), reads=[("ps", bk)], writes=[("Ws", u)])
                yield
            for (cc, hd, p) in units:
                u = uid(cc, hd, p)
                hs = slice(hd * 64, hd * 64 + 64); tk = slice(p * 128, (p + 1) * 128)
                bk = bank()
                S.op("pe", lambda: nc.tensor.matmul(ps[:, bk, 0:64], Xs[u][:], Ws[u][:], start=True, stop=True), reads=[Xkeyf(u), ("Ws", u)], writes=[("ps", bk)])
                S.op("act", lambda: nc.scalar.copy(out=UV[u][:], in_=ps[:, bk, 0:64]), reads=[("ps", bk)], writes=[("UV", u)])
                bk2 = bank()
                S.op("pe", lambda: nc.tensor.matmul(ps[hs, bk2, 0:128], ATM[u][:], AM[u][:, 128:256], start=True, stop=True), reads=[("ATM", u), ("AM", u)], writes=[("ps", bk2)])
                S.op("dve", lambda: nc.vector.tensor_tensor(out=RH[cc][hs, tk], in0=ps[hs, bk2, 0:128], in1=RAf1[par][cc][hs, tk], op=ALU.add),
                     reads=[("ps", bk2), ("RAf1", par, cc)], writes=[("RH", cc, hd)])
                yield
            for (cc, hd, p) in units:
                u = uid(cc, hd, p)
                hs = slice(hd * 64, hd * 64 + 64)
                bk = bank()
                S.op("pe", lambda: nc.tensor.matmul(ps[:, bk, 0:64], AM[u][:, 128:256], UV[u][:], start=True, stop=False), reads=[("AM", u), ("UV", u)], writes=[("ps", bk)], signal=False)
                S.op("pe", lambda: nc.tensor.matmul(ps[:, bk, 0:64], AM[u][:, 384:512], TM[par][cc][p][:, 3, hs], start=False, stop=True), reads=[("AM", u), ("TM", par, cc, p)], writes=[("ps", bk)])
                S.op("act", lambda: nc.scalar.copy(out=YV[cc][p][:, hs], in_=ps[:, bk, 0:64]), reads=[("ps", bk)], writes=[("YV", cc, p, hd)])
                for q in range(2):
                    pb = slice(q * 64, q * 64 + 64)
                    ck = p * 2 + q
                    e = ck * CH + CH - 1
                    bk = bank()
                    S.op("pe", lambda: nc.tensor.matmul(ps[hs, bk, 0:64], ATM[u][pb, :], TM[par][cc][p][pb, 1, hs], start=True, stop=True), reads=[("ATM", u), ("TM", par, cc, p)], writes=[("ps", bk)])
                    S.op("dve", lambda: nc.vector.scalar_tensor_tensor(out=GT[cc][hs, ck, :], in0=identf[hs, hs], scalar=E1[par][cc][hs, e:e + 1], in1=ps[hs, bk, 0:64], op0=ALU.mult, op1=ALU.add),
                         reads=[("ps", bk), ("cst",), ("E1", par, cc)], writes=[("GT", cc, hd)])
                    bk = bank()
                    S.op("pe", lambda: nc.tensor.matmul(ps[hs, bk, 0:64], TM[par][cc][p][pb, 1, hs], UV[u][pb, :], start=True, stop=False), reads=[("TM", par, cc, p), ("UV", u)], writes=[("ps", bk)], signal=False)
                    S.op("pe", lambda: nc.tensor.matmul(ps[hs, bk, 0:64], TM[par][cc][p][pb, 2, hs], TM[par][cc][p][pb, 3, hs], start=False, stop=True), reads=[("TM", par, cc, p)], writes=[("ps", bk)])
                    S.op("act", lambda: nc.scalar.copy(out=Hf[cc][hs, ck, :], in_=ps[hs, bk, 0:64]), reads=[("ps", bk)], writes=[("Hf", cc, hd)])
                    yield
        for ck in range(NCK):
            p = ck // 2; q = ck % 2
            pb = slice(q * 64, q * 64 + 64)
            for cc in range(4):
                for hd in range(2):
                    hs = slice(hd * 64, hd * 64 + 64)
                    bkY = bank(); bkS = bank()
                    S.op("pe", lambda: nc.tensor.matmul(ps[pb, bkY, 0:64], RH[cc][hs, ck * CH:(ck + 1) * CH], ST[cc][hs, :], start=True, stop=True),
                         reads=[("RH", cc, hd), ("ST", cc, hd)], writes=[("ps", bkY)])
                    S.op("pe", lambda: nc.tensor.matmul(ps[hs, bkS, 0:64], GT[cc][hs, ck, :], ST[cc][hs, :], start=True, stop=True),
                         reads=[("GT", cc, hd), ("ST", cc, hd)], writes=[("ps", bkS)])
                    S.op("dve", lambda: nc.vector.tensor_tensor(out=ST[cc][hs, :], in0=ps[hs, bkS, 0:64], in1=Hf[cc][hs, ck, :], op=ALU.add),
                         reads=[("ps", bkS), ("Hf", cc, hd)], writes=[("ST", cc, hd)])
                    S.op("dve", lambda: nc.vector.tensor_tensor(out=yt[cc][p][pb, hs], in0=ps[pb, bkY, 0:64], in1=YV[cc][p][pb, hs], op=ALU.add),
                         reads=[("ps", bkY), ("YV", cc, p, hd)], writes=[("yt", cc, p)])
                yield
                if q == 1:
                    S.dma("sp", d["y_out"][t0 + p * 128:t0 + (p + 1) * 128, cc * 128:(cc + 1) * 128], yt[cc][p][:], reads=[("yt", cc, p)])
        yield

    def drain(g):
        n = 0
        for _ in g:
            n += 1
        return n

    n_abc = drain(gen_abc(0))
    n_de = None
    for ti in range(NTI):
        ga = gen_abc(ti + 1) if ti + 1 < NTI else None
        gd = gen_de(ti)
        ca = 0; cd = 0
        while ga is not None or gd is not None:
            fa = ca / n_abc if ga is not None else 2.0
            fd = cd / n_de if (gd is not None and n_de) else (ca / n_abc if gd is not None else 2.0)
            if gd is not None and (ga is None or fd <= fa):
                try:
                    next(gd); cd += 1
                except StopIteration:
                    gd = None
                    if n_de is None:
                        n_de = max(cd, 1)
            else:
                try:
                    next(ga); ca += 1
                except StopIteration:
                    ga = None
        if n_de is None:
            n_de = max(cd, 1)
    S.finish("sp")


def build_rw1_prog(SL, TS=256, SD=BF16):
    nc = bass.Bass("TRN2", target_bir_lowering=False)
    dt = lambda n, s: nc.dram_tensor(n, s, F32, kind="ExternalInput").ap()
    d = {"xT": dt("xT", [D, SL]), "mix": dt("mix", [6, D]), "consts": dt("consts", [128, 896])}
    for nm in ("w_r", "w_k", "w_v"):
        d[nm] = dt(nm, [D, 512])
    d["w1"] = dt("w1", [D, 64]); d["w2"] = dt("w2", [64, 512]); d["w0"] = dt("w0", [512])
    d["a1"] = dt("a1", [D, 64]); d["a2"] = dt("a2", [64, 512]); d["a0"] = dt("a0", [512])
    d["g1"] = dt("g1", [D, 160]); d["g2"] = dt("g2", [160, 512])
    for nm in ("k_k", "k_a", "r_k"):
        d[nm] = dt(nm, [512])
    do = lambda n, s: nc.dram_tensor(n, s, F32, kind="ExternalOutput").ap()
    d["y_out"] = do("y_out", [SL, 512]); d["p_out"] = do("p_out", [512, SL]); d["v_out"] = do("v_out", [512, SL]); d["g_out"] = do("g_out", [512, SL])
    with ExitStack() as es:
        S = Sched(nc, es)
        ps = es.enter_context(nc.psum_tensor("ps", [128, 8, 512], F32))
        emit_rw1(S, nc, es, ps, d, SL, TS, SD)
    return nc


GN_EPS = 64e-5


def emit_rw2(S, nc, es, ps, d, T, xT, xb, LB):
    NT = T // 512
    tsl = lambda tt: slice(tt * 512, (tt + 1) * 512)
    sb = lambda es_, n, s, dt: es_.enter_context(nc.sbuf_tensor(n, s, dt))
    with ExitStack() as es2:
        cst = sb(es2, "r2_cst", [128, 896], F32)
        S.dma("sp", cst[:], d["consts"], writes=[("cst2",)])
        blkf = cst[:, 768:896]
        wout = sb(es2, "r2_wout", [128, 8, 1024], BF16)
        S.dma("pool", wout[:], d["w_out"].rearrange("(c p) n -> p c n", p=128), writes=[("r2wout",)])
        gcol = sb(es2, "r2_gcol", [128, 8], F32); bcol = sb(es2, "r2_bcol", [128, 8], F32); epsg = sb(es2, "r2_eps", [128, 1], F32)
        S.dma("sp", gcol[:], d["lnx_g"].rearrange("(c p) -> p c", p=128), writes=[("r2g",)], allow_slow_non_contiguous=True)
        S.dma("sp", bcol[:], d["lnx_b"].rearrange("(c p) -> p c", p=128), writes=[("r2b",)], allow_slow_non_contiguous=True)
        S.op("dve", lambda: nc.vector.memset(epsg[:], GN_EPS), writes=[("r2eps",)])
        names = ["yf", "yb", "pf", "pb", "v", "g"]
        tin = {n: [sb(es2, f"r2_{n}{i}", [128, 512], F32) for i in range(2)] for n in names}
        tmp = {n: [sb(es2, f"r2_t{n}{i}", [128, 512], F32) for i in range(2)] for n in ["y", "ysq", "mean", "a", "b", "pp"]}
        dv = {n: d[n].rearrange("(c p) t -> p c t", p=128) for n in names + ["x"]}
        it = 0
        for tt in range(NT):
            S.dma("sp", xT[:, :, tsl(tt)], dv["x"][:, :, tsl(tt)], writes=[("xT", c, tt) for c in range(8)])
            for c in range(8):
                i = it % 2; it += 1
                for n in names:
                    S.dma("sp", tin[n][i][:], dv[n][:, c, tsl(tt)], writes=[("r2in", n, i)])
                Y = tmp["y"][i]; YS = tmp["ysq"][i]; MN = tmp["mean"][i]; A = tmp["a"][i]; Bt = tmp["b"][i]; PP = tmp["pp"][i]
                S.op("dve", lambda: nc.vector.tensor_tensor(out=Y[:], in0=tin["yf"][i][:], in1=tin["yb"][i][:], op=ALU.add),
                     reads=[("r2in", "yf", i), ("r2in", "yb", i)], writes=[("r2y", i)])
                S.op("act", lambda: nc.scalar.activation(out=YS[:], in_=Y[:], func=AF.Square), reads=[("r2y", i)], writes=[("r2ysq", i)])
                S.op("pool", lambda: nc.gpsimd.tensor_tensor(out=PP[:], in0=tin["pf"][i][:], in1=tin["pb"][i][:], op=ALU.add),
                     reads=[("r2in", "pf", i), ("r2in", "pb", i)], writes=[("r2pp", i)])
                b1 = 0 + 3 * (it % 2); b2 = b1 + 1; b3 = b1 + 2
                S.op("pe", lambda: nc.tensor.matmul(ps[:, b1, :], blkf, Y[:], start=True, stop=True), reads=[("cst2",), ("r2y", i)], writes=[("ps", b1)])
                S.op("pe", lambda: nc.tensor.matmul(ps[:, b2, :], blkf, YS[:], start=True, stop=True), reads=[("cst2",), ("r2ysq", i)], writes=[("ps", b2)])
                S.op("pe", lambda: nc.tensor.matmul(ps[:, b3, :], blkf, PP[:], start=True, stop=True), reads=[("cst2",), ("r2pp", i)], writes=[("ps", b3)])
                S.op("act", lambda: nc.scalar.activation(out=MN[:], in_=ps[:, b1, :], func=AF.Identity, scale=1.0 / 64), reads=[("ps", b1)], writes=[("r2mean", i)])
                S.op("dve", lambda: nc.vector.tensor_tensor(out=A[:], in0=ps[:, b1, :], in1=MN[:], op=ALU.mult), reads=[("ps", b1), ("r2mean", i)], writes=[("r2a", i)])
                S.op("dve", lambda: nc.vector.tensor_tensor(out=A[:], in0=ps[:, b2, :], in1=A[:], op=ALU.subtract), reads=[("ps", b2), ("r2a", i)], writes=[("r2a", i)])
                S.op("act", lambda: nc.scalar.activation(out=A[:], in_=A[:], func=AF.Ln, bias=epsg[:], scale=1.0 / 64), reads=[("r2a", i), ("r2eps",)], writes=[("r2a", i)])
                S.op("act", lambda: nc.scalar.activation(out=A[:], in_=A[:], func=AF.Exp, scale=-0.5), reads=[("r2a", i)], writes=[("r2a", i)])
                S.op("pool", lambda: nc.gpsimd.tensor_tensor(out=Bt[:], in0=Y[:], in1=MN[:], op=ALU.subtract), reads=[("r2y", i), ("r2mean", i)], writes=[("r2b_", i)])
                S.op("pool", lambda: nc.gpsimd.tensor_tensor(out=Bt[:], in0=Bt[:], in1=A[:], op=ALU.mult), reads=[("r2b_", i), ("r2a", i)], writes=[("r2b_", i)])
                S.op("act", lambda: nc.scalar.activation(out=Bt[:], in_=Bt[:], func=AF.Identity, bias=bcol[:, c:c + 1], scale=gcol[:, c:c + 1]),
                     reads=[("r2b_", i), ("r2g",), ("r2b",)], writes=[("r2b_", i)])
                S.op("dve", lambda: nc.vector.tensor_tensor(out=YS[:], in0=ps[:, b3, :], in1=tin["v"][i][:], op=ALU.mult), reads=[("ps", b3), ("r2in", "v", i)], writes=[("r2ysq", i)])
                S.op("pool", lambda: nc.gpsimd.tensor_tensor(out=Bt[:], in0=Bt[:], in1=YS[:], op=ALU.add), reads=[("r2b_", i), ("r2ysq", i)], writes=[("r2b_", i)])
                S.op("dve", lambda: nc.vector.tensor_tensor(out=xb[:, c, tsl(tt)], in0=Bt[:], in1=tin["g"][i][:], op=ALU.mult), reads=[("r2b_", i), ("r2in", "g", i)], writes=[("xb", c, tt)])
            for dc in range(8):
                bank = 6 + (dc % 2)
                for c in range(8):
                    S.op("pe", lambda: nc.tensor.matmul(ps[:, bank, :], wout[:, c, dc * 128:(dc + 1) * 128], xb[:, c, tsl(tt)], start=(c == 0), stop=(c == 7)),
                         reads=[("r2wout",), ("xb", c, tt)], writes=[("ps", bank)], signal=(c == 7))
                S.op("pool", lambda: nc.gpsimd.tensor_scalar(out=xT[:, dc, tsl(tt)], in0=xT[:, dc, tsl(tt)], scalar1=float(ALPHA), scalar2=0.0, op0=ALU.mult, op1=ALU.add),
                     reads=[("xT", dc, tt)], writes=[("xT", dc, tt)])
                S.op("dve", lambda: nc.vector.tensor_tensor(out=xT[:, dc, tsl(tt)], in0=ps[:, bank, :], in1=xT[:, dc, tsl(tt)], op=ALU.add),
                     reads=[("ps", bank), ("xT", dc, tt)], writes=[("xT", dc, tt)])
    S.fence()


def build_rw2_prog(T):
    nc = bass.Bass("TRN2", target_bir_lowering=False)
    dt = lambda n, s: nc.dram_tensor(n, s, F32, kind="ExternalInput").ap()
    d = {n: dt(n, [D, T]) for n in ["x", "yf", "yb", "pf", "pb", "v", "g"]}
    d["consts"] = dt("consts", [128, 896])
    d["lnx_g"] = dt("lnx_g", [D]); d["lnx_b"] = dt("lnx_b", [D]); d["w_out"] = dt("w_out", [D, D])
    lng = dt("lng", [2, D]); lnb = dt("lnb", [2, D])
    w_in = dt("w_in", [D, 2 * FF]); w_outf = dt("w_outf", [FF, D])
    yT_d = nc.dram_tensor("yT", [D, T], F32, kind="ExternalOutput").ap()
    with ExitStack() as es:
        S = Sched(nc, es)
        ps = es.enter_context(nc.psum_tensor("ps", [128, 8, 512], F32))
        xT = es.enter_context(nc.sbuf_tensor("xTs", [128, 8, T], F32))
        xb = es.enter_context(nc.sbuf_tensor("xbs", [128, 8, T], BF16))
        B = FfnBufs(nc, es, T, with_ffn=False)
        B.init_consts(S, nc)
        emit_rw2(S, nc, es, ps, d, T, xT, xb, B)
        NT = T // 512
        emit_ln(S, nc, B, xT, xb, ps, lng[0], lnb[0], NT)
        B.alloc_ffn(nc, es)
        emit_ffn_ln(S, nc, B, xT, xb, ps, w_in, w_outf, lng[1], lnb[1], "f")
        yv = yT_d.rearrange("(c p) t -> p c t", p=128)
        for tt in range(NT):
            S.dma("sp", yv[:, :, tt * 512:(tt + 1) * 512], xT[:, :, tt * 512:(tt + 1) * 512], reads=[("xT", c, tt) for c in range(8)])
        S.finish("sp")
    return nc


_PROGS = {}


def _prog(key, fn):
    if key not in _PROGS:
        _PROGS[key] = fn()
    return _PROGS[key]


def _run(nc, in_maps):
    res = run_bass_kernel_spmd(nc, in_maps, core_ids=list(range(8)))
    return res.results


def _tok(c):
    b = c // 4
    t0 = (c % 4) * 2048
    return b, t0


def _tr(a):
    return np.ascontiguousarray(a.T)


def kernel(x, ffn_w_in, ffn_w_out, ln_g, ln_b, na_w_qkv, na_b_qkv, na_rpb, na_w_o, na_b_o,
           rw_mix, rw_w_rkv, rw_w0, rw_w1, rw_w2, rw_a0, rw_a1, rw_a2, rw_g1, rw_g2,
           rw_k_k, rw_k_a, rw_r_k, rw_lnx_g, rw_lnx_b, rw_w_out):
    f32 = np.float32
    A = lambda a: np.ascontiguousarray(np.asarray(a, dtype=f32))
    x = A(x)
    Bsz, SEQ, Dm = x.shape
    T = 2048
    nc1 = _prog("ffn1", lambda: build_ffn_prog(T, 1))
    ins = []
    for c in range(8):
        b, t0 = _tok(c)
        ins.append({"xT": _tr(x[b, t0:t0 + T]), "w_in": A(ffn_w_in[0, 0:1]), "w_out": A(ffn_w_out[0, 0:1]),
                    "lng": A(ln_g[0, 0:1]), "lnb": A(ln_b[0, 0:1])})
    r = _run(nc1, ins)
    x1 = np.empty_like(x)
    for c in range(8):
        b, t0 = _tok(c)
        x1[b, t0:t0 + T] = r[c]["yT"].T
    nc2 = _prog("na_ffn2", lambda: build_na_ffn_prog(32, 2))
    w_in2 = A(np.stack([ffn_w_in[0, 1], ffn_w_in[1, 0]])); w_out2 = A(np.stack([ffn_w_out[0, 1], ffn_w_out[1, 0]]))
    lng2 = A(np.stack([ln_g[0, 2], ln_g[1, 0]])); lnb2 = A(np.stack([ln_b[0, 2], ln_b[1, 0]]))
    ins = []
    for c in range(8):
        b, t0 = _tok(c)
        r0 = (c % 4) * 32
        G = na_halo_rows(r0, 32, 128)
        xg = x1[b].reshape(128, 64, Dm)
        ins.append({"xh": _tr(xg[G].reshape(-1, Dm)), "xo": _tr(x1[b, t0:t0 + T]), "w_qkv": A(na_w_qkv[0]), "b_qkv": A(na_b_qkv[0]),
                    "tab": na_tables(A(na_rpb[0]), r0, 32, 128), "w_o": A(na_w_o[0]), "b_o": A(na_b_o[0]),
                    "lng": A(ln_g[0, 1]), "lnb": A(ln_b[0, 1]),
                    "w_in": w_in2, "w_out": w_out2, "flng": lng2, "flnb": lnb2})
    r = _run(nc2, ins)
    x4 = np.empty_like(x)
    for c in range(8):
        b, t0 = _tok(c)
        x4[b, t0:t0 + T] = r[c]["yT"].T
    nc4 = _prog("rw1", lambda: build_rw1_prog(SEQ, 256, BF16))
    consts = rw_consts()
    ins = []
    for c in range(8):
        b = c // 4; z = (c // 2) % 2; hg = c % 2
        cs_ = slice(hg * 512, (hg + 1) * 512)
        xs_ = x4[b][::-1] if z == 1 else x4[b]
        ins.append({"xT": _tr(xs_), "mix": A(rw_mix[0]), "consts": consts,
                    "w_r": A(rw_w_rkv[0, 0][:, cs_]), "w_k": A(rw_w_rkv[0, 1][:, cs_]), "w_v": A(rw_w_rkv[0, 2][:, cs_]),
                    "w1": A(rw_w1[0, z]), "w2": A(rw_w2[0, z][:, cs_]), "w0": A(rw_w0[0, z][cs_]),
                    "a1": A(rw_a1[0, z]), "a2": A(rw_a2[0, z][:, cs_]), "a0": A(rw_a0[0, z][cs_]),
                    "g1": A(rw_g1[0]), "g2": A(rw_g2[0][:, cs_]),
                    "k_k": A(rw_k_k[0][cs_]), "k_a": A(rw_k_a[0][cs_]), "r_k": A(np.reshape(rw_r_k[0], (-1,))[cs_])})
    r = _run(nc4, ins)
    yT = np.empty((2, Bsz, Dm, SEQ), f32); pT = np.empty((2, Bsz, Dm, SEQ), f32)
    vT = np.empty((Bsz, Dm, SEQ), f32); gT = np.empty((Bsz, Dm, SEQ), f32)
    for c in range(8):
        b = c // 4; z = (c // 2) % 2; hg = c % 2
        cs_ = slice(hg * 512, (hg + 1) * 512)
        yo = r[c]["y_out"]; po = r[c]["p_out"]
        if z == 1:
            yT[z, b, cs_] = yo[::-1].T
            pT[z, b, cs_] = po[:, ::-1]
        else:
            yT[z, b, cs_] = yo.T
            pT[z, b, cs_] = po
            vT[b, cs_] = r[c]["v_out"]; gT[b, cs_] = r[c]["g_out"]
    nc5 = _prog("rw2", lambda: build_rw2_prog(T))
    ins = []
    lng5 = A(np.stack([ln_g[1, 1], ln_g[1, 2]])); lnb5 = A(np.stack([ln_b[1, 1], ln_b[1, 2]]))
    for c in range(8):
        b, t0 = _tok(c)
        ts_ = slice(t0, t0 + T)
        cp = lambda a: np.ascontiguousarray(a[:, ts_])
        ins.append({"x": _tr(x4[b, ts_]), "yf": cp(yT[0, b]), "yb": cp(yT[1, b]), "pf": cp(pT[0, b]), "pb": cp(pT[1, b]),
                    "v": cp(vT[b]), "g": cp(gT[b]), "consts": consts, "lnx_g": A(rw_lnx_g[0]), "lnx_b": A(rw_lnx_b[0]),
                    "w_out": A(rw_w_out[0]), "lng": lng5, "lnb": lnb5, "w_in": A(ffn_w_in[1, 1]), "w_outf": A(ffn_w_out[1, 1])})
    r = _run(nc5, ins)
    out = np.empty_like(x)
    for c in range(8):
        b, t0 = _tok(c)
        out[b, t0:t0 + T] = r[c]["yT"].T
    return out
```
